# Optimizing a Trainium2 kernel written in Bass

```python
import math
import jax, jax.numpy as jnp
from jax import lax
import numpy as np

D_MODEL = 2048
BATCH = 1
SEQ = 8192
DEPTH = 2

ATTN_HEADS = 8
ATTN_HEAD_DIM = 128
ATTN_WIDTH = ATTN_HEADS * ATTN_HEAD_DIM
MOBA_BLOCK = 256
MOBA_TOPK = 3
Q_BLOCK = 128
REL_BUCKETS = 32
REL_MAX_DIST = 128
SSD_HEADS = 32
SSD_HEAD_DIM = 64
SSD_WIDTH = SSD_HEADS * SSD_HEAD_DIM
SSD_GROUPS = 4
SSD_STATE = 128
SSD_CONV = 4
SSD_CHUNK = 256
SSD_XBC = SSD_WIDTH + 2 * SSD_GROUPS * SSD_STATE
CONV_WIDTH = 1024
CONV_K = 3
FFN_HIDDEN = 5632
FFN_RESIDUAL = 0.5
N_SUBLAYERS = 3
SEQ_ALIGN = math.lcm(MOBA_BLOCK, SSD_CHUNK)
MIX_SIZES = (ATTN_WIDTH, ATTN_WIDTH, ATTN_WIDTH,
             SSD_WIDTH, SSD_XBC, SSD_HEADS,
             CONV_WIDTH, CONV_WIDTH, CONV_WIDTH,
             D_MODEL, D_MODEL, D_MODEL)
MIX_IN = sum(MIX_SIZES)
NORM_EPS = 1e-6
NEG_INF = -1e30

kernel_name = "hybrid_moba_ssd_shortconv_block"


def rms_norm(x, gain):
    xf = x.astype(jnp.float32)
    y = xf * lax.rsqrt(jnp.mean(xf * xf, axis=-1, keepdims=True) + NORM_EPS)
    return (y * gain.astype(jnp.float32)).astype(x.dtype)


def causal_depthwise_conv(x, w):
    width = w.shape[0]
    return lax.conv_general_dilated(
        x, w[:, None, :].astype(x.dtype), window_strides=(1,), padding=[(width - 1, 0)],
        dimension_numbers=('NWC', 'WIO', 'NWC'), feature_group_count=x.shape[-1])


def swiglu_ffn(h, w_in, w_out):
    gate, up = jnp.split(h @ w_in, 2, axis=-1)
    return (jax.nn.silu(gate) * up) @ w_out


def t5_causal_bucket(dist):
    n = jnp.maximum(dist, 0)
    max_exact = REL_BUCKETS // 2
    ratio = jnp.log(jnp.maximum(n, 1).astype(jnp.float32) / max_exact) / math.log(REL_MAX_DIST / max_exact)
    large = max_exact + (ratio * (REL_BUCKETS - max_exact)).astype(jnp.int32)
    large = jnp.minimum(large, REL_BUCKETS - 1)
    return jnp.where(n < max_exact, n, large)


def moba_attention(q, k, v, rel_bias):
    b, h, s, d = q.shape
    nb = s // MOBA_BLOCK
    n_sel = min(MOBA_TOPK, nb)
    nq = s // Q_BLOCK
    scale = d ** -0.5
    kb = k.reshape(b, h, nb, MOBA_BLOCK, d)
    vb = v.reshape(b, h, nb, MOBA_BLOCK, d)
    k_mean = jnp.mean(kb.astype(jnp.float32), axis=3).astype(k.dtype)
    q_blk = jnp.arange(s) // MOBA_BLOCK
    gate = jnp.einsum('bhsd,bhnd->bhsn', q, k_mean).astype(jnp.float32)
    past = jnp.arange(nb)[None, :] < q_blk[:, None]
    gate = jnp.where(past, gate, -jnp.inf)
    _, sel = lax.top_k(gate, n_sel)
    sel_valid = sel < q_blk[:, None]
    q_c = q.reshape(b, h, nq, Q_BLOCK, d).transpose(2, 0, 1, 3, 4)
    sel_c = sel.reshape(b, h, nq, Q_BLOCK, n_sel).transpose(2, 0, 1, 3, 4)
    valid_c = sel_valid.reshape(b, h, nq, Q_BLOCK, n_sel).transpose(2, 0, 1, 3, 4)
    table = rel_bias.T
    head_idx = jnp.arange(h)[None, :, None, None, None]
    gather_blocks = jax.vmap(jax.vmap(lambda blocks, idx: blocks[idx]))

    def one_query_block(args):
        ci, qb, selb, validb = args
        q_pos = ci * Q_BLOCK + jnp.arange(Q_BLOCK)
        own = (ci * Q_BLOCK) // MOBA_BLOCK
        own_pos = own * MOBA_BLOCK + jnp.arange(MOBA_BLOCK)
        k_own = lax.dynamic_index_in_dim(kb, own, axis=2, keepdims=False)
        v_own = lax.dynamic_index_in_dim(vb, own, axis=2, keepdims=False)
        k_sel = gather_blocks(kb, selb)
        v_sel = gather_blocks(vb, selb)
        sel_pos = selb[..., None] * MOBA_BLOCK + jnp.arange(MOBA_BLOCK)
        bias_sel = table[head_idx, t5_causal_bucket(q_pos[:, None, None] - sel_pos)]
        l_sel = jnp.einsum('bhqd,bhqjkd->bhqjk', qb, k_sel).astype(jnp.float32) * scale + bias_sel
        l_sel = jnp.where(validb[..., None], l_sel, NEG_INF)
        bias_own = table[:, t5_causal_bucket(q_pos[:, None] - own_pos[None, :])]
        l_own = jnp.einsum('bhqd,bhkd->bhqk', qb, k_own).astype(jnp.float32) * scale + bias_own
        l_own = jnp.where(own_pos[None, :] <= q_pos[:, None], l_own, NEG_INF)
        logits = jnp.concatenate([l_sel.reshape(b, h, Q_BLOCK, n_sel * MOBA_BLOCK), l_own], axis=-1)
        p = jax.nn.softmax(logits, axis=-1)
        p_sel = p[..., :n_sel * MOBA_BLOCK].reshape(b, h, Q_BLOCK, n_sel, MOBA_BLOCK).astype(v.dtype)
        p_own = p[..., n_sel * MOBA_BLOCK:].astype(v.dtype)
        return (jnp.einsum('bhqjk,bhqjkd->bhqd', p_sel, v_sel)
                + jnp.einsum('bhqk,bhkd->bhqd', p_own, v_own))

    out = lax.map(one_query_block, (jnp.arange(nq), q_c, sel_c, valid_c))
    return out.transpose(1, 2, 0, 3, 4).reshape(b, h, s, d)


def ssd_chunked_scan(xs, dt, a, bm, cm):
    b, s = xs.shape[0], xs.shape[1]
    nc = s // SSD_CHUNK
    r = SSD_HEADS // SSD_GROUPS
    f32 = jnp.float32
    xd = (xs * dt[..., None]).reshape(b, nc, SSD_CHUNK, SSD_GROUPS, r, SSD_HEAD_DIM)
    ad = (dt.astype(f32) * a.astype(f32)).reshape(b, nc, SSD_CHUNK, SSD_GROUPS, r).transpose(0, 3, 4, 1, 2)
    bc = bm.reshape(b, nc, SSD_CHUNK, SSD_GROUPS, SSD_STATE)
    cc = cm.reshape(b, nc, SSD_CHUNK, SSD_GROUPS, SSD_STATE)
    a_cs = jnp.cumsum(ad, axis=-1)
    causal = jnp.tril(jnp.ones((SSD_CHUNK, SSD_CHUNK), dtype=bool))
    decay = jnp.exp(jnp.where(causal, a_cs[..., :, None] - a_cs[..., None, :], -jnp.inf))
    cb = jnp.einsum('bclgn,bcsgn->bgcls', cc, bc)
    y_diag = jnp.einsum('bgrcls,bcsgrp->bclgrp', cb[:, :, None] * decay, xd)
    decay_to_end = jnp.exp(a_cs[..., -1:] - a_cs)
    states = jnp.einsum('bclgn,bgrcl,bclgrp->cbgrpn', bc, decay_to_end, xd)
    chunk_decay = jnp.exp(a_cs[..., -1]).transpose(3, 0, 1, 2)

    def carry_state(hstate, inp):
        st, dec = inp
        return dec[..., None, None] * hstate + st, hstate

    h0 = jnp.zeros(states.shape[1:], states.dtype)
    _, prev = lax.scan(carry_state, h0, (states, chunk_decay))
    y_off = jnp.einsum('bclgn,cbgrpn,bgrcl->bclgrp', cc, prev, jnp.exp(a_cs))
    return (y_diag + y_off).reshape(b, s, SSD_HEADS, SSD_HEAD_DIM).astype(xs.dtype)


def token_mix(h, w_mix_in, qk_norm, rel_bias, w_ssd_conv, b_ssd_conv, ssd_dt_bias, ssd_a_log,
              ssd_d, ssd_norm, w_sc_conv, w_br_attn, w_br_ssd, w_br_conv, w_mix_out):
    b, s, _ = h.shape
    s_pad = -(-s // SEQ_ALIGN) * SEQ_ALIGN
    hp = jnp.pad(h, ((0, 0), (0, s_pad - s), (0, 0)))
    splits = [int(v) for v in np.cumsum(MIX_SIZES)[:-1]]
    (q, k, v, z, xbc, dt, conv_b, conv_c, conv_x,
     g_attn, g_ssd, g_conv) = jnp.split(hp @ w_mix_in, splits, axis=-1)

    def heads(t):
        return t.reshape(b, s_pad, ATTN_HEADS, ATTN_HEAD_DIM)
    qh = rms_norm(heads(q), qk_norm[0]).transpose(0, 2, 1, 3)
    kh = rms_norm(heads(k), qk_norm[1]).transpose(0, 2, 1, 3)
    vh = heads(v).transpose(0, 2, 1, 3)
    y_attn = moba_attention(qh, kh, vh, rel_bias).transpose(0, 2, 1, 3).reshape(b, s_pad, ATTN_WIDTH)

    xbc = jax.nn.silu(causal_depthwise_conv(xbc, w_ssd_conv) + b_ssd_conv)
    xs, bm, cm = jnp.split(xbc, [SSD_WIDTH, SSD_WIDTH + SSD_GROUPS * SSD_STATE], axis=-1)
    xs = xs.reshape(b, s_pad, SSD_HEADS, SSD_HEAD_DIM)
    dt = jax.nn.softplus(dt + ssd_dt_bias)
    a = -jnp.exp(ssd_a_log)
    y = ssd_chunked_scan(xs, dt, a, bm.reshape(b, s_pad, SSD_GROUPS, SSD_STATE),
                         cm.reshape(b, s_pad, SSD_GROUPS, SSD_STATE))
    y = (y + xs * ssd_d[:, None]).reshape(b, s_pad, SSD_WIDTH) * jax.nn.silu(z)
    group_w = SSD_WIDTH // SSD_GROUPS
    y_ssd = rms_norm(y.reshape(b, s_pad, SSD_GROUPS, group_w),
                     ssd_norm.reshape(SSD_GROUPS, group_w)).reshape(b, s_pad, SSD_WIDTH)

    y_conv = conv_b * causal_depthwise_conv(conv_c * conv_x, w_sc_conv)

    merged = (jax.nn.sigmoid(g_attn) * (y_attn @ w_br_attn)
              + jax.nn.sigmoid(g_ssd) * (y_ssd @ w_br_ssd)
              + jax.nn.sigmoid(g_conv) * (y_conv @ w_br_conv))
    return (merged @ w_mix_out)[:, :s]


def setup_inputs(seed: int = 0) -> dict:
    key = jax.random.key(seed)
    ks = jax.random.split(key, 24)
    f32 = jnp.float32

    def dense(k, shape, fan_in):
        return jax.random.normal(k, shape, f32) * fan_in ** -0.5

    def near_one(k, shape):
        return 1.0 + 0.1 * jax.random.normal(k, shape, f32)

    dt0 = jnp.exp(jax.random.uniform(ks[12], (DEPTH, SSD_HEADS), f32, math.log(1e-3), math.log(1e-1)))
    return {
        'x': jax.random.normal(ks[0], (BATCH, SEQ, D_MODEL), f32),
        'c': jax.random.normal(ks[1], (BATCH, D_MODEL), f32),
        'w_ada': dense(ks[2], (DEPTH, D_MODEL, N_SUBLAYERS * 3 * D_MODEL), D_MODEL),
        'b_ada': 0.02 * jax.random.normal(ks[3], (DEPTH, N_SUBLAYERS * 3 * D_MODEL), f32),
        'norm_gain': near_one(ks[4], (DEPTH, N_SUBLAYERS, D_MODEL)),
        'w_ffn_in': dense(ks[5], (DEPTH, 2, D_MODEL, 2 * FFN_HIDDEN), D_MODEL),
        'w_ffn_out': dense(ks[6], (DEPTH, 2, FFN_HIDDEN, D_MODEL), FFN_HIDDEN),
        'w_mix_in': dense(ks[7], (DEPTH, D_MODEL, MIX_IN), D_MODEL),
        'qk_norm': near_one(ks[8], (DEPTH, 2, ATTN_HEAD_DIM)),
        'rel_bias': 0.5 * jax.random.normal(ks[9], (REL_BUCKETS, ATTN_HEADS), f32),
        'w_ssd_conv': dense(ks[10], (DEPTH, SSD_CONV, SSD_XBC), SSD_CONV),
        'b_ssd_conv': 0.02 * jax.random.normal(ks[11], (DEPTH, SSD_XBC), f32),
        'ssd_dt_bias': dt0 + jnp.log(-jnp.expm1(-dt0)),
        'ssd_a_log': jnp.log(jax.random.uniform(ks[13], (DEPTH, SSD_HEADS), f32, 1.0, 16.0)),
        'ssd_d': near_one(ks[14], (DEPTH, SSD_HEADS)),
        'ssd_norm': near_one(ks[15], (DEPTH, SSD_WIDTH)),
        'w_sc_conv': dense(ks[16], (DEPTH, CONV_K, CONV_WIDTH), CONV_K),
        'w_br_attn': dense(ks[17], (DEPTH, ATTN_WIDTH, D_MODEL), ATTN_WIDTH),
        'w_br_ssd': dense(ks[18], (DEPTH, SSD_WIDTH, D_MODEL), SSD_WIDTH),
        'w_br_conv': dense(ks[19], (DEPTH, CONV_WIDTH, D_MODEL), CONV_WIDTH),
        'w_mix_out': dense(ks[20], (DEPTH, D_MODEL, D_MODEL), D_MODEL),
    }


def reference(x, c, w_ada, b_ada, norm_gain, w_ffn_in, w_ffn_out, w_mix_in, qk_norm, rel_bias,
              w_ssd_conv, b_ssd_conv, ssd_dt_bias, ssd_a_log, ssd_d, ssd_norm, w_sc_conv,
              w_br_attn, w_br_ssd, w_br_conv, w_mix_out):
    b = x.shape[0]
    cond = jax.nn.silu(c)
    for l in range(DEPTH):
        ada = (cond @ w_ada[l] + b_ada[l]).reshape(b, N_SUBLAYERS, 3, D_MODEL)
        shift, scale, gate = ada[:, :, 0, None], ada[:, :, 1, None], ada[:, :, 2, None]

        def modulated(t, i):
            return rms_norm(t, norm_gain[l, i]) * (1.0 + scale[:, i]) + shift[:, i]

        h = modulated(x, 0)
        x = x + FFN_RESIDUAL * gate[:, 0] * swiglu_ffn(h, w_ffn_in[l, 0], w_ffn_out[l, 0])
        h = modulated(x, 1)
        x = x + gate[:, 1] * token_mix(h, w_mix_in[l], qk_norm[l], rel_bias, w_ssd_conv[l], b_ssd_conv[l],
                                       ssd_dt_bias[l], ssd_a_log[l], ssd_d[l], ssd_norm[l], w_sc_conv[l],
                                       w_br_attn[l], w_br_ssd[l], w_br_conv[l], w_mix_out[l])
        h = modulated(x, 2)
        x = x + FFN_RESIDUAL * gate[:, 2] * swiglu_ffn(h, w_ffn_in[l, 1], w_ffn_out[l, 1])
    return x
```

```python
import contextlib
import numpy as np
import concourse.bass as bass
import concourse.mybir as mybir
from concourse.bass_utils import run_bass_kernel_spmd

F32 = mybir.dt.float32
BF16 = mybir.dt.bfloat16
AF = mybir.ActivationFunctionType
ALU = mybir.AluOpType
AX = mybir.AxisListType

NCORES = 8
D = 2048
KC = 16
SEQ = 8192
TOK = SEQ // NCORES
FFH = 5632
NJ = FFH // 128
NQ = 4
JQ = NJ // NQ
EPS = 1e-6


class T:
    __slots__ = ("name", "last_w", "readers", "excl")

    def __init__(self, name="", excl=False):
        self.name = name
        self.last_w = None
        self.readers = []
        self.excl = excl


class Op:
    __slots__ = ("eng", "fn", "deps", "is_dma", "sig", "sigval", "dsem", "dval", "dprev")


ENGS = ("pe", "act", "dve", "pool", "sp")
N_DMA_SEMS = 4


class Sched:
    def __init__(self, nc):
        self.nc = nc
        self.ops = []
        self.dma_rr = {e: 0 for e in ENGS}
        self.dma_cnt = {}

    def add(self, eng, fn, reads=(), writes=(), dma=False):
        op = Op()
        op.eng = eng
        op.fn = fn
        op.is_dma = dma
        op.sig = False
        op.sigval = None
        deps = []
        excl_r = [t for t in reads if t.excl and t not in writes]
        reads = [t for t in reads if not t.excl]
        writes = list(writes) + excl_r
        for t in reads:
            if t.last_w is not None:
                deps.append(t.last_w)
        for t in writes:
            if t.last_w is not None:
                deps.append(t.last_w)
            deps.extend(t.readers)
        for t in reads:
            t.readers.append(op)
        for t in writes:
            t.last_w = op
            t.readers = []
        op.deps = [d for d in dict.fromkeys(deps) if d is not op]
        if dma:
            k = self.dma_rr[eng]
            self.dma_rr[eng] = (k + 1) % N_DMA_SEMS
            key = (eng, k)
            c = self.dma_cnt.get(key, 0) + 1
            self.dma_cnt[key] = c
            op.dsem = key
            op.dval = 16 * c
            op.dprev = 16 * (c - 1)
        self.ops.append(op)
        return op

    def emit(self):
        nc = self.nc
        ops = self.ops
        for op in ops:
            for d in op.deps:
                if d.is_dma:
                    continue
                if d.eng == op.eng and d.eng == "pe" and not op.is_dma:
                    continue
                d.sig = True
        cnt = {e: 0 for e in ENGS}
        for op in ops:
            if not op.is_dma and op.sig:
                cnt[op.eng] += 1
                op.sigval = cnt[op.eng]
        with contextlib.ExitStack() as st:
            esem = {e: st.enter_context(nc.semaphore("s_" + e)) for e in ENGS}
            dsem = {}
            for key in self.dma_cnt:
                dsem[key] = st.enter_context(nc.semaphore("d_%s%d" % key))
            block = st.enter_context(nc.Block())
            by_eng = {e: [o for o in ops if o.eng == e] for e in ENGS}

            def run(eng_name, eng):
                known = {}

                def wait(sem_key, sem, val):
                    if known.get(sem_key, 0) >= val:
                        return
                    known[sem_key] = val
                    eng.wait_ge(sem, val)

                for op in by_eng[eng_name]:
                    for d in op.deps:
                        if d.is_dma:
                            wait(d.dsem, dsem[d.dsem], d.dval)
                        elif d.sigval is not None:
                            wait(d.eng, esem[d.eng], d.sigval)
                    if op.is_dma:
                        if op.dprev > 0:
                            wait(op.dsem, dsem[op.dsem], op.dprev)
                        op.fn(eng).then_inc(dsem[op.dsem], 16)
                    else:
                        ins = op.fn(eng)
                        if op.sig:
                            ins.then_inc(esem[op.eng], 1)
                if eng_name == "sp":
                    for key, c in self.dma_cnt.items():
                        wait(key, dsem[key], 16 * c)

            block.tensor(lambda e: run("pe", e))
            block.scalar(lambda e: run("act", e))
            block.vector(lambda e: run("dve", e))
            block.gpsimd(lambda e: run("pool", e))
            block.sync(lambda e: run("sp", e))


class Ctx:
    def __init__(self):
        self.nc = bass.Bass("TRN2", target_bir_lowering=False)
        self.S = Sched(self.nc)
        self.st = contextlib.ExitStack()
        self.ps = []
        self.ps_i = 0
        self.n = 0

    def din(self, name, shape, dt=F32):
        return self.nc.dram_tensor(name, list(shape), dt, kind="ExternalInput").ap()

    def dout(self, name, shape, dt=F32):
        return self.nc.dram_tensor(name, list(shape), dt, kind="ExternalOutput").ap()

    def sb(self, shape, dt, name=None):
        self.n += 1
        t = self.st.enter_context(self.nc.sbuf_tensor("%s_%d" % (name or "t", self.n), list(shape), dt))
        return t, T(name or "t")

    def init_psum(self):
        for i in range(8):
            t = self.st.enter_context(self.nc.psum_tensor("ps%d" % i, [128, 512], F32))
            self.ps.append((t, T("ps%d" % i, excl=True)))

    def psum(self):
        r = self.ps[self.ps_i]
        self.ps_i = (self.ps_i + 1) % 8
        return r

    def finish(self):
        self.S.emit()
        self.st.close()
        return self.nc


def mm_group(S, out_ap, pairs, reads, writes):
    n = len(pairs)

    def fn(e):
        ins = None
        for i, (l, r) in enumerate(pairs):
            ins = e.matmul(out_ap, l, r, start=(i == 0), stop=(i == n - 1))
        return ins

    return S.add("pe", fn, reads=reads, writes=writes)


def load_consts(C, ones_ap):
    ones, t_ones = C.sb([128, 128], F32, "ones")
    C.S.add("sp", lambda e: e.dma_start(out=ones[:], in_=ones_ap), writes=[t_ones], dma=True)
    return ones, t_ones


def modulate(C, xT, t_x, hT, t_h, a_col, shift_col, t_par, ones, t_ones, ntok, out_dram=None):
    S = C.S
    sq = [C.sb([128, 512], F32, "sq") for _ in range(2)]
    rs, t_rs = C.sb([128, 512], F32, "rstd")
    tmp = [C.sb([128, 512], F32, "modtmp") for _ in range(2)]
    tmp32 = [C.sb([128, 512], F32, "modtmp32") for _ in range(2)] if out_dram is not None else None
    for tt in range(ntok // 512):
        sl = slice(tt * 512, (tt + 1) * 512)
        ps, t_ps = C.psum()
        for kc in range(KC):
            sqt, t_sq = sq[kc % 2]
            S.add("act", lambda e, sqt=sqt, kc=kc, sl=sl: e.activation(out=sqt[:], in_=xT[:, kc, sl], func=AF.Square),
                  reads=[t_x], writes=[t_sq])
            S.add("pe", lambda e, sqt=sqt, kc=kc, ps=ps: e.matmul(ps[:, :], ones[:], sqt[:], start=(kc == 0), stop=(kc == KC - 1)),
                  reads=[t_sq, t_ones], writes=[t_ps])
        S.add("act", lambda e, ps=ps: e.activation(out=rs[:], in_=ps[:, :], func=AF.Sqrt, scale=1.0 / D, bias=EPS),
              reads=[t_ps], writes=[t_rs])
        S.add("dve", lambda e: e.reciprocal(out=rs[:], in_=rs[:]), reads=[t_rs], writes=[t_rs])
        for kc in range(KC):
            tm, t_tm = tmp[kc % 2]
            S.add("dve", lambda e, tm=tm, kc=kc, sl=sl: e.scalar_tensor_tensor(
                out=tm[:], in0=xT[:, kc, sl], scalar=a_col[:, kc:kc + 1], in1=rs[:], op0=ALU.mult, op1=ALU.mult),
                reads=[t_x, t_rs] + t_par, writes=[t_tm])
            S.add("act", lambda e, tm=tm, kc=kc, sl=sl: e.activation(
                out=hT[:, kc, sl], in_=tm[:], func=AF.Identity, bias=shift_col[:, kc:kc + 1], scale=1.0),
                reads=[t_tm] + t_par, writes=[t_h])
            if out_dram is not None:
                t32, t_t32 = tmp32[kc % 2]
                S.add("act", lambda e, tm=tm, t32=t32, kc=kc: e.activation(
                    out=t32[:], in_=tm[:], func=AF.Identity, bias=shift_col[:, kc:kc + 1], scale=1.0),
                    reads=[t_tm] + t_par, writes=[t_t32])
                S.add("sp", lambda e, t32=t32, kc=kc, sl=sl: e.dma_start(out=out_dram[kc * 128:(kc + 1) * 128, sl], in_=t32[:]),
                      reads=[t_t32], dma=True)


def prep_mod_params(C, par_ap, n_sets):
    S = C.S
    par, t_par = C.sb([128, n_sets * 4 * KC], F32, "par")
    acol, t_a = C.sb([128, n_sets * KC], F32, "acol")
    S.add("sp", lambda e: e.dma_start(out=par[:], in_=par_ap), writes=[t_par], dma=True)
    out = []
    for s in range(n_sets):
        b = s * 4 * KC
        S.add("dve", lambda e, b=b, s=s: e.scalar_tensor_tensor(
            out=acol[:, s * KC:(s + 1) * KC], in0=par[:, b + 2 * KC:b + 3 * KC], scalar=1.0, in1=par[:, b:b + KC],
            op0=ALU.add, op1=ALU.mult), reads=[t_par], writes=[t_a])
        out.append((acol[:, s * KC:(s + 1) * KC], par[:, b + KC:b + 2 * KC], par[:, b + 3 * KC:b + 4 * KC]))
    return out, [t_par, t_a]


ADA_COLS = 2 * 18432 // NCORES
ADA_NB = ADA_COLS // 512


def build_ada():
    C = Ctx()
    S = C.S
    c_ap = C.din("c", [128, KC])
    w_ap = C.din("w", [ADA_NB, 128, KC, 512])
    b_ap = C.din("b", [1, ADA_COLS])
    o_ap = C.dout("o", [1, ADA_COLS])
    C.init_psum()
    ct, t_c = C.sb([128, KC], F32, "c")
    bt, t_b = C.sb([1, ADA_COLS], F32, "b")
    ot, t_o = C.sb([1, ADA_COLS], F32, "o")
    wt = [C.sb([128, KC, 512], F32, "w") for _ in range(2)]
    S.add("sp", lambda e: e.dma_start(out=ct[:], in_=c_ap), writes=[t_c], dma=True)
    S.add("sp", lambda e: e.dma_start(out=bt[:], in_=b_ap), writes=[t_b], dma=True)
    S.add("act", lambda e: e.activation(out=ct[:], in_=ct[:], func=AF.Silu), reads=[t_c], writes=[t_c])
    for nb in range(ADA_NB):
        w, t_w = wt[nb % 2]
        for kc in range(KC):
            S.add("sp", lambda e, w=w, nb=nb, kc=kc: e.dma_start(out=w[:, kc, :], in_=w_ap[nb, :, kc, :]), writes=[t_w], dma=True)
        ps, t_ps = C.psum()
        mm_group(S, ps[0:1, :], [(ct[:, kc:kc + 1], w[:, kc, :]) for kc in range(KC)], [t_c, t_w], [t_ps])
        S.add("dve", lambda e, ps=ps, nb=nb: e.tensor_tensor(
            out=ot[:, nb * 512:(nb + 1) * 512], in0=ps[0:1, :], in1=bt[:, nb * 512:(nb + 1) * 512], op=ALU.add),
            reads=[t_ps, t_b], writes=[t_o])
    S.add("sp", lambda e: e.dma_start(out=o_ap, in_=ot[:]), reads=[t_o], dma=True)
    return C.finish()


def build_ffn(emit_h_next, dbg=None):
    C = Ctx()
    S = C.S
    nsets = 2 if emit_h_next else 1
    x_ap = C.din("xT", [D, TOK])
    par_ap = C.din("par", [128, nsets * 4 * KC])
    win_ap = C.din("win", [NJ, 128, KC, 256])
    wout_ap = C.din("wout", [NQ, KC, 128, JQ, 128])
    ones_ap = C.din("ones", [128, 128])
    xo_ap = C.dout("xo", [D, TOK])
    ho_ap = C.dout("ho", [D, TOK]) if emit_h_next else None
    C.init_psum()
    ones, t_ones = load_consts(C, ones_ap)
    xT, t_x = C.sb([128, KC, TOK], F32, "xT")
    hT, t_h = C.sb([128, KC, TOK], BF16, "hT")
    actT, t_act = C.sb([128, JQ, TOK], BF16, "actT")
    wi = [C.sb([128, KC, 256], BF16, "wi") for _ in range(2)]
    wo = [C.sb([128, JQ, 128], BF16, "wo") for _ in range(2)]
    sg = [C.sb([128, 512], F32, "sg") for _ in range(2)]
    g05, t_g = C.sb([128, KC], F32, "g05")
    for kc in range(KC):
        S.add("sp", lambda e, kc=kc: e.dma_start(out=xT[:, kc, :], in_=x_ap[kc * 128:(kc + 1) * 128, :]),
              writes=[t_x], dma=True)
    sets, t_par = prep_mod_params(C, par_ap, nsets)
    a_col, shift_col, gate_col = sets[0]
    S.add("dve", lambda e: e.tensor_scalar(out=g05[:], in0=gate_col, scalar1=0.5, scalar2=None, op0=ALU.mult),
          reads=t_par, writes=[t_g])
    modulate(C, xT, t_x, hT, t_h, a_col, shift_col, t_par, ones, t_ones, TOK, out_dram=(ho_ap if dbg == "mod" else None))
    if dbg == "mod":
        for kc in range(KC):
            S.add("sp", lambda e, kc=kc: e.dma_start(out=xo_ap[kc * 128:(kc + 1) * 128, :], in_=xT[:, kc, :]),
                  reads=[t_x], dma=True)
        return C.finish()
    nwi = 0
    nwo = 0
    nsg = 0
    for qh in range(NQ):
        for jj in range(JQ):
            j = qh * JQ + jj
            w, t_w = wi[nwi % 2]
            nwi += 1
            S.add("pool", lambda e, w=w, j=j: e.dma_start(out=w[:], in_=win_ap[j]), writes=[t_w], dma=True)
            for tt in range(TOK // 512):
                sl = slice(tt * 512, (tt + 1) * 512)
                pg, t_pg = C.psum()
                pu, t_pu = C.psum()
                mm_group(S, pg[:, :], [(w[:, kc, 0:128], hT[:, kc, sl]) for kc in range(KC)], [t_w, t_h], [t_pg])
                mm_group(S, pu[:, :], [(w[:, kc, 128:256], hT[:, kc, sl]) for kc in range(KC)], [t_w, t_h], [t_pu])
                s_, t_s = sg[nsg % 2]
                nsg += 1
                S.add("act", lambda e, s_=s_, pg=pg: e.activation(out=s_[:], in_=pg[:, :], func=AF.Silu),
                      reads=[t_pg], writes=[t_s])
                S.add("dve", lambda e, s_=s_, pu=pu, jj=jj, sl=sl: e.tensor_tensor(
                    out=actT[:, jj, sl], in0=s_[:], in1=pu[:, :], op=ALU.mult),
                    reads=[t_s, t_pu], writes=[t_act])
        for m in range(KC):
            w, t_w = wo[nwo % 2]
            nwo += 1
            S.add("pool", lambda e, w=w, qh=qh, m=m: e.dma_start(out=w[:], in_=wout_ap[qh, m]), writes=[t_w], dma=True)
            for tt in range(TOK // 512):
                sl = slice(tt * 512, (tt + 1) * 512)
                po, t_po = C.psum()
                mm_group(S, po[:, :], [(w[:, jj, :], actT[:, jj, sl]) for jj in range(JQ)], [t_w, t_act], [t_po])
                S.add("dve", lambda e, po=po, m=m, sl=sl: e.scalar_tensor_tensor(
                    out=xT[:, m, sl], in0=po[:, :], scalar=g05[:, m:m + 1], in1=xT[:, m, sl], op0=ALU.mult, op1=ALU.add),
                    reads=[t_po, t_g, t_x], writes=[t_x])
    for kc in range(KC):
        S.add("sp", lambda e, kc=kc: e.dma_start(out=xo_ap[kc * 128:(kc + 1) * 128, :], in_=xT[:, kc, :]),
              reads=[t_x], dma=True)
    if emit_h_next:
        a2, sh2, _ = sets[1]
        modulate(C, xT, t_x, hT, t_h, a2, sh2, t_par, ones, t_ones, TOK, out_dram=ho_ap)
    return C.finish()


def cols16(v):
    return np.ascontiguousarray(np.asarray(v, np.float32).reshape(KC, 128).T)


def tile_win(w_in):
    w = w_in.reshape(KC, 128, 2, NJ, 128)
    return np.ascontiguousarray(w.transpose(3, 1, 0, 2, 4).reshape(NJ, 128, KC, 256))


def tile_wout(w_out):
    w = w_out.reshape(NQ, JQ, 128, KC, 128)
    return np.ascontiguousarray(w.transpose(0, 3, 2, 1, 4))


_PROG = {}


def prog(name, fn, *a):
    key = (name,) + a
    if key not in _PROG:
        _PROG[key] = fn(*a)
    return _PROG[key]


def run(nc, in_maps):
    res = run_bass_kernel_spmd(nc, in_maps, core_ids=list(range(NCORES)))
    return res.results


def compute_ada(c, w_ada, b_ada):
    cc = cols16(c.reshape(-1))
    wa = np.concatenate([w_ada[0], w_ada[1]], axis=1)
    ba = np.concatenate([b_ada[0], b_ada[1]], axis=0)
    in_maps = []
    for i in range(NCORES):
        ws = wa[:, i * ADA_COLS:(i + 1) * ADA_COLS].reshape(KC, 128, ADA_NB, 512)
        in_maps.append({
            "c": cc,
            "w": np.ascontiguousarray(ws.transpose(2, 1, 0, 3)),
            "b": np.ascontiguousarray(ba[i * ADA_COLS:(i + 1) * ADA_COLS].reshape(1, -1)),
        })
    r = run(prog("ada", build_ada), in_maps)
    ada = np.concatenate([r[i]["o"].reshape(-1) for i in range(NCORES)])
    return ada.reshape(2, 3, 3, D)


def par_block(norm_gain_li, ada_li):
    return np.concatenate([cols16(norm_gain_li), cols16(ada_li[0]), cols16(ada_li[1]), cols16(ada_li[2])], axis=1)


def run_ffn(xT, par, w_in, w_out, emit_h_next, dbg=None):
    win_t = tile_win(w_in)
    wout_t = tile_wout(w_out)
    ones = np.ones((128, 128), np.float32)
    in_maps = []
    for i in range(NCORES):
        in_maps.append({"xT": np.ascontiguousarray(xT[:, i * TOK:(i + 1) * TOK]), "par": par,
                        "win": win_t, "wout": wout_t, "ones": ones})
    r = run(prog("ffn", build_ffn, emit_h_next, dbg), in_maps)
    xo = np.concatenate([r[i]["xo"] for i in range(NCORES)], axis=1)
    ho = np.concatenate([r[i]["ho"] for i in range(NCORES)], axis=1) if emit_h_next else None
    return xo, ho


NCOLB = 1540
C_Q, C_K, C_Z, C_XS, C_B, C_C, C_CB, C_CC, C_CX, C_V = 0, 128, 256, 512, 768, 896, 1024, 1152, 1280, 1408
SCALE = 128 ** -0.5
NEG = -1e30


PARTS = ('conv', 'ssd', 'attn')
STOP = 0


def build_mixB(NT):
    C = Ctx()
    S = C.S
    ntok = NT * 512
    h_ap = C.din("hT", [D, ntok])
    w_ap = C.din("wB", [128, KC, NCOLB])
    sp_ap = C.din("sp", [128, 24])
    sp64_ap = C.din("sp64", [64, 24])
    cst_ap = C.din("cst", [128, 5 * 128 + 512 + 512])
    ya_ap = C.dout("ya", [ntok, 128])
    yg_ap = C.dout("yg", [256, ntok])
    yc_ap = C.dout("yc", [128, ntok])
    C.init_psum()
    cst, t_cst = C.sb([128, 5 * 128 + 1024], F32, "cst")
    S.add("sp", lambda e: e.dma_start(out=cst[:], in_=cst_ap), writes=[t_cst], dma=True)
    ident = cst[:, 0:128]
    ones = cst[:, 128:256]
    TA = cst[:, 256:384]
    TB = cst[:, 384:512]
    NEGM = cst[:, 512:640]
    U0 = cst[:, 640:896]
    UTRI = cst[:, 640:768]
    U1 = cst[:, 896:1152]
    SELH = cst[0:4, 1152:1664]
    spt, t_sp = C.sb([128, 24], F32, "sp")
    sp64, t_sp64 = C.sb([64, 24], F32, "sp64")
    S.add("sp", lambda e: e.dma_start(out=spt[:], in_=sp_ap), writes=[t_sp], dma=True)
    S.add("sp", lambda e: e.dma_start(out=sp64[:], in_=sp64_ap), writes=[t_sp64], dma=True)
    S.add("dve", lambda e: e.tensor_tensor(out=TA, in0=TA, in1=NEGM, op=ALU.add), reads=[t_cst], writes=[t_cst])
    aneg, t_an = C.sb([128, 4], F32, "aneg")
    S.add("act", lambda e: e.activation(out=aneg[:], in_=spt[:, 20:24], func=AF.Exp), reads=[t_sp], writes=[t_an])
    S.add("dve", lambda e: e.tensor_scalar(out=aneg[:], in0=aneg[:], scalar1=-1.0, scalar2=None, op0=ALU.mult),
          reads=[t_an], writes=[t_an])
    wB, _ = C.sb([128, KC, NCOLB], BF16, "wB")
    t_wB = [T("wB%d" % kc) for kc in range(KC)]
    for kc in range(KC):
        S.add("pool", lambda e, kc=kc: e.dma_start(out=wB[:, kc, :], in_=w_ap[:, kc, :]), writes=[t_wB[kc]], dma=True)
    hbuf = [C.sb([128, KC, 512], BF16, "hb")[0] for _ in range(2)]
    t_hb = [[T("hb") for _ in range(KC)] for _ in range(2)]
    KT, t_KT = C.sb([128, ntok], BF16, "KT")
    vaug, t_v = C.sb([128, NT * 4, 130], BF16, "vaug")
    S.add("pool", lambda e: e.memset(vaug[:], 1.0), writes=[t_v])
    kmean, t_km = C.sb([128, 32], F32, "kmean")
    S.add("pool", lambda e: e.memset(kmean[:], 0.0), writes=[t_km])

    def sbt(shape, dt, name):
        return C.sb(shape, dt, name)

    sq, t_sq = sbt([128, 512], F32, "sq")
    rs, t_rs = sbt([128, 512], F32, "rs")
    qn32, t_qn = sbt([128, 512], F32, "qn32")
    qnb, t_qnb = sbt([128, 512], BF16, "qnb")
    kn32, t_kn = sbt([128, 2, 256], F32, "kn32")
    kms, t_kms = sbt([128, 2], F32, "kms")
    sz = [sbt([64, 512], F32, "sz") for _ in range(4)]
    xpre = [sbt([64, 515], F32, "xpre") for _ in range(4)]
    bpre, t_bpre = sbt([128, 515], F32, "bpre")
    cpre, t_cpre = sbt([128, 515], F32, "cpre")
    for (t_, tt_) in xpre + [(bpre, t_bpre), (cpre, t_cpre)]:
        S.add("pool", lambda e, t_=t_: e.memset(t_[:, 0:3], 0.0), writes=[tt_])
    xsT = [sbt([64, 512], F32, "xsT") for _ in range(4)]
    BT32, t_BT32 = sbt([128, 512], F32, "BT32")
    CT32, t_CT32 = sbt([128, 512], F32, "CT32")
    BTb, t_BTb = sbt([128, 512], BF16, "BTb")
    CTb, t_CTb = sbt([128, 512], BF16, "CTb")
    cacc = [sbt([128, 512], F32, "cacc") for _ in range(2)]
    cBs, t_cBs = sbt([128, 512], F32, "cBs")
    cCs, t_cCs = sbt([128, 512], F32, "cCs")
    ucv, t_ucv = sbt([128, 514], F32, "ucv")
    S.add("pool", lambda e: e.memset(ucv[:, 0:2], 0.0), writes=[t_ucv])
    ycs, t_ycs = sbt([128, 512], F32, "ycs")
    dtraw, t_dtr = sbt([128, 4, 4], F32, "dtraw")
    dtt, t_dtt = sbt([128, 4, 4], F32, "dtt")
    dtA, t_dtA = sbt([128, 4, 4], F32, "dtA")
    wd, t_wd = sbt([128, 4, 4], F32, "wd")
    xd = [sbt([128, 256], BF16, "xd") for _ in range(4)]
    xdd = [sbt([128, 256], BF16, "xdd") for _ in range(4)]
    Btok = [sbt([128, 128], BF16, "Btok") for _ in range(4)]
    acs, t_acs = sbt([128, 12], F32, "acs")
    d2, t_d2 = sbt([128, 8], F32, "d2")
    acsT, t_acsT = sbt([4, 256], F32, "acsT")
    cbm, t_cbm = sbt([128, 384], F32, "cbm")
    E = [sbt([128, 256], F32, "E") for _ in range(4)]
    Cdec, t_Cdec = sbt([128, 256], BF16, "Cdec")
    Dm, t_Dm = sbt([128, 384], F32, "Dm")
    MT, t_MT = sbt([128, 384], BF16, "MT")
    prev32, t_p32 = sbt([128, 256], F32, "prev32")
    prevb, t_pb = sbt([128, 256], BF16, "prevb")
    S.add("pool", lambda e: e.memset(prev32[:], 0.0), writes=[t_p32])
    S.add("pool", lambda e: e.memset(prevb[:], 0.0), writes=[t_pb])
    yt1, t_yt1 = sbt([64, 256], F32, "yt1")
    ygs = [sbt([64, 256], F32, "ygs") for _ in range(2)]
    gsb, t_gsb = sbt([128, 32], F32, "gsb")
    top8, t_top8 = sbt([128, 8], F32, "top8")
    msel, t_msel = sbt([128, 32], F32, "msel")
    acc, t_acc = sbt([128, 130], F32, "acc")
    tmpS, t_tmpS = sbt([128, 256], F32, "tmpS")
    pt, t_pt = sbt([128, 2, 128], BF16, "pt")
    rec, t_rec = sbt([128, 1], F32, "rec")
    yq = [sbt([128, 128], F32, "yq") for _ in range(2)]
    nyg = 0
    nyq = 0
    nosb = 0
    osb = [sbt([128, 130], F32, "osb") for _ in range(2)]
    if STOP == 1:
        return C.finish()

    for tt in range(NT):
        tok0 = tt * 512
        hb = hbuf[tt % 2]
        thb = t_hb[tt % 2]
        for kc in range(KC):
            S.add("pool", lambda e, kc=kc, hb=hb, tok0=tok0: e.dma_start(
                out=hb[:, kc, :], in_=h_ap[kc * 128:(kc + 1) * 128, tok0:tok0 + 512]), writes=[thb[kc]], dma=True)

        def proj(col0, M):
            ps, t_ps = C.psum()
            mm_group(S, ps[0:M, :], [(wB[:, kc, col0:col0 + M], hb[:, kc, :]) for kc in range(KC)], thb + t_wB, [t_ps])
            return ps, t_ps

        for which in range(2):
            ps, t_ps = proj(C_Q if which == 0 else C_K, 128)
            S.add("act", lambda e, ps=ps: e.activation(out=sq[:], in_=ps[:, :], func=AF.Square), reads=[t_ps], writes=[t_sq])
            ps2, t_ps2 = C.psum()
            S.add("pe", lambda e, ps2=ps2: e.matmul(ps2[:, :], ones, sq[:], start=True, stop=True),
                  reads=[t_sq, t_cst], writes=[t_ps2])
            S.add("act", lambda e, ps2=ps2: e.activation(out=rs[:], in_=ps2[:, :], func=AF.Sqrt, scale=1.0 / 128, bias=EPS),
                  reads=[t_ps2], writes=[t_rs])
            S.add("dve", lambda e: e.reciprocal(out=rs[:], in_=rs[:]), reads=[t_rs], writes=[t_rs])
            if which == 0:
                S.add("dve", lambda e, ps=ps: e.scalar_tensor_tensor(
                    out=qn32[:], in0=ps[:, :], scalar=spt[:, 0:1], in1=rs[:], op0=ALU.mult, op1=ALU.mult),
                    reads=[t_ps, t_rs, t_sp], writes=[t_qn])
                S.add("act", lambda e: e.activation(out=qnb[:], in_=qn32[:], func=AF.Identity), reads=[t_qn], writes=[t_qnb])
            else:
                for a in range(2):
                    S.add("dve", lambda e, ps=ps, a=a: e.scalar_tensor_tensor(
                        out=kn32[:, a, :], in0=ps[:, a * 256:(a + 1) * 256], scalar=spt[:, 1:2], in1=rs[:, a * 256:(a + 1) * 256],
                        op0=ALU.mult, op1=ALU.mult), reads=[t_ps, t_rs, t_sp], writes=[t_kn])
                    S.add("act", lambda e, a=a, tok0=tok0: e.activation(
                        out=KT[:, tok0 + a * 256:tok0 + (a + 1) * 256], in_=kn32[:, a, :], func=AF.Identity),
                        reads=[t_kn], writes=[t_KT])
                S.add("dve", lambda e: e.tensor_reduce(out=kms[:], in_=kn32[:], axis=AX.X, op=ALU.add), reads=[t_kn], writes=[t_kms])
                S.add("dve", lambda e, tt=tt: e.tensor_scalar(
                    out=kmean[:, 2 * tt:2 * tt + 2], in0=kms[:], scalar1=1.0 / 256, scalar2=None, op0=ALU.mult),
                    reads=[t_kms], writes=[t_km])
        if STOP == 2:
            return C.finish()
        for h in range(4):
            ps, t_ps = proj(C_Z + h * 64, 64)
            S.add("act", lambda e, ps=ps, h=h: e.activation(out=sz[h][0][:], in_=ps[0:64, :], func=AF.Silu),
                  reads=[t_ps], writes=[sz[h][1]])
        for h in range(4):
            ps, t_ps = proj(C_XS + h * 64, 64)
            S.add("act", lambda e, ps=ps, h=h: e.activation(out=xpre[h][0][:, 3:515], in_=ps[0:64, :], func=AF.Identity),
                  reads=[t_ps], writes=[xpre[h][1]])
        ps, t_ps = proj(C_B, 128)
        S.add("act", lambda e, ps=ps: e.activation(out=bpre[:, 3:515], in_=ps[:, :], func=AF.Identity), reads=[t_ps], writes=[t_bpre])
        ps, t_ps = proj(C_C, 128)
        S.add("act", lambda e, ps=ps: e.activation(out=cpre[:, 3:515], in_=ps[:, :], func=AF.Identity), reads=[t_ps], writes=[t_cpre])
        ps, t_ps = proj(C_CB, 128)
        S.add("act", lambda e, ps=ps: e.activation(out=cBs[:], in_=ps[:, :], func=AF.Identity), reads=[t_ps], writes=[t_cBs])
        ps, t_ps = proj(C_CC, 128)
        S.add("act", lambda e, ps=ps: e.activation(out=cCs[:], in_=ps[:, :], func=AF.Identity), reads=[t_ps], writes=[t_cCs])
        ps, t_ps = proj(C_CX, 128)
        S.add("dve", lambda e, ps=ps: e.tensor_tensor(out=ucv[:, 2:514], in0=cCs[:], in1=ps[:, :], op=ALU.mult),
              reads=[t_ps, t_cCs], writes=[t_ucv])
        if STOP == 3:
            return C.finish()
        for s in range(4):
            ps, t_ps = C.psum()
            mm_group(S, ps[:, 0:128], [(hb[:, kc, s * 128:(s + 1) * 128], wB[:, kc, C_V:C_V + 128]) for kc in range(KC)],
                     thb + t_wB, [t_ps])
            if STOP != 8:
                mm_group(S, ps[:, 128:132], [(hb[:, kc, s * 128:(s + 1) * 128], wB[:, kc, C_V + 128:C_V + 132]) for kc in range(KC)],
                         thb + t_wB, [t_ps])
            if STOP == 4:
                continue
            S.add("act", lambda e, ps=ps, s=s, tt=tt: e.activation(out=vaug[:, 4 * tt + s, 0:128], in_=ps[:, 0:128], func=AF.Identity),
                  reads=[t_ps], writes=[t_v])
            if STOP in (5, 7):
                continue
            S.add("act", lambda e, ps=ps, s=s: e.activation(out=dtraw[:, s, :], in_=ps[:, 128:132], func=AF.Identity),
                  reads=[t_ps], writes=[t_dtr])
            S.add("pool", lambda e, s=s: e.tensor_tensor(out=dtraw[:, s, :], in0=dtraw[:, s, :], in1=spt[:, 16:20], op=ALU.add),
                  reads=[t_dtr, t_sp], writes=[t_dtr])
            if STOP == 6:
                continue
        if 'conv' in PARTS:
            a0, t_a0 = cacc[0]
            S.add("dve", lambda e: e.tensor_scalar(out=a0[:], in0=ucv[:, 0:512], scalar1=spt[:, 13:14], scalar2=None, op0=ALU.mult),
                  reads=[t_ucv, t_sp], writes=[t_a0])
            for j in (1, 2):
                S.add("dve", lambda e, j=j: e.scalar_tensor_tensor(
                    out=a0[:], in0=ucv[:, j:j + 512], scalar=spt[:, 13 + j:14 + j], in1=a0[:], op0=ALU.mult, op1=ALU.add),
                    reads=[t_ucv, t_sp, t_a0], writes=[t_a0])
            S.add("dve", lambda e: e.tensor_tensor(out=ycs[:], in0=cBs[:], in1=a0[:], op=ALU.mult), reads=[t_cBs, t_a0], writes=[t_ycs])
            S.add("sp", lambda e, tok0=tok0: e.dma_start(out=yc_ap[:, tok0:tok0 + 512], in_=ycs[:]), reads=[t_ycs], dma=True)
            S.add("dve", lambda e: e.tensor_copy(out=ucv[:, 0:2], in_=ucv[:, 512:514]), reads=[t_ucv], writes=[t_ucv])
        if 'ssd' in PARTS:
            def ssd_conv(pre, t_pre, P, wsrc, t_w, c0, outs):
                ac, t_ac = cacc[1]
                S.add("dve", lambda e: e.tensor_scalar(out=ac[0:P, :], in0=pre[:, 0:512], scalar1=wsrc[:, c0:c0 + 1], scalar2=None, op0=ALU.mult),
                      reads=[t_pre, t_w], writes=[t_ac])
                for j in (1, 2, 3):
                    S.add("dve", lambda e, j=j: e.scalar_tensor_tensor(
                        out=ac[0:P, :], in0=pre[:, j:j + 512], scalar=wsrc[:, c0 + j:c0 + j + 1], in1=ac[0:P, :], op0=ALU.mult, op1=ALU.add),
                        reads=[t_pre, t_w, t_ac], writes=[t_ac])
                o, t_o = outs
                S.add("act", lambda e: e.activation(out=o[:], in_=ac[0:P, :], func=AF.Silu, bias=wsrc[:, c0 + 4:c0 + 5], scale=1.0),
                      reads=[t_ac, t_w], writes=[t_o])
                S.add("dve", lambda e: e.tensor_copy(out=pre[:, 0:3], in_=pre[:, 512:515]), reads=[t_pre], writes=[t_pre])

            for h in range(4):
                ssd_conv(xpre[h][0], xpre[h][1], 64, sp64, t_sp64, h * 6, xsT[h])
            ssd_conv(bpre, t_bpre, 128, spt, t_sp, 3, (BT32, t_BT32))
            ssd_conv(cpre, t_cpre, 128, spt, t_sp, 8, (CT32, t_CT32))
            S.add("pool", lambda e: e.tensor_copy(out=BTb[:], in_=BT32[:]), reads=[t_BT32], writes=[t_BTb])
            S.add("pool", lambda e: e.tensor_copy(out=CTb[:], in_=CT32[:]), reads=[t_CT32], writes=[t_CTb])
            S.add("act", lambda e: e.activation(out=dtt[:], in_=dtraw[:], func=AF.Exp), reads=[t_dtr], writes=[t_dtt])
            S.add("act", lambda e: e.activation(out=dtt[:], in_=dtt[:], func=AF.Ln, bias=1.0, scale=1.0), reads=[t_dtt], writes=[t_dtt])
            for s in range(4):
                S.add("dve", lambda e, s=s: e.tensor_tensor(out=dtA[:, s, :], in0=dtt[:, s, :], in1=aneg[:], op=ALU.mult),
                      reads=[t_dtt, t_an], writes=[t_dtA])
            for cc in range(2):
                ccol = slice(cc * 256, (cc + 1) * 256)
                s0, s1 = 2 * cc, 2 * cc + 1
                psa, t_psa = C.psum()
                S.add("pe", lambda e, psa=psa, s0=s0: e.matmul(psa[:, 0:4], UTRI, dtA[:, s0, :], start=True, stop=True),
                      reads=[t_dtA, t_cst], writes=[t_psa])
                mm_group(S, psa[:, 4:8], [(ones, dtA[:, s0, :]), (UTRI, dtA[:, s1, :])], [t_dtA, t_cst], [t_psa])
                mm_group(S, psa[:, 8:12], [(ones, dtA[:, s0, :]), (ones, dtA[:, s1, :])], [t_dtA, t_cst], [t_psa])
                S.add("dve", lambda e, psa=psa: e.tensor_copy(out=acs[:], in_=psa[:, 0:12]), reads=[t_psa], writes=[t_acs])
                for s in range(2):
                    S.add("dve", lambda e, s=s: e.tensor_tensor(out=d2[:, 4 * s:4 * s + 4], in0=acs[:, 8:12], in1=acs[:, 4 * s:4 * s + 4], op=ALU.subtract),
                          reads=[t_acs], writes=[t_d2])
                S.add("act", lambda e: e.activation(out=d2[:], in_=d2[:], func=AF.Exp), reads=[t_d2], writes=[t_d2])
                for s in range(2):
                    S.add("dve", lambda e, s=s, cc=cc: e.tensor_tensor(out=wd[:, 2 * cc + s, :], in0=dtt[:, 2 * cc + s, :], in1=d2[:, 4 * s:4 * s + 4], op=ALU.mult),
                          reads=[t_dtt, t_d2], writes=[t_wd])
                pst, t_pst = C.psum()
                mm_group(S, pst[0:4, 0:256], [(dtA[:, s0, :], U0), (dtA[:, s1, :], U1)], [t_dtA, t_cst], [t_pst])
                S.add("dve", lambda e, pst=pst: e.tensor_copy(out=acsT[:], in_=pst[0:4, 0:256]), reads=[t_pst], writes=[t_acsT])
                for s in (s0, s1):
                    psx, t_psx = C.psum()

                    def tfn(e, psx=psx, s=s):
                        ins = None
                        for h in range(4):
                            ins = e.transpose(out=psx[:, h * 64:(h + 1) * 64], in_=xsT[h][0][:, s * 128:(s + 1) * 128], identity=cst[0:64, 0:64])
                        return ins
                    S.add("pe", tfn, reads=[x[1] for x in xsT] + [t_cst], writes=[t_psx])
                    for h in range(4):
                        S.add("dve", lambda e, psx=psx, s=s, h=h: e.tensor_scalar(
                            out=xd[s][0][:, h * 64:(h + 1) * 64], in0=psx[:, h * 64:(h + 1) * 64], scalar1=dtt[:, s, h:h + 1], scalar2=None, op0=ALU.mult),
                            reads=[t_psx, t_dtt], writes=[xd[s][1]])
                        S.add("dve", lambda e, psx=psx, s=s, h=h: e.tensor_scalar(
                            out=xdd[s][0][:, h * 64:(h + 1) * 64], in0=psx[:, h * 64:(h + 1) * 64], scalar1=wd[:, s, h:h + 1], scalar2=None, op0=ALU.mult),
                            reads=[t_psx, t_wd], writes=[xdd[s][1]])
                    psb, t_psb = C.psum()
                    S.add("pe", lambda e, psb=psb, s=s: e.transpose(out=psb[:, 0:128], in_=BT32[:, s * 128:(s + 1) * 128], identity=ident),
                          reads=[t_BT32, t_cst], writes=[t_psb])
                    S.add("act", lambda e, psb=psb, s=s: e.activation(out=Btok[s][0][:], in_=psb[:, 0:128], func=AF.Identity),
                          reads=[t_psb], writes=[Btok[s][1]])
                psc, t_psc = C.psum()
                S.add("pe", lambda e, psc=psc, cc=cc: e.matmul(psc[:, 0:256], BTb[:, cc * 256:cc * 256 + 128], CTb[:, cc * 256:cc * 256 + 256], start=True, stop=True),
                      reads=[t_BTb, t_CTb], writes=[t_psc])
                S.add("pe", lambda e, psc=psc, cc=cc: e.matmul(psc[:, 256:384], BTb[:, cc * 256 + 128:cc * 256 + 256], CTb[:, cc * 256 + 128:cc * 256 + 256], start=True, stop=True),
                      reads=[t_BTb, t_CTb], writes=[t_psc])
                S.add("dve", lambda e, psc=psc: e.tensor_tensor(out=cbm[:, 0:256], in0=psc[:, 0:256], in1=U0, op=ALU.mult),
                      reads=[t_psc, t_cst], writes=[t_cbm])
                S.add("dve", lambda e, psc=psc: e.tensor_tensor(out=cbm[:, 256:384], in0=psc[:, 256:384], in1=UTRI, op=ALU.mult),
                      reads=[t_psc, t_cst], writes=[t_cbm])
                for h in range(4):
                    psbc, t_psbc = C.psum()
                    S.add("pe", lambda e, psbc=psbc, h=h: e.matmul(psbc[:, 0:256], SELH[:, h * 128:(h + 1) * 128], acsT[:], start=True, stop=True),
                          reads=[t_acsT, t_cst], writes=[t_psbc])
                    Eh, t_Eh = E[h]
                    S.add("act", lambda e, psbc=psbc, Eh=Eh: e.activation(out=Eh[:], in_=psbc[:, 0:256], func=AF.Exp), reads=[t_psbc], writes=[t_Eh])
                    S.add("pool", lambda e, Eh=Eh, ccol=ccol: e.tensor_tensor(out=Cdec[:], in0=CT32[:, ccol], in1=Eh[:], op=ALU.mult),
                          reads=[t_CT32, t_Eh], writes=[t_Cdec])
                    S.add("dve", lambda e, psbc=psbc, h=h: e.tensor_scalar(
                        out=Dm[:, 0:256], in0=psbc[:, 0:256], scalar1=acs[:, h:h + 1], scalar2=0.0, op0=ALU.subtract, op1=ALU.min),
                        reads=[t_psbc, t_acs], writes=[t_Dm])
                    S.add("dve", lambda e, psbc=psbc, h=h: e.tensor_scalar(
                        out=Dm[:, 256:384], in0=psbc[:, 128:256], scalar1=acs[:, 4 + h:5 + h], scalar2=0.0, op0=ALU.subtract, op1=ALU.min),
                        reads=[t_psbc, t_acs], writes=[t_Dm])
                    S.add("act", lambda e: e.activation(out=Dm[:], in_=Dm[:], func=AF.Exp), reads=[t_Dm], writes=[t_Dm])
                    S.add("dve", lambda e: e.tensor_tensor(out=MT[:], in0=Dm[:], in1=cbm[:], op=ALU.mult), reads=[t_Dm, t_cbm], writes=[t_MT])
                    psy, t_psy = C.psum()
                    hs = slice(h * 64, (h + 1) * 64)

                    def yfn(e, psy=psy, hs=hs, s0=s0, s1=s1):
                        e.matmul(psy[0:64, 0:256], prevb[:, hs], Cdec[:], start=True, stop=False)
                        e.matmul(psy[0:64, 0:256], xd[s0][0][:, hs], MT[:, 0:256], start=False, stop=False)
                        return e.matmul(psy[0:64, 128:256], xd[s1][0][:, hs], MT[:, 256:384], start=False, stop=True)
                    S.add("pe", yfn, reads=[t_pb, t_Cdec, xd[s0][1], xd[s1][1], t_MT], writes=[t_psy])
                    S.add("dve", lambda e, psy=psy, h=h, ccol=ccol: e.scalar_tensor_tensor(
                        out=yt1[:], in0=xsT[h][0][:, ccol], scalar=sp64[:, h * 6 + 5:h * 6 + 6], in1=psy[0:64, 0:256], op0=ALU.mult, op1=ALU.add),
                        reads=[xsT[h][1], t_sp64, t_psy], writes=[t_yt1])
                    yg, t_yg = ygs[nyg % 2]
                    nyg += 1
                    S.add("pool", lambda e, yg=yg, h=h, ccol=ccol: e.tensor_tensor(out=yg[:], in0=yt1[:], in1=sz[h][0][:, ccol], op=ALU.mult),
                          reads=[t_yt1, sz[h][1]], writes=[t_yg])
                    S.add("sp", lambda e, yg=yg, h=h, tok0=tok0, cc=cc: e.dma_start(
                        out=yg_ap[h * 64:(h + 1) * 64, tok0 + cc * 256:tok0 + (cc + 1) * 256], in_=yg[:]), reads=[t_yg], dma=True)
                pss, t_pss = C.psum()
                mm_group(S, pss[:, 0:256], [(Btok[s0][0][:], xdd[s0][0][:]), (Btok[s1][0][:], xdd[s1][0][:])],
                         [Btok[s0][1], Btok[s1][1], xdd[s0][1], xdd[s1][1]], [t_pss])
                for h in range(4):
                    hs = slice(h * 64, (h + 1) * 64)
                    S.add("dve", lambda e, h=h, hs=hs, pss=pss: e.scalar_tensor_tensor(
                        out=prev32[:, hs], in0=prev32[:, hs], scalar=E[h][0][:, 255:256], in1=pss[:, hs], op0=ALU.mult, op1=ALU.add),
                        reads=[t_p32, E[h][1], t_pss], writes=[t_p32])
                S.add("act", lambda e: e.activation(out=prevb[:], in_=prev32[:], func=AF.Identity), reads=[t_p32], writes=[t_pb])
        if 'attn' in PARTS:
            for i in range(4):
                qi = 4 * tt + i
                J = qi // 2
                eo = qi % 2
                qc = slice(i * 128, (i + 1) * 128)
                use_sel = J > 3
                if use_sel:
                    psg, t_psg = C.psum()
                    S.add("pe", lambda e, psg=psg, qc=qc: e.matmul(psg[:, 0:32], qn32[:, qc], kmean[:], start=True, stop=True),
                          reads=[t_qn, t_km], writes=[t_psg])
                    S.add("pool", lambda e: e.memset(gsb[:], NEG), writes=[t_gsb])
                    S.add("dve", lambda e, psg=psg, J=J: e.tensor_copy(out=gsb[:, 0:J], in_=psg[:, 0:J]), reads=[t_psg], writes=[t_gsb])
                    S.add("dve", lambda e: e.max(out=top8[:], in_=gsb[:]), reads=[t_gsb], writes=[t_top8])
                    S.add("dve", lambda e: e.tensor_scalar(out=msel[:], in0=gsb[:], scalar1=top8[:, 2:3], scalar2=None, op0=ALU.is_ge),
                          reads=[t_gsb, t_top8], writes=[t_msel])
                blocks = [J] + list(range(J))
                for bi, n in enumerate(blocks):
                    if n == J:
                        halves = [(0, "A")] if eo == 0 else [(0, "B"), (1, "A")]
                    elif n == J - 1:
                        halves = [(0, "c"), (1, "B")] if eo == 0 else [(0, "c"), (1, "c")]
                    else:
                        halves = [(0, "c"), (1, "c")]
                    pss2, t_pss2 = C.psum()

                    def sfn(e, pss2=pss2, halves=halves, n=n, qc=qc):
                        ins = None
                        for hh, _k in halves:
                            ins = e.matmul(pss2[:, hh * 128:(hh + 1) * 128], KT[:, n * 256 + hh * 128:n * 256 + (hh + 1) * 128], qnb[:, qc], start=True, stop=True)
                        return ins
                    S.add("pe", sfn, reads=[t_KT, t_qnb], writes=[t_pss2])
                    if STOP == 10:
                        continue
                    for hh, kind in halves:
                        if (STOP == 14 and kind != "c") or (STOP == 15 and kind == "c"):
                            continue
                        if kind == "c":
                            S.add("act", lambda e, pss2=pss2, hh=hh: e.activation(
                                out=pt[:, hh, :], in_=pss2[:, hh * 128:(hh + 1) * 128], func=AF.Exp, bias=spt[:, 2:3], scale=SCALE),
                                reads=[t_pss2, t_sp], writes=[t_pt])
                        else:
                            bt_ = TA if kind == "A" else TB
                            S.add("dve", lambda e, pss2=pss2, hh=hh, bt_=bt_: e.scalar_tensor_tensor(
                                out=tmpS[:, hh * 128:(hh + 1) * 128], in0=pss2[:, hh * 128:(hh + 1) * 128], scalar=SCALE, in1=bt_, op0=ALU.mult, op1=ALU.add),
                                reads=[t_pss2, t_cst], writes=[t_tmpS])
                            S.add("act", lambda e, hh=hh: e.activation(out=pt[:, hh, :], in_=tmpS[:, hh * 128:(hh + 1) * 128], func=AF.Exp),
                                  reads=[t_tmpS], writes=[t_pt])
                    if STOP in (11, 14, 15):
                        continue
                    pso, t_pso = C.psum()
                    nh = len(halves)

                    def ofn(e, pso=pso, halves=halves, n=n, nh=nh):
                        ins = None
                        for ii, (hh, _k) in enumerate(halves):
                            ins = e.matmul(pso[:, 0:130], pt[:, hh, :], vaug[:, 2 * n + hh, :], start=(ii == 0), stop=(ii == nh - 1))
                        return ins
                    S.add("pe", ofn, reads=[t_pt, t_v], writes=[t_pso])
                    if STOP == 12:
                        continue
                    if bi == 0:
                        S.add("act", lambda e, pso=pso: e.activation(out=acc[:], in_=pso[:, 0:130], func=AF.Identity), reads=[t_pso], writes=[t_acc])
                    else:
                        osb_, t_osb = osb[nosb % 2]
                        nosb += 1
                        S.add("act", lambda e, pso=pso, osb_=osb_: e.activation(out=osb_[:], in_=pso[:, 0:130], func=AF.Identity),
                              reads=[t_pso], writes=[t_osb])
                        sc_ = msel[:, n:n + 1] if use_sel else 1.0
                        S.add("dve", lambda e, osb_=osb_, sc_=sc_: e.scalar_tensor_tensor(
                            out=acc[:], in0=osb_[:], scalar=sc_, in1=acc[:], op0=ALU.mult, op1=ALU.add),
                            reads=[t_osb, t_msel, t_acc], writes=[t_acc])
                if STOP in (10, 11, 12, 13, 14, 15):
                    continue
                S.add("dve", lambda e: e.reciprocal(out=rec[:], in_=acc[:, 128:129]), reads=[t_acc], writes=[t_rec])
                yq_, t_yq = yq[nyq % 2]
                nyq += 1
                S.add("dve", lambda e, yq_=yq_: e.tensor_scalar(out=yq_[:], in0=acc[:, 0:128], scalar1=rec[:, 0:1], scalar2=None, op0=ALU.mult),
                      reads=[t_acc, t_rec], writes=[t_yq])
                S.add("sp", lambda e, yq_=yq_, qi=qi: e.dma_start(out=ya_ap[qi * 128:(qi + 1) * 128, :], in_=yq_[:]), reads=[t_yq], dma=True)
    return C.finish()


def t5_bucket_np(dist):
    n = np.maximum(dist, 0)
    max_exact = 16
    ratio = np.log(np.maximum(n, 1).astype(np.float32) / max_exact) / np.float32(np.log(128 / max_exact))
    large = max_exact + (ratio * (32 - max_exact)).astype(np.int32)
    large = np.minimum(large, 31)
    return np.where(n < max_exact, n, large)


def mix_consts(rel_bias, head):
    kk = np.arange(128)[:, None]
    qq = np.arange(128)[None, :]
    cst = np.zeros((128, 5 * 128 + 1024), np.float32)
    cst[:, 0:128] = np.eye(128, dtype=np.float32)
    cst[:, 128:256] = 1.0
    cst[:, 256:384] = rel_bias[t5_bucket_np(qq - kk), head]
    cst[:, 384:512] = rel_bias[t5_bucket_np(128 + qq - kk), head]
    cst[:, 512:640] = np.where(qq >= kk, 0.0, NEG)
    utri = (kk <= qq).astype(np.float32)
    cst[:, 640:768] = utri
    cst[:, 768:896] = 1.0
    cst[:, 896:1024] = 0.0
    cst[:, 1024:1152] = utri
    for h in range(4):
        cst[h, 1152 + h * 128:1152 + (h + 1) * 128] = 1.0
    return cst


def run_mixB(hT, l, inp, NT=16):
    w = inp["w_mix_in"][l]
    in_maps = []
    for c in range(NCORES):
        g = c // 2
        cols = np.concatenate([
            np.arange(c * 128, (c + 1) * 128), 1024 + np.arange(c * 128, (c + 1) * 128),
            3072 + np.arange(c * 256, (c + 1) * 256), 5120 + np.arange(c * 256, (c + 1) * 256),
            5120 + 2048 + np.arange(g * 128, (g + 1) * 128), 5120 + 2560 + np.arange(g * 128, (g + 1) * 128),
            8224 + np.arange(c * 128, (c + 1) * 128), 9248 + np.arange(c * 128, (c + 1) * 128),
            10272 + np.arange(c * 128, (c + 1) * 128), 2048 + np.arange(c * 128, (c + 1) * 128),
            8192 + np.arange(c * 4, (c + 1) * 4)])
        wB = np.ascontiguousarray(w[:, cols].reshape(KC, 128, NCOLB).transpose(1, 0, 2))
        sp = np.zeros((128, 24), np.float32)
        sp[:, 0] = inp["qk_norm"][l, 0]
        sp[:, 1] = inp["qk_norm"][l, 1]
        sp[:, 2] = inp["rel_bias"][31, c]
        wc = inp["w_ssd_conv"][l]
        bc = inp["b_ssd_conv"][l]
        chB = 2048 + g * 128 + np.arange(128)
        chC = 2560 + g * 128 + np.arange(128)
        sp[:, 3:7] = wc[:, chB].T
        sp[:, 7] = bc[chB]
        sp[:, 8:12] = wc[:, chC].T
        sp[:, 12] = bc[chC]
        sp[:, 13:16] = inp["w_sc_conv"][l][:, c * 128:(c + 1) * 128].T
        sp[:, 16:20] = inp["ssd_dt_bias"][l][None, 4 * c:4 * c + 4]
        sp[:, 20:24] = inp["ssd_a_log"][l][None, 4 * c:4 * c + 4]
        sp64 = np.zeros((64, 24), np.float32)
        for h in range(4):
            ch = c * 256 + h * 64 + np.arange(64)
            sp64[:, h * 6:h * 6 + 4] = wc[:, ch].T
            sp64[:, h * 6 + 4] = bc[ch]
            sp64[:, h * 6 + 5] = inp["ssd_d"][l][4 * c + h]
        in_maps.append({"hT": hT, "wB": wB, "sp": sp, "sp64": sp64, "cst": mix_consts(inp["rel_bias"], c)})
    r = run(prog("mixB", build_mixB, NT), in_maps)
    ntok = NT * 512
    YT = np.zeros((4096, ntok), np.float32)
    for c in range(NCORES):
        YT[c * 128:(c + 1) * 128] = r[c]["ya"].T
        YT[1024 + c * 256:1024 + (c + 1) * 256] = r[c]["yg"]
        YT[3072 + c * 128:3072 + (c + 1) * 128] = r[c]["yc"]
    return YT


NYC = 32
BR_CH = ((0, 8), (8, 24), (24, 32))


def build_mixC():
    C = Ctx()
    S = C.S
    x_ap = C.din("xT", [D, TOK])
    h_ap = C.din("hT", [D, TOK])
    y_ap = C.din("YT", [4096, TOK])
    wg_ap = C.din("wg", [KC, 128, KC, 384])
    wbr_ap = C.din("wbr", [KC, 128, NYC, 128])
    wo_ap = C.din("wo", [KC, 128, KC, 128])
    par_ap = C.din("par", [128, 2 * KC])
    ones_ap = C.din("ones", [128, 128])
    xo_ap = C.dout("xo", [D, TOK])
    C.init_psum()
    ones, t_ones = load_consts(C, ones_ap)
    par, t_par = C.sb([128, 2 * KC], F32, "par")
    S.add("sp", lambda e: e.dma_start(out=par[:], in_=par_ap), writes=[t_par], dma=True)
    xT, _ = C.sb([128, KC, 512], F32, "xT")
    t_x = [T("x%d" % k) for k in range(KC)]
    hb, _ = C.sb([128, KC, 512], BF16, "hb")
    t_h = [T("h%d" % k) for k in range(KC)]
    Yb, _ = C.sb([128, NYC, 512], BF16, "Yb")
    t_y = [T("y%d" % k) for k in range(NYC)]
    stg = [C.sb([128, 512], F32, "stg") for _ in range(4)]
    sq = [C.sb([128, 512], F32, "sq") for _ in range(2)]
    rs, t_rs = C.sb([128, 512], F32, "rs")
    tmpn = [C.sb([128, 512], F32, "tmpn") for _ in range(2)]
    mg, t_mg = C.sb([128, 512], F32, "mg")
    sig = [C.sb([128, 512], F32, "sig") for _ in range(2)]
    tmpm, t_tmpm = C.sb([128, 512], F32, "tmpm")
    mT, _ = C.sb([128, KC, 512], BF16, "mT")
    t_m = [T("m%d" % k) for k in range(KC)]
    wg = [C.sb([128, KC, 384], BF16, "wg") for _ in range(2)]
    wbr = [C.sb([128, NYC, 128], BF16, "wbr") for _ in range(2)]
    wo = [C.sb([128, KC, 128], BF16, "wo") for _ in range(2)]
    nsig = 0
    nw = 0
    nwo = 0
    for tt in range(TOK // 512):
        sl = slice(tt * 512, (tt + 1) * 512)
        for kc in range(KC):
            S.add("sp", lambda e, kc=kc, sl=sl: e.dma_start(out=xT[:, kc, :], in_=x_ap[kc * 128:(kc + 1) * 128, sl]),
                  writes=[t_x[kc]], dma=True)
            S.add("pool", lambda e, kc=kc, sl=sl: e.dma_start(out=hb[:, kc, :], in_=h_ap[kc * 128:(kc + 1) * 128, sl]),
                  writes=[t_h[kc]], dma=True)
        for ch in list(range(0, 8)) + list(range(24, 32)):
            S.add("pool", lambda e, ch=ch, sl=sl: e.dma_start(out=Yb[:, ch, :], in_=y_ap[ch * 128:(ch + 1) * 128, sl]),
                  writes=[t_y[ch]], dma=True)
        for gq in range(4):
            ps, t_ps = C.psum()
            for c4 in range(4):
                ch = 8 + gq * 4 + c4
                st_, t_st = stg[c4]
                S.add("sp", lambda e, st_=st_, ch=ch, sl=sl: e.dma_start(out=st_[:], in_=y_ap[ch * 128:(ch + 1) * 128, sl]),
                      writes=[t_st], dma=True)
                sq_, t_sq = sq[c4 % 2]
                S.add("act", lambda e, sq_=sq_, st_=st_: e.activation(out=sq_[:], in_=st_[:], func=AF.Square), reads=[t_st], writes=[t_sq])
                S.add("pe", lambda e, ps=ps, sq_=sq_, c4=c4: e.matmul(ps[:, :], ones[:], sq_[:], start=(c4 == 0), stop=(c4 == 3)),
                      reads=[t_sq, t_ones], writes=[t_ps])
            S.add("act", lambda e, ps=ps: e.activation(out=rs[:], in_=ps[:, :], func=AF.Sqrt, scale=1.0 / 512, bias=EPS),
                  reads=[t_ps], writes=[t_rs])
            S.add("dve", lambda e: e.reciprocal(out=rs[:], in_=rs[:]), reads=[t_rs], writes=[t_rs])
            for c4 in range(4):
                ch = 8 + gq * 4 + c4
                st_, t_st = stg[c4]
                S.add("dve", lambda e, st_=st_, ch=ch: e.scalar_tensor_tensor(
                    out=Yb[:, ch, :], in0=st_[:], scalar=par[:, ch - 8:ch - 7], in1=rs[:], op0=ALU.mult, op1=ALU.mult),
                    reads=[t_st, t_rs, t_par], writes=[t_y[ch]])
        for m in range(KC):
            wg_, t_wg = wg[nw % 2]
            wbr_, t_wbr = wbr[nw % 2]
            nw += 1
            S.add("pool", lambda e, wg_=wg_, m=m: e.dma_start(out=wg_[:], in_=wg_ap[m]), writes=[t_wg], dma=True)
            S.add("pool", lambda e, wbr_=wbr_, m=m: e.dma_start(out=wbr_[:], in_=wbr_ap[m]), writes=[t_wbr], dma=True)
            for b in range(3):
                pg, t_pg = C.psum()
                mm_group(S, pg[:, :], [(wg_[:, kc, b * 128:(b + 1) * 128], hb[:, kc, :]) for kc in range(KC)], t_h + [t_wg], [t_pg])
                sg_, t_sg = sig[nsig % 2]
                nsig += 1
                S.add("act", lambda e, sg_=sg_, pg=pg: e.activation(out=sg_[:], in_=pg[:, :], func=AF.Sigmoid), reads=[t_pg], writes=[t_sg])
                pb, t_pb = C.psum()
                c0, c1 = BR_CH[b]
                mm_group(S, pb[:, :], [(wbr_[:, ch, :], Yb[:, ch, :]) for ch in range(c0, c1)], t_y[c0:c1] + [t_wbr], [t_pb])
                if b == 0:
                    S.add("dve", lambda e, sg_=sg_, pb=pb: e.tensor_tensor(out=mg[:], in0=sg_[:], in1=pb[:, :], op=ALU.mult),
                          reads=[t_sg, t_pb], writes=[t_mg])
                else:
                    S.add("dve", lambda e, sg_=sg_, pb=pb: e.tensor_tensor(out=tmpm[:], in0=sg_[:], in1=pb[:, :], op=ALU.mult),
                          reads=[t_sg, t_pb], writes=[t_tmpm])
                    if b == 1:
                        S.add("dve", lambda e: e.tensor_tensor(out=mg[:], in0=mg[:], in1=tmpm[:], op=ALU.add),
                              reads=[t_mg, t_tmpm], writes=[t_mg])
                    else:
                        S.add("dve", lambda e, m=m: e.tensor_tensor(out=mT[:, m, :], in0=mg[:], in1=tmpm[:], op=ALU.add),
                              reads=[t_mg, t_tmpm], writes=[t_m[m]])
        for m in range(KC):
            wo_, t_wo = wo[nwo % 2]
            nwo += 1
            S.add("pool", lambda e, wo_=wo_, m=m: e.dma_start(out=wo_[:], in_=wo_ap[m]), writes=[t_wo], dma=True)
            po, t_po = C.psum()
            mm_group(S, po[:, :], [(wo_[:, kc, :], mT[:, kc, :]) for kc in range(KC)], t_m + [t_wo], [t_po])
            S.add("dve", lambda e, po=po, m=m: e.scalar_tensor_tensor(
                out=xT[:, m, :], in0=po[:, :], scalar=par[:, KC + m:KC + m + 1], in1=xT[:, m, :], op0=ALU.mult, op1=ALU.add),
                reads=[t_po, t_par, t_x[m]], writes=[t_x[m]])
            S.add("sp", lambda e, m=m, sl=sl: e.dma_start(out=xo_ap[m * 128:(m + 1) * 128, sl], in_=xT[:, m, :]),
                  reads=[t_x[m]], dma=True)
    return C.finish()


def run_mixC(xT, hT, YT, l, inp, gate1):
    w = inp["w_mix_in"][l]
    G0 = 11296
    wg = np.stack([w[:, G0 + b * 2048:G0 + (b + 1) * 2048].reshape(KC, 128, KC, 128) for b in range(3)], axis=0)
    wg_t = np.ascontiguousarray(wg.transpose(3, 2, 1, 0, 4).reshape(KC, 128, KC, 384))
    wbr = np.concatenate([inp["w_br_attn"][l], inp["w_br_ssd"][l], inp["w_br_conv"][l]], axis=0)
    wbr_t = np.ascontiguousarray(wbr.reshape(NYC, 128, KC, 128).transpose(2, 1, 0, 3))
    wo_t = np.ascontiguousarray(inp["w_mix_out"][l].reshape(KC, 128, KC, 128).transpose(2, 1, 0, 3))
    par = np.concatenate([cols16(inp["ssd_norm"][l]), cols16(gate1)], axis=1)
    ones = np.ones((128, 128), np.float32)
    in_maps = []
    for i in range(NCORES):
        ts = slice(i * TOK, (i + 1) * TOK)
        in_maps.append({"xT": np.ascontiguousarray(xT[:, ts]), "hT": np.ascontiguousarray(hT[:, ts]),
                        "YT": np.ascontiguousarray(YT[:, ts]), "wg": wg_t, "wbr": wbr_t, "wo": wo_t, "par": par, "ones": ones})
    r = run(prog("mixC", build_mixC), in_maps)
    return np.concatenate([r[i]["xo"] for i in range(NCORES)], axis=1)


def kernel(x, c, w_ada, b_ada, norm_gain, w_ffn_in, w_ffn_out, w_mix_in, qk_norm, rel_bias,
           w_ssd_conv, b_ssd_conv, ssd_dt_bias, ssd_a_log, ssd_d, ssd_norm, w_sc_conv,
           w_br_attn, w_br_ssd, w_br_conv, w_mix_out):
    inp = dict(w_mix_in=w_mix_in, qk_norm=qk_norm, rel_bias=rel_bias, w_ssd_conv=w_ssd_conv, b_ssd_conv=b_ssd_conv,
               ssd_dt_bias=ssd_dt_bias, ssd_a_log=ssd_a_log, ssd_d=ssd_d, ssd_norm=ssd_norm, w_sc_conv=w_sc_conv,
               w_br_attn=w_br_attn, w_br_ssd=w_br_ssd, w_br_conv=w_br_conv, w_mix_out=w_mix_out)
    inp = {k: np.asarray(v, np.float32) for k, v in inp.items()}
    w_ada = np.asarray(w_ada, np.float32)
    b_ada = np.asarray(b_ada, np.float32)
    norm_gain = np.asarray(norm_gain, np.float32)
    w_ffn_in = np.asarray(w_ffn_in, np.float32)
    w_ffn_out = np.asarray(w_ffn_out, np.float32)
    ada = compute_ada(np.asarray(c, np.float32), w_ada, b_ada)
    xT = np.ascontiguousarray(np.asarray(x, np.float32)[0].T)
    for l in range(2):
        par = np.concatenate([par_block(norm_gain[l, 0], ada[l, 0]), par_block(norm_gain[l, 1], ada[l, 1])], axis=1)
        xT, hT = run_ffn(xT, par, w_ffn_in[l, 0], w_ffn_out[l, 0], True)
        YT = run_mixB(hT, l, inp, NT=16)
        xT = run_mixC(xT, hT, YT, l, inp, ada[l, 1, 2])
        par = par_block(norm_gain[l, 2], ada[l, 2])
        xT, _ = run_ffn(xT, par, w_ffn_in[l, 1], w_ffn_out[l, 1], False)
    return np.ascontiguousarray(xT.T)[None].astype(np.float32)
```

```python
import contextlib
import threading
import numpy as np
import concourse.bass as bass
import concourse.mybir as mybir
from concourse.bass_utils import run_bass_kernel_spmd

F32 = mybir.dt.float32
BF16 = mybir.dt.bfloat16
AF = mybir.ActivationFunctionType
ALU = mybir.AluOpType
AX = mybir.AxisListType

NCORES = 8
D = 2048
KC = 16
SEQ = 8192
TOK = SEQ // NCORES
FFH = 5632
NJ = FFH // 128
NQ = 4
JQ = NJ // NQ
EPS = 1e-6


class T:
    __slots__ = ("name", "last_w", "readers", "excl")

    def __init__(self, name="", excl=False):
        self.name = name
        self.last_w = None
        self.readers = []
        self.excl = excl


class Op:
    __slots__ = ("eng", "fn", "deps", "is_dma", "sig", "sigval", "dsem", "dval", "dprev")


ENGS = ("pe", "act", "dve", "pool", "sp")
N_DMA_SEMS = {"pe": 1, "act": 1, "dve": 1, "pool": 16, "sp": 8}


class Sched:
    def __init__(self, nc):
        self.nc = nc
        self.ops = []
        self.dma_rr = {e: 0 for e in ENGS}
        self.dma_cnt = {}
        self.hook = None

    def add(self, eng, fn, reads=(), writes=(), dma=False):
        op = Op()
        op.eng = eng
        op.fn = fn
        op.is_dma = dma
        op.sig = False
        op.sigval = None
        deps = []
        excl_r = [t for t in reads if t.excl and t not in writes]
        reads = [t for t in reads if not t.excl]
        writes = list(writes) + excl_r
        for t in reads:
            if t.last_w is not None:
                deps.append(t.last_w)
        for t in writes:
            if t.last_w is not None:
                deps.append(t.last_w)
            deps.extend(t.readers)
        for t in reads:
            t.readers.append(op)
        for t in writes:
            t.last_w = op
            t.readers = []
        op.deps = [d for d in dict.fromkeys(deps) if d is not op]
        if dma:
            k = self.dma_rr[eng]
            self.dma_rr[eng] = (k + 1) % N_DMA_SEMS[eng]
            key = (eng, k)
            c = self.dma_cnt.get(key, 0) + 1
            self.dma_cnt[key] = c
            op.dsem = key
            op.dval = 16 * c
            op.dprev = 16 * (c - 1)
        self.ops.append(op)
        if self.hook is not None:
            self.hook()
        return op

    def emit(self):
        nc = self.nc
        ops = self.ops
        for op in ops:
            for d in op.deps:
                if d.is_dma:
                    continue
                if d.eng == op.eng and d.eng == "pe" and not op.is_dma:
                    continue
                d.sig = True
        cnt = {e: 0 for e in ENGS}
        for op in ops:
            if not op.is_dma and op.sig:
                cnt[op.eng] += 1
                op.sigval = cnt[op.eng]
        with contextlib.ExitStack() as st:
            esem = {e: st.enter_context(nc.semaphore("s_" + e)) for e in ENGS}
            dsem = {}
            for key in self.dma_cnt:
                dsem[key] = st.enter_context(nc.semaphore("d_%s%d" % key))
            block = st.enter_context(nc.Block())
            by_eng = {e: [o for o in ops if o.eng == e] for e in ENGS}

            def run(eng_name, eng):
                known = {}

                def wait(sem_key, sem, val):
                    if known.get(sem_key, 0) >= val:
                        return
                    known[sem_key] = val
                    eng.wait_ge(sem, val)

                for op in by_eng[eng_name]:
                    for d in op.deps:
                        if d.is_dma:
                            wait(d.dsem, dsem[d.dsem], d.dval)
                        elif d.sigval is not None:
                            wait(d.eng, esem[d.eng], d.sigval)
                    if op.is_dma:
                        if op.dprev > 0:
                            wait(op.dsem, dsem[op.dsem], op.dprev)
                        op.fn(eng).then_inc(dsem[op.dsem], 16)
                    else:
                        ins = op.fn(eng)
                        if op.sig:
                            ins.then_inc(esem[op.eng], 1)
                if eng_name == "sp":
                    for key, c in self.dma_cnt.items():
                        wait(key, dsem[key], 16 * c)

            block.tensor(lambda e: run("pe", e))
            block.scalar(lambda e: run("act", e))
            block.vector(lambda e: run("dve", e))
            block.gpsimd(lambda e: run("pool", e))
            block.sync(lambda e: run("sp", e))


class Ctx:
    def __init__(self):
        self.nc = bass.Bass("TRN2", target_bir_lowering=False)
        self.S = Sched(self.nc)
        self.st = contextlib.ExitStack()
        self.ps = []
        self.ps_i = 0
        self.n = 0
        self.tls = threading.local()
        self.pool_i = {1: 0, 2: 0}

    def din(self, name, shape, dt=F32):
        return self.nc.dram_tensor(name, list(shape), dt, kind="ExternalInput").ap()

    def dout(self, name, shape, dt=F32):
        return self.nc.dram_tensor(name, list(shape), dt, kind="ExternalOutput").ap()

    def sb(self, shape, dt, name=None):
        self.n += 1
        t = self.st.enter_context(self.nc.sbuf_tensor("%s_%d" % (name or "t", self.n), list(shape), dt))
        return t, T(name or "t")

    def init_psum(self):
        for i in range(8):
            t = self.st.enter_context(self.nc.psum_tensor("ps%d" % i, [128, 512], F32))
            self.ps.append((t, T("ps%d" % i, excl=True)))

    def psum(self):
        pool = getattr(self.tls, "pool", 0)
        if pool == 0:
            r = self.ps[self.ps_i]
            self.ps_i = (self.ps_i + 1) % 8
            return r
        k = self.pool_i[pool]
        self.pool_i[pool] = (k + 1) % 4
        return self.ps[(pool - 1) * 4 + k]

    def finish(self):
        self.S.emit()
        self.st.close()
        return self.nc


def interleave(C, fns):
    S = C.S
    n = len(fns)
    if n == 0:
        return
    if n == 1:
        fns[0]()
        return
    sems = [threading.Semaphore(0) for _ in range(n)]
    alive = [True] * n
    idx = {}
    err = []
    done = threading.Event()

    def nxt(i):
        for k in range(1, n + 1):
            j = (i + k) % n
            if alive[j]:
                return j
        return None

    def hook():
        i = idx[threading.get_ident()]
        j = nxt(i)
        if j is not None and j != i:
            sems[j].release()
            sems[i].acquire()

    def worker(i):
        sems[i].acquire()
        idx[threading.get_ident()] = i
        try:
            C.tls.pool = i + 1
            fns[i]()
        except BaseException as ex:
            err.append(ex)
        finally:
            alive[i] = False
            j = nxt(i)
            if j is not None:
                sems[j].release()
            else:
                done.set()

    ths = [threading.Thread(target=worker, args=(i,)) for i in range(n)]
    S.hook = hook
    for t in ths:
        t.start()
    sems[0].release()
    done.wait()
    for t in ths:
        t.join()
    S.hook = None
    if err:
        raise err[0]


def mm_group(S, out_ap, pairs, reads, writes):
    n = len(pairs)

    def fn(e):
        ins = None
        for i, (l, r) in enumerate(pairs):
            ins = e.matmul(out_ap, l, r, start=(i == 0), stop=(i == n - 1))
        return ins

    return S.add("pe", fn, reads=reads, writes=writes)


def load_consts(C, ones_ap):
    ones, t_ones = C.sb([128, 128], F32, "ones")
    C.S.add("sp", lambda e: e.dma_start(out=ones[:], in_=ones_ap), writes=[t_ones], dma=True)
    return ones, t_ones


def modulate(C, xT, t_x, hT, t_h, a_col, shift_col, t_par, ones, t_ones, ntok, out_dram=None):
    S = C.S
    sq = [C.sb([128, 512], F32, "sq") for _ in range(2)]
    rs, t_rs = C.sb([128, 512], F32, "rstd")
    tmp = [C.sb([128, 512], F32, "modtmp") for _ in range(2)]
    tmp32 = [C.sb([128, 512], F32, "modtmp32") for _ in range(2)] if out_dram is not None else None
    for tt in range(ntok // 512):
        sl = slice(tt * 512, (tt + 1) * 512)
        ps, t_ps = C.psum()
        for kc in range(KC):
            sqt, t_sq = sq[kc % 2]
            S.add("act", lambda e, sqt=sqt, kc=kc, sl=sl: e.activation(out=sqt[:], in_=xT[:, kc, sl], func=AF.Square),
                  reads=[t_x], writes=[t_sq])
            S.add("pe", lambda e, sqt=sqt, kc=kc, ps=ps: e.matmul(ps[:, :], ones[:], sqt[:], start=(kc == 0), stop=(kc == KC - 1)),
                  reads=[t_sq, t_ones], writes=[t_ps])
        S.add("act", lambda e, ps=ps: e.activation(out=rs[:], in_=ps[:, :], func=AF.Sqrt, scale=1.0 / D, bias=EPS),
              reads=[t_ps], writes=[t_rs])
        S.add("dve", lambda e: e.reciprocal(out=rs[:], in_=rs[:]), reads=[t_rs], writes=[t_rs])
        for kc in range(KC):
            tm, t_tm = tmp[kc % 2]
            S.add("dve", lambda e, tm=tm, kc=kc, sl=sl: e.scalar_tensor_tensor(
                out=tm[:], in0=xT[:, kc, sl], scalar=a_col[:, kc:kc + 1], in1=rs[:], op0=ALU.mult, op1=ALU.mult),
                reads=[t_x, t_rs] + t_par, writes=[t_tm])
            S.add("act", lambda e, tm=tm, kc=kc, sl=sl: e.activation(
                out=hT[:, kc, sl], in_=tm[:], func=AF.Identity, bias=shift_col[:, kc:kc + 1], scale=1.0),
                reads=[t_tm] + t_par, writes=[t_h])
            if out_dram is not None:
                t32, t_t32 = tmp32[kc % 2]
                S.add("act", lambda e, tm=tm, t32=t32, kc=kc: e.activation(
                    out=t32[:], in_=tm[:], func=AF.Identity, bias=shift_col[:, kc:kc + 1], scale=1.0),
                    reads=[t_tm] + t_par, writes=[t_t32])
                S.add("sp", lambda e, t32=t32, kc=kc, sl=sl: e.dma_start(out=out_dram[kc * 128:(kc + 1) * 128, sl], in_=t32[:]),
                      reads=[t_t32], dma=True)


def prep_mod_params(C, par_ap, n_sets):
    S = C.S
    par, t_par = C.sb([128, n_sets * 4 * KC], F32, "par")
    acol, t_a = C.sb([128, n_sets * KC], F32, "acol")
    S.add("sp", lambda e: e.dma_start(out=par[:], in_=par_ap), writes=[t_par], dma=True)
    out = []
    for s in range(n_sets):
        b = s * 4 * KC
        S.add("dve", lambda e, b=b, s=s: e.scalar_tensor_tensor(
            out=acol[:, s * KC:(s + 1) * KC], in0=par[:, b + 2 * KC:b + 3 * KC], scalar=1.0, in1=par[:, b:b + KC],
            op0=ALU.add, op1=ALU.mult), reads=[t_par], writes=[t_a])
        out.append((acol[:, s * KC:(s + 1) * KC], par[:, b + KC:b + 2 * KC], par[:, b + 3 * KC:b + 4 * KC]))
    return out, [t_par, t_a]


ADA_COLS = 2 * 18432 // NCORES
ADA_NB = ADA_COLS // 512


def build_ada():
    C = Ctx()
    S = C.S
    c_ap = C.din("c", [128, KC])
    w_ap = C.din("w", [ADA_NB, 128, KC, 512])
    b_ap = C.din("b", [1, ADA_COLS])
    o_ap = C.dout("o", [1, ADA_COLS])
    C.init_psum()
    ct, t_c = C.sb([128, KC], F32, "c")
    bt, t_b = C.sb([1, ADA_COLS], F32, "b")
    ot, t_o = C.sb([1, ADA_COLS], F32, "o")
    wt = [C.sb([128, KC, 512], F32, "w") for _ in range(2)]
    S.add("sp", lambda e: e.dma_start(out=ct[:], in_=c_ap), writes=[t_c], dma=True)
    S.add("sp", lambda e: e.dma_start(out=bt[:], in_=b_ap), writes=[t_b], dma=True)
    S.add("act", lambda e: e.activation(out=ct[:], in_=ct[:], func=AF.Silu), reads=[t_c], writes=[t_c])
    for nb in range(ADA_NB):
        w, t_w = wt[nb % 2]
        for kc in range(KC):
            S.add("sp", lambda e, w=w, nb=nb, kc=kc: e.dma_start(out=w[:, kc, :], in_=w_ap[nb, :, kc, :]), writes=[t_w], dma=True)
        ps, t_ps = C.psum()
        mm_group(S, ps[0:1, :], [(ct[:, kc:kc + 1], w[:, kc, :]) for kc in range(KC)], [t_c, t_w], [t_ps])
        S.add("dve", lambda e, ps=ps, nb=nb: e.tensor_tensor(
            out=ot[:, nb * 512:(nb + 1) * 512], in0=ps[0:1, :], in1=bt[:, nb * 512:(nb + 1) * 512], op=ALU.add),
            reads=[t_ps, t_b], writes=[t_o])
    S.add("sp", lambda e: e.dma_start(out=o_ap, in_=ot[:]), reads=[t_o], dma=True)
    return C.finish()


def build_ffn(emit_h_next, dbg=None):
    C = Ctx()
    S = C.S
    nsets = 2 if emit_h_next else 1
    x_ap = C.din("xT", [D, TOK])
    par_ap = C.din("par", [128, nsets * 4 * KC])
    win_ap = C.din("win", [NJ, 128, KC, 256])
    wout_ap = C.din("wout", [NQ, KC, 128, JQ, 128])
    ones_ap = C.din("ones", [128, 128])
    xo_ap = C.dout("xo", [D, TOK])
    ho_ap = C.dout("ho", [D, TOK]) if emit_h_next else None
    C.init_psum()
    ones, t_ones = load_consts(C, ones_ap)
    xT, t_x = C.sb([128, KC, TOK], F32, "xT")
    hT, t_h = C.sb([128, KC, TOK], BF16, "hT")
    actT, t_act = C.sb([128, JQ, TOK], BF16, "actT")
    wi = [C.sb([128, KC, 256], BF16, "wi") for _ in range(2)]
    wo = [C.sb([128, JQ, 128], BF16, "wo") for _ in range(2)]
    sg = [C.sb([128, 512], F32, "sg") for _ in range(2)]
    g05, t_g = C.sb([128, KC], F32, "g05")
    for kc in range(KC):
        S.add("sp", lambda e, kc=kc: e.dma_start(out=xT[:, kc, :], in_=x_ap[kc * 128:(kc + 1) * 128, :]),
              writes=[t_x], dma=True)
    sets, t_par = prep_mod_params(C, par_ap, nsets)
    a_col, shift_col, gate_col = sets[0]
    S.add("dve", lambda e: e.tensor_scalar(out=g05[:], in0=gate_col, scalar1=0.5, scalar2=None, op0=ALU.mult),
          reads=t_par, writes=[t_g])
    modulate(C, xT, t_x, hT, t_h, a_col, shift_col, t_par, ones, t_ones, TOK, out_dram=(ho_ap if dbg == "mod" else None))
    if dbg == "mod":
        for kc in range(KC):
            S.add("sp", lambda e, kc=kc: e.dma_start(out=xo_ap[kc * 128:(kc + 1) * 128, :], in_=xT[:, kc, :]),
                  reads=[t_x], dma=True)
        return C.finish()
    nwi = 0
    nwo = 0
    nsg = 0
    for qh in range(NQ):
        for jj in range(JQ):
            j = qh * JQ + jj
            w, t_w = wi[nwi % 2]
            nwi += 1
            S.add("pool", lambda e, w=w, j=j: e.dma_start(out=w[:], in_=win_ap[j]), writes=[t_w], dma=True)
            for tt in range(TOK // 512):
                sl = slice(tt * 512, (tt + 1) * 512)
                pg, t_pg = C.psum()
                pu, t_pu = C.psum()
                mm_group(S, pg[:, :], [(w[:, kc, 0:128], hT[:, kc, sl]) for kc in range(KC)], [t_w, t_h], [t_pg])
                mm_group(S, pu[:, :], [(w[:, kc, 128:256], hT[:, kc, sl]) for kc in range(KC)], [t_w, t_h], [t_pu])
                s_, t_s = sg[nsg % 2]
                nsg += 1
                S.add("act", lambda e, s_=s_, pg=pg: e.activation(out=s_[:], in_=pg[:, :], func=AF.Silu),
                      reads=[t_pg], writes=[t_s])
                S.add("dve", lambda e, s_=s_, pu=pu, jj=jj, sl=sl: e.tensor_tensor(
                    out=actT[:, jj, sl], in0=s_[:], in1=pu[:, :], op=ALU.mult),
                    reads=[t_s, t_pu], writes=[t_act])
        for m in range(KC):
            w, t_w = wo[nwo % 2]
            nwo += 1
            S.add("pool", lambda e, w=w, qh=qh, m=m: e.dma_start(out=w[:], in_=wout_ap[qh, m]), writes=[t_w], dma=True)
            for tt in range(TOK // 512):
                sl = slice(tt * 512, (tt + 1) * 512)
                po, t_po = C.psum()
                mm_group(S, po[:, :], [(w[:, jj, :], actT[:, jj, sl]) for jj in range(JQ)], [t_w, t_act], [t_po])
                S.add("dve", lambda e, po=po, m=m, sl=sl: e.scalar_tensor_tensor(
                    out=xT[:, m, sl], in0=po[:, :], scalar=g05[:, m:m + 1], in1=xT[:, m, sl], op0=ALU.mult, op1=ALU.add),
                    reads=[t_po, t_g, t_x], writes=[t_x])
    for kc in range(KC):
        S.add("sp", lambda e, kc=kc: e.dma_start(out=xo_ap[kc * 128:(kc + 1) * 128, :], in_=xT[:, kc, :]),
              reads=[t_x], dma=True)
    if emit_h_next:
        a2, sh2, _ = sets[1]
        modulate(C, xT, t_x, hT, t_h, a2, sh2, t_par, ones, t_ones, TOK, out_dram=ho_ap)
    return C.finish()


def cols16(v):
    return np.ascontiguousarray(np.asarray(v, np.float32).reshape(KC, 128).T)


def tile_win(w_in):
    w = w_in.reshape(KC, 128, 2, NJ, 128)
    return np.ascontiguousarray(w.transpose(3, 1, 0, 2, 4).reshape(NJ, 128, KC, 256))


def tile_wout(w_out):
    w = w_out.reshape(NQ, JQ, 128, KC, 128)
    return np.ascontiguousarray(w.transpose(0, 3, 2, 1, 4))


_PROG = {}


def prog(name, fn, *a):
    key = (name,) + a
    if key not in _PROG:
        _PROG[key] = fn(*a)
    return _PROG[key]


def run(nc, in_maps):
    res = run_bass_kernel_spmd(nc, in_maps, core_ids=list(range(NCORES)))
    return res.results


def compute_ada(c, w_ada, b_ada):
    cc = cols16(c.reshape(-1))
    wa = np.concatenate([w_ada[0], w_ada[1]], axis=1)
    ba = np.concatenate([b_ada[0], b_ada[1]], axis=0)
    in_maps = []
    for i in range(NCORES):
        ws = wa[:, i * ADA_COLS:(i + 1) * ADA_COLS].reshape(KC, 128, ADA_NB, 512)
        in_maps.append({
            "c": cc,
            "w": np.ascontiguousarray(ws.transpose(2, 1, 0, 3)),
            "b": np.ascontiguousarray(ba[i * ADA_COLS:(i + 1) * ADA_COLS].reshape(1, -1)),
        })
    r = run(prog("ada", build_ada), in_maps)
    ada = np.concatenate([r[i]["o"].reshape(-1) for i in range(NCORES)])
    return ada.reshape(2, 3, 3, D)


def par_block(norm_gain_li, ada_li):
    return np.concatenate([cols16(norm_gain_li), cols16(ada_li[0]), cols16(ada_li[1]), cols16(ada_li[2])], axis=1)


def run_ffn(xT, par, w_in, w_out, emit_h_next, dbg=None):
    win_t = tile_win(w_in)
    wout_t = tile_wout(w_out)
    ones = np.ones((128, 128), np.float32)
    in_maps = []
    for i in range(NCORES):
        in_maps.append({"xT": np.ascontiguousarray(xT[:, i * TOK:(i + 1) * TOK]), "par": par,
                        "win": win_t, "wout": wout_t, "ones": ones})
    r = run(prog("ffn", build_ffn, emit_h_next, dbg), in_maps)
    xo = np.concatenate([r[i]["xo"] for i in range(NCORES)], axis=1)
    ho = np.concatenate([r[i]["ho"] for i in range(NCORES)], axis=1) if emit_h_next else None
    return xo, ho


NCOLB = 1540
C_Q, C_K, C_Z, C_XS, C_B, C_C, C_CB, C_CC, C_CX, C_V = 0, 128, 256, 512, 768, 896, 1024, 1152, 1280, 1408
SCALE = 128 ** -0.5
NEG = -1e30


PARTS = ('conv', 'ssd', 'attn')
STOP = 0


def build_mixB(NT):
    C = Ctx()
    S = C.S
    ntok = NT * 512
    h_ap = C.din("hT", [D, ntok])
    w_ap = C.din("wB", [128, KC, NCOLB])
    sp_ap = C.din("sp", [128, 24])
    sp64_ap = C.din("sp64", [64, 24])
    cst_ap = C.din("cst", [128, 5 * 128 + 512 + 512])
    ya_ap = C.dout("ya", [ntok, 128])
    yg_ap = C.dout("yg", [256, ntok])
    yc_ap = C.dout("yc", [128, ntok])
    C.init_psum()
    cst, t_cst = C.sb([128, 5 * 128 + 1024], F32, "cst")
    S.add("sp", lambda e: e.dma_start(out=cst[:], in_=cst_ap), writes=[t_cst], dma=True)
    ident = cst[:, 0:128]
    ones = cst[:, 128:256]
    TA = cst[:, 256:384]
    TB = cst[:, 384:512]
    NEGM = cst[:, 512:640]
    U0 = cst[:, 640:896]
    UTRI = cst[:, 640:768]
    U1 = cst[:, 896:1152]
    SELH = cst[0:4, 1152:1664]
    spt, t_sp = C.sb([128, 24], F32, "sp")
    sp64, t_sp64 = C.sb([64, 24], F32, "sp64")
    S.add("sp", lambda e: e.dma_start(out=spt[:], in_=sp_ap), writes=[t_sp], dma=True)
    S.add("sp", lambda e: e.dma_start(out=sp64[:], in_=sp64_ap), writes=[t_sp64], dma=True)
    S.add("dve", lambda e: e.tensor_tensor(out=TA, in0=TA, in1=NEGM, op=ALU.add), reads=[t_cst], writes=[t_cst])
    aneg, t_an = C.sb([128, 4], F32, "aneg")
    S.add("act", lambda e: e.activation(out=aneg[:], in_=spt[:, 20:24], func=AF.Exp), reads=[t_sp], writes=[t_an])
    S.add("dve", lambda e: e.tensor_scalar(out=aneg[:], in0=aneg[:], scalar1=-1.0, scalar2=None, op0=ALU.mult),
          reads=[t_an], writes=[t_an])
    wB, _ = C.sb([128, KC, NCOLB], BF16, "wB")
    t_wB = [T("wB%d" % kc) for kc in range(KC)]
    for kc in range(KC):
        S.add("pool", lambda e, kc=kc: e.dma_start(out=wB[:, kc, :], in_=w_ap[:, kc, :]), writes=[t_wB[kc]], dma=True)
    hbuf = [C.sb([128, KC, 512], BF16, "hb")[0] for _ in range(2)]
    t_hb = [[T("hb") for _ in range(KC)] for _ in range(2)]
    KT, t_KT = C.sb([128, ntok], BF16, "KT")
    vaug, t_v = C.sb([128, NT * 4, 130], BF16, "vaug")
    S.add("pool", lambda e: e.memset(vaug[:], 1.0), writes=[t_v])
    kmean, t_km = C.sb([128, 32], F32, "kmean")
    S.add("pool", lambda e: e.memset(kmean[:], 0.0), writes=[t_km])

    def sbt(shape, dt, name):
        return C.sb(shape, dt, name)

    sq, t_sq = sbt([128, 512], F32, "sq")
    rs, t_rs = sbt([128, 512], F32, "rs")
    qn32, t_qn = sbt([128, 512], F32, "qn32")
    qnb, t_qnb = sbt([128, 512], BF16, "qnb")
    kn32, t_kn = sbt([128, 2, 256], F32, "kn32")
    kms, t_kms = sbt([128, 2], F32, "kms")
    sz = [sbt([64, 512], F32, "sz") for _ in range(4)]
    xpre = [sbt([64, 515], F32, "xpre") for _ in range(4)]
    bpre, t_bpre = sbt([128, 515], F32, "bpre")
    cpre, t_cpre = sbt([128, 515], F32, "cpre")
    for (t_, tt_) in xpre + [(bpre, t_bpre), (cpre, t_cpre)]:
        S.add("pool", lambda e, t_=t_: e.memset(t_[:, 0:3], 0.0), writes=[tt_])
    xsT = [sbt([64, 512], F32, "xsT") for _ in range(4)]
    BT32, t_BT32 = sbt([128, 512], F32, "BT32")
    CT32, t_CT32 = sbt([128, 512], F32, "CT32")
    BTb, t_BTb = sbt([128, 512], BF16, "BTb")
    CTb, t_CTb = sbt([128, 512], BF16, "CTb")
    cacc = [sbt([128, 512], F32, "cacc") for _ in range(2)]
    cBs, t_cBs = sbt([128, 512], F32, "cBs")
    cCs, t_cCs = sbt([128, 512], F32, "cCs")
    ucv, t_ucv = sbt([128, 514], F32, "ucv")
    S.add("pool", lambda e: e.memset(ucv[:, 0:2], 0.0), writes=[t_ucv])
    ycs, t_ycs = sbt([128, 512], F32, "ycs")
    dtraw, t_dtr = sbt([128, 4, 4], F32, "dtraw")
    dtt, t_dtt = sbt([128, 4, 4], F32, "dtt")
    dtA, t_dtA = sbt([128, 4, 4], F32, "dtA")
    wd, t_wd = sbt([128, 4, 4], F32, "wd")
    xd = [sbt([128, 256], BF16, "xd") for _ in range(4)]
    xdd = [sbt([128, 256], BF16, "xdd") for _ in range(4)]
    Btok = [sbt([128, 128], BF16, "Btok") for _ in range(4)]
    acs, t_acs = sbt([128, 12], F32, "acs")
    d2, t_d2 = sbt([128, 8], F32, "d2")
    acsT, t_acsT = sbt([4, 256], F32, "acsT")
    cbm, t_cbm = sbt([128, 384], F32, "cbm")
    E = [sbt([128, 256], F32, "E") for _ in range(4)]
    Cdec, t_Cdec = sbt([128, 256], BF16, "Cdec")
    Dm, t_Dm = sbt([128, 384], F32, "Dm")
    MT, t_MT = sbt([128, 384], BF16, "MT")
    prev32, t_p32 = sbt([128, 256], F32, "prev32")
    prevb, t_pb = sbt([128, 256], BF16, "prevb")
    S.add("pool", lambda e: e.memset(prev32[:], 0.0), writes=[t_p32])
    S.add("pool", lambda e: e.memset(prevb[:], 0.0), writes=[t_pb])
    yt1, t_yt1 = sbt([64, 256], F32, "yt1")
    ygs = [sbt([64, 256], F32, "ygs") for _ in range(2)]
    gsb, t_gsb = sbt([128, 32], F32, "gsb")
    top8, t_top8 = sbt([128, 8], F32, "top8")
    msel, t_msel = sbt([128, 32], F32, "msel")
    acc, t_acc = sbt([128, 130], F32, "acc")
    tmpS2 = [sbt([128, 256], F32, "tmpS") for _ in range(2)]
    pt2 = [sbt([128, 2, 128], BF16, "pt") for _ in range(2)]
    npt = 0
    rec, t_rec = sbt([128, 1], F32, "rec")
    yq = [sbt([128, 128], F32, "yq") for _ in range(2)]
    nyg = 0
    nyq = 0
    nosb = 0
    osb = [sbt([128, 130], F32, "osb") for _ in range(2)]
    if STOP == 1:
        return C.finish()

    def load_h(tt_):
        hb_ = hbuf[tt_ % 2]
        for kc in range(KC):
            S.add("pool", lambda e, kc=kc, hb_=hb_, t0=tt_ * 512: e.dma_start(
                out=hb_[:, kc, :], in_=h_ap[kc * 128:(kc + 1) * 128, t0:t0 + 512]), writes=[t_hb[tt_ % 2][kc]], dma=True)

    load_h(0)
    for tt in range(NT):
        tok0 = tt * 512
        hb = hbuf[tt % 2]
        thb = t_hb[tt % 2]

        def proj(col0, M):
            ps, t_ps = C.psum()
            mm_group(S, ps[0:M, :], [(wB[:, kc, col0:col0 + M], hb[:, kc, :]) for kc in range(KC)], thb + t_wB, [t_ps])
            return ps, t_ps

        for which in range(2):
            ps, t_ps = proj(C_Q if which == 0 else C_K, 128)
            S.add("act", lambda e, ps=ps: e.activation(out=sq[:], in_=ps[:, :], func=AF.Square), reads=[t_ps], writes=[t_sq])
            ps2, t_ps2 = C.psum()
            S.add("pe", lambda e, ps2=ps2: e.matmul(ps2[:, :], ones, sq[:], start=True, stop=True),
                  reads=[t_sq, t_cst], writes=[t_ps2])
            S.add("act", lambda e, ps2=ps2: e.activation(out=rs[:], in_=ps2[:, :], func=AF.Sqrt, scale=1.0 / 128, bias=EPS),
                  reads=[t_ps2], writes=[t_rs])
            S.add("dve", lambda e: e.reciprocal(out=rs[:], in_=rs[:]), reads=[t_rs], writes=[t_rs])
            if which == 0:
                S.add("dve", lambda e, ps=ps: e.scalar_tensor_tensor(
                    out=qn32[:], in0=ps[:, :], scalar=spt[:, 0:1], in1=rs[:], op0=ALU.mult, op1=ALU.mult),
                    reads=[t_ps, t_rs, t_sp], writes=[t_qn])
                S.add("act", lambda e: e.activation(out=qnb[:], in_=qn32[:], func=AF.Identity), reads=[t_qn], writes=[t_qnb])
            else:
                for a in range(2):
                    S.add("dve", lambda e, ps=ps, a=a: e.scalar_tensor_tensor(
                        out=kn32[:, a, :], in0=ps[:, a * 256:(a + 1) * 256], scalar=spt[:, 1:2], in1=rs[:, a * 256:(a + 1) * 256],
                        op0=ALU.mult, op1=ALU.mult), reads=[t_ps, t_rs, t_sp], writes=[t_kn])
                    S.add("act", lambda e, a=a, tok0=tok0: e.activation(
                        out=KT[:, tok0 + a * 256:tok0 + (a + 1) * 256], in_=kn32[:, a, :], func=AF.Identity),
                        reads=[t_kn], writes=[t_KT])
                S.add("dve", lambda e: e.tensor_reduce(out=kms[:], in_=kn32[:], axis=AX.X, op=ALU.add), reads=[t_kn], writes=[t_kms])
                S.add("dve", lambda e, tt=tt: e.tensor_scalar(
                    out=kmean[:, 2 * tt:2 * tt + 2], in0=kms[:], scalar1=1.0 / 256, scalar2=None, op0=ALU.mult),
                    reads=[t_kms], writes=[t_km])
        if STOP == 2:
            return C.finish()
        for h in range(4):
            ps, t_ps = proj(C_Z + h * 64, 64)
            S.add("act", lambda e, ps=ps, h=h: e.activation(out=sz[h][0][:], in_=ps[0:64, :], func=AF.Silu),
                  reads=[t_ps], writes=[sz[h][1]])
        for h in range(4):
            ps, t_ps = proj(C_XS + h * 64, 64)
            S.add("act", lambda e, ps=ps, h=h: e.activation(out=xpre[h][0][:, 3:515], in_=ps[0:64, :], func=AF.Identity),
                  reads=[t_ps], writes=[xpre[h][1]])
        ps, t_ps = proj(C_B, 128)
        S.add("act", lambda e, ps=ps: e.activation(out=bpre[:, 3:515], in_=ps[:, :], func=AF.Identity), reads=[t_ps], writes=[t_bpre])
        ps, t_ps = proj(C_C, 128)
        S.add("act", lambda e, ps=ps: e.activation(out=cpre[:, 3:515], in_=ps[:, :], func=AF.Identity), reads=[t_ps], writes=[t_cpre])
        ps, t_ps = proj(C_CB, 128)
        S.add("act", lambda e, ps=ps: e.activation(out=cBs[:], in_=ps[:, :], func=AF.Identity), reads=[t_ps], writes=[t_cBs])
        ps, t_ps = proj(C_CC, 128)
        S.add("act", lambda e, ps=ps: e.activation(out=cCs[:], in_=ps[:, :], func=AF.Identity), reads=[t_ps], writes=[t_cCs])
        ps, t_ps = proj(C_CX, 128)
        S.add("dve", lambda e, ps=ps: e.tensor_tensor(out=ucv[:, 2:514], in0=cCs[:], in1=ps[:, :], op=ALU.mult),
              reads=[t_ps, t_cCs], writes=[t_ucv])
        if STOP == 3:
            return C.finish()
        for s in range(4):
            ps, t_ps = C.psum()
            mm_group(S, ps[:, 0:128], [(hb[:, kc, s * 128:(s + 1) * 128], wB[:, kc, C_V:C_V + 128]) for kc in range(KC)],
                     thb + t_wB, [t_ps])
            if STOP != 8:
                mm_group(S, ps[:, 128:132], [(hb[:, kc, s * 128:(s + 1) * 128], wB[:, kc, C_V + 128:C_V + 132]) for kc in range(KC)],
                         thb + t_wB, [t_ps])
            if STOP == 4:
                continue
            S.add("act", lambda e, ps=ps, s=s, tt=tt: e.activation(out=vaug[:, 4 * tt + s, 0:128], in_=ps[:, 0:128], func=AF.Identity),
                  reads=[t_ps], writes=[t_v])
            if STOP in (5, 7):
                continue
            S.add("act", lambda e, ps=ps, s=s: e.activation(out=dtraw[:, s, :], in_=ps[:, 128:132], func=AF.Identity),
                  reads=[t_ps], writes=[t_dtr])
            S.add("pool", lambda e, s=s: e.tensor_tensor(out=dtraw[:, s, :], in0=dtraw[:, s, :], in1=spt[:, 16:20], op=ALU.add),
                  reads=[t_dtr, t_sp], writes=[t_dtr])
            if STOP == 6:
                continue
        if tt + 1 < NT:
            load_h(tt + 1)
        if 'conv' in PARTS:
            a0, t_a0 = cacc[0]
            S.add("dve", lambda e: e.tensor_scalar(out=a0[:], in0=ucv[:, 0:512], scalar1=spt[:, 13:14], scalar2=None, op0=ALU.mult),
                  reads=[t_ucv, t_sp], writes=[t_a0])
            for j in (1, 2):
                S.add("dve", lambda e, j=j: e.scalar_tensor_tensor(
                    out=a0[:], in0=ucv[:, j:j + 512], scalar=spt[:, 13 + j:14 + j], in1=a0[:], op0=ALU.mult, op1=ALU.add),
                    reads=[t_ucv, t_sp, t_a0], writes=[t_a0])
            S.add("dve", lambda e: e.tensor_tensor(out=ycs[:], in0=cBs[:], in1=a0[:], op=ALU.mult), reads=[t_cBs, t_a0], writes=[t_ycs])
            S.add("sp", lambda e, tok0=tok0: e.dma_start(out=yc_ap[:, tok0:tok0 + 512], in_=ycs[:]), reads=[t_ycs], dma=True)
            S.add("dve", lambda e: e.tensor_copy(out=ucv[:, 0:2], in_=ucv[:, 512:514]), reads=[t_ucv], writes=[t_ucv])
        def sec_ssd():
            nonlocal nyg
            def ssd_conv(pre, t_pre, P, wsrc, t_w, c0, outs):
                ac, t_ac = cacc[1]
                S.add("dve", lambda e: e.tensor_scalar(out=ac[0:P, :], in0=pre[:, 0:512], scalar1=wsrc[:, c0:c0 + 1], scalar2=None, op0=ALU.mult),
                      reads=[t_pre, t_w], writes=[t_ac])
                for j in (1, 2, 3):
                    S.add("dve", lambda e, j=j: e.scalar_tensor_tensor(
                        out=ac[0:P, :], in0=pre[:, j:j + 512], scalar=wsrc[:, c0 + j:c0 + j + 1], in1=ac[0:P, :], op0=ALU.mult, op1=ALU.add),
                        reads=[t_pre, t_w, t_ac], writes=[t_ac])
                o, t_o = outs
                S.add("act", lambda e: e.activation(out=o[:], in_=ac[0:P, :], func=AF.Silu, bias=wsrc[:, c0 + 4:c0 + 5], scale=1.0),
                      reads=[t_ac, t_w], writes=[t_o])
                S.add("dve", lambda e: e.tensor_copy(out=pre[:, 0:3], in_=pre[:, 512:515]), reads=[t_pre], writes=[t_pre])

            for h in range(4):
                ssd_conv(xpre[h][0], xpre[h][1], 64, sp64, t_sp64, h * 6, xsT[h])
            ssd_conv(bpre, t_bpre, 128, spt, t_sp, 3, (BT32, t_BT32))
            ssd_conv(cpre, t_cpre, 128, spt, t_sp, 8, (CT32, t_CT32))
            S.add("pool", lambda e: e.tensor_copy(out=BTb[:], in_=BT32[:]), reads=[t_BT32], writes=[t_BTb])
            S.add("pool", lambda e: e.tensor_copy(out=CTb[:], in_=CT32[:]), reads=[t_CT32], writes=[t_CTb])
            S.add("act", lambda e: e.activation(out=dtt[:], in_=dtraw[:], func=AF.Exp), reads=[t_dtr], writes=[t_dtt])
            S.add("act", lambda e: e.activation(out=dtt[:], in_=dtt[:], func=AF.Ln, bias=1.0, scale=1.0), reads=[t_dtt], writes=[t_dtt])
            for s in range(4):
                S.add("dve", lambda e, s=s: e.tensor_tensor(out=dtA[:, s, :], in0=dtt[:, s, :], in1=aneg[:], op=ALU.mult),
                      reads=[t_dtt, t_an], writes=[t_dtA])
            for cc in range(2):
                ccol = slice(cc * 256, (cc + 1) * 256)
                s0, s1 = 2 * cc, 2 * cc + 1
                psa, t_psa = C.psum()
                S.add("pe", lambda e, psa=psa, s0=s0: e.matmul(psa[:, 0:4], UTRI, dtA[:, s0, :], start=True, stop=True),
                      reads=[t_dtA, t_cst], writes=[t_psa])
                mm_group(S, psa[:, 4:8], [(ones, dtA[:, s0, :]), (UTRI, dtA[:, s1, :])], [t_dtA, t_cst], [t_psa])
                mm_group(S, psa[:, 8:12], [(ones, dtA[:, s0, :]), (ones, dtA[:, s1, :])], [t_dtA, t_cst], [t_psa])
                S.add("dve", lambda e, psa=psa: e.tensor_copy(out=acs[:], in_=psa[:, 0:12]), reads=[t_psa], writes=[t_acs])
                for s in range(2):
                    S.add("dve", lambda e, s=s: e.tensor_tensor(out=d2[:, 4 * s:4 * s + 4], in0=acs[:, 8:12], in1=acs[:, 4 * s:4 * s + 4], op=ALU.subtract),
                          reads=[t_acs], writes=[t_d2])
                S.add("act", lambda e: e.activation(out=d2[:], in_=d2[:], func=AF.Exp), reads=[t_d2], writes=[t_d2])
                for s in range(2):
                    S.add("dve", lambda e, s=s, cc=cc: e.tensor_tensor(out=wd[:, 2 * cc + s, :], in0=dtt[:, 2 * cc + s, :], in1=d2[:, 4 * s:4 * s + 4], op=ALU.mult),
                          reads=[t_dtt, t_d2], writes=[t_wd])
                pst, t_pst = C.psum()
                mm_group(S, pst[0:4, 0:256], [(dtA[:, s0, :], U0), (dtA[:, s1, :], U1)], [t_dtA, t_cst], [t_pst])
                S.add("dve", lambda e, pst=pst: e.tensor_copy(out=acsT[:], in_=pst[0:4, 0:256]), reads=[t_pst], writes=[t_acsT])
                for s in (s0, s1):
                    psx, t_psx = C.psum()

                    def tfn(e, psx=psx, s=s):
                        ins = None
                        for h in range(4):
                            ins = e.transpose(out=psx[:, h * 64:(h + 1) * 64], in_=xsT[h][0][:, s * 128:(s + 1) * 128], identity=cst[0:64, 0:64])
                        return ins
                    S.add("pe", tfn, reads=[x[1] for x in xsT] + [t_cst], writes=[t_psx])
                    for h in range(4):
                        S.add("dve", lambda e, psx=psx, s=s, h=h: e.tensor_scalar(
                            out=xd[s][0][:, h * 64:(h + 1) * 64], in0=psx[:, h * 64:(h + 1) * 64], scalar1=dtt[:, s, h:h + 1], scalar2=None, op0=ALU.mult),
                            reads=[t_psx, t_dtt], writes=[xd[s][1]])
                        S.add("dve", lambda e, psx=psx, s=s, h=h: e.tensor_scalar(
                            out=xdd[s][0][:, h * 64:(h + 1) * 64], in0=psx[:, h * 64:(h + 1) * 64], scalar1=wd[:, s, h:h + 1], scalar2=None, op0=ALU.mult),
                            reads=[t_psx, t_wd], writes=[xdd[s][1]])
                    psb, t_psb = C.psum()
                    S.add("pe", lambda e, psb=psb, s=s: e.transpose(out=psb[:, 0:128], in_=BT32[:, s * 128:(s + 1) * 128], identity=ident),
                          reads=[t_BT32, t_cst], writes=[t_psb])
                    S.add("act", lambda e, psb=psb, s=s: e.activation(out=Btok[s][0][:], in_=psb[:, 0:128], func=AF.Identity),
                          reads=[t_psb], writes=[Btok[s][1]])
                psc, t_psc = C.psum()
                S.add("pe", lambda e, psc=psc, cc=cc: e.matmul(psc[:, 0:256], BTb[:, cc * 256:cc * 256 + 128], CTb[:, cc * 256:cc * 256 + 256], start=True, stop=True),
                      reads=[t_BTb, t_CTb], writes=[t_psc])
                S.add("pe", lambda e, psc=psc, cc=cc: e.matmul(psc[:, 256:384], BTb[:, cc * 256 + 128:cc * 256 + 256], CTb[:, cc * 256 + 128:cc * 256 + 256], start=True, stop=True),
                      reads=[t_BTb, t_CTb], writes=[t_psc])
                S.add("dve", lambda e, psc=psc: e.tensor_tensor(out=cbm[:, 0:256], in0=psc[:, 0:256], in1=U0, op=ALU.mult),
                      reads=[t_psc, t_cst], writes=[t_cbm])
                S.add("dve", lambda e, psc=psc: e.tensor_tensor(out=cbm[:, 256:384], in0=psc[:, 256:384], in1=UTRI, op=ALU.mult),
                      reads=[t_psc, t_cst], writes=[t_cbm])
                for h in range(4):
                    psbc, t_psbc = C.psum()
                    S.add("pe", lambda e, psbc=psbc, h=h: e.matmul(psbc[:, 0:256], SELH[:, h * 128:(h + 1) * 128], acsT[:], start=True, stop=True),
                          reads=[t_acsT, t_cst], writes=[t_psbc])
                    Eh, t_Eh = E[h]
                    S.add("act", lambda e, psbc=psbc, Eh=Eh: e.activation(out=Eh[:], in_=psbc[:, 0:256], func=AF.Exp), reads=[t_psbc], writes=[t_Eh])
                    S.add("pool", lambda e, Eh=Eh, ccol=ccol: e.tensor_tensor(out=Cdec[:], in0=CT32[:, ccol], in1=Eh[:], op=ALU.mult),
                          reads=[t_CT32, t_Eh], writes=[t_Cdec])
                    S.add("dve", lambda e, psbc=psbc, h=h: e.tensor_scalar(
                        out=Dm[:, 0:256], in0=psbc[:, 0:256], scalar1=acs[:, h:h + 1], scalar2=0.0, op0=ALU.subtract, op1=ALU.min),
                        reads=[t_psbc, t_acs], writes=[t_Dm])
                    S.add("dve", lambda e, psbc=psbc, h=h: e.tensor_scalar(
                        out=Dm[:, 256:384], in0=psbc[:, 128:256], scalar1=acs[:, 4 + h:5 + h], scalar2=0.0, op0=ALU.subtract, op1=ALU.min),
                        reads=[t_psbc, t_acs], writes=[t_Dm])
                    S.add("act", lambda e: e.activation(out=Dm[:], in_=Dm[:], func=AF.Exp), reads=[t_Dm], writes=[t_Dm])
                    S.add("dve", lambda e: e.tensor_tensor(out=MT[:], in0=Dm[:], in1=cbm[:], op=ALU.mult), reads=[t_Dm, t_cbm], writes=[t_MT])
                    psy, t_psy = C.psum()
                    hs = slice(h * 64, (h + 1) * 64)

                    def yfn(e, psy=psy, hs=hs, s0=s0, s1=s1):
                        e.matmul(psy[0:64, 0:256], prevb[:, hs], Cdec[:], start=True, stop=False)
                        e.matmul(psy[0:64, 0:256], xd[s0][0][:, hs], MT[:, 0:256], start=False, stop=False)
                        return e.matmul(psy[0:64, 128:256], xd[s1][0][:, hs], MT[:, 256:384], start=False, stop=True)
                    S.add("pe", yfn, reads=[t_pb, t_Cdec, xd[s0][1], xd[s1][1], t_MT], writes=[t_psy])
                    S.add("dve", lambda e, psy=psy, h=h, ccol=ccol: e.scalar_tensor_tensor(
                        out=yt1[:], in0=xsT[h][0][:, ccol], scalar=sp64[:, h * 6 + 5:h * 6 + 6], in1=psy[0:64, 0:256], op0=ALU.mult, op1=ALU.add),
                        reads=[xsT[h][1], t_sp64, t_psy], writes=[t_yt1])
                    yg, t_yg = ygs[nyg % 2]
                    nyg += 1
                    S.add("pool", lambda e, yg=yg, h=h, ccol=ccol: e.tensor_tensor(out=yg[:], in0=yt1[:], in1=sz[h][0][:, ccol], op=ALU.mult),
                          reads=[t_yt1, sz[h][1]], writes=[t_yg])
                    S.add("sp", lambda e, yg=yg, h=h, tok0=tok0, cc=cc: e.dma_start(
                        out=yg_ap[h * 64:(h + 1) * 64, tok0 + cc * 256:tok0 + (cc + 1) * 256], in_=yg[:]), reads=[t_yg], dma=True)
                pss, t_pss = C.psum()
                mm_group(S, pss[:, 0:256], [(Btok[s0][0][:], xdd[s0][0][:]), (Btok[s1][0][:], xdd[s1][0][:])],
                         [Btok[s0][1], Btok[s1][1], xdd[s0][1], xdd[s1][1]], [t_pss])
                for h in range(4):
                    hs = slice(h * 64, (h + 1) * 64)
                    S.add("dve", lambda e, h=h, hs=hs, pss=pss: e.scalar_tensor_tensor(
                        out=prev32[:, hs], in0=prev32[:, hs], scalar=E[h][0][:, 255:256], in1=pss[:, hs], op0=ALU.mult, op1=ALU.add),
                        reads=[t_p32, E[h][1], t_pss], writes=[t_p32])
                S.add("act", lambda e: e.activation(out=prevb[:], in_=prev32[:], func=AF.Identity), reads=[t_p32], writes=[t_pb])
        def sec_attn():
            nonlocal nyq, nosb, npt
            for i in range(4):
                qi = 4 * tt + i
                J = qi // 2
                eo = qi % 2
                qc = slice(i * 128, (i + 1) * 128)
                use_sel = J > 3
                if use_sel:
                    psg, t_psg = C.psum()
                    S.add("pe", lambda e, psg=psg, qc=qc: e.matmul(psg[:, 0:32], qn32[:, qc], kmean[:], start=True, stop=True),
                          reads=[t_qn, t_km], writes=[t_psg])
                    S.add("pool", lambda e: e.memset(gsb[:], NEG), writes=[t_gsb])
                    S.add("dve", lambda e, psg=psg, J=J: e.tensor_copy(out=gsb[:, 0:J], in_=psg[:, 0:J]), reads=[t_psg], writes=[t_gsb])
                    S.add("dve", lambda e: e.max(out=top8[:], in_=gsb[:]), reads=[t_gsb], writes=[t_top8])
                    S.add("dve", lambda e: e.tensor_scalar(out=msel[:], in0=gsb[:], scalar1=top8[:, 2:3], scalar2=None, op0=ALU.is_ge),
                          reads=[t_gsb, t_top8], writes=[t_msel])
                blocks = [J] + list(range(J))
                for bi, n in enumerate(blocks):
                    if n == J:
                        halves = [(0, "A")] if eo == 0 else [(0, "B"), (1, "A")]
                    elif n == J - 1:
                        halves = [(0, "c"), (1, "B")] if eo == 0 else [(0, "c"), (1, "c")]
                    else:
                        halves = [(0, "c"), (1, "c")]
                    pss2, t_pss2 = C.psum()
                    pt, t_pt = pt2[npt % 2]
                    tmpS, t_tmpS = tmpS2[npt % 2]
                    npt += 1

                    def sfn(e, pss2=pss2, halves=halves, n=n, qc=qc):
                        ins = None
                        for hh, _k in halves:
                            ins = e.matmul(pss2[:, hh * 128:(hh + 1) * 128], KT[:, n * 256 + hh * 128:n * 256 + (hh + 1) * 128], qnb[:, qc], start=True, stop=True)
                        return ins
                    S.add("pe", sfn, reads=[t_KT, t_qnb], writes=[t_pss2])
                    if STOP == 10:
                        continue
                    for hh, kind in halves:
                        if (STOP == 14 and kind != "c") or (STOP == 15 and kind == "c"):
                            continue
                        if kind == "c":
                            S.add("act", lambda e, pss2=pss2, hh=hh, pt=pt: e.activation(
                                out=pt[:, hh, :], in_=pss2[:, hh * 128:(hh + 1) * 128], func=AF.Exp, bias=spt[:, 2:3], scale=SCALE),
                                reads=[t_pss2, t_sp], writes=[t_pt])
                        else:
                            bt_ = TA if kind == "A" else TB
                            S.add("dve", lambda e, pss2=pss2, hh=hh, bt_=bt_, tmpS=tmpS: e.scalar_tensor_tensor(
                                out=tmpS[:, hh * 128:(hh + 1) * 128], in0=pss2[:, hh * 128:(hh + 1) * 128], scalar=SCALE, in1=bt_, op0=ALU.mult, op1=ALU.add),
                                reads=[t_pss2, t_cst], writes=[t_tmpS])
                            S.add("act", lambda e, hh=hh, pt=pt, tmpS=tmpS: e.activation(out=pt[:, hh, :], in_=tmpS[:, hh * 128:(hh + 1) * 128], func=AF.Exp),
                                  reads=[t_tmpS], writes=[t_pt])
                    if STOP in (11, 14, 15):
                        continue
                    pso, t_pso = C.psum()
                    nh = len(halves)

                    def ofn(e, pso=pso, halves=halves, n=n, nh=nh, pt=pt):
                        ins = None
                        for ii, (hh, _k) in enumerate(halves):
                            ins = e.matmul(pso[:, 0:130], pt[:, hh, :], vaug[:, 2 * n + hh, :], start=(ii == 0), stop=(ii == nh - 1))
                        return ins
                    S.add("pe", ofn, reads=[t_pt, t_v], writes=[t_pso])
                    if STOP == 12:
                        continue
                    if bi == 0:
                        S.add("act", lambda e, pso=pso: e.activation(out=acc[:], in_=pso[:, 0:130], func=AF.Identity), reads=[t_pso], writes=[t_acc])
                    else:
                        sc_ = msel[:, n:n + 1] if use_sel else 1.0
                        S.add("dve", lambda e, pso=pso, sc_=sc_: e.scalar_tensor_tensor(
                            out=acc[:], in0=pso[:, 0:130], scalar=sc_, in1=acc[:], op0=ALU.mult, op1=ALU.add),
                            reads=[t_pso, t_msel, t_acc], writes=[t_acc])
                if STOP in (10, 11, 12, 13, 14, 15):
                    continue
                S.add("dve", lambda e: e.reciprocal(out=rec[:], in_=acc[:, 128:129]), reads=[t_acc], writes=[t_rec])
                yq_, t_yq = yq[nyq % 2]
                nyq += 1
                S.add("dve", lambda e, yq_=yq_: e.tensor_scalar(out=yq_[:], in0=acc[:, 0:128], scalar1=rec[:, 0:1], scalar2=None, op0=ALU.mult),
                      reads=[t_acc, t_rec], writes=[t_yq])
                S.add("sp", lambda e, yq_=yq_, qi=qi: e.dma_start(out=ya_ap[qi * 128:(qi + 1) * 128, :], in_=yq_[:]), reads=[t_yq], dma=True)
        secs = []
        if 'ssd' in PARTS:
            secs.append(sec_ssd)
        if 'attn' in PARTS:
            secs.append(sec_attn)
        interleave(C, secs)
    return C.finish()


def t5_bucket_np(dist):
    n = np.maximum(dist, 0)
    max_exact = 16
    ratio = np.log(np.maximum(n, 1).astype(np.float32) / max_exact) / np.float32(np.log(128 / max_exact))
    large = max_exact + (ratio * (32 - max_exact)).astype(np.int32)
    large = np.minimum(large, 31)
    return np.where(n < max_exact, n, large)


def mix_consts(rel_bias, head):
    kk = np.arange(128)[:, None]
    qq = np.arange(128)[None, :]
    cst = np.zeros((128, 5 * 128 + 1024), np.float32)
    cst[:, 0:128] = np.eye(128, dtype=np.float32)
    cst[:, 128:256] = 1.0
    cst[:, 256:384] = rel_bias[t5_bucket_np(qq - kk), head]
    cst[:, 384:512] = rel_bias[t5_bucket_np(128 + qq - kk), head]
    cst[:, 512:640] = np.where(qq >= kk, 0.0, NEG)
    utri = (kk <= qq).astype(np.float32)
    cst[:, 640:768] = utri
    cst[:, 768:896] = 1.0
    cst[:, 896:1024] = 0.0
    cst[:, 1024:1152] = utri
    for h in range(4):
        cst[h, 1152 + h * 128:1152 + (h + 1) * 128] = 1.0
    return cst


def run_mixB(hT, l, inp, NT=16):
    w = inp["w_mix_in"][l]
    in_maps = []
    for c in range(NCORES):
        g = c // 2
        cols = np.concatenate([
            np.arange(c * 128, (c + 1) * 128), 1024 + np.arange(c * 128, (c + 1) * 128),
            3072 + np.arange(c * 256, (c + 1) * 256), 5120 + np.arange(c * 256, (c + 1) * 256),
            5120 + 2048 + np.arange(g * 128, (g + 1) * 128), 5120 + 2560 + np.arange(g * 128, (g + 1) * 128),
            8224 + np.arange(c * 128, (c + 1) * 128), 9248 + np.arange(c * 128, (c + 1) * 128),
            10272 + np.arange(c * 128, (c + 1) * 128), 2048 + np.arange(c * 128, (c + 1) * 128),
            8192 + np.arange(c * 4, (c + 1) * 4)])
        wB = np.ascontiguousarray(w[:, cols].reshape(KC, 128, NCOLB).transpose(1, 0, 2))
        sp = np.zeros((128, 24), np.float32)
        sp[:, 0] = inp["qk_norm"][l, 0]
        sp[:, 1] = inp["qk_norm"][l, 1]
        sp[:, 2] = inp["rel_bias"][31, c]
        wc = inp["w_ssd_conv"][l]
        bc = inp["b_ssd_conv"][l]
        chB = 2048 + g * 128 + np.arange(128)
        chC = 2560 + g * 128 + np.arange(128)
        sp[:, 3:7] = wc[:, chB].T
        sp[:, 7] = bc[chB]
        sp[:, 8:12] = wc[:, chC].T
        sp[:, 12] = bc[chC]
        sp[:, 13:16] = inp["w_sc_conv"][l][:, c * 128:(c + 1) * 128].T
        sp[:, 16:20] = inp["ssd_dt_bias"][l][None, 4 * c:4 * c + 4]
        sp[:, 20:24] = inp["ssd_a_log"][l][None, 4 * c:4 * c + 4]
        sp64 = np.zeros((64, 24), np.float32)
        for h in range(4):
            ch = c * 256 + h * 64 + np.arange(64)
            sp64[:, h * 6:h * 6 + 4] = wc[:, ch].T
            sp64[:, h * 6 + 4] = bc[ch]
            sp64[:, h * 6 + 5] = inp["ssd_d"][l][4 * c + h]
        in_maps.append({"hT": hT, "wB": wB, "sp": sp, "sp64": sp64, "cst": mix_consts(inp["rel_bias"], c)})
    r = run(prog("mixB", build_mixB, NT), in_maps)
    ntok = NT * 512
    YT = np.zeros((4096, ntok), np.float32)
    for c in range(NCORES):
        YT[c * 128:(c + 1) * 128] = r[c]["ya"].T
        YT[1024 + c * 256:1024 + (c + 1) * 256] = r[c]["yg"]
        YT[3072 + c * 128:3072 + (c + 1) * 128] = r[c]["yc"]
    return YT


NYC = 32
BR_CH = ((0, 8), (8, 24), (24, 32))


def build_mixC():
    C = Ctx()
    S = C.S
    x_ap = C.din("xT", [D, TOK])
    h_ap = C.din("hT", [D, TOK])
    y_ap = C.din("YT", [4096, TOK])
    wg_ap = C.din("wg", [KC, 128, KC, 384])
    wbr_ap = C.din("wbr", [KC, 128, NYC, 128])
    wo_ap = C.din("wo", [KC, 128, KC, 128])
    par_ap = C.din("par", [128, 2 * KC])
    ones_ap = C.din("ones", [128, 128])
    xo_ap = C.dout("xo", [D, TOK])
    C.init_psum()
    ones, t_ones = load_consts(C, ones_ap)
    par, t_par = C.sb([128, 2 * KC], F32, "par")
    S.add("sp", lambda e: e.dma_start(out=par[:], in_=par_ap), writes=[t_par], dma=True)
    xT, _ = C.sb([128, KC, 512], F32, "xT")
    t_x = [T("x%d" % k) for k in range(KC)]
    hb, _ = C.sb([128, KC, 512], BF16, "hb")
    t_h = [T("h%d" % k) for k in range(KC)]
    Yb, _ = C.sb([128, NYC, 512], BF16, "Yb")
    t_y = [T("y%d" % k) for k in range(NYC)]
    stg = [C.sb([128, 512], F32, "stg") for _ in range(4)]
    sq = [C.sb([128, 512], F32, "sq") for _ in range(2)]
    rs, t_rs = C.sb([128, 512], F32, "rs")
    tmpn = [C.sb([128, 512], F32, "tmpn") for _ in range(2)]
    mg, t_mg = C.sb([128, 512], F32, "mg")
    sig = [C.sb([128, 512], F32, "sig") for _ in range(2)]
    tmpm, t_tmpm = C.sb([128, 512], F32, "tmpm")
    mT, _ = C.sb([128, KC, 512], BF16, "mT")
    t_m = [T("m%d" % k) for k in range(KC)]
    wg = [C.sb([128, KC, 384], BF16, "wg") for _ in range(2)]
    wbr = [C.sb([128, NYC, 128], BF16, "wbr") for _ in range(2)]
    wo = [C.sb([128, KC, 128], BF16, "wo") for _ in range(2)]
    nsig = 0
    nw = 0
    nwo = 0
    for tt in range(TOK // 512):
        sl = slice(tt * 512, (tt + 1) * 512)
        for kc in range(KC):
            S.add("sp", lambda e, kc=kc, sl=sl: e.dma_start(out=xT[:, kc, :], in_=x_ap[kc * 128:(kc + 1) * 128, sl]),
                  writes=[t_x[kc]], dma=True)
            S.add("pool", lambda e, kc=kc, sl=sl: e.dma_start(out=hb[:, kc, :], in_=h_ap[kc * 128:(kc + 1) * 128, sl]),
                  writes=[t_h[kc]], dma=True)
        for ch in list(range(0, 8)) + list(range(24, 32)):
            S.add("pool", lambda e, ch=ch, sl=sl: e.dma_start(out=Yb[:, ch, :], in_=y_ap[ch * 128:(ch + 1) * 128, sl]),
                  writes=[t_y[ch]], dma=True)
        for gq in range(4):
            ps, t_ps = C.psum()
            for c4 in range(4):
                ch = 8 + gq * 4 + c4
                st_, t_st = stg[c4]
                S.add("sp", lambda e, st_=st_, ch=ch, sl=sl: e.dma_start(out=st_[:], in_=y_ap[ch * 128:(ch + 1) * 128, sl]),
                      writes=[t_st], dma=True)
                sq_, t_sq = sq[c4 % 2]
                S.add("act", lambda e, sq_=sq_, st_=st_: e.activation(out=sq_[:], in_=st_[:], func=AF.Square), reads=[t_st], writes=[t_sq])
                S.add("pe", lambda e, ps=ps, sq_=sq_, c4=c4: e.matmul(ps[:, :], ones[:], sq_[:], start=(c4 == 0), stop=(c4 == 3)),
                      reads=[t_sq, t_ones], writes=[t_ps])
            S.add("act", lambda e, ps=ps: e.activation(out=rs[:], in_=ps[:, :], func=AF.Sqrt, scale=1.0 / 512, bias=EPS),
                  reads=[t_ps], writes=[t_rs])
            S.add("dve", lambda e: e.reciprocal(out=rs[:], in_=rs[:]), reads=[t_rs], writes=[t_rs])
            for c4 in range(4):
                ch = 8 + gq * 4 + c4
                st_, t_st = stg[c4]
                S.add("dve", lambda e, st_=st_, ch=ch: e.scalar_tensor_tensor(
                    out=Yb[:, ch, :], in0=st_[:], scalar=par[:, ch - 8:ch - 7], in1=rs[:], op0=ALU.mult, op1=ALU.mult),
                    reads=[t_st, t_rs, t_par], writes=[t_y[ch]])
        for m in range(KC):
            wg_, t_wg = wg[nw % 2]
            wbr_, t_wbr = wbr[nw % 2]
            nw += 1
            S.add("pool", lambda e, wg_=wg_, m=m: e.dma_start(out=wg_[:], in_=wg_ap[m]), writes=[t_wg], dma=True)
            S.add("pool", lambda e, wbr_=wbr_, m=m: e.dma_start(out=wbr_[:], in_=wbr_ap[m]), writes=[t_wbr], dma=True)
            for b in range(3):
                pg, t_pg = C.psum()
                mm_group(S, pg[:, :], [(wg_[:, kc, b * 128:(b + 1) * 128], hb[:, kc, :]) for kc in range(KC)], t_h + [t_wg], [t_pg])
                sg_, t_sg = sig[nsig % 2]
                nsig += 1
                S.add("act", lambda e, sg_=sg_, pg=pg: e.activation(out=sg_[:], in_=pg[:, :], func=AF.Sigmoid), reads=[t_pg], writes=[t_sg])
                pb, t_pb = C.psum()
                c0, c1 = BR_CH[b]
                mm_group(S, pb[:, :], [(wbr_[:, ch, :], Yb[:, ch, :]) for ch in range(c0, c1)], t_y[c0:c1] + [t_wbr], [t_pb])
                if b == 0:
                    S.add("dve", lambda e, sg_=sg_, pb=pb: e.tensor_tensor(out=mg[:], in0=sg_[:], in1=pb[:, :], op=ALU.mult),
                          reads=[t_sg, t_pb], writes=[t_mg])
                else:
                    S.add("dve", lambda e, sg_=sg_, pb=pb: e.tensor_tensor(out=tmpm[:], in0=sg_[:], in1=pb[:, :], op=ALU.mult),
                          reads=[t_sg, t_pb], writes=[t_tmpm])
                    if b == 1:
                        S.add("dve", lambda e: e.tensor_tensor(out=mg[:], in0=mg[:], in1=tmpm[:], op=ALU.add),
                              reads=[t_mg, t_tmpm], writes=[t_mg])
                    else:
                        S.add("dve", lambda e, m=m: e.tensor_tensor(out=mT[:, m, :], in0=mg[:], in1=tmpm[:], op=ALU.add),
                              reads=[t_mg, t_tmpm], writes=[t_m[m]])
        for m in range(KC):
            wo_, t_wo = wo[nwo % 2]
            nwo += 1
            S.add("pool", lambda e, wo_=wo_, m=m: e.dma_start(out=wo_[:], in_=wo_ap[m]), writes=[t_wo], dma=True)
            po, t_po = C.psum()
            mm_group(S, po[:, :], [(wo_[:, kc, :], mT[:, kc, :]) for kc in range(KC)], t_m + [t_wo], [t_po])
            S.add("dve", lambda e, po=po, m=m: e.scalar_tensor_tensor(
                out=xT[:, m, :], in0=po[:, :], scalar=par[:, KC + m:KC + m + 1], in1=xT[:, m, :], op0=ALU.mult, op1=ALU.add),
                reads=[t_po, t_par, t_x[m]], writes=[t_x[m]])
            S.add("sp", lambda e, m=m, sl=sl: e.dma_start(out=xo_ap[m * 128:(m + 1) * 128, sl], in_=xT[:, m, :]),
                  reads=[t_x[m]], dma=True)
    return C.finish()


def run_mixC(xT, hT, YT, l, inp, gate1):
    w = inp["w_mix_in"][l]
    G0 = 11296
    wg = np.stack([w[:, G0 + b * 2048:G0 + (b + 1) * 2048].reshape(KC, 128, KC, 128) for b in range(3)], axis=0)
    wg_t = np.ascontiguousarray(wg.transpose(3, 2, 1, 0, 4).reshape(KC, 128, KC, 384))
    wbr = np.concatenate([inp["w_br_attn"][l], inp["w_br_ssd"][l], inp["w_br_conv"][l]], axis=0)
    wbr_t = np.ascontiguousarray(wbr.reshape(NYC, 128, KC, 128).transpose(2, 1, 0, 3))
    wo_t = np.ascontiguousarray(inp["w_mix_out"][l].reshape(KC, 128, KC, 128).transpose(2, 1, 0, 3))
    par = np.concatenate([cols16(inp["ssd_norm"][l]), cols16(gate1)], axis=1)
    ones = np.ones((128, 128), np.float32)
    in_maps = []
    for i in range(NCORES):
        ts = slice(i * TOK, (i + 1) * TOK)
        in_maps.append({"xT": np.ascontiguousarray(xT[:, ts]), "hT": np.ascontiguousarray(hT[:, ts]),
                        "YT": np.ascontiguousarray(YT[:, ts]), "wg": wg_t, "wbr": wbr_t, "wo": wo_t, "par": par, "ones": ones})
    r = run(prog("mixC", build_mixC), in_maps)
    return np.concatenate([r[i]["xo"] for i in range(NCORES)], axis=1)


def kernel(x, c, w_ada, b_ada, norm_gain, w_ffn_in, w_ffn_out, w_mix_in, qk_norm, rel_bias,
           w_ssd_conv, b_ssd_conv, ssd_dt_bias, ssd_a_log, ssd_d, ssd_norm, w_sc_conv,
           w_br_attn, w_br_ssd, w_br_conv, w_mix_out):
    inp = dict(w_mix_in=w_mix_in, qk_norm=qk_norm, rel_bias=rel_bias, w_ssd_conv=w_ssd_conv, b_ssd_conv=b_ssd_conv,
               ssd_dt_bias=ssd_dt_bias, ssd_a_log=ssd_a_log, ssd_d=ssd_d, ssd_norm=ssd_norm, w_sc_conv=w_sc_conv,
               w_br_attn=w_br_attn, w_br_ssd=w_br_ssd, w_br_conv=w_br_conv, w_mix_out=w_mix_out)
    inp = {k: np.asarray(v, np.float32) for k, v in inp.items()}
    w_ada = np.asarray(w_ada, np.float32)
    b_ada = np.asarray(b_ada, np.float32)
    norm_gain = np.asarray(norm_gain, np.float32)
    w_ffn_in = np.asarray(w_ffn_in, np.float32)
    w_ffn_out = np.asarray(w_ffn_out, np.float32)
    ada = compute_ada(np.asarray(c, np.float32), w_ada, b_ada)
    xT = np.ascontiguousarray(np.asarray(x, np.float32)[0].T)
    for l in range(2):
        par = np.concatenate([par_block(norm_gain[l, 0], ada[l, 0]), par_block(norm_gain[l, 1], ada[l, 1])], axis=1)
        xT, hT = run_ffn(xT, par, w_ffn_in[l, 0], w_ffn_out[l, 0], True)
        YT = run_mixB(hT, l, inp, NT=16)
        xT = run_mixC(xT, hT, YT, l, inp, ada[l, 1, 2])
        par = par_block(norm_gain[l, 2], ada[l, 2])
        xT, _ = run_ffn(xT, par, w_ffn_in[l, 1], w_ffn_out[l, 1], False)
    return np.ascontiguousarray(xT.T)[None].astype(np.float32)
```

```python
import contextlib
import threading
import numpy as np
import concourse.bass as bass
import concourse.mybir as mybir
from concourse.bass_utils import run_bass_kernel_spmd

F32 = mybir.dt.float32
BF16 = mybir.dt.bfloat16
AF = mybir.ActivationFunctionType
ALU = mybir.AluOpType
AX = mybir.AxisListType

NCORES = 8
D = 2048
KC = 16
SEQ = 8192
TOK = SEQ // NCORES
FFH = 5632
NJ = FFH // 128
NQ = 4
JQ = NJ // NQ
EPS = 1e-6


class T:
    __slots__ = ("name", "last_w", "readers", "excl")

    def __init__(self, name="", excl=False):
        self.name = name
        self.last_w = None
        self.readers = []
        self.excl = excl


class Op:
    __slots__ = ("eng", "fn", "deps", "is_dma", "sig", "sigval", "dsem", "dval", "dprev")


ENGS = ("pe", "act", "dve", "pool", "sp")
N_DMA_SEMS = {"pe": 1, "act": 1, "dve": 1, "pool": 16, "sp": 8}


class Sched:
    def __init__(self, nc):
        self.nc = nc
        self.ops = []
        self.dma_rr = {e: 0 for e in ENGS}
        self.dma_cnt = {}
        self.hook = None

    def add(self, eng, fn, reads=(), writes=(), dma=False):
        op = Op()
        op.eng = eng
        op.fn = fn
        op.is_dma = dma
        op.sig = False
        op.sigval = None
        deps = []
        excl_r = [t for t in reads if t.excl and t not in writes]
        reads = [t for t in reads if not t.excl]
        writes = list(writes) + excl_r
        for t in reads:
            if t.last_w is not None:
                deps.append(t.last_w)
        for t in writes:
            if t.last_w is not None:
                deps.append(t.last_w)
            deps.extend(t.readers)
        for t in reads:
            t.readers.append(op)
        for t in writes:
            t.last_w = op
            t.readers = []
        op.deps = [d for d in dict.fromkeys(deps) if d is not op]
        if dma:
            k = self.dma_rr[eng]
            self.dma_rr[eng] = (k + 1) % N_DMA_SEMS[eng]
            key = (eng, k)
            c = self.dma_cnt.get(key, 0) + 1
            self.dma_cnt[key] = c
            op.dsem = key
            op.dval = 16 * c
            op.dprev = 16 * (c - 1)
        self.ops.append(op)
        if self.hook is not None:
            self.hook()
        return op

    def emit(self):
        nc = self.nc
        ops = self.ops
        for op in ops:
            for d in op.deps:
                if d.is_dma:
                    continue
                if d.eng == op.eng and d.eng == "pe" and not op.is_dma:
                    continue
                d.sig = True
        cnt = {e: 0 for e in ENGS}
        for op in ops:
            if not op.is_dma and op.sig:
                cnt[op.eng] += 1
                op.sigval = cnt[op.eng]
        with contextlib.ExitStack() as st:
            esem = {e: st.enter_context(nc.semaphore("s_" + e)) for e in ENGS}
            dsem = {}
            for key in self.dma_cnt:
                dsem[key] = st.enter_context(nc.semaphore("d_%s%d" % key))
            block = st.enter_context(nc.Block())
            by_eng = {e: [o for o in ops if o.eng == e] for e in ENGS}

            def run(eng_name, eng):
                known = {}

                def wait(sem_key, sem, val):
                    if known.get(sem_key, 0) >= val:
                        return
                    known[sem_key] = val
                    eng.wait_ge(sem, val)

                for op in by_eng[eng_name]:
                    for d in op.deps:
                        if d.is_dma:
                            wait(d.dsem, dsem[d.dsem], d.dval)
                        elif d.sigval is not None:
                            wait(d.eng, esem[d.eng], d.sigval)
                    if op.is_dma:
                        if op.dprev > 0:
                            wait(op.dsem, dsem[op.dsem], op.dprev)
                        op.fn(eng).then_inc(dsem[op.dsem], 16)
                    else:
                        ins = op.fn(eng)
                        if op.sig:
                            ins.then_inc(esem[op.eng], 1)
                if eng_name == "sp":
                    for key, c in self.dma_cnt.items():
                        wait(key, dsem[key], 16 * c)

            block.tensor(lambda e: run("pe", e))
            block.scalar(lambda e: run("act", e))
            block.vector(lambda e: run("dve", e))
            block.gpsimd(lambda e: run("pool", e))
            block.sync(lambda e: run("sp", e))


class Ctx:
    def __init__(self):
        self.nc = bass.Bass("TRN2", target_bir_lowering=False)
        self.S = Sched(self.nc)
        self.st = contextlib.ExitStack()
        self.ps = []
        self.ps_i = 0
        self.n = 0
        self.tls = threading.local()
        self.pool_i = {}
        self.pool_banks = [[0, 1, 2, 3], [4, 5, 6, 7]]

    def din(self, name, shape, dt=F32):
        return self.nc.dram_tensor(name, list(shape), dt, kind="ExternalInput").ap()

    def dout(self, name, shape, dt=F32):
        return self.nc.dram_tensor(name, list(shape), dt, kind="ExternalOutput").ap()

    def sb(self, shape, dt, name=None):
        self.n += 1
        t = self.st.enter_context(self.nc.sbuf_tensor("%s_%d" % (name or "t", self.n), list(shape), dt))
        return t, T(name or "t")

    def init_psum(self):
        for i in range(8):
            t = self.st.enter_context(self.nc.psum_tensor("ps%d" % i, [128, 512], F32))
            self.ps.append((t, T("ps%d" % i, excl=True)))

    def psum(self):
        pool = getattr(self.tls, "pool", 0)
        if pool == 0:
            r = self.ps[self.ps_i]
            self.ps_i = (self.ps_i + 1) % 8
            return r
        banks = self.pool_banks[pool - 1]
        k = self.pool_i.get(pool, 0)
        self.pool_i[pool] = (k + 1) % len(banks)
        return self.ps[banks[k]]

    def finish(self):
        self.S.emit()
        self.st.close()
        return self.nc


def interleave(C, fns, banks=None):
    S = C.S
    n = len(fns)
    if n == 0:
        return
    if n == 1:
        fns[0]()
        return
    if banks is not None:
        C.pool_banks = banks
        C.pool_i = {}
    sems = [threading.Semaphore(0) for _ in range(n)]
    alive = [True] * n
    idx = {}
    err = []
    done = threading.Event()

    def nxt(i):
        for k in range(1, n + 1):
            j = (i + k) % n
            if alive[j]:
                return j
        return None

    def hook():
        i = idx[threading.get_ident()]
        j = nxt(i)
        if j is not None and j != i:
            sems[j].release()
            sems[i].acquire()

    def worker(i):
        sems[i].acquire()
        idx[threading.get_ident()] = i
        try:
            C.tls.pool = i + 1
            fns[i]()
        except BaseException as ex:
            err.append(ex)
        finally:
            alive[i] = False
            j = nxt(i)
            if j is not None:
                sems[j].release()
            else:
                done.set()

    ths = [threading.Thread(target=worker, args=(i,)) for i in range(n)]
    S.hook = hook
    for t in ths:
        t.start()
    sems[0].release()
    done.wait()
    for t in ths:
        t.join()
    S.hook = None
    if err:
        raise err[0]


def mm_group(S, out_ap, pairs, reads, writes):
    n = len(pairs)

    def fn(e):
        ins = None
        for i, (l, r) in enumerate(pairs):
            ins = e.matmul(out_ap, l, r, start=(i == 0), stop=(i == n - 1))
        return ins

    return S.add("pe", fn, reads=reads, writes=writes)


def load_consts(C, ones_ap):
    ones, t_ones = C.sb([128, 128], F32, "ones")
    C.S.add("sp", lambda e: e.dma_start(out=ones[:], in_=ones_ap), writes=[t_ones], dma=True)
    return ones, t_ones


def modulate(C, xT, t_x, hT, t_h, a_col, shift_col, t_par, ones, t_ones, ntok, out_dram=None):
    S = C.S
    sq = [C.sb([128, 512], F32, "sq") for _ in range(2)]
    rs, t_rs = C.sb([128, 512], F32, "rstd")
    tmp = [C.sb([128, 512], F32, "modtmp") for _ in range(2)]
    tmp32 = [C.sb([128, 512], F32, "modtmp32") for _ in range(2)] if out_dram is not None else None
    for tt in range(ntok // 512):
        sl = slice(tt * 512, (tt + 1) * 512)
        ps, t_ps = C.psum()
        for kc in range(KC):
            sqt, t_sq = sq[kc % 2]
            S.add("act", lambda e, sqt=sqt, kc=kc, sl=sl: e.activation(out=sqt[:], in_=xT[:, kc, sl], func=AF.Square),
                  reads=[t_x], writes=[t_sq])
            S.add("pe", lambda e, sqt=sqt, kc=kc, ps=ps: e.matmul(ps[:, :], ones[:], sqt[:], start=(kc == 0), stop=(kc == KC - 1)),
                  reads=[t_sq, t_ones], writes=[t_ps])
        S.add("act", lambda e, ps=ps: e.activation(out=rs[:], in_=ps[:, :], func=AF.Sqrt, scale=1.0 / D, bias=EPS),
              reads=[t_ps], writes=[t_rs])
        S.add("dve", lambda e: e.reciprocal(out=rs[:], in_=rs[:]), reads=[t_rs], writes=[t_rs])
        for kc in range(KC):
            tm, t_tm = tmp[kc % 2]
            S.add("dve", lambda e, tm=tm, kc=kc, sl=sl: e.scalar_tensor_tensor(
                out=tm[:], in0=xT[:, kc, sl], scalar=a_col[:, kc:kc + 1], in1=rs[:], op0=ALU.mult, op1=ALU.mult),
                reads=[t_x, t_rs] + t_par, writes=[t_tm])
            S.add("act", lambda e, tm=tm, kc=kc, sl=sl: e.activation(
                out=hT[:, kc, sl], in_=tm[:], func=AF.Identity, bias=shift_col[:, kc:kc + 1], scale=1.0),
                reads=[t_tm] + t_par, writes=[t_h])
            if out_dram is not None:
                t32, t_t32 = tmp32[kc % 2]
                S.add("act", lambda e, tm=tm, t32=t32, kc=kc: e.activation(
                    out=t32[:], in_=tm[:], func=AF.Identity, bias=shift_col[:, kc:kc + 1], scale=1.0),
                    reads=[t_tm] + t_par, writes=[t_t32])
                S.add("sp", lambda e, t32=t32, kc=kc, sl=sl: e.dma_start(out=out_dram[kc * 128:(kc + 1) * 128, sl], in_=t32[:]),
                      reads=[t_t32], dma=True)


def prep_mod_params(C, par_ap, n_sets):
    S = C.S
    par, t_par = C.sb([128, n_sets * 4 * KC], F32, "par")
    acol, t_a = C.sb([128, n_sets * KC], F32, "acol")
    S.add("sp", lambda e: e.dma_start(out=par[:], in_=par_ap), writes=[t_par], dma=True)
    out = []
    for s in range(n_sets):
        b = s * 4 * KC
        S.add("dve", lambda e, b=b, s=s: e.scalar_tensor_tensor(
            out=acol[:, s * KC:(s + 1) * KC], in0=par[:, b + 2 * KC:b + 3 * KC], scalar=1.0, in1=par[:, b:b + KC],
            op0=ALU.add, op1=ALU.mult), reads=[t_par], writes=[t_a])
        out.append((acol[:, s * KC:(s + 1) * KC], par[:, b + KC:b + 2 * KC], par[:, b + 3 * KC:b + 4 * KC]))
    return out, [t_par, t_a]


ADA_COLS = 2 * 18432 // NCORES
ADA_NB = ADA_COLS // 512


def build_ada():
    C = Ctx()
    S = C.S
    c_ap = C.din("c", [128, KC])
    w_ap = C.din("w", [ADA_NB, 128, KC, 512])
    b_ap = C.din("b", [1, ADA_COLS])
    o_ap = C.dout("o", [1, ADA_COLS])
    C.init_psum()
    ct, t_c = C.sb([128, KC], F32, "c")
    bt, t_b = C.sb([1, ADA_COLS], F32, "b")
    ot, t_o = C.sb([1, ADA_COLS], F32, "o")
    wt = [C.sb([128, KC, 512], F32, "w") for _ in range(2)]
    S.add("sp", lambda e: e.dma_start(out=ct[:], in_=c_ap), writes=[t_c], dma=True)
    S.add("sp", lambda e: e.dma_start(out=bt[:], in_=b_ap), writes=[t_b], dma=True)
    S.add("act", lambda e: e.activation(out=ct[:], in_=ct[:], func=AF.Silu), reads=[t_c], writes=[t_c])
    for nb in range(ADA_NB):
        w, t_w = wt[nb % 2]
        for kc in range(KC):
            S.add("sp", lambda e, w=w, nb=nb, kc=kc: e.dma_start(out=w[:, kc, :], in_=w_ap[nb, :, kc, :]), writes=[t_w], dma=True)
        ps, t_ps = C.psum()
        mm_group(S, ps[0:1, :], [(ct[:, kc:kc + 1], w[:, kc, :]) for kc in range(KC)], [t_c, t_w], [t_ps])
        S.add("dve", lambda e, ps=ps, nb=nb: e.tensor_tensor(
            out=ot[:, nb * 512:(nb + 1) * 512], in0=ps[0:1, :], in1=bt[:, nb * 512:(nb + 1) * 512], op=ALU.add),
            reads=[t_ps, t_b], writes=[t_o])
    S.add("sp", lambda e: e.dma_start(out=o_ap, in_=ot[:]), reads=[t_o], dma=True)
    return C.finish()


def build_ffn(emit_h_next, dbg=None):
    C = Ctx()
    S = C.S
    nsets = 2 if emit_h_next else 1
    x_ap = C.din("xT", [D, TOK])
    par_ap = C.din("par", [128, nsets * 4 * KC])
    win_ap = C.din("win", [NJ, 128, KC, 256])
    wout_ap = C.din("wout", [NQ, KC, 128, JQ, 128])
    ones_ap = C.din("ones", [128, 128])
    xo_ap = C.dout("xo", [D, TOK])
    ho_ap = C.dout("ho", [D, TOK]) if emit_h_next else None
    C.init_psum()
    ones, t_ones = load_consts(C, ones_ap)
    xT, t_x = C.sb([128, KC, TOK], F32, "xT")
    hT, t_h = C.sb([128, KC, TOK], BF16, "hT")
    actT, t_act = C.sb([128, JQ, TOK], BF16, "actT")
    wi = [C.sb([128, KC, 256], BF16, "wi") for _ in range(2)]
    wo = [C.sb([128, JQ, 128], BF16, "wo") for _ in range(2)]
    sg = [C.sb([128, 512], F32, "sg") for _ in range(2)]
    g05, t_g = C.sb([128, KC], F32, "g05")
    for kc in range(KC):
        S.add("sp", lambda e, kc=kc: e.dma_start(out=xT[:, kc, :], in_=x_ap[kc * 128:(kc + 1) * 128, :]),
              writes=[t_x], dma=True)
    sets, t_par = prep_mod_params(C, par_ap, nsets)
    a_col, shift_col, gate_col = sets[0]
    S.add("dve", lambda e: e.tensor_scalar(out=g05[:], in0=gate_col, scalar1=0.5, scalar2=None, op0=ALU.mult),
          reads=t_par, writes=[t_g])
    modulate(C, xT, t_x, hT, t_h, a_col, shift_col, t_par, ones, t_ones, TOK, out_dram=(ho_ap if dbg == "mod" else None))
    if dbg == "mod":
        for kc in range(KC):
            S.add("sp", lambda e, kc=kc: e.dma_start(out=xo_ap[kc * 128:(kc + 1) * 128, :], in_=xT[:, kc, :]),
                  reads=[t_x], dma=True)
        return C.finish()
    nwi = 0
    nwo = 0
    nsg = 0
    for qh in range(NQ):
        for jj in range(JQ):
            j = qh * JQ + jj
            w, t_w = wi[nwi % 2]
            nwi += 1
            S.add("pool", lambda e, w=w, j=j: e.dma_start(out=w[:], in_=win_ap[j]), writes=[t_w], dma=True)
            for tt in range(TOK // 512):
                sl = slice(tt * 512, (tt + 1) * 512)
                pg, t_pg = C.psum()
                pu, t_pu = C.psum()
                mm_group(S, pg[:, :], [(w[:, kc, 0:128], hT[:, kc, sl]) for kc in range(KC)], [t_w, t_h], [t_pg])
                mm_group(S, pu[:, :], [(w[:, kc, 128:256], hT[:, kc, sl]) for kc in range(KC)], [t_w, t_h], [t_pu])
                s_, t_s = sg[nsg % 2]
                nsg += 1
                S.add("act", lambda e, s_=s_, pg=pg: e.activation(out=s_[:], in_=pg[:, :], func=AF.Silu),
                      reads=[t_pg], writes=[t_s])
                S.add("dve", lambda e, s_=s_, pu=pu, jj=jj, sl=sl: e.tensor_tensor(
                    out=actT[:, jj, sl], in0=s_[:], in1=pu[:, :], op=ALU.mult),
                    reads=[t_s, t_pu], writes=[t_act])
        for m in range(KC):
            w, t_w = wo[nwo % 2]
            nwo += 1
            S.add("pool", lambda e, w=w, qh=qh, m=m: e.dma_start(out=w[:], in_=wout_ap[qh, m]), writes=[t_w], dma=True)
            for tt in range(TOK // 512):
                sl = slice(tt * 512, (tt + 1) * 512)
                po, t_po = C.psum()
                mm_group(S, po[:, :], [(w[:, jj, :], actT[:, jj, sl]) for jj in range(JQ)], [t_w, t_act], [t_po])
                S.add("dve", lambda e, po=po, m=m, sl=sl: e.scalar_tensor_tensor(
                    out=xT[:, m, sl], in0=po[:, :], scalar=g05[:, m:m + 1], in1=xT[:, m, sl], op0=ALU.mult, op1=ALU.add),
                    reads=[t_po, t_g, t_x], writes=[t_x])
    for kc in range(KC):
        S.add("sp", lambda e, kc=kc: e.dma_start(out=xo_ap[kc * 128:(kc + 1) * 128, :], in_=xT[:, kc, :]),
              reads=[t_x], dma=True)
    if emit_h_next:
        a2, sh2, _ = sets[1]
        modulate(C, xT, t_x, hT, t_h, a2, sh2, t_par, ones, t_ones, TOK, out_dram=ho_ap)
    return C.finish()


def cols16(v):
    return np.ascontiguousarray(np.asarray(v, np.float32).reshape(KC, 128).T)


def tile_win(w_in):
    w = w_in.reshape(KC, 128, 2, NJ, 128)
    return np.ascontiguousarray(w.transpose(3, 1, 0, 2, 4).reshape(NJ, 128, KC, 256))


def tile_wout(w_out):
    w = w_out.reshape(NQ, JQ, 128, KC, 128)
    return np.ascontiguousarray(w.transpose(0, 3, 2, 1, 4))


_PROG = {}


def prog(name, fn, *a):
    key = (name,) + a
    if key not in _PROG:
        _PROG[key] = fn(*a)
    return _PROG[key]


def run(nc, in_maps):
    res = run_bass_kernel_spmd(nc, in_maps, core_ids=list(range(NCORES)))
    return res.results


def compute_ada(c, w_ada, b_ada):
    cc = cols16(c.reshape(-1))
    wa = np.concatenate([w_ada[0], w_ada[1]], axis=1)
    ba = np.concatenate([b_ada[0], b_ada[1]], axis=0)
    in_maps = []
    for i in range(NCORES):
        ws = wa[:, i * ADA_COLS:(i + 1) * ADA_COLS].reshape(KC, 128, ADA_NB, 512)
        in_maps.append({
            "c": cc,
            "w": np.ascontiguousarray(ws.transpose(2, 1, 0, 3)),
            "b": np.ascontiguousarray(ba[i * ADA_COLS:(i + 1) * ADA_COLS].reshape(1, -1)),
        })
    r = run(prog("ada", build_ada), in_maps)
    ada = np.concatenate([r[i]["o"].reshape(-1) for i in range(NCORES)])
    return ada.reshape(2, 3, 3, D)


def par_block(norm_gain_li, ada_li):
    return np.concatenate([cols16(norm_gain_li), cols16(ada_li[0]), cols16(ada_li[1]), cols16(ada_li[2])], axis=1)


def run_ffn(xT, par, w_in, w_out, emit_h_next, dbg=None):
    win_t = tile_win(w_in)
    wout_t = tile_wout(w_out)
    ones = np.ones((128, 128), np.float32)
    in_maps = []
    for i in range(NCORES):
        in_maps.append({"xT": np.ascontiguousarray(xT[:, i * TOK:(i + 1) * TOK]), "par": par,
                        "win": win_t, "wout": wout_t, "ones": ones})
    r = run(prog("ffn", build_ffn, emit_h_next, dbg), in_maps)
    xo = np.concatenate([r[i]["xo"] for i in range(NCORES)], axis=1)
    ho = np.concatenate([r[i]["ho"] for i in range(NCORES)], axis=1) if emit_h_next else None
    return xo, ho


NCOLB = 1540
C_Q, C_K, C_Z, C_XS, C_B, C_C, C_CB, C_CC, C_CX, C_V = 0, 128, 256, 512, 768, 896, 1024, 1152, 1280, 1408
SCALE = 128 ** -0.5
NEG = -1e30


PARTS = ('conv', 'ssd', 'attn')
STOP = 0


def build_mixB(NT):
    C = Ctx()
    S = C.S
    ntok = NT * 512
    h_ap = C.din("hT", [D, ntok])
    w_ap = C.din("wB", [128, KC, NCOLB])
    sp_ap = C.din("sp", [128, 24])
    sp64_ap = C.din("sp64", [64, 24])
    cst_ap = C.din("cst", [128, 5 * 128 + 512 + 512])
    ya_ap = C.dout("ya", [ntok, 128])
    yg_ap = C.dout("yg", [256, ntok])
    yc_ap = C.dout("yc", [128, ntok])
    C.init_psum()
    cst, t_cst = C.sb([128, 5 * 128 + 1024], F32, "cst")
    S.add("sp", lambda e: e.dma_start(out=cst[:], in_=cst_ap), writes=[t_cst], dma=True)
    ident = cst[:, 0:128]
    ones = cst[:, 128:256]
    TA = cst[:, 256:384]
    TB = cst[:, 384:512]
    NEGM = cst[:, 512:640]
    U0 = cst[:, 640:896]
    UTRI = cst[:, 640:768]
    U1 = cst[:, 896:1152]
    SELH = cst[0:4, 1152:1664]
    spt, t_sp = C.sb([128, 24], F32, "sp")
    sp64, t_sp64 = C.sb([64, 24], F32, "sp64")
    S.add("sp", lambda e: e.dma_start(out=spt[:], in_=sp_ap), writes=[t_sp], dma=True)
    S.add("sp", lambda e: e.dma_start(out=sp64[:], in_=sp64_ap), writes=[t_sp64], dma=True)
    S.add("dve", lambda e: e.tensor_tensor(out=TA, in0=TA, in1=NEGM, op=ALU.add), reads=[t_cst], writes=[t_cst])
    aneg, t_an = C.sb([128, 4], F32, "aneg")
    S.add("act", lambda e: e.activation(out=aneg[:], in_=spt[:, 20:24], func=AF.Exp), reads=[t_sp], writes=[t_an])
    S.add("dve", lambda e: e.tensor_scalar(out=aneg[:], in0=aneg[:], scalar1=-1.0, scalar2=None, op0=ALU.mult),
          reads=[t_an], writes=[t_an])
    wB, _ = C.sb([128, KC, NCOLB], BF16, "wB")
    t_wB = [T("wB%d" % kc) for kc in range(KC)]
    for kc in range(KC):
        S.add("pool", lambda e, kc=kc: e.dma_start(out=wB[:, kc, :], in_=w_ap[:, kc, :]), writes=[t_wB[kc]], dma=True)
    hbuf = [C.sb([128, KC, 512], BF16, "hb")[0] for _ in range(2)]
    t_hb = [[T("hb") for _ in range(KC)] for _ in range(2)]
    KT, t_KT = C.sb([128, ntok], BF16, "KT")
    vaug, t_v = C.sb([128, NT * 4, 130], BF16, "vaug")
    S.add("pool", lambda e: e.memset(vaug[:], 1.0), writes=[t_v])
    kmean, t_km = C.sb([128, 32], F32, "kmean")
    S.add("pool", lambda e: e.memset(kmean[:], 0.0), writes=[t_km])

    def sbt(shape, dt, name):
        return C.sb(shape, dt, name)

    sq, t_sq = sbt([128, 512], F32, "sq")
    rs, t_rs = sbt([128, 512], F32, "rs")
    qn32, t_qn = sbt([128, 512], F32, "qn32")
    qnb, t_qnb = sbt([128, 512], BF16, "qnb")
    kn32, t_kn = sbt([128, 2, 256], F32, "kn32")
    kms, t_kms = sbt([128, 2], F32, "kms")
    sz = [sbt([64, 512], F32, "sz") for _ in range(4)]
    xpre = [sbt([64, 515], F32, "xpre") for _ in range(4)]
    bpre, t_bpre = sbt([128, 515], F32, "bpre")
    cpre, t_cpre = sbt([128, 515], F32, "cpre")
    for (t_, tt_) in xpre + [(bpre, t_bpre), (cpre, t_cpre)]:
        S.add("pool", lambda e, t_=t_: e.memset(t_[:, 0:3], 0.0), writes=[tt_])
    xsT = [sbt([64, 512], F32, "xsT") for _ in range(4)]
    BT32, t_BT32 = sbt([128, 512], F32, "BT32")
    CT32, t_CT32 = sbt([128, 512], F32, "CT32")
    BTb, t_BTb = sbt([128, 512], BF16, "BTb")
    CTb, t_CTb = sbt([128, 512], BF16, "CTb")
    cacc = [sbt([128, 512], F32, "cacc") for _ in range(2)]
    cBs, t_cBs = sbt([128, 512], F32, "cBs")
    cCs, t_cCs = sbt([128, 512], F32, "cCs")
    ucv, t_ucv = sbt([128, 514], F32, "ucv")
    S.add("pool", lambda e: e.memset(ucv[:, 0:2], 0.0), writes=[t_ucv])
    ycs, t_ycs = sbt([128, 512], F32, "ycs")
    dtraw, t_dtr = sbt([128, 4, 4], F32, "dtraw")
    dtt, t_dtt = sbt([128, 4, 4], F32, "dtt")
    dtA, t_dtA = sbt([128, 4, 4], F32, "dtA")
    wd, t_wd = sbt([128, 4, 4], F32, "wd")
    xd = [sbt([128, 256], BF16, "xd") for _ in range(4)]
    xdd = [sbt([128, 256], BF16, "xdd") for _ in range(4)]
    Btok = [sbt([128, 128], BF16, "Btok") for _ in range(4)]
    acs, t_acs = sbt([128, 12], F32, "acs")
    d2, t_d2 = sbt([128, 8], F32, "d2")
    acsT, t_acsT = sbt([4, 256], F32, "acsT")
    cbm, t_cbm = sbt([128, 384], F32, "cbm")
    E = [sbt([128, 256], F32, "E") for _ in range(4)]
    Cdec, t_Cdec = sbt([128, 256], BF16, "Cdec")
    Dm, t_Dm = sbt([128, 384], F32, "Dm")
    MT, t_MT = sbt([128, 384], BF16, "MT")
    prev32, t_p32 = sbt([128, 256], F32, "prev32")
    prevb, t_pb = sbt([128, 256], BF16, "prevb")
    S.add("pool", lambda e: e.memset(prev32[:], 0.0), writes=[t_p32])
    S.add("pool", lambda e: e.memset(prevb[:], 0.0), writes=[t_pb])
    yt1, t_yt1 = sbt([64, 256], F32, "yt1")
    ygs = [sbt([64, 256], F32, "ygs") for _ in range(2)]
    gsbL = [sbt([128, 32], F32, "gsb") for _ in range(2)]
    top8L = [sbt([128, 8], F32, "top8") for _ in range(2)]
    mselL = [sbt([128, 32], F32, "msel") for _ in range(2)]
    accL = [sbt([128, 130], F32, "acc") for _ in range(2)]
    tmpS2L = [[sbt([128, 256], F32, "tmpS") for _ in range(2)] for _ in range(2)]
    pt2L = [[sbt([128, 2, 128], BF16, "pt") for _ in range(2)] for _ in range(2)]
    nptL = [0, 0]
    recL = [sbt([128, 1], F32, "rec") for _ in range(2)]
    yqL = [[sbt([128, 128], F32, "yq") for _ in range(2)] for _ in range(2)]
    nyqL = [0, 0]
    nyg = 0
    nyq = 0
    nosb = 0
    osb = [sbt([128, 130], F32, "osb") for _ in range(2)]
    if STOP == 1:
        return C.finish()

    def load_h(tt_):
        hb_ = hbuf[tt_ % 2]
        for kc in range(KC):
            S.add("pool", lambda e, kc=kc, hb_=hb_, t0=tt_ * 512: e.dma_start(
                out=hb_[:, kc, :], in_=h_ap[kc * 128:(kc + 1) * 128, t0:t0 + 512]), writes=[t_hb[tt_ % 2][kc]], dma=True)

    load_h(0)
    for tt in range(NT):
        tok0 = tt * 512
        hb = hbuf[tt % 2]
        thb = t_hb[tt % 2]

        def proj(col0, M):
            ps, t_ps = C.psum()
            mm_group(S, ps[0:M, :], [(wB[:, kc, col0:col0 + M], hb[:, kc, :]) for kc in range(KC)], thb + t_wB, [t_ps])
            return ps, t_ps

        for which in range(2):
            ps, t_ps = proj(C_Q if which == 0 else C_K, 128)
            S.add("act", lambda e, ps=ps: e.activation(out=sq[:], in_=ps[:, :], func=AF.Square), reads=[t_ps], writes=[t_sq])
            ps2, t_ps2 = C.psum()
            S.add("pe", lambda e, ps2=ps2: e.matmul(ps2[:, :], ones, sq[:], start=True, stop=True),
                  reads=[t_sq, t_cst], writes=[t_ps2])
            S.add("act", lambda e, ps2=ps2: e.activation(out=rs[:], in_=ps2[:, :], func=AF.Sqrt, scale=1.0 / 128, bias=EPS),
                  reads=[t_ps2], writes=[t_rs])
            S.add("dve", lambda e: e.reciprocal(out=rs[:], in_=rs[:]), reads=[t_rs], writes=[t_rs])
            if which == 0:
                S.add("dve", lambda e, ps=ps: e.scalar_tensor_tensor(
                    out=qn32[:], in0=ps[:, :], scalar=spt[:, 0:1], in1=rs[:], op0=ALU.mult, op1=ALU.mult),
                    reads=[t_ps, t_rs, t_sp], writes=[t_qn])
                S.add("act", lambda e: e.activation(out=qnb[:], in_=qn32[:], func=AF.Identity), reads=[t_qn], writes=[t_qnb])
            else:
                for a in range(2):
                    S.add("dve", lambda e, ps=ps, a=a: e.scalar_tensor_tensor(
                        out=kn32[:, a, :], in0=ps[:, a * 256:(a + 1) * 256], scalar=spt[:, 1:2], in1=rs[:, a * 256:(a + 1) * 256],
                        op0=ALU.mult, op1=ALU.mult), reads=[t_ps, t_rs, t_sp], writes=[t_kn])
                    S.add("act", lambda e, a=a, tok0=tok0: e.activation(
                        out=KT[:, tok0 + a * 256:tok0 + (a + 1) * 256], in_=kn32[:, a, :], func=AF.Identity),
                        reads=[t_kn], writes=[t_KT])
                S.add("dve", lambda e: e.tensor_reduce(out=kms[:], in_=kn32[:], axis=AX.X, op=ALU.add), reads=[t_kn], writes=[t_kms])
                S.add("dve", lambda e, tt=tt: e.tensor_scalar(
                    out=kmean[:, 2 * tt:2 * tt + 2], in0=kms[:], scalar1=1.0 / 256, scalar2=None, op0=ALU.mult),
                    reads=[t_kms], writes=[t_km])
        if STOP == 2:
            return C.finish()
        for h in range(4):
            ps, t_ps = proj(C_Z + h * 64, 64)
            S.add("act", lambda e, ps=ps, h=h: e.activation(out=sz[h][0][:], in_=ps[0:64, :], func=AF.Silu),
                  reads=[t_ps], writes=[sz[h][1]])
        for h in range(4):
            ps, t_ps = proj(C_XS + h * 64, 64)
            S.add("act", lambda e, ps=ps, h=h: e.activation(out=xpre[h][0][:, 3:515], in_=ps[0:64, :], func=AF.Identity),
                  reads=[t_ps], writes=[xpre[h][1]])
        ps, t_ps = proj(C_B, 128)
        S.add("act", lambda e, ps=ps: e.activation(out=bpre[:, 3:515], in_=ps[:, :], func=AF.Identity), reads=[t_ps], writes=[t_bpre])
        ps, t_ps = proj(C_C, 128)
        S.add("act", lambda e, ps=ps: e.activation(out=cpre[:, 3:515], in_=ps[:, :], func=AF.Identity), reads=[t_ps], writes=[t_cpre])
        ps, t_ps = proj(C_CB, 128)
        S.add("act", lambda e, ps=ps: e.activation(out=cBs[:], in_=ps[:, :], func=AF.Identity), reads=[t_ps], writes=[t_cBs])
        ps, t_ps = proj(C_CC, 128)
        S.add("act", lambda e, ps=ps: e.activation(out=cCs[:], in_=ps[:, :], func=AF.Identity), reads=[t_ps], writes=[t_cCs])
        ps, t_ps = proj(C_CX, 128)
        S.add("dve", lambda e, ps=ps: e.tensor_tensor(out=ucv[:, 2:514], in0=cCs[:], in1=ps[:, :], op=ALU.mult),
              reads=[t_ps, t_cCs], writes=[t_ucv])
        if STOP == 3:
            return C.finish()
        for s in range(4):
            ps, t_ps = C.psum()
            mm_group(S, ps[:, 0:128], [(hb[:, kc, s * 128:(s + 1) * 128], wB[:, kc, C_V:C_V + 128]) for kc in range(KC)],
                     thb + t_wB, [t_ps])
            if STOP != 8:
                mm_group(S, ps[:, 128:132], [(hb[:, kc, s * 128:(s + 1) * 128], wB[:, kc, C_V + 128:C_V + 132]) for kc in range(KC)],
                         thb + t_wB, [t_ps])
            if STOP == 4:
                continue
            S.add("act", lambda e, ps=ps, s=s, tt=tt: e.activation(out=vaug[:, 4 * tt + s, 0:128], in_=ps[:, 0:128], func=AF.Identity),
                  reads=[t_ps], writes=[t_v])
            if STOP in (5, 7):
                continue
            S.add("act", lambda e, ps=ps, s=s: e.activation(out=dtraw[:, s, :], in_=ps[:, 128:132], func=AF.Identity),
                  reads=[t_ps], writes=[t_dtr])
            S.add("pool", lambda e, s=s: e.tensor_tensor(out=dtraw[:, s, :], in0=dtraw[:, s, :], in1=spt[:, 16:20], op=ALU.add),
                  reads=[t_dtr, t_sp], writes=[t_dtr])
            if STOP == 6:
                continue
        if tt + 1 < NT:
            load_h(tt + 1)
        if 'conv' in PARTS:
            a0, t_a0 = cacc[0]
            S.add("dve", lambda e: e.tensor_scalar(out=a0[:], in0=ucv[:, 0:512], scalar1=spt[:, 13:14], scalar2=None, op0=ALU.mult),
                  reads=[t_ucv, t_sp], writes=[t_a0])
            for j in (1, 2):
                S.add("dve", lambda e, j=j: e.scalar_tensor_tensor(
                    out=a0[:], in0=ucv[:, j:j + 512], scalar=spt[:, 13 + j:14 + j], in1=a0[:], op0=ALU.mult, op1=ALU.add),
                    reads=[t_ucv, t_sp, t_a0], writes=[t_a0])
            S.add("dve", lambda e: e.tensor_tensor(out=ycs[:], in0=cBs[:], in1=a0[:], op=ALU.mult), reads=[t_cBs, t_a0], writes=[t_ycs])
            S.add("sp", lambda e, tok0=tok0: e.dma_start(out=yc_ap[:, tok0:tok0 + 512], in_=ycs[:]), reads=[t_ycs], dma=True)
            S.add("dve", lambda e: e.tensor_copy(out=ucv[:, 0:2], in_=ucv[:, 512:514]), reads=[t_ucv], writes=[t_ucv])
        def sec_ssd():
            nonlocal nyg
            def ssd_conv(pre, t_pre, P, wsrc, t_w, c0, outs):
                ac, t_ac = cacc[1]
                S.add("dve", lambda e: e.tensor_scalar(out=ac[0:P, :], in0=pre[:, 0:512], scalar1=wsrc[:, c0:c0 + 1], scalar2=None, op0=ALU.mult),
                      reads=[t_pre, t_w], writes=[t_ac])
                for j in (1, 2, 3):
                    S.add("dve", lambda e, j=j: e.scalar_tensor_tensor(
                        out=ac[0:P, :], in0=pre[:, j:j + 512], scalar=wsrc[:, c0 + j:c0 + j + 1], in1=ac[0:P, :], op0=ALU.mult, op1=ALU.add),
                        reads=[t_pre, t_w, t_ac], writes=[t_ac])
                o, t_o = outs
                S.add("act", lambda e: e.activation(out=o[:], in_=ac[0:P, :], func=AF.Silu, bias=wsrc[:, c0 + 4:c0 + 5], scale=1.0),
                      reads=[t_ac, t_w], writes=[t_o])
                S.add("dve", lambda e: e.tensor_copy(out=pre[:, 0:3], in_=pre[:, 512:515]), reads=[t_pre], writes=[t_pre])

            for h in range(4):
                ssd_conv(xpre[h][0], xpre[h][1], 64, sp64, t_sp64, h * 6, xsT[h])
            ssd_conv(bpre, t_bpre, 128, spt, t_sp, 3, (BT32, t_BT32))
            ssd_conv(cpre, t_cpre, 128, spt, t_sp, 8, (CT32, t_CT32))
            S.add("pool", lambda e: e.tensor_copy(out=BTb[:], in_=BT32[:]), reads=[t_BT32], writes=[t_BTb])
            S.add("pool", lambda e: e.tensor_copy(out=CTb[:], in_=CT32[:]), reads=[t_CT32], writes=[t_CTb])
            S.add("act", lambda e: e.activation(out=dtt[:], in_=dtraw[:], func=AF.Exp), reads=[t_dtr], writes=[t_dtt])
            S.add("act", lambda e: e.activation(out=dtt[:], in_=dtt[:], func=AF.Ln, bias=1.0, scale=1.0), reads=[t_dtt], writes=[t_dtt])
            for s in range(4):
                S.add("dve", lambda e, s=s: e.tensor_tensor(out=dtA[:, s, :], in0=dtt[:, s, :], in1=aneg[:], op=ALU.mult),
                      reads=[t_dtt, t_an], writes=[t_dtA])
            for cc in range(2):
                ccol = slice(cc * 256, (cc + 1) * 256)
                s0, s1 = 2 * cc, 2 * cc + 1
                psa, t_psa = C.psum()
                S.add("pe", lambda e, psa=psa, s0=s0: e.matmul(psa[:, 0:4], UTRI, dtA[:, s0, :], start=True, stop=True),
                      reads=[t_dtA, t_cst], writes=[t_psa])
                mm_group(S, psa[:, 4:8], [(ones, dtA[:, s0, :]), (UTRI, dtA[:, s1, :])], [t_dtA, t_cst], [t_psa])
                mm_group(S, psa[:, 8:12], [(ones, dtA[:, s0, :]), (ones, dtA[:, s1, :])], [t_dtA, t_cst], [t_psa])
                S.add("dve", lambda e, psa=psa: e.tensor_copy(out=acs[:], in_=psa[:, 0:12]), reads=[t_psa], writes=[t_acs])
                for s in range(2):
                    S.add("dve", lambda e, s=s: e.tensor_tensor(out=d2[:, 4 * s:4 * s + 4], in0=acs[:, 8:12], in1=acs[:, 4 * s:4 * s + 4], op=ALU.subtract),
                          reads=[t_acs], writes=[t_d2])
                S.add("act", lambda e: e.activation(out=d2[:], in_=d2[:], func=AF.Exp), reads=[t_d2], writes=[t_d2])
                for s in range(2):
                    S.add("dve", lambda e, s=s, cc=cc: e.tensor_tensor(out=wd[:, 2 * cc + s, :], in0=dtt[:, 2 * cc + s, :], in1=d2[:, 4 * s:4 * s + 4], op=ALU.mult),
                          reads=[t_dtt, t_d2], writes=[t_wd])
                pst, t_pst = C.psum()
                mm_group(S, pst[0:4, 0:256], [(dtA[:, s0, :], U0), (dtA[:, s1, :], U1)], [t_dtA, t_cst], [t_pst])
                S.add("dve", lambda e, pst=pst: e.tensor_copy(out=acsT[:], in_=pst[0:4, 0:256]), reads=[t_pst], writes=[t_acsT])
                for s in (s0, s1):
                    psx, t_psx = C.psum()

                    def tfn(e, psx=psx, s=s):
                        ins = None
                        for h in range(4):
                            ins = e.transpose(out=psx[:, h * 64:(h + 1) * 64], in_=xsT[h][0][:, s * 128:(s + 1) * 128], identity=cst[0:64, 0:64])
                        return ins
                    S.add("pe", tfn, reads=[x[1] for x in xsT] + [t_cst], writes=[t_psx])
                    for h in range(4):
                        S.add("dve", lambda e, psx=psx, s=s, h=h: e.tensor_scalar(
                            out=xd[s][0][:, h * 64:(h + 1) * 64], in0=psx[:, h * 64:(h + 1) * 64], scalar1=dtt[:, s, h:h + 1], scalar2=None, op0=ALU.mult),
                            reads=[t_psx, t_dtt], writes=[xd[s][1]])
                        S.add("dve", lambda e, psx=psx, s=s, h=h: e.tensor_scalar(
                            out=xdd[s][0][:, h * 64:(h + 1) * 64], in0=psx[:, h * 64:(h + 1) * 64], scalar1=wd[:, s, h:h + 1], scalar2=None, op0=ALU.mult),
                            reads=[t_psx, t_wd], writes=[xdd[s][1]])
                    psb, t_psb = C.psum()
                    S.add("pe", lambda e, psb=psb, s=s: e.transpose(out=psb[:, 0:128], in_=BT32[:, s * 128:(s + 1) * 128], identity=ident),
                          reads=[t_BT32, t_cst], writes=[t_psb])
                    S.add("act", lambda e, psb=psb, s=s: e.activation(out=Btok[s][0][:], in_=psb[:, 0:128], func=AF.Identity),
                          reads=[t_psb], writes=[Btok[s][1]])
                psc, t_psc = C.psum()
                S.add("pe", lambda e, psc=psc, cc=cc: e.matmul(psc[:, 0:256], BTb[:, cc * 256:cc * 256 + 128], CTb[:, cc * 256:cc * 256 + 256], start=True, stop=True),
                      reads=[t_BTb, t_CTb], writes=[t_psc])
                S.add("pe", lambda e, psc=psc, cc=cc: e.matmul(psc[:, 256:384], BTb[:, cc * 256 + 128:cc * 256 + 256], CTb[:, cc * 256 + 128:cc * 256 + 256], start=True, stop=True),
                      reads=[t_BTb, t_CTb], writes=[t_psc])
                S.add("dve", lambda e, psc=psc: e.tensor_tensor(out=cbm[:, 0:256], in0=psc[:, 0:256], in1=U0, op=ALU.mult),
                      reads=[t_psc, t_cst], writes=[t_cbm])
                S.add("dve", lambda e, psc=psc: e.tensor_tensor(out=cbm[:, 256:384], in0=psc[:, 256:384], in1=UTRI, op=ALU.mult),
                      reads=[t_psc, t_cst], writes=[t_cbm])
                for h in range(4):
                    psbc, t_psbc = C.psum()
                    S.add("pe", lambda e, psbc=psbc, h=h: e.matmul(psbc[:, 0:256], SELH[:, h * 128:(h + 1) * 128], acsT[:], start=True, stop=True),
                          reads=[t_acsT, t_cst], writes=[t_psbc])
                    Eh, t_Eh = E[h]
                    S.add("act", lambda e, psbc=psbc, Eh=Eh: e.activation(out=Eh[:], in_=psbc[:, 0:256], func=AF.Exp), reads=[t_psbc], writes=[t_Eh])
                    S.add("pool", lambda e, Eh=Eh, ccol=ccol: e.tensor_tensor(out=Cdec[:], in0=CT32[:, ccol], in1=Eh[:], op=ALU.mult),
                          reads=[t_CT32, t_Eh], writes=[t_Cdec])
                    S.add("dve", lambda e, psbc=psbc, h=h: e.tensor_scalar(
                        out=Dm[:, 0:256], in0=psbc[:, 0:256], scalar1=acs[:, h:h + 1], scalar2=0.0, op0=ALU.subtract, op1=ALU.min),
                        reads=[t_psbc, t_acs], writes=[t_Dm])
                    S.add("dve", lambda e, psbc=psbc, h=h: e.tensor_scalar(
                        out=Dm[:, 256:384], in0=psbc[:, 128:256], scalar1=acs[:, 4 + h:5 + h], scalar2=0.0, op0=ALU.subtract, op1=ALU.min),
                        reads=[t_psbc, t_acs], writes=[t_Dm])
                    S.add("act", lambda e: e.activation(out=Dm[:], in_=Dm[:], func=AF.Exp), reads=[t_Dm], writes=[t_Dm])
                    S.add("dve", lambda e: e.tensor_tensor(out=MT[:], in0=Dm[:], in1=cbm[:], op=ALU.mult), reads=[t_Dm, t_cbm], writes=[t_MT])
                    psy, t_psy = C.psum()
                    hs = slice(h * 64, (h + 1) * 64)

                    def yfn(e, psy=psy, hs=hs, s0=s0, s1=s1):
                        e.matmul(psy[0:64, 0:256], prevb[:, hs], Cdec[:], start=True, stop=False)
                        e.matmul(psy[0:64, 0:256], xd[s0][0][:, hs], MT[:, 0:256], start=False, stop=False)
                        return e.matmul(psy[0:64, 128:256], xd[s1][0][:, hs], MT[:, 256:384], start=False, stop=True)
                    S.add("pe", yfn, reads=[t_pb, t_Cdec, xd[s0][1], xd[s1][1], t_MT], writes=[t_psy])
                    S.add("dve", lambda e, psy=psy, h=h, ccol=ccol: e.scalar_tensor_tensor(
                        out=yt1[:], in0=xsT[h][0][:, ccol], scalar=sp64[:, h * 6 + 5:h * 6 + 6], in1=psy[0:64, 0:256], op0=ALU.mult, op1=ALU.add),
                        reads=[xsT[h][1], t_sp64, t_psy], writes=[t_yt1])
                    yg, t_yg = ygs[nyg % 2]
                    nyg += 1
                    S.add("pool", lambda e, yg=yg, h=h, ccol=ccol: e.tensor_tensor(out=yg[:], in0=yt1[:], in1=sz[h][0][:, ccol], op=ALU.mult),
                          reads=[t_yt1, sz[h][1]], writes=[t_yg])
                    S.add("sp", lambda e, yg=yg, h=h, tok0=tok0, cc=cc: e.dma_start(
                        out=yg_ap[h * 64:(h + 1) * 64, tok0 + cc * 256:tok0 + (cc + 1) * 256], in_=yg[:]), reads=[t_yg], dma=True)
                pss, t_pss = C.psum()
                mm_group(S, pss[:, 0:256], [(Btok[s0][0][:], xdd[s0][0][:]), (Btok[s1][0][:], xdd[s1][0][:])],
                         [Btok[s0][1], Btok[s1][1], xdd[s0][1], xdd[s1][1]], [t_pss])
                for h in range(4):
                    hs = slice(h * 64, (h + 1) * 64)
                    S.add("dve", lambda e, h=h, hs=hs, pss=pss: e.scalar_tensor_tensor(
                        out=prev32[:, hs], in0=prev32[:, hs], scalar=E[h][0][:, 255:256], in1=pss[:, hs], op0=ALU.mult, op1=ALU.add),
                        reads=[t_p32, E[h][1], t_pss], writes=[t_p32])
                S.add("act", lambda e: e.activation(out=prevb[:], in_=prev32[:], func=AF.Identity), reads=[t_p32], writes=[t_pb])
        def sec_attn(qis=(0, 1, 2, 3), bs=0):
            gsb, t_gsb = gsbL[bs]
            top8, t_top8 = top8L[bs]
            msel, t_msel = mselL[bs]
            acc, t_acc = accL[bs]
            rec, t_rec = recL[bs]
            pt2 = pt2L[bs]
            tmpS2 = tmpS2L[bs]
            yq = yqL[bs]
            for i in qis:
                qi = 4 * tt + i
                J = qi // 2
                eo = qi % 2
                qc = slice(i * 128, (i + 1) * 128)
                use_sel = J > 3
                if use_sel:
                    psg, t_psg = C.psum()
                    S.add("pe", lambda e, psg=psg, qc=qc: e.matmul(psg[:, 0:32], qn32[:, qc], kmean[:], start=True, stop=True),
                          reads=[t_qn, t_km], writes=[t_psg])
                    S.add("pool", lambda e: e.memset(gsb[:], NEG), writes=[t_gsb])
                    S.add("dve", lambda e, psg=psg, J=J: e.tensor_copy(out=gsb[:, 0:J], in_=psg[:, 0:J]), reads=[t_psg], writes=[t_gsb])
                    S.add("dve", lambda e: e.max(out=top8[:], in_=gsb[:]), reads=[t_gsb], writes=[t_top8])
                    S.add("dve", lambda e: e.tensor_scalar(out=msel[:], in0=gsb[:], scalar1=top8[:, 2:3], scalar2=None, op0=ALU.is_ge),
                          reads=[t_gsb, t_top8], writes=[t_msel])
                blocks = [J] + list(range(J))
                for bi, n in enumerate(blocks):
                    if n == J:
                        halves = [(0, "A")] if eo == 0 else [(0, "B"), (1, "A")]
                    elif n == J - 1:
                        halves = [(0, "c"), (1, "B")] if eo == 0 else [(0, "c"), (1, "c")]
                    else:
                        halves = [(0, "c"), (1, "c")]
                    pss2, t_pss2 = C.psum()
                    pt, t_pt = pt2[nptL[bs] % 2]
                    tmpS, t_tmpS = tmpS2[nptL[bs] % 2]
                    nptL[bs] += 1

                    def sfn(e, pss2=pss2, halves=halves, n=n, qc=qc):
                        ins = None
                        for hh, _k in halves:
                            ins = e.matmul(pss2[:, hh * 128:(hh + 1) * 128], KT[:, n * 256 + hh * 128:n * 256 + (hh + 1) * 128], qnb[:, qc], start=True, stop=True)
                        return ins
                    S.add("pe", sfn, reads=[t_KT, t_qnb], writes=[t_pss2])
                    if STOP == 10:
                        continue
                    for hh, kind in halves:
                        if (STOP == 14 and kind != "c") or (STOP == 15 and kind == "c"):
                            continue
                        if kind == "c":
                            S.add("act", lambda e, pss2=pss2, hh=hh, pt=pt: e.activation(
                                out=pt[:, hh, :], in_=pss2[:, hh * 128:(hh + 1) * 128], func=AF.Exp, bias=spt[:, 2:3], scale=SCALE),
                                reads=[t_pss2, t_sp], writes=[t_pt])
                        else:
                            bt_ = TA if kind == "A" else TB
                            S.add("dve", lambda e, pss2=pss2, hh=hh, bt_=bt_, tmpS=tmpS: e.scalar_tensor_tensor(
                                out=tmpS[:, hh * 128:(hh + 1) * 128], in0=pss2[:, hh * 128:(hh + 1) * 128], scalar=SCALE, in1=bt_, op0=ALU.mult, op1=ALU.add),
                                reads=[t_pss2, t_cst], writes=[t_tmpS])
                            S.add("act", lambda e, hh=hh, pt=pt, tmpS=tmpS: e.activation(out=pt[:, hh, :], in_=tmpS[:, hh * 128:(hh + 1) * 128], func=AF.Exp),
                                  reads=[t_tmpS], writes=[t_pt])
                    if STOP in (11, 14, 15):
                        continue
                    pso, t_pso = C.psum()
                    nh = len(halves)

                    def ofn(e, pso=pso, halves=halves, n=n, nh=nh, pt=pt):
                        ins = None
                        for ii, (hh, _k) in enumerate(halves):
                            ins = e.matmul(pso[:, 0:130], pt[:, hh, :], vaug[:, 2 * n + hh, :], start=(ii == 0), stop=(ii == nh - 1))
                        return ins
                    S.add("pe", ofn, reads=[t_pt, t_v], writes=[t_pso])
                    if STOP == 12:
                        continue
                    if bi == 0:
                        S.add("act", lambda e, pso=pso: e.activation(out=acc[:], in_=pso[:, 0:130], func=AF.Identity), reads=[t_pso], writes=[t_acc])
                    else:
                        sc_ = msel[:, n:n + 1] if use_sel else 1.0
                        S.add("dve", lambda e, pso=pso, sc_=sc_: e.scalar_tensor_tensor(
                            out=acc[:], in0=pso[:, 0:130], scalar=sc_, in1=acc[:], op0=ALU.mult, op1=ALU.add),
                            reads=[t_pso, t_msel, t_acc], writes=[t_acc])
                if STOP in (10, 11, 12, 13, 14, 15):
                    continue
                S.add("dve", lambda e: e.reciprocal(out=rec[:], in_=acc[:, 128:129]), reads=[t_acc], writes=[t_rec])
                yq_, t_yq = yq[nyqL[bs] % 2]
                nyqL[bs] += 1
                S.add("dve", lambda e, yq_=yq_: e.tensor_scalar(out=yq_[:], in0=acc[:, 0:128], scalar1=rec[:, 0:1], scalar2=None, op0=ALU.mult),
                      reads=[t_acc, t_rec], writes=[t_yq])
                S.add("sp", lambda e, yq_=yq_, qi=qi: e.dma_start(out=ya_ap[qi * 128:(qi + 1) * 128, :], in_=yq_[:]), reads=[t_yq], dma=True)
        secs = []
        if 'ssd' in PARTS:
            secs.append(sec_ssd)
        if 'attn' in PARTS:
            secs.append(lambda: sec_attn((0, 2), 0))
            secs.append(lambda: sec_attn((1, 3), 1))
        interleave(C, secs, banks=[[0, 1, 2, 3], [4, 5], [6, 7]] if len(secs) == 3 else [[0, 1, 2, 3], [4, 5, 6, 7]])
    return C.finish()


def t5_bucket_np(dist):
    n = np.maximum(dist, 0)
    max_exact = 16
    ratio = np.log(np.maximum(n, 1).astype(np.float32) / max_exact) / np.float32(np.log(128 / max_exact))
    large = max_exact + (ratio * (32 - max_exact)).astype(np.int32)
    large = np.minimum(large, 31)
    return np.where(n < max_exact, n, large)


def mix_consts(rel_bias, head):
    kk = np.arange(128)[:, None]
    qq = np.arange(128)[None, :]
    cst = np.zeros((128, 5 * 128 + 1024), np.float32)
    cst[:, 0:128] = np.eye(128, dtype=np.float32)
    cst[:, 128:256] = 1.0
    cst[:, 256:384] = rel_bias[t5_bucket_np(qq - kk), head]
    cst[:, 384:512] = rel_bias[t5_bucket_np(128 + qq - kk), head]
    cst[:, 512:640] = np.where(qq >= kk, 0.0, NEG)
    utri = (kk <= qq).astype(np.float32)
    cst[:, 640:768] = utri
    cst[:, 768:896] = 1.0
    cst[:, 896:1024] = 0.0
    cst[:, 1024:1152] = utri
    for h in range(4):
        cst[h, 1152 + h * 128:1152 + (h + 1) * 128] = 1.0
    return cst


def run_mixB(hT, l, inp, NT=16):
    w = inp["w_mix_in"][l]
    in_maps = []
    for c in range(NCORES):
        g = c // 2
        cols = np.concatenate([
            np.arange(c * 128, (c + 1) * 128), 1024 + np.arange(c * 128, (c + 1) * 128),
            3072 + np.arange(c * 256, (c + 1) * 256), 5120 + np.arange(c * 256, (c + 1) * 256),
            5120 + 2048 + np.arange(g * 128, (g + 1) * 128), 5120 + 2560 + np.arange(g * 128, (g + 1) * 128),
            8224 + np.arange(c * 128, (c + 1) * 128), 9248 + np.arange(c * 128, (c + 1) * 128),
            10272 + np.arange(c * 128, (c + 1) * 128), 2048 + np.arange(c * 128, (c + 1) * 128),
            8192 + np.arange(c * 4, (c + 1) * 4)])
        wB = np.ascontiguousarray(w[:, cols].reshape(KC, 128, NCOLB).transpose(1, 0, 2))
        sp = np.zeros((128, 24), np.float32)
        sp[:, 0] = inp["qk_norm"][l, 0]
        sp[:, 1] = inp["qk_norm"][l, 1]
        sp[:, 2] = inp["rel_bias"][31, c]
        wc = inp["w_ssd_conv"][l]
        bc = inp["b_ssd_conv"][l]
        chB = 2048 + g * 128 + np.arange(128)
        chC = 2560 + g * 128 + np.arange(128)
        sp[:, 3:7] = wc[:, chB].T
        sp[:, 7] = bc[chB]
        sp[:, 8:12] = wc[:, chC].T
        sp[:, 12] = bc[chC]
        sp[:, 13:16] = inp["w_sc_conv"][l][:, c * 128:(c + 1) * 128].T
        sp[:, 16:20] = inp["ssd_dt_bias"][l][None, 4 * c:4 * c + 4]
        sp[:, 20:24] = inp["ssd_a_log"][l][None, 4 * c:4 * c + 4]
        sp64 = np.zeros((64, 24), np.float32)
        for h in range(4):
            ch = c * 256 + h * 64 + np.arange(64)
            sp64[:, h * 6:h * 6 + 4] = wc[:, ch].T
            sp64[:, h * 6 + 4] = bc[ch]
            sp64[:, h * 6 + 5] = inp["ssd_d"][l][4 * c + h]
        in_maps.append({"hT": hT, "wB": wB, "sp": sp, "sp64": sp64, "cst": mix_consts(inp["rel_bias"], c)})
    r = run(prog("mixB", build_mixB, NT), in_maps)
    ntok = NT * 512
    YT = np.zeros((4096, ntok), np.float32)
    for c in range(NCORES):
        YT[c * 128:(c + 1) * 128] = r[c]["ya"].T
        YT[1024 + c * 256:1024 + (c + 1) * 256] = r[c]["yg"]
        YT[3072 + c * 128:3072 + (c + 1) * 128] = r[c]["yc"]
    return YT


NYC = 32
BR_CH = ((0, 8), (8, 24), (24, 32))


def build_mixC():
    C = Ctx()
    S = C.S
    x_ap = C.din("xT", [D, TOK])
    h_ap = C.din("hT", [D, TOK])
    y_ap = C.din("YT", [4096, TOK])
    wg_ap = C.din("wg", [KC, 128, KC, 384])
    wbr_ap = C.din("wbr", [KC, 128, NYC, 128])
    wo_ap = C.din("wo", [KC, 128, KC, 128])
    par_ap = C.din("par", [128, 2 * KC])
    ones_ap = C.din("ones", [128, 128])
    xo_ap = C.dout("xo", [D, TOK])
    C.init_psum()
    ones, t_ones = load_consts(C, ones_ap)
    par, t_par = C.sb([128, 2 * KC], F32, "par")
    S.add("sp", lambda e: e.dma_start(out=par[:], in_=par_ap), writes=[t_par], dma=True)
    xT, _ = C.sb([128, KC, 512], F32, "xT")
    t_x = [T("x%d" % k) for k in range(KC)]
    hb, _ = C.sb([128, KC, 512], BF16, "hb")
    t_h = [T("h%d" % k) for k in range(KC)]
    Yb, _ = C.sb([128, NYC, 512], BF16, "Yb")
    t_y = [T("y%d" % k) for k in range(NYC)]
    stg = [C.sb([128, 512], F32, "stg") for _ in range(4)]
    sq = [C.sb([128, 512], F32, "sq") for _ in range(2)]
    rs, t_rs = C.sb([128, 512], F32, "rs")
    tmpn = [C.sb([128, 512], F32, "tmpn") for _ in range(2)]
    mg, t_mg = C.sb([128, 512], F32, "mg")
    sig = [C.sb([128, 512], F32, "sig") for _ in range(2)]
    tmpm, t_tmpm = C.sb([128, 512], F32, "tmpm")
    mT, _ = C.sb([128, KC, 512], BF16, "mT")
    t_m = [T("m%d" % k) for k in range(KC)]
    wg = [C.sb([128, KC, 384], BF16, "wg") for _ in range(2)]
    wbr = [C.sb([128, NYC, 128], BF16, "wbr") for _ in range(2)]
    wo = [C.sb([128, KC, 128], BF16, "wo") for _ in range(2)]
    nsig = 0
    nw = 0
    nwo = 0
    for tt in range(TOK // 512):
        sl = slice(tt * 512, (tt + 1) * 512)
        for kc in range(KC):
            S.add("sp", lambda e, kc=kc, sl=sl: e.dma_start(out=xT[:, kc, :], in_=x_ap[kc * 128:(kc + 1) * 128, sl]),
                  writes=[t_x[kc]], dma=True)
            S.add("pool", lambda e, kc=kc, sl=sl: e.dma_start(out=hb[:, kc, :], in_=h_ap[kc * 128:(kc + 1) * 128, sl]),
                  writes=[t_h[kc]], dma=True)
        for ch in list(range(0, 8)) + list(range(24, 32)):
            S.add("pool", lambda e, ch=ch, sl=sl: e.dma_start(out=Yb[:, ch, :], in_=y_ap[ch * 128:(ch + 1) * 128, sl]),
                  writes=[t_y[ch]], dma=True)
        for gq in range(4):
            ps, t_ps = C.psum()
            for c4 in range(4):
                ch = 8 + gq * 4 + c4
                st_, t_st = stg[c4]
                S.add("sp", lambda e, st_=st_, ch=ch, sl=sl: e.dma_start(out=st_[:], in_=y_ap[ch * 128:(ch + 1) * 128, sl]),
                      writes=[t_st], dma=True)
                sq_, t_sq = sq[c4 % 2]
                S.add("act", lambda e, sq_=sq_, st_=st_: e.activation(out=sq_[:], in_=st_[:], func=AF.Square), reads=[t_st], writes=[t_sq])
                S.add("pe", lambda e, ps=ps, sq_=sq_, c4=c4: e.matmul(ps[:, :], ones[:], sq_[:], start=(c4 == 0), stop=(c4 == 3)),
                      reads=[t_sq, t_ones], writes=[t_ps])
            S.add("act", lambda e, ps=ps: e.activation(out=rs[:], in_=ps[:, :], func=AF.Sqrt, scale=1.0 / 512, bias=EPS),
                  reads=[t_ps], writes=[t_rs])
            S.add("dve", lambda e: e.reciprocal(out=rs[:], in_=rs[:]), reads=[t_rs], writes=[t_rs])
            for c4 in range(4):
                ch = 8 + gq * 4 + c4
                st_, t_st = stg[c4]
                S.add("dve", lambda e, st_=st_, ch=ch: e.scalar_tensor_tensor(
                    out=Yb[:, ch, :], in0=st_[:], scalar=par[:, ch - 8:ch - 7], in1=rs[:], op0=ALU.mult, op1=ALU.mult),
                    reads=[t_st, t_rs, t_par], writes=[t_y[ch]])
        for m in range(KC):
            wg_, t_wg = wg[nw % 2]
            wbr_, t_wbr = wbr[nw % 2]
            nw += 1
            S.add("pool", lambda e, wg_=wg_, m=m: e.dma_start(out=wg_[:], in_=wg_ap[m]), writes=[t_wg], dma=True)
            S.add("pool", lambda e, wbr_=wbr_, m=m: e.dma_start(out=wbr_[:], in_=wbr_ap[m]), writes=[t_wbr], dma=True)
            for b in range(3):
                pg, t_pg = C.psum()
                mm_group(S, pg[:, :], [(wg_[:, kc, b * 128:(b + 1) * 128], hb[:, kc, :]) for kc in range(KC)], t_h + [t_wg], [t_pg])
                sg_, t_sg = sig[nsig % 2]
                nsig += 1
                S.add("act", lambda e, sg_=sg_, pg=pg: e.activation(out=sg_[:], in_=pg[:, :], func=AF.Sigmoid), reads=[t_pg], writes=[t_sg])
                pb, t_pb = C.psum()
                c0, c1 = BR_CH[b]
                mm_group(S, pb[:, :], [(wbr_[:, ch, :], Yb[:, ch, :]) for ch in range(c0, c1)], t_y[c0:c1] + [t_wbr], [t_pb])
                if b == 0:
                    S.add("dve", lambda e, sg_=sg_, pb=pb: e.tensor_tensor(out=mg[:], in0=sg_[:], in1=pb[:, :], op=ALU.mult),
                          reads=[t_sg, t_pb], writes=[t_mg])
                else:
                    S.add("dve", lambda e, sg_=sg_, pb=pb: e.tensor_tensor(out=tmpm[:], in0=sg_[:], in1=pb[:, :], op=ALU.mult),
                          reads=[t_sg, t_pb], writes=[t_tmpm])
                    if b == 1:
                        S.add("dve", lambda e: e.tensor_tensor(out=mg[:], in0=mg[:], in1=tmpm[:], op=ALU.add),
                              reads=[t_mg, t_tmpm], writes=[t_mg])
                    else:
                        S.add("dve", lambda e, m=m: e.tensor_tensor(out=mT[:, m, :], in0=mg[:], in1=tmpm[:], op=ALU.add),
                              reads=[t_mg, t_tmpm], writes=[t_m[m]])
        for m in range(KC):
            wo_, t_wo = wo[nwo % 2]
            nwo += 1
            S.add("pool", lambda e, wo_=wo_, m=m: e.dma_start(out=wo_[:], in_=wo_ap[m]), writes=[t_wo], dma=True)
            po, t_po = C.psum()
            mm_group(S, po[:, :], [(wo_[:, kc, :], mT[:, kc, :]) for kc in range(KC)], t_m + [t_wo], [t_po])
            S.add("dve", lambda e, po=po, m=m: e.scalar_tensor_tensor(
                out=xT[:, m, :], in0=po[:, :], scalar=par[:, KC + m:KC + m + 1], in1=xT[:, m, :], op0=ALU.mult, op1=ALU.add),
                reads=[t_po, t_par, t_x[m]], writes=[t_x[m]])
            S.add("sp", lambda e, m=m, sl=sl: e.dma_start(out=xo_ap[m * 128:(m + 1) * 128, sl], in_=xT[:, m, :]),
                  reads=[t_x[m]], dma=True)
    return C.finish()


def run_mixC(xT, hT, YT, l, inp, gate1):
    w = inp["w_mix_in"][l]
    G0 = 11296
    wg = np.stack([w[:, G0 + b * 2048:G0 + (b + 1) * 2048].reshape(KC, 128, KC, 128) for b in range(3)], axis=0)
    wg_t = np.ascontiguousarray(wg.transpose(3, 2, 1, 0, 4).reshape(KC, 128, KC, 384))
    wbr = np.concatenate([inp["w_br_attn"][l], inp["w_br_ssd"][l], inp["w_br_conv"][l]], axis=0)
    wbr_t = np.ascontiguousarray(wbr.reshape(NYC, 128, KC, 128).transpose(2, 1, 0, 3))
    wo_t = np.ascontiguousarray(inp["w_mix_out"][l].reshape(KC, 128, KC, 128).transpose(2, 1, 0, 3))
    par = np.concatenate([cols16(inp["ssd_norm"][l]), cols16(gate1)], axis=1)
    ones = np.ones((128, 128), np.float32)
    in_maps = []
    for i in range(NCORES):
        ts = slice(i * TOK, (i + 1) * TOK)
        in_maps.append({"xT": np.ascontiguousarray(xT[:, ts]), "hT": np.ascontiguousarray(hT[:, ts]),
                        "YT": np.ascontiguousarray(YT[:, ts]), "wg": wg_t, "wbr": wbr_t, "wo": wo_t, "par": par, "ones": ones})
    r = run(prog("mixC", build_mixC), in_maps)
    return np.concatenate([r[i]["xo"] for i in range(NCORES)], axis=1)


def kernel(x, c, w_ada, b_ada, norm_gain, w_ffn_in, w_ffn_out, w_mix_in, qk_norm, rel_bias,
           w_ssd_conv, b_ssd_conv, ssd_dt_bias, ssd_a_log, ssd_d, ssd_norm, w_sc_conv,
           w_br_attn, w_br_ssd, w_br_conv, w_mix_out):
    inp = dict(w_mix_in=w_mix_in, qk_norm=qk_norm, rel_bias=rel_bias, w_ssd_conv=w_ssd_conv, b_ssd_conv=b_ssd_conv,
               ssd_dt_bias=ssd_dt_bias, ssd_a_log=ssd_a_log, ssd_d=ssd_d, ssd_norm=ssd_norm, w_sc_conv=w_sc_conv,
               w_br_attn=w_br_attn, w_br_ssd=w_br_ssd, w_br_conv=w_br_conv, w_mix_out=w_mix_out)
    inp = {k: np.asarray(v, np.float32) for k, v in inp.items()}
    w_ada = np.asarray(w_ada, np.float32)
    b_ada = np.asarray(b_ada, np.float32)
    norm_gain = np.asarray(norm_gain, np.float32)
    w_ffn_in = np.asarray(w_ffn_in, np.float32)
    w_ffn_out = np.asarray(w_ffn_out, np.float32)
    ada = compute_ada(np.asarray(c, np.float32), w_ada, b_ada)
    xT = np.ascontiguousarray(np.asarray(x, np.float32)[0].T)
    for l in range(2):
        par = np.concatenate([par_block(norm_gain[l, 0], ada[l, 0]), par_block(norm_gain[l, 1], ada[l, 1])], axis=1)
        xT, hT = run_ffn(xT, par, w_ffn_in[l, 0], w_ffn_out[l, 0], True)
        YT = run_mixB(hT, l, inp, NT=16)
        xT = run_mixC(xT, hT, YT, l, inp, ada[l, 1, 2])
        par = par_block(norm_gain[l, 2], ada[l, 2])
        xT, _ = run_ffn(xT, par, w_ffn_in[l, 1], w_ffn_out[l, 1], False)
    return np.ascontiguousarray(xT.T)[None].astype(np.float32)
```

```python
import contextlib
import threading
import numpy as np
import concourse.bass as bass
import concourse.mybir as mybir
from concourse.bass_utils import run_bass_kernel_spmd

F32 = mybir.dt.float32
BF16 = mybir.dt.bfloat16
AF = mybir.ActivationFunctionType
ALU = mybir.AluOpType
AX = mybir.AxisListType

NCORES = 8
D = 2048
KC = 16
SEQ = 8192
TOK = SEQ // NCORES
FFH = 5632
NJ = FFH // 128
NQ = 4
JQ = NJ // NQ
EPS = 1e-6


class T:
    __slots__ = ("name", "last_w", "readers", "excl")

    def __init__(self, name="", excl=False):
        self.name = name
        self.last_w = None
        self.readers = []
        self.excl = excl


class Op:
    __slots__ = ("eng", "fn", "deps", "is_dma", "sig", "sigval", "dsem", "dval", "dprev")


ENGS = ("pe", "act", "dve", "pool", "sp")
N_DMA_SEMS = {"pe": 1, "act": 1, "dve": 1, "pool": 16, "sp": 8}


class Sched:
    def __init__(self, nc):
        self.nc = nc
        self.ops = []
        self.dma_rr = {e: 0 for e in ENGS}
        self.dma_cnt = {}
        self.hook = None

    def add(self, eng, fn, reads=(), writes=(), dma=False):
        op = Op()
        op.eng = eng
        op.fn = fn
        op.is_dma = dma
        op.sig = False
        op.sigval = None
        deps = []
        excl_r = [t for t in reads if t.excl and t not in writes]
        reads = [t for t in reads if not t.excl]
        writes = list(writes) + excl_r
        for t in reads:
            if t.last_w is not None:
                deps.append(t.last_w)
        for t in writes:
            if t.last_w is not None:
                deps.append(t.last_w)
            deps.extend(t.readers)
        for t in reads:
            t.readers.append(op)
        for t in writes:
            t.last_w = op
            t.readers = []
        op.deps = [d for d in dict.fromkeys(deps) if d is not op]
        if dma:
            k = self.dma_rr[eng]
            self.dma_rr[eng] = (k + 1) % N_DMA_SEMS[eng]
            key = (eng, k)
            c = self.dma_cnt.get(key, 0) + 1
            self.dma_cnt[key] = c
            op.dsem = key
            op.dval = 16 * c
            op.dprev = 16 * (c - 1)
        self.ops.append(op)
        if self.hook is not None:
            self.hook()
        return op

    def emit(self):
        nc = self.nc
        ops = self.ops
        for op in ops:
            for d in op.deps:
                if d.is_dma:
                    continue
                if d.eng == op.eng and d.eng == "pe" and not op.is_dma:
                    continue
                d.sig = True
        cnt = {e: 0 for e in ENGS}
        for op in ops:
            if not op.is_dma and op.sig:
                cnt[op.eng] += 1
                op.sigval = cnt[op.eng]
        with contextlib.ExitStack() as st:
            esem = {e: st.enter_context(nc.semaphore("s_" + e)) for e in ENGS}
            dsem = {}
            for key in self.dma_cnt:
                dsem[key] = st.enter_context(nc.semaphore("d_%s%d" % key))
            block = st.enter_context(nc.Block())
            by_eng = {e: [o for o in ops if o.eng == e] for e in ENGS}

            def run(eng_name, eng):
                known = {}

                def wait(sem_key, sem, val):
                    if known.get(sem_key, 0) >= val:
                        return
                    known[sem_key] = val
                    eng.wait_ge(sem, val)

                for op in by_eng[eng_name]:
                    for d in op.deps:
                        if d.is_dma:
                            wait(d.dsem, dsem[d.dsem], d.dval)
                        elif d.sigval is not None:
                            wait(d.eng, esem[d.eng], d.sigval)
                    if op.is_dma:
                        if op.dprev > 0:
                            wait(op.dsem, dsem[op.dsem], op.dprev)
                        op.fn(eng).then_inc(dsem[op.dsem], 16)
                    else:
                        ins = op.fn(eng)
                        if op.sig:
                            ins.then_inc(esem[op.eng], 1)
                if eng_name == "sp":
                    for key, c in self.dma_cnt.items():
                        wait(key, dsem[key], 16 * c)

            block.tensor(lambda e: run("pe", e))
            block.scalar(lambda e: run("act", e))
            block.vector(lambda e: run("dve", e))
            block.gpsimd(lambda e: run("pool", e))
            block.sync(lambda e: run("sp", e))


class Ctx:
    def __init__(self):
        self.nc = bass.Bass("TRN2", target_bir_lowering=False)
        self.S = Sched(self.nc)
        self.st = contextlib.ExitStack()
        self.ps = []
        self.ps_i = 0
        self.n = 0
        self.tls = threading.local()
        self.pool_i = {}
        self.pool_banks = [[0, 1, 2, 3], [4, 5, 6, 7]]

    def din(self, name, shape, dt=F32):
        return self.nc.dram_tensor(name, list(shape), dt, kind="ExternalInput").ap()

    def dout(self, name, shape, dt=F32):
        return self.nc.dram_tensor(name, list(shape), dt, kind="ExternalOutput").ap()

    def sb(self, shape, dt, name=None):
        self.n += 1
        t = self.st.enter_context(self.nc.sbuf_tensor("%s_%d" % (name or "t", self.n), list(shape), dt))
        return t, T(name or "t")

    def init_psum(self):
        for i in range(8):
            t = self.st.enter_context(self.nc.psum_tensor("ps%d" % i, [128, 512], F32))
            self.ps.append((t, T("ps%d" % i, excl=True)))

    def psum(self):
        pool = getattr(self.tls, "pool", 0)
        if pool == 0:
            r = self.ps[self.ps_i]
            self.ps_i = (self.ps_i + 1) % 8
            return r
        banks = self.pool_banks[pool - 1]
        k = self.pool_i.get(pool, 0)
        self.pool_i[pool] = (k + 1) % len(banks)
        return self.ps[banks[k]]

    def finish(self):
        self.S.emit()
        self.st.close()
        return self.nc


def interleave(C, fns, banks=None):
    S = C.S
    n = len(fns)
    if n == 0:
        return
    if n == 1:
        fns[0]()
        return
    if banks is not None:
        C.pool_banks = banks
        C.pool_i = {}
    sems = [threading.Semaphore(0) for _ in range(n)]
    alive = [True] * n
    idx = {}
    err = []
    done = threading.Event()

    def nxt(i):
        for k in range(1, n + 1):
            j = (i + k) % n
            if alive[j]:
                return j
        return None

    def hook():
        i = idx[threading.get_ident()]
        j = nxt(i)
        if j is not None and j != i:
            sems[j].release()
            sems[i].acquire()

    def worker(i):
        sems[i].acquire()
        idx[threading.get_ident()] = i
        try:
            C.tls.pool = i + 1
            fns[i]()
        except BaseException as ex:
            err.append(ex)
        finally:
            alive[i] = False
            j = nxt(i)
            if j is not None:
                sems[j].release()
            else:
                done.set()

    ths = [threading.Thread(target=worker, args=(i,)) for i in range(n)]
    S.hook = hook
    for t in ths:
        t.start()
    sems[0].release()
    done.wait()
    for t in ths:
        t.join()
    S.hook = None
    if err:
        raise err[0]


def mm_group(S, out_ap, pairs, reads, writes):
    n = len(pairs)

    def fn(e):
        ins = None
        for i, (l, r) in enumerate(pairs):
            ins = e.matmul(out_ap, l, r, start=(i == 0), stop=(i == n - 1))
        return ins

    return S.add("pe", fn, reads=reads, writes=writes)


def load_consts(C, ones_ap):
    ones, t_ones = C.sb([128, 128], F32, "ones")
    C.S.add("sp", lambda e: e.dma_start(out=ones[:], in_=ones_ap), writes=[t_ones], dma=True)
    return ones, t_ones


def modulate(C, xT, t_x, hT, t_h, a_col, shift_col, t_par, ones, t_ones, ntok, out_dram=None):
    S = C.S
    sq = [C.sb([128, 512], F32, "sq") for _ in range(2)]
    rs, t_rs = C.sb([128, 512], F32, "rstd")
    tmp = [C.sb([128, 512], F32, "modtmp") for _ in range(2)]
    tmp32 = [C.sb([128, 512], F32, "modtmp32") for _ in range(2)] if out_dram is not None else None
    for tt in range(ntok // 512):
        sl = slice(tt * 512, (tt + 1) * 512)
        ps, t_ps = C.psum()
        for kc in range(KC):
            sqt, t_sq = sq[kc % 2]
            S.add("act", lambda e, sqt=sqt, kc=kc, sl=sl: e.activation(out=sqt[:], in_=xT[:, kc, sl], func=AF.Square),
                  reads=[t_x], writes=[t_sq])
            S.add("pe", lambda e, sqt=sqt, kc=kc, ps=ps: e.matmul(ps[:, :], ones[:], sqt[:], start=(kc == 0), stop=(kc == KC - 1)),
                  reads=[t_sq, t_ones], writes=[t_ps])
        S.add("act", lambda e, ps=ps: e.activation(out=rs[:], in_=ps[:, :], func=AF.Sqrt, scale=1.0 / D, bias=EPS),
              reads=[t_ps], writes=[t_rs])
        S.add("dve", lambda e: e.reciprocal(out=rs[:], in_=rs[:]), reads=[t_rs], writes=[t_rs])
        for kc in range(KC):
            tm, t_tm = tmp[kc % 2]
            S.add("dve", lambda e, tm=tm, kc=kc, sl=sl: e.scalar_tensor_tensor(
                out=tm[:], in0=xT[:, kc, sl], scalar=a_col[:, kc:kc + 1], in1=rs[:], op0=ALU.mult, op1=ALU.mult),
                reads=[t_x, t_rs] + t_par, writes=[t_tm])
            S.add("act", lambda e, tm=tm, kc=kc, sl=sl: e.activation(
                out=hT[:, kc, sl], in_=tm[:], func=AF.Identity, bias=shift_col[:, kc:kc + 1], scale=1.0),
                reads=[t_tm] + t_par, writes=[t_h])
            if out_dram is not None:
                t32, t_t32 = tmp32[kc % 2]
                S.add("act", lambda e, tm=tm, t32=t32, kc=kc: e.activation(
                    out=t32[:], in_=tm[:], func=AF.Identity, bias=shift_col[:, kc:kc + 1], scale=1.0),
                    reads=[t_tm] + t_par, writes=[t_t32])
                S.add("sp", lambda e, t32=t32, kc=kc, sl=sl: e.dma_start(out=out_dram[kc * 128:(kc + 1) * 128, sl], in_=t32[:]),
                      reads=[t_t32], dma=True)


def prep_mod_params(C, par_ap, n_sets):
    S = C.S
    par, t_par = C.sb([128, n_sets * 4 * KC], F32, "par")
    acol, t_a = C.sb([128, n_sets * KC], F32, "acol")
    S.add("sp", lambda e: e.dma_start(out=par[:], in_=par_ap), writes=[t_par], dma=True)
    out = []
    for s in range(n_sets):
        b = s * 4 * KC
        S.add("dve", lambda e, b=b, s=s: e.scalar_tensor_tensor(
            out=acol[:, s * KC:(s + 1) * KC], in0=par[:, b + 2 * KC:b + 3 * KC], scalar=1.0, in1=par[:, b:b + KC],
            op0=ALU.add, op1=ALU.mult), reads=[t_par], writes=[t_a])
        out.append((acol[:, s * KC:(s + 1) * KC], par[:, b + KC:b + 2 * KC], par[:, b + 3 * KC:b + 4 * KC]))
    return out, [t_par, t_a]


ADA_COLS = 2 * 18432 // NCORES
ADA_NB = ADA_COLS // 512


def build_ada():
    C = Ctx()
    S = C.S
    c_ap = C.din("c", [128, KC])
    w_ap = C.din("w", [ADA_NB, 128, KC, 512])
    b_ap = C.din("b", [1, ADA_COLS])
    o_ap = C.dout("o", [1, ADA_COLS])
    C.init_psum()
    ct, t_c = C.sb([128, KC], F32, "c")
    bt, t_b = C.sb([1, ADA_COLS], F32, "b")
    ot, t_o = C.sb([1, ADA_COLS], F32, "o")
    wt = [C.sb([128, KC, 512], F32, "w") for _ in range(2)]
    S.add("sp", lambda e: e.dma_start(out=ct[:], in_=c_ap), writes=[t_c], dma=True)
    S.add("sp", lambda e: e.dma_start(out=bt[:], in_=b_ap), writes=[t_b], dma=True)
    S.add("act", lambda e: e.activation(out=ct[:], in_=ct[:], func=AF.Silu), reads=[t_c], writes=[t_c])
    for nb in range(ADA_NB):
        w, t_w = wt[nb % 2]
        for kc in range(KC):
            S.add("sp", lambda e, w=w, nb=nb, kc=kc: e.dma_start(out=w[:, kc, :], in_=w_ap[nb, :, kc, :]), writes=[t_w], dma=True)
        ps, t_ps = C.psum()
        mm_group(S, ps[0:1, :], [(ct[:, kc:kc + 1], w[:, kc, :]) for kc in range(KC)], [t_c, t_w], [t_ps])
        S.add("dve", lambda e, ps=ps, nb=nb: e.tensor_tensor(
            out=ot[:, nb * 512:(nb + 1) * 512], in0=ps[0:1, :], in1=bt[:, nb * 512:(nb + 1) * 512], op=ALU.add),
            reads=[t_ps, t_b], writes=[t_o])
    S.add("sp", lambda e: e.dma_start(out=o_ap, in_=ot[:]), reads=[t_o], dma=True)
    return C.finish()


def build_ffn(emit_h_next, dbg=None):
    C = Ctx()
    S = C.S
    nsets = 2 if emit_h_next else 1
    x_ap = C.din("xT", [D, TOK])
    par_ap = C.din("par", [128, nsets * 4 * KC])
    win_ap = C.din("win", [NJ, 128, KC, 256])
    wout_ap = C.din("wout", [NQ, KC, 128, JQ, 128])
    ones_ap = C.din("ones", [128, 128])
    xo_ap = C.dout("xo", [D, TOK])
    ho_ap = C.dout("ho", [D, TOK]) if emit_h_next else None
    C.init_psum()
    ones, t_ones = load_consts(C, ones_ap)
    xT, t_x = C.sb([128, KC, TOK], F32, "xT")
    hT, t_h = C.sb([128, KC, TOK], BF16, "hT")
    actT, t_act = C.sb([128, JQ, TOK], BF16, "actT")
    wi = [C.sb([128, KC, 256], BF16, "wi") for _ in range(2)]
    wo = [C.sb([128, JQ, 128], BF16, "wo") for _ in range(2)]
    sg = [C.sb([128, 512], F32, "sg") for _ in range(2)]
    g05, t_g = C.sb([128, KC], F32, "g05")
    for kc in range(KC):
        S.add("sp", lambda e, kc=kc: e.dma_start(out=xT[:, kc, :], in_=x_ap[kc * 128:(kc + 1) * 128, :]),
              writes=[t_x], dma=True)
    sets, t_par = prep_mod_params(C, par_ap, nsets)
    a_col, shift_col, gate_col = sets[0]
    S.add("dve", lambda e: e.tensor_scalar(out=g05[:], in0=gate_col, scalar1=0.5, scalar2=None, op0=ALU.mult),
          reads=t_par, writes=[t_g])
    modulate(C, xT, t_x, hT, t_h, a_col, shift_col, t_par, ones, t_ones, TOK, out_dram=(ho_ap if dbg == "mod" else None))
    if dbg == "mod":
        for kc in range(KC):
            S.add("sp", lambda e, kc=kc: e.dma_start(out=xo_ap[kc * 128:(kc + 1) * 128, :], in_=xT[:, kc, :]),
                  reads=[t_x], dma=True)
        return C.finish()
    nwi = 0
    nwo = 0
    nsg = 0
    for qh in range(NQ):
        for jj in range(JQ):
            j = qh * JQ + jj
            w, t_w = wi[nwi % 2]
            nwi += 1
            S.add("pool", lambda e, w=w, j=j: e.dma_start(out=w[:], in_=win_ap[j]), writes=[t_w], dma=True)
            for tt in range(TOK // 512):
                sl = slice(tt * 512, (tt + 1) * 512)
                pg, t_pg = C.psum()
                pu, t_pu = C.psum()
                mm_group(S, pg[:, :], [(w[:, kc, 0:128], hT[:, kc, sl]) for kc in range(KC)], [t_w, t_h], [t_pg])
                mm_group(S, pu[:, :], [(w[:, kc, 128:256], hT[:, kc, sl]) for kc in range(KC)], [t_w, t_h], [t_pu])
                s_, t_s = sg[nsg % 2]
                nsg += 1
                S.add("act", lambda e, s_=s_, pg=pg: e.activation(out=s_[:], in_=pg[:, :], func=AF.Silu),
                      reads=[t_pg], writes=[t_s])
                S.add("dve", lambda e, s_=s_, pu=pu, jj=jj, sl=sl: e.tensor_tensor(
                    out=actT[:, jj, sl], in0=s_[:], in1=pu[:, :], op=ALU.mult),
                    reads=[t_s, t_pu], writes=[t_act])
        for m in range(KC):
            w, t_w = wo[nwo % 2]
            nwo += 1
            S.add("pool", lambda e, w=w, qh=qh, m=m: e.dma_start(out=w[:], in_=wout_ap[qh, m]), writes=[t_w], dma=True)
            for tt in range(TOK // 512):
                sl = slice(tt * 512, (tt + 1) * 512)
                po, t_po = C.psum()
                mm_group(S, po[:, :], [(w[:, jj, :], actT[:, jj, sl]) for jj in range(JQ)], [t_w, t_act], [t_po])
                S.add("dve", lambda e, po=po, m=m, sl=sl: e.scalar_tensor_tensor(
                    out=xT[:, m, sl], in0=po[:, :], scalar=g05[:, m:m + 1], in1=xT[:, m, sl], op0=ALU.mult, op1=ALU.add),
                    reads=[t_po, t_g, t_x], writes=[t_x])
    for kc in range(KC):
        S.add("sp", lambda e, kc=kc: e.dma_start(out=xo_ap[kc * 128:(kc + 1) * 128, :], in_=xT[:, kc, :]),
              reads=[t_x], dma=True)
    if emit_h_next:
        a2, sh2, _ = sets[1]
        modulate(C, xT, t_x, hT, t_h, a2, sh2, t_par, ones, t_ones, TOK, out_dram=ho_ap)
    return C.finish()


def cols16(v):
    return np.ascontiguousarray(np.asarray(v, np.float32).reshape(KC, 128).T)


def tile_win(w_in):
    w = w_in.reshape(KC, 128, 2, NJ, 128)
    return np.ascontiguousarray(w.transpose(3, 1, 0, 2, 4).reshape(NJ, 128, KC, 256))


def tile_wout(w_out):
    w = w_out.reshape(NQ, JQ, 128, KC, 128)
    return np.ascontiguousarray(w.transpose(0, 3, 2, 1, 4))


_PROG = {}


def prog(name, fn, *a):
    key = (name,) + a
    if key not in _PROG:
        _PROG[key] = fn(*a)
    return _PROG[key]


def run(nc, in_maps):
    res = run_bass_kernel_spmd(nc, in_maps, core_ids=list(range(NCORES)))
    return res.results


def compute_ada(c, w_ada, b_ada):
    cc = cols16(c.reshape(-1))
    wa = np.concatenate([w_ada[0], w_ada[1]], axis=1)
    ba = np.concatenate([b_ada[0], b_ada[1]], axis=0)
    in_maps = []
    for i in range(NCORES):
        ws = wa[:, i * ADA_COLS:(i + 1) * ADA_COLS].reshape(KC, 128, ADA_NB, 512)
        in_maps.append({
            "c": cc,
            "w": np.ascontiguousarray(ws.transpose(2, 1, 0, 3)),
            "b": np.ascontiguousarray(ba[i * ADA_COLS:(i + 1) * ADA_COLS].reshape(1, -1)),
        })
    r = run(prog("ada", build_ada), in_maps)
    ada = np.concatenate([r[i]["o"].reshape(-1) for i in range(NCORES)])
    return ada.reshape(2, 3, 3, D)


def par_block(norm_gain_li, ada_li):
    return np.concatenate([cols16(norm_gain_li), cols16(ada_li[0]), cols16(ada_li[1]), cols16(ada_li[2])], axis=1)


def run_ffn(xT, par, w_in, w_out, emit_h_next, dbg=None):
    win_t = tile_win(w_in)
    wout_t = tile_wout(w_out)
    ones = np.ones((128, 128), np.float32)
    in_maps = []
    for i in range(NCORES):
        in_maps.append({"xT": np.ascontiguousarray(xT[:, i * TOK:(i + 1) * TOK]), "par": par,
                        "win": win_t, "wout": wout_t, "ones": ones})
    r = run(prog("ffn", build_ffn, emit_h_next, dbg), in_maps)
    xo = np.concatenate([r[i]["xo"] for i in range(NCORES)], axis=1)
    ho = np.concatenate([r[i]["ho"] for i in range(NCORES)], axis=1) if emit_h_next else None
    return xo, ho


NCOLB = 1540
C_Q, C_K, C_Z, C_XS, C_B, C_C, C_CB, C_CC, C_CX, C_V = 0, 128, 256, 512, 768, 896, 1024, 1152, 1280, 1408
SCALE = 128 ** -0.5
NEG = -1e30


PARTS = ('conv', 'ssd', 'attn')
STOP = 0


def build_mixB(NT):
    C = Ctx()
    S = C.S
    ntok = NT * 512
    h_ap = C.din("hT", [D, ntok])
    w_ap = C.din("wB", [128, KC, NCOLB])
    sp_ap = C.din("sp", [128, 24])
    sp64_ap = C.din("sp64", [64, 24])
    cst_ap = C.din("cst", [128, 5 * 128 + 512 + 512])
    ya_ap = C.dout("ya", [ntok, 128])
    yg_ap = C.dout("yg", [256, ntok])
    yc_ap = C.dout("yc", [128, ntok])
    C.init_psum()
    cst, t_cst = C.sb([128, 5 * 128 + 1024], F32, "cst")
    S.add("sp", lambda e: e.dma_start(out=cst[:], in_=cst_ap), writes=[t_cst], dma=True)
    ident = cst[:, 0:128]
    ones = cst[:, 128:256]
    TA = cst[:, 256:384]
    TB = cst[:, 384:512]
    NEGM = cst[:, 512:640]
    U0 = cst[:, 640:896]
    UTRI = cst[:, 640:768]
    U1 = cst[:, 896:1152]
    SELH = cst[0:4, 1152:1664]
    spt, t_sp = C.sb([128, 24], F32, "sp")
    sp64, t_sp64 = C.sb([64, 24], F32, "sp64")
    S.add("sp", lambda e: e.dma_start(out=spt[:], in_=sp_ap), writes=[t_sp], dma=True)
    S.add("sp", lambda e: e.dma_start(out=sp64[:], in_=sp64_ap), writes=[t_sp64], dma=True)
    S.add("dve", lambda e: e.tensor_tensor(out=TA, in0=TA, in1=NEGM, op=ALU.add), reads=[t_cst], writes=[t_cst])
    aneg, t_an = C.sb([128, 4], F32, "aneg")
    S.add("act", lambda e: e.activation(out=aneg[:], in_=spt[:, 20:24], func=AF.Exp), reads=[t_sp], writes=[t_an])
    S.add("dve", lambda e: e.tensor_scalar(out=aneg[:], in0=aneg[:], scalar1=-1.0, scalar2=None, op0=ALU.mult),
          reads=[t_an], writes=[t_an])
    wB, _ = C.sb([128, KC, NCOLB], BF16, "wB")
    t_wB = [T("wB%d" % kc) for kc in range(KC)]
    for kc in range(KC):
        S.add("pool", lambda e, kc=kc: e.dma_start(out=wB[:, kc, :], in_=w_ap[:, kc, :]), writes=[t_wB[kc]], dma=True)
    hbuf = [C.sb([128, KC, 512], BF16, "hb")[0] for _ in range(2)]
    t_hb = [[T("hb") for _ in range(KC)] for _ in range(2)]
    KT, t_KT = C.sb([128, ntok], BF16, "KT")
    vaug, t_v = C.sb([128, NT * 4, 130], BF16, "vaug")
    S.add("pool", lambda e: e.memset(vaug[:], 1.0), writes=[t_v])
    kmean, t_km = C.sb([128, 32], F32, "kmean")
    S.add("pool", lambda e: e.memset(kmean[:], 0.0), writes=[t_km])

    def sbt(shape, dt, name):
        return C.sb(shape, dt, name)

    sq, t_sq = sbt([128, 512], F32, "sq")
    rs, t_rs = sbt([128, 512], F32, "rs")
    qn32, t_qn = sbt([128, 512], F32, "qn32")
    qnb, t_qnb = sbt([128, 512], BF16, "qnb")
    kn32, t_kn = sbt([128, 2, 256], F32, "kn32")
    kms, t_kms = sbt([128, 2], F32, "kms")
    sz = [sbt([64, 512], F32, "sz") for _ in range(4)]
    xpre = [sbt([64, 515], F32, "xpre") for _ in range(4)]
    bpre, t_bpre = sbt([128, 515], F32, "bpre")
    cpre, t_cpre = sbt([128, 515], F32, "cpre")
    for (t_, tt_) in xpre + [(bpre, t_bpre), (cpre, t_cpre)]:
        S.add("pool", lambda e, t_=t_: e.memset(t_[:, 0:3], 0.0), writes=[tt_])
    xsT = [sbt([64, 512], F32, "xsT") for _ in range(4)]
    BT32, t_BT32 = sbt([128, 512], F32, "BT32")
    CT32, t_CT32 = sbt([128, 512], F32, "CT32")
    BTb, t_BTb = sbt([128, 512], BF16, "BTb")
    CTb, t_CTb = sbt([128, 512], BF16, "CTb")
    cacc = [sbt([128, 512], F32, "cacc") for _ in range(2)]
    cBs, t_cBs = sbt([128, 512], F32, "cBs")
    cCs, t_cCs = sbt([128, 512], F32, "cCs")
    ucv, t_ucv = sbt([128, 514], F32, "ucv")
    S.add("pool", lambda e: e.memset(ucv[:, 0:2], 0.0), writes=[t_ucv])
    ycs, t_ycs = sbt([128, 512], F32, "ycs")
    dtraw, t_dtr = sbt([128, 4, 4], F32, "dtraw")
    dtt, t_dtt = sbt([128, 4, 4], F32, "dtt")
    dtA, t_dtA = sbt([128, 4, 4], F32, "dtA")
    wd, t_wd = sbt([128, 4, 4], F32, "wd")
    xd = [sbt([128, 256], BF16, "xd") for _ in range(4)]
    xdd = [sbt([128, 256], BF16, "xdd") for _ in range(4)]
    Btok = [sbt([128, 128], BF16, "Btok") for _ in range(4)]
    acs, t_acs = sbt([128, 12], F32, "acs")
    d2, t_d2 = sbt([128, 8], F32, "d2")
    acsT, t_acsT = sbt([4, 256], F32, "acsT")
    cbm, t_cbm = sbt([128, 384], F32, "cbm")
    E = [sbt([128, 256], F32, "E") for _ in range(4)]
    Cdec, t_Cdec = sbt([128, 256], BF16, "Cdec")
    Dm, t_Dm = sbt([128, 384], F32, "Dm")
    MT, t_MT = sbt([128, 384], BF16, "MT")
    prev32, t_p32 = sbt([128, 256], F32, "prev32")
    prevb, t_pb = sbt([128, 256], BF16, "prevb")
    S.add("pool", lambda e: e.memset(prev32[:], 0.0), writes=[t_p32])
    S.add("pool", lambda e: e.memset(prevb[:], 0.0), writes=[t_pb])
    yt1, t_yt1 = sbt([64, 256], F32, "yt1")
    ygs = [sbt([64, 256], F32, "ygs") for _ in range(2)]
    gsbL = [sbt([128, 32], F32, "gsb") for _ in range(4)]
    top8L = [sbt([128, 8], F32, "top8") for _ in range(4)]
    mselL = [sbt([128, 32], F32, "msel") for _ in range(4)]
    accL = [sbt([128, 130], F32, "acc") for _ in range(4)]
    tmpS2L = [[sbt([128, 256], F32, "tmpS")] * 2 for _ in range(4)]
    pt2L = [[sbt([128, 2, 128], BF16, "pt") for _ in range(2)] for _ in range(4)]
    nptL = [0, 0, 0, 0]
    recL = [sbt([128, 1], F32, "rec") for _ in range(4)]
    yqL = [[sbt([128, 128], F32, "yq")] * 2 for _ in range(4)]
    nyqL = [0, 0, 0, 0]
    nyg = 0
    nyq = 0
    nosb = 0
    if STOP == 1:
        return C.finish()

    def load_h(tt_):
        hb_ = hbuf[tt_ % 2]
        for kc in range(KC):
            S.add("pool", lambda e, kc=kc, hb_=hb_, t0=tt_ * 512: e.dma_start(
                out=hb_[:, kc, :], in_=h_ap[kc * 128:(kc + 1) * 128, t0:t0 + 512]), writes=[t_hb[tt_ % 2][kc]], dma=True)

    load_h(0)
    for tt in range(NT):
        tok0 = tt * 512
        hb = hbuf[tt % 2]
        thb = t_hb[tt % 2]

        def proj(col0, M):
            ps, t_ps = C.psum()
            mm_group(S, ps[0:M, :], [(wB[:, kc, col0:col0 + M], hb[:, kc, :]) for kc in range(KC)], thb + t_wB, [t_ps])
            return ps, t_ps

        for which in range(2):
            ps, t_ps = proj(C_Q if which == 0 else C_K, 128)
            S.add("act", lambda e, ps=ps: e.activation(out=sq[:], in_=ps[:, :], func=AF.Square), reads=[t_ps], writes=[t_sq])
            ps2, t_ps2 = C.psum()
            S.add("pe", lambda e, ps2=ps2: e.matmul(ps2[:, :], ones, sq[:], start=True, stop=True),
                  reads=[t_sq, t_cst], writes=[t_ps2])
            S.add("act", lambda e, ps2=ps2: e.activation(out=rs[:], in_=ps2[:, :], func=AF.Sqrt, scale=1.0 / 128, bias=EPS),
                  reads=[t_ps2], writes=[t_rs])
            S.add("dve", lambda e: e.reciprocal(out=rs[:], in_=rs[:]), reads=[t_rs], writes=[t_rs])
            if which == 0:
                S.add("dve", lambda e, ps=ps: e.scalar_tensor_tensor(
                    out=qn32[:], in0=ps[:, :], scalar=spt[:, 0:1], in1=rs[:], op0=ALU.mult, op1=ALU.mult),
                    reads=[t_ps, t_rs, t_sp], writes=[t_qn])
                S.add("act", lambda e: e.activation(out=qnb[:], in_=qn32[:], func=AF.Identity), reads=[t_qn], writes=[t_qnb])
            else:
                for a in range(2):
                    S.add("dve", lambda e, ps=ps, a=a: e.scalar_tensor_tensor(
                        out=kn32[:, a, :], in0=ps[:, a * 256:(a + 1) * 256], scalar=spt[:, 1:2], in1=rs[:, a * 256:(a + 1) * 256],
                        op0=ALU.mult, op1=ALU.mult), reads=[t_ps, t_rs, t_sp], writes=[t_kn])
                    S.add("act", lambda e, a=a, tok0=tok0: e.activation(
                        out=KT[:, tok0 + a * 256:tok0 + (a + 1) * 256], in_=kn32[:, a, :], func=AF.Identity),
                        reads=[t_kn], writes=[t_KT])
                S.add("dve", lambda e: e.tensor_reduce(out=kms[:], in_=kn32[:], axis=AX.X, op=ALU.add), reads=[t_kn], writes=[t_kms])
                S.add("dve", lambda e, tt=tt: e.tensor_scalar(
                    out=kmean[:, 2 * tt:2 * tt + 2], in0=kms[:], scalar1=1.0 / 256, scalar2=None, op0=ALU.mult),
                    reads=[t_kms], writes=[t_km])
        if STOP == 2:
            return C.finish()
        for h in range(4):
            ps, t_ps = proj(C_Z + h * 64, 64)
            S.add("act", lambda e, ps=ps, h=h: e.activation(out=sz[h][0][:], in_=ps[0:64, :], func=AF.Silu),
                  reads=[t_ps], writes=[sz[h][1]])
        for h in range(4):
            ps, t_ps = proj(C_XS + h * 64, 64)
            S.add("act", lambda e, ps=ps, h=h: e.activation(out=xpre[h][0][:, 3:515], in_=ps[0:64, :], func=AF.Identity),
                  reads=[t_ps], writes=[xpre[h][1]])
        ps, t_ps = proj(C_B, 128)
        S.add("act", lambda e, ps=ps: e.activation(out=bpre[:, 3:515], in_=ps[:, :], func=AF.Identity), reads=[t_ps], writes=[t_bpre])
        ps, t_ps = proj(C_C, 128)
        S.add("act", lambda e, ps=ps: e.activation(out=cpre[:, 3:515], in_=ps[:, :], func=AF.Identity), reads=[t_ps], writes=[t_cpre])
        ps, t_ps = proj(C_CB, 128)
        S.add("act", lambda e, ps=ps: e.activation(out=cBs[:], in_=ps[:, :], func=AF.Identity), reads=[t_ps], writes=[t_cBs])
        ps, t_ps = proj(C_CC, 128)
        S.add("act", lambda e, ps=ps: e.activation(out=cCs[:], in_=ps[:, :], func=AF.Identity), reads=[t_ps], writes=[t_cCs])
        ps, t_ps = proj(C_CX, 128)
        S.add("dve", lambda e, ps=ps: e.tensor_tensor(out=ucv[:, 2:514], in0=cCs[:], in1=ps[:, :], op=ALU.mult),
              reads=[t_ps, t_cCs], writes=[t_ucv])
        if STOP == 3:
            return C.finish()
        for s in range(4):
            ps, t_ps = C.psum()
            mm_group(S, ps[:, 0:128], [(hb[:, kc, s * 128:(s + 1) * 128], wB[:, kc, C_V:C_V + 128]) for kc in range(KC)],
                     thb + t_wB, [t_ps])
            if STOP != 8:
                mm_group(S, ps[:, 128:132], [(hb[:, kc, s * 128:(s + 1) * 128], wB[:, kc, C_V + 128:C_V + 132]) for kc in range(KC)],
                         thb + t_wB, [t_ps])
            if STOP == 4:
                continue
            S.add("act", lambda e, ps=ps, s=s, tt=tt: e.activation(out=vaug[:, 4 * tt + s, 0:128], in_=ps[:, 0:128], func=AF.Identity),
                  reads=[t_ps], writes=[t_v])
            if STOP in (5, 7):
                continue
            S.add("act", lambda e, ps=ps, s=s: e.activation(out=dtraw[:, s, :], in_=ps[:, 128:132], func=AF.Identity),
                  reads=[t_ps], writes=[t_dtr])
            S.add("pool", lambda e, s=s: e.tensor_tensor(out=dtraw[:, s, :], in0=dtraw[:, s, :], in1=spt[:, 16:20], op=ALU.add),
                  reads=[t_dtr, t_sp], writes=[t_dtr])
            if STOP == 6:
                continue
        if tt + 1 < NT:
            load_h(tt + 1)
        if 'conv' in PARTS:
            a0, t_a0 = cacc[0]
            S.add("dve", lambda e: e.tensor_scalar(out=a0[:], in0=ucv[:, 0:512], scalar1=spt[:, 13:14], scalar2=None, op0=ALU.mult),
                  reads=[t_ucv, t_sp], writes=[t_a0])
            for j in (1, 2):
                S.add("dve", lambda e, j=j: e.scalar_tensor_tensor(
                    out=a0[:], in0=ucv[:, j:j + 512], scalar=spt[:, 13 + j:14 + j], in1=a0[:], op0=ALU.mult, op1=ALU.add),
                    reads=[t_ucv, t_sp, t_a0], writes=[t_a0])
            S.add("dve", lambda e: e.tensor_tensor(out=ycs[:], in0=cBs[:], in1=a0[:], op=ALU.mult), reads=[t_cBs, t_a0], writes=[t_ycs])
            S.add("sp", lambda e, tok0=tok0: e.dma_start(out=yc_ap[:, tok0:tok0 + 512], in_=ycs[:]), reads=[t_ycs], dma=True)
            S.add("dve", lambda e: e.tensor_copy(out=ucv[:, 0:2], in_=ucv[:, 512:514]), reads=[t_ucv], writes=[t_ucv])
        def sec_ssd():
            nonlocal nyg
            def ssd_conv(pre, t_pre, P, wsrc, t_w, c0, outs):
                ac, t_ac = cacc[1]
                S.add("dve", lambda e: e.tensor_scalar(out=ac[0:P, :], in0=pre[:, 0:512], scalar1=wsrc[:, c0:c0 + 1], scalar2=None, op0=ALU.mult),
                      reads=[t_pre, t_w], writes=[t_ac])
                for j in (1, 2, 3):
                    S.add("dve", lambda e, j=j: e.scalar_tensor_tensor(
                        out=ac[0:P, :], in0=pre[:, j:j + 512], scalar=wsrc[:, c0 + j:c0 + j + 1], in1=ac[0:P, :], op0=ALU.mult, op1=ALU.add),
                        reads=[t_pre, t_w, t_ac], writes=[t_ac])
                o, t_o = outs
                S.add("act", lambda e: e.activation(out=o[:], in_=ac[0:P, :], func=AF.Silu, bias=wsrc[:, c0 + 4:c0 + 5], scale=1.0),
                      reads=[t_ac, t_w], writes=[t_o])
                S.add("dve", lambda e: e.tensor_copy(out=pre[:, 0:3], in_=pre[:, 512:515]), reads=[t_pre], writes=[t_pre])

            for h in range(4):
                ssd_conv(xpre[h][0], xpre[h][1], 64, sp64, t_sp64, h * 6, xsT[h])
            ssd_conv(bpre, t_bpre, 128, spt, t_sp, 3, (BT32, t_BT32))
            ssd_conv(cpre, t_cpre, 128, spt, t_sp, 8, (CT32, t_CT32))
            S.add("pool", lambda e: e.tensor_copy(out=BTb[:], in_=BT32[:]), reads=[t_BT32], writes=[t_BTb])
            S.add("pool", lambda e: e.tensor_copy(out=CTb[:], in_=CT32[:]), reads=[t_CT32], writes=[t_CTb])
            S.add("act", lambda e: e.activation(out=dtt[:], in_=dtraw[:], func=AF.Exp), reads=[t_dtr], writes=[t_dtt])
            S.add("act", lambda e: e.activation(out=dtt[:], in_=dtt[:], func=AF.Ln, bias=1.0, scale=1.0), reads=[t_dtt], writes=[t_dtt])
            for s in range(4):
                S.add("dve", lambda e, s=s: e.tensor_tensor(out=dtA[:, s, :], in0=dtt[:, s, :], in1=aneg[:], op=ALU.mult),
                      reads=[t_dtt, t_an], writes=[t_dtA])
            for cc in range(2):
                ccol = slice(cc * 256, (cc + 1) * 256)
                s0, s1 = 2 * cc, 2 * cc + 1
                psa, t_psa = C.psum()
                S.add("pe", lambda e, psa=psa, s0=s0: e.matmul(psa[:, 0:4], UTRI, dtA[:, s0, :], start=True, stop=True),
                      reads=[t_dtA, t_cst], writes=[t_psa])
                mm_group(S, psa[:, 4:8], [(ones, dtA[:, s0, :]), (UTRI, dtA[:, s1, :])], [t_dtA, t_cst], [t_psa])
                mm_group(S, psa[:, 8:12], [(ones, dtA[:, s0, :]), (ones, dtA[:, s1, :])], [t_dtA, t_cst], [t_psa])
                S.add("dve", lambda e, psa=psa: e.tensor_copy(out=acs[:], in_=psa[:, 0:12]), reads=[t_psa], writes=[t_acs])
                for s in range(2):
                    S.add("dve", lambda e, s=s: e.tensor_tensor(out=d2[:, 4 * s:4 * s + 4], in0=acs[:, 8:12], in1=acs[:, 4 * s:4 * s + 4], op=ALU.subtract),
                          reads=[t_acs], writes=[t_d2])
                S.add("act", lambda e: e.activation(out=d2[:], in_=d2[:], func=AF.Exp), reads=[t_d2], writes=[t_d2])
                for s in range(2):
                    S.add("dve", lambda e, s=s, cc=cc: e.tensor_tensor(out=wd[:, 2 * cc + s, :], in0=dtt[:, 2 * cc + s, :], in1=d2[:, 4 * s:4 * s + 4], op=ALU.mult),
                          reads=[t_dtt, t_d2], writes=[t_wd])
                pst, t_pst = C.psum()
                mm_group(S, pst[0:4, 0:256], [(dtA[:, s0, :], U0), (dtA[:, s1, :], U1)], [t_dtA, t_cst], [t_pst])
                S.add("dve", lambda e, pst=pst: e.tensor_copy(out=acsT[:], in_=pst[0:4, 0:256]), reads=[t_pst], writes=[t_acsT])
                for s in (s0, s1):
                    psx, t_psx = C.psum()

                    def tfn(e, psx=psx, s=s):
                        ins = None
                        for h in range(4):
                            ins = e.transpose(out=psx[:, h * 64:(h + 1) * 64], in_=xsT[h][0][:, s * 128:(s + 1) * 128], identity=cst[0:64, 0:64])
                        return ins
                    S.add("pe", tfn, reads=[x[1] for x in xsT] + [t_cst], writes=[t_psx])
                    for h in range(4):
                        S.add("dve", lambda e, psx=psx, s=s, h=h: e.tensor_scalar(
                            out=xd[s][0][:, h * 64:(h + 1) * 64], in0=psx[:, h * 64:(h + 1) * 64], scalar1=dtt[:, s, h:h + 1], scalar2=None, op0=ALU.mult),
                            reads=[t_psx, t_dtt], writes=[xd[s][1]])
                        S.add("dve", lambda e, psx=psx, s=s, h=h: e.tensor_scalar(
                            out=xdd[s][0][:, h * 64:(h + 1) * 64], in0=psx[:, h * 64:(h + 1) * 64], scalar1=wd[:, s, h:h + 1], scalar2=None, op0=ALU.mult),
                            reads=[t_psx, t_wd], writes=[xdd[s][1]])
                    psb, t_psb = C.psum()
                    S.add("pe", lambda e, psb=psb, s=s: e.transpose(out=psb[:, 0:128], in_=BT32[:, s * 128:(s + 1) * 128], identity=ident),
                          reads=[t_BT32, t_cst], writes=[t_psb])
                    S.add("act", lambda e, psb=psb, s=s: e.activation(out=Btok[s][0][:], in_=psb[:, 0:128], func=AF.Identity),
                          reads=[t_psb], writes=[Btok[s][1]])
                psc, t_psc = C.psum()
                S.add("pe", lambda e, psc=psc, cc=cc: e.matmul(psc[:, 0:256], BTb[:, cc * 256:cc * 256 + 128], CTb[:, cc * 256:cc * 256 + 256], start=True, stop=True),
                      reads=[t_BTb, t_CTb], writes=[t_psc])
                S.add("pe", lambda e, psc=psc, cc=cc: e.matmul(psc[:, 256:384], BTb[:, cc * 256 + 128:cc * 256 + 256], CTb[:, cc * 256 + 128:cc * 256 + 256], start=True, stop=True),
                      reads=[t_BTb, t_CTb], writes=[t_psc])
                S.add("dve", lambda e, psc=psc: e.tensor_tensor(out=cbm[:, 0:256], in0=psc[:, 0:256], in1=U0, op=ALU.mult),
                      reads=[t_psc, t_cst], writes=[t_cbm])
                S.add("dve", lambda e, psc=psc: e.tensor_tensor(out=cbm[:, 256:384], in0=psc[:, 256:384], in1=UTRI, op=ALU.mult),
                      reads=[t_psc, t_cst], writes=[t_cbm])
                for h in range(4):
                    psbc, t_psbc = C.psum()
                    S.add("pe", lambda e, psbc=psbc, h=h: e.matmul(psbc[:, 0:256], SELH[:, h * 128:(h + 1) * 128], acsT[:], start=True, stop=True),
                          reads=[t_acsT, t_cst], writes=[t_psbc])
                    Eh, t_Eh = E[h]
                    S.add("act", lambda e, psbc=psbc, Eh=Eh: e.activation(out=Eh[:], in_=psbc[:, 0:256], func=AF.Exp), reads=[t_psbc], writes=[t_Eh])
                    S.add("pool", lambda e, Eh=Eh, ccol=ccol: e.tensor_tensor(out=Cdec[:], in0=CT32[:, ccol], in1=Eh[:], op=ALU.mult),
                          reads=[t_CT32, t_Eh], writes=[t_Cdec])
                    S.add("dve", lambda e, psbc=psbc, h=h: e.tensor_scalar(
                        out=Dm[:, 0:256], in0=psbc[:, 0:256], scalar1=acs[:, h:h + 1], scalar2=0.0, op0=ALU.subtract, op1=ALU.min),
                        reads=[t_psbc, t_acs], writes=[t_Dm])
                    S.add("dve", lambda e, psbc=psbc, h=h: e.tensor_scalar(
                        out=Dm[:, 256:384], in0=psbc[:, 128:256], scalar1=acs[:, 4 + h:5 + h], scalar2=0.0, op0=ALU.subtract, op1=ALU.min),
                        reads=[t_psbc, t_acs], writes=[t_Dm])
                    S.add("act", lambda e: e.activation(out=Dm[:], in_=Dm[:], func=AF.Exp), reads=[t_Dm], writes=[t_Dm])
                    S.add("dve", lambda e: e.tensor_tensor(out=MT[:], in0=Dm[:], in1=cbm[:], op=ALU.mult), reads=[t_Dm, t_cbm], writes=[t_MT])
                    psy, t_psy = C.psum()
                    hs = slice(h * 64, (h + 1) * 64)

                    def yfn(e, psy=psy, hs=hs, s0=s0, s1=s1):
                        e.matmul(psy[0:64, 0:256], prevb[:, hs], Cdec[:], start=True, stop=False)
                        e.matmul(psy[0:64, 0:256], xd[s0][0][:, hs], MT[:, 0:256], start=False, stop=False)
                        return e.matmul(psy[0:64, 128:256], xd[s1][0][:, hs], MT[:, 256:384], start=False, stop=True)
                    S.add("pe", yfn, reads=[t_pb, t_Cdec, xd[s0][1], xd[s1][1], t_MT], writes=[t_psy])
                    S.add("dve", lambda e, psy=psy, h=h, ccol=ccol: e.scalar_tensor_tensor(
                        out=yt1[:], in0=xsT[h][0][:, ccol], scalar=sp64[:, h * 6 + 5:h * 6 + 6], in1=psy[0:64, 0:256], op0=ALU.mult, op1=ALU.add),
                        reads=[xsT[h][1], t_sp64, t_psy], writes=[t_yt1])
                    yg, t_yg = ygs[nyg % 2]
                    nyg += 1
                    S.add("pool", lambda e, yg=yg, h=h, ccol=ccol: e.tensor_tensor(out=yg[:], in0=yt1[:], in1=sz[h][0][:, ccol], op=ALU.mult),
                          reads=[t_yt1, sz[h][1]], writes=[t_yg])
                    S.add("sp", lambda e, yg=yg, h=h, tok0=tok0, cc=cc: e.dma_start(
                        out=yg_ap[h * 64:(h + 1) * 64, tok0 + cc * 256:tok0 + (cc + 1) * 256], in_=yg[:]), reads=[t_yg], dma=True)
                pss, t_pss = C.psum()
                mm_group(S, pss[:, 0:256], [(Btok[s0][0][:], xdd[s0][0][:]), (Btok[s1][0][:], xdd[s1][0][:])],
                         [Btok[s0][1], Btok[s1][1], xdd[s0][1], xdd[s1][1]], [t_pss])
                for h in range(4):
                    hs = slice(h * 64, (h + 1) * 64)
                    S.add("dve", lambda e, h=h, hs=hs, pss=pss: e.scalar_tensor_tensor(
                        out=prev32[:, hs], in0=prev32[:, hs], scalar=E[h][0][:, 255:256], in1=pss[:, hs], op0=ALU.mult, op1=ALU.add),
                        reads=[t_p32, E[h][1], t_pss], writes=[t_p32])
                S.add("act", lambda e: e.activation(out=prevb[:], in_=prev32[:], func=AF.Identity), reads=[t_p32], writes=[t_pb])
        def sec_attn(qis=(0, 1, 2, 3), bs=0):
            gsb, t_gsb = gsbL[bs]
            top8, t_top8 = top8L[bs]
            msel, t_msel = mselL[bs]
            acc, t_acc = accL[bs]
            rec, t_rec = recL[bs]
            pt2 = pt2L[bs]
            tmpS2 = tmpS2L[bs]
            yq = yqL[bs]
            for i in qis:
                qi = 4 * tt + i
                J = qi // 2
                eo = qi % 2
                qc = slice(i * 128, (i + 1) * 128)
                use_sel = J > 3
                if use_sel:
                    psg, t_psg = C.psum()
                    S.add("pe", lambda e, psg=psg, qc=qc: e.matmul(psg[:, 0:32], qn32[:, qc], kmean[:], start=True, stop=True),
                          reads=[t_qn, t_km], writes=[t_psg])
                    S.add("pool", lambda e: e.memset(gsb[:], NEG), writes=[t_gsb])
                    S.add("dve", lambda e, psg=psg, J=J: e.tensor_copy(out=gsb[:, 0:J], in_=psg[:, 0:J]), reads=[t_psg], writes=[t_gsb])
                    S.add("dve", lambda e: e.max(out=top8[:], in_=gsb[:]), reads=[t_gsb], writes=[t_top8])
                    S.add("dve", lambda e: e.tensor_scalar(out=msel[:], in0=gsb[:], scalar1=top8[:, 2:3], scalar2=None, op0=ALU.is_ge),
                          reads=[t_gsb, t_top8], writes=[t_msel])
                blocks = [J] + list(range(J))
                for bi, n in enumerate(blocks):
                    if n == J:
                        halves = [(0, "A")] if eo == 0 else [(0, "B"), (1, "A")]
                    elif n == J - 1:
                        halves = [(0, "c"), (1, "B")] if eo == 0 else [(0, "c"), (1, "c")]
                    else:
                        halves = [(0, "c"), (1, "c")]
                    pss2, t_pss2 = C.psum()
                    pt, t_pt = pt2[nptL[bs] % 2]
                    tmpS, t_tmpS = tmpS2[nptL[bs] % 2]
                    nptL[bs] += 1

                    def sfn(e, pss2=pss2, halves=halves, n=n, qc=qc):
                        ins = None
                        for hh, _k in halves:
                            ins = e.matmul(pss2[:, hh * 128:(hh + 1) * 128], KT[:, n * 256 + hh * 128:n * 256 + (hh + 1) * 128], qnb[:, qc], start=True, stop=True)
                        return ins
                    S.add("pe", sfn, reads=[t_KT, t_qnb], writes=[t_pss2])
                    if STOP == 10:
                        continue
                    for hh, kind in halves:
                        if (STOP == 14 and kind != "c") or (STOP == 15 and kind == "c"):
                            continue
                        if kind == "c":
                            S.add("act", lambda e, pss2=pss2, hh=hh, pt=pt: e.activation(
                                out=pt[:, hh, :], in_=pss2[:, hh * 128:(hh + 1) * 128], func=AF.Exp, bias=spt[:, 2:3], scale=SCALE),
                                reads=[t_pss2, t_sp], writes=[t_pt])
                        else:
                            bt_ = TA if kind == "A" else TB
                            S.add("dve", lambda e, pss2=pss2, hh=hh, bt_=bt_, tmpS=tmpS: e.scalar_tensor_tensor(
                                out=tmpS[:, hh * 128:(hh + 1) * 128], in0=pss2[:, hh * 128:(hh + 1) * 128], scalar=SCALE, in1=bt_, op0=ALU.mult, op1=ALU.add),
                                reads=[t_pss2, t_cst], writes=[t_tmpS])
                            S.add("act", lambda e, hh=hh, pt=pt, tmpS=tmpS: e.activation(out=pt[:, hh, :], in_=tmpS[:, hh * 128:(hh + 1) * 128], func=AF.Exp),
                                  reads=[t_tmpS], writes=[t_pt])
                    if STOP in (11, 14, 15):
                        continue
                    pso, t_pso = C.psum()
                    nh = len(halves)

                    def ofn(e, pso=pso, halves=halves, n=n, nh=nh, pt=pt):
                        ins = None
                        for ii, (hh, _k) in enumerate(halves):
                            ins = e.matmul(pso[:, 0:130], pt[:, hh, :], vaug[:, 2 * n + hh, :], start=(ii == 0), stop=(ii == nh - 1))
                        return ins
                    S.add("pe", ofn, reads=[t_pt, t_v], writes=[t_pso])
                    if STOP == 12:
                        continue
                    if bi == 0:
                        S.add("act", lambda e, pso=pso: e.activation(out=acc[:], in_=pso[:, 0:130], func=AF.Identity), reads=[t_pso], writes=[t_acc])
                    else:
                        sc_ = msel[:, n:n + 1] if use_sel else 1.0
                        S.add("dve", lambda e, pso=pso, sc_=sc_: e.scalar_tensor_tensor(
                            out=acc[:], in0=pso[:, 0:130], scalar=sc_, in1=acc[:], op0=ALU.mult, op1=ALU.add),
                            reads=[t_pso, t_msel, t_acc], writes=[t_acc])
                if STOP in (10, 11, 12, 13, 14, 15):
                    continue
                S.add("dve", lambda e: e.reciprocal(out=rec[:], in_=acc[:, 128:129]), reads=[t_acc], writes=[t_rec])
                yq_, t_yq = yq[nyqL[bs] % 2]
                nyqL[bs] += 1
                S.add("dve", lambda e, yq_=yq_: e.tensor_scalar(out=yq_[:], in0=acc[:, 0:128], scalar1=rec[:, 0:1], scalar2=None, op0=ALU.mult),
                      reads=[t_acc, t_rec], writes=[t_yq])
                S.add("sp", lambda e, yq_=yq_, qi=qi: e.dma_start(out=ya_ap[qi * 128:(qi + 1) * 128, :], in_=yq_[:]), reads=[t_yq], dma=True)
        secs = []
        if 'ssd' in PARTS:
            secs.append(sec_ssd)
        if 'attn' in PARTS:
            for qb in range(4):
                secs.append(lambda qb=qb: sec_attn((qb,), qb))
        interleave(C, secs, banks=[[0, 1, 2, 3], [4], [5], [6], [7]] if len(secs) == 5 else [[0, 1], [2, 3], [4, 5], [6, 7]])
    return C.finish()


def t5_bucket_np(dist):
    n = np.maximum(dist, 0)
    max_exact = 16
    ratio = np.log(np.maximum(n, 1).astype(np.float32) / max_exact) / np.float32(np.log(128 / max_exact))
    large = max_exact + (ratio * (32 - max_exact)).astype(np.int32)
    large = np.minimum(large, 31)
    return np.where(n < max_exact, n, large)


def mix_consts(rel_bias, head):
    kk = np.arange(128)[:, None]
    qq = np.arange(128)[None, :]
    cst = np.zeros((128, 5 * 128 + 1024), np.float32)
    cst[:, 0:128] = np.eye(128, dtype=np.float32)
    cst[:, 128:256] = 1.0
    cst[:, 256:384] = rel_bias[t5_bucket_np(qq - kk), head]
    cst[:, 384:512] = rel_bias[t5_bucket_np(128 + qq - kk), head]
    cst[:, 512:640] = np.where(qq >= kk, 0.0, NEG)
    utri = (kk <= qq).astype(np.float32)
    cst[:, 640:768] = utri
    cst[:, 768:896] = 1.0
    cst[:, 896:1024] = 0.0
    cst[:, 1024:1152] = utri
    for h in range(4):
        cst[h, 1152 + h * 128:1152 + (h + 1) * 128] = 1.0
    return cst


def run_mixB(hT, l, inp, NT=16):
    w = inp["w_mix_in"][l]
    in_maps = []
    for c in range(NCORES):
        g = c // 2
        cols = np.concatenate([
            np.arange(c * 128, (c + 1) * 128), 1024 + np.arange(c * 128, (c + 1) * 128),
            3072 + np.arange(c * 256, (c + 1) * 256), 5120 + np.arange(c * 256, (c + 1) * 256),
            5120 + 2048 + np.arange(g * 128, (g + 1) * 128), 5120 + 2560 + np.arange(g * 128, (g + 1) * 128),
            8224 + np.arange(c * 128, (c + 1) * 128), 9248 + np.arange(c * 128, (c + 1) * 128),
            10272 + np.arange(c * 128, (c + 1) * 128), 2048 + np.arange(c * 128, (c + 1) * 128),
            8192 + np.arange(c * 4, (c + 1) * 4)])
        wB = np.ascontiguousarray(w[:, cols].reshape(KC, 128, NCOLB).transpose(1, 0, 2))
        sp = np.zeros((128, 24), np.float32)
        sp[:, 0] = inp["qk_norm"][l, 0]
        sp[:, 1] = inp["qk_norm"][l, 1]
        sp[:, 2] = inp["rel_bias"][31, c]
        wc = inp["w_ssd_conv"][l]
        bc = inp["b_ssd_conv"][l]
        chB = 2048 + g * 128 + np.arange(128)
        chC = 2560 + g * 128 + np.arange(128)
        sp[:, 3:7] = wc[:, chB].T
        sp[:, 7] = bc[chB]
        sp[:, 8:12] = wc[:, chC].T
        sp[:, 12] = bc[chC]
        sp[:, 13:16] = inp["w_sc_conv"][l][:, c * 128:(c + 1) * 128].T
        sp[:, 16:20] = inp["ssd_dt_bias"][l][None, 4 * c:4 * c + 4]
        sp[:, 20:24] = inp["ssd_a_log"][l][None, 4 * c:4 * c + 4]
        sp64 = np.zeros((64, 24), np.float32)
        for h in range(4):
            ch = c * 256 + h * 64 + np.arange(64)
            sp64[:, h * 6:h * 6 + 4] = wc[:, ch].T
            sp64[:, h * 6 + 4] = bc[ch]
            sp64[:, h * 6 + 5] = inp["ssd_d"][l][4 * c + h]
        in_maps.append({"hT": hT, "wB": wB, "sp": sp, "sp64": sp64, "cst": mix_consts(inp["rel_bias"], c)})
    r = run(prog("mixB", build_mixB, NT), in_maps)
    ntok = NT * 512
    YT = np.zeros((4096, ntok), np.float32)
    for c in range(NCORES):
        YT[c * 128:(c + 1) * 128] = r[c]["ya"].T
        YT[1024 + c * 256:1024 + (c + 1) * 256] = r[c]["yg"]
        YT[3072 + c * 128:3072 + (c + 1) * 128] = r[c]["yc"]
    return YT


NYC = 32
BR_CH = ((0, 8), (8, 24), (24, 32))


def build_mixC():
    C = Ctx()
    S = C.S
    x_ap = C.din("xT", [D, TOK])
    h_ap = C.din("hT", [D, TOK])
    y_ap = C.din("YT", [4096, TOK])
    wg_ap = C.din("wg", [KC, 128, KC, 384])
    wbr_ap = C.din("wbr", [KC, 128, NYC, 128])
    wo_ap = C.din("wo", [KC, 128, KC, 128])
    par_ap = C.din("par", [128, 2 * KC])
    ones_ap = C.din("ones", [128, 128])
    xo_ap = C.dout("xo", [D, TOK])
    C.init_psum()
    ones, t_ones = load_consts(C, ones_ap)
    par, t_par = C.sb([128, 2 * KC], F32, "par")
    S.add("sp", lambda e: e.dma_start(out=par[:], in_=par_ap), writes=[t_par], dma=True)
    xT, _ = C.sb([128, KC, 512], F32, "xT")
    t_x = [T("x%d" % k) for k in range(KC)]
    hb, _ = C.sb([128, KC, 512], BF16, "hb")
    t_h = [T("h%d" % k) for k in range(KC)]
    Yb, _ = C.sb([128, NYC, 512], BF16, "Yb")
    t_y = [T("y%d" % k) for k in range(NYC)]
    stg = [C.sb([128, 512], F32, "stg") for _ in range(4)]
    sq = [C.sb([128, 512], F32, "sq") for _ in range(2)]
    rs, t_rs = C.sb([128, 512], F32, "rs")
    tmpn = [C.sb([128, 512], F32, "tmpn") for _ in range(2)]
    mg, t_mg = C.sb([128, 512], F32, "mg")
    sig = [C.sb([128, 512], F32, "sig") for _ in range(2)]
    tmpm, t_tmpm = C.sb([128, 512], F32, "tmpm")
    mT, _ = C.sb([128, KC, 512], BF16, "mT")
    t_m = [T("m%d" % k) for k in range(KC)]
    wg = [C.sb([128, KC, 384], BF16, "wg") for _ in range(2)]
    wbr = [C.sb([128, NYC, 128], BF16, "wbr") for _ in range(2)]
    wo = [C.sb([128, KC, 128], BF16, "wo") for _ in range(2)]
    nsig = 0
    nw = 0
    nwo = 0
    for tt in range(TOK // 512):
        sl = slice(tt * 512, (tt + 1) * 512)
        for kc in range(KC):
            S.add("sp", lambda e, kc=kc, sl=sl: e.dma_start(out=xT[:, kc, :], in_=x_ap[kc * 128:(kc + 1) * 128, sl]),
                  writes=[t_x[kc]], dma=True)
            S.add("pool", lambda e, kc=kc, sl=sl: e.dma_start(out=hb[:, kc, :], in_=h_ap[kc * 128:(kc + 1) * 128, sl]),
                  writes=[t_h[kc]], dma=True)
        for ch in list(range(0, 8)) + list(range(24, 32)):
            S.add("pool", lambda e, ch=ch, sl=sl: e.dma_start(out=Yb[:, ch, :], in_=y_ap[ch * 128:(ch + 1) * 128, sl]),
                  writes=[t_y[ch]], dma=True)
        for gq in range(4):
            ps, t_ps = C.psum()
            for c4 in range(4):
                ch = 8 + gq * 4 + c4
                st_, t_st = stg[c4]
                S.add("sp", lambda e, st_=st_, ch=ch, sl=sl: e.dma_start(out=st_[:], in_=y_ap[ch * 128:(ch + 1) * 128, sl]),
                      writes=[t_st], dma=True)
                sq_, t_sq = sq[c4 % 2]
                S.add("act", lambda e, sq_=sq_, st_=st_: e.activation(out=sq_[:], in_=st_[:], func=AF.Square), reads=[t_st], writes=[t_sq])
                S.add("pe", lambda e, ps=ps, sq_=sq_, c4=c4: e.matmul(ps[:, :], ones[:], sq_[:], start=(c4 == 0), stop=(c4 == 3)),
                      reads=[t_sq, t_ones], writes=[t_ps])
            S.add("act", lambda e, ps=ps: e.activation(out=rs[:], in_=ps[:, :], func=AF.Sqrt, scale=1.0 / 512, bias=EPS),
                  reads=[t_ps], writes=[t_rs])
            S.add("dve", lambda e: e.reciprocal(out=rs[:], in_=rs[:]), reads=[t_rs], writes=[t_rs])
            for c4 in range(4):
                ch = 8 + gq * 4 + c4
                st_, t_st = stg[c4]
                S.add("dve", lambda e, st_=st_, ch=ch: e.scalar_tensor_tensor(
                    out=Yb[:, ch, :], in0=st_[:], scalar=par[:, ch - 8:ch - 7], in1=rs[:], op0=ALU.mult, op1=ALU.mult),
                    reads=[t_st, t_rs, t_par], writes=[t_y[ch]])
        for m in range(KC):
            wg_, t_wg = wg[nw % 2]
            wbr_, t_wbr = wbr[nw % 2]
            nw += 1
            S.add("pool", lambda e, wg_=wg_, m=m: e.dma_start(out=wg_[:], in_=wg_ap[m]), writes=[t_wg], dma=True)
            S.add("pool", lambda e, wbr_=wbr_, m=m: e.dma_start(out=wbr_[:], in_=wbr_ap[m]), writes=[t_wbr], dma=True)
            for b in range(3):
                pg, t_pg = C.psum()
                mm_group(S, pg[:, :], [(wg_[:, kc, b * 128:(b + 1) * 128], hb[:, kc, :]) for kc in range(KC)], t_h + [t_wg], [t_pg])
                sg_, t_sg = sig[nsig % 2]
                nsig += 1
                S.add("act", lambda e, sg_=sg_, pg=pg: e.activation(out=sg_[:], in_=pg[:, :], func=AF.Sigmoid), reads=[t_pg], writes=[t_sg])
                pb, t_pb = C.psum()
                c0, c1 = BR_CH[b]
                mm_group(S, pb[:, :], [(wbr_[:, ch, :], Yb[:, ch, :]) for ch in range(c0, c1)], t_y[c0:c1] + [t_wbr], [t_pb])
                if b == 0:
                    S.add("dve", lambda e, sg_=sg_, pb=pb: e.tensor_tensor(out=mg[:], in0=sg_[:], in1=pb[:, :], op=ALU.mult),
                          reads=[t_sg, t_pb], writes=[t_mg])
                else:
                    S.add("dve", lambda e, sg_=sg_, pb=pb: e.tensor_tensor(out=tmpm[:], in0=sg_[:], in1=pb[:, :], op=ALU.mult),
                          reads=[t_sg, t_pb], writes=[t_tmpm])
                    if b == 1:
                        S.add("dve", lambda e: e.tensor_tensor(out=mg[:], in0=mg[:], in1=tmpm[:], op=ALU.add),
                              reads=[t_mg, t_tmpm], writes=[t_mg])
                    else:
                        S.add("dve", lambda e, m=m: e.tensor_tensor(out=mT[:, m, :], in0=mg[:], in1=tmpm[:], op=ALU.add),
                              reads=[t_mg, t_tmpm], writes=[t_m[m]])
        for m in range(KC):
            wo_, t_wo = wo[nwo % 2]
            nwo += 1
            S.add("pool", lambda e, wo_=wo_, m=m: e.dma_start(out=wo_[:], in_=wo_ap[m]), writes=[t_wo], dma=True)
            po, t_po = C.psum()
            mm_group(S, po[:, :], [(wo_[:, kc, :], mT[:, kc, :]) for kc in range(KC)], t_m + [t_wo], [t_po])
            S.add("dve", lambda e, po=po, m=m: e.scalar_tensor_tensor(
                out=xT[:, m, :], in0=po[:, :], scalar=par[:, KC + m:KC + m + 1], in1=xT[:, m, :], op0=ALU.mult, op1=ALU.add),
                reads=[t_po, t_par, t_x[m]], writes=[t_x[m]])
            S.add("sp", lambda e, m=m, sl=sl: e.dma_start(out=xo_ap[m * 128:(m + 1) * 128, sl], in_=xT[:, m, :]),
                  reads=[t_x[m]], dma=True)
    return C.finish()


def run_mixC(xT, hT, YT, l, inp, gate1):
    w = inp["w_mix_in"][l]
    G0 = 11296
    wg = np.stack([w[:, G0 + b * 2048:G0 + (b + 1) * 2048].reshape(KC, 128, KC, 128) for b in range(3)], axis=0)
    wg_t = np.ascontiguousarray(wg.transpose(3, 2, 1, 0, 4).reshape(KC, 128, KC, 384))
    wbr = np.concatenate([inp["w_br_attn"][l], inp["w_br_ssd"][l], inp["w_br_conv"][l]], axis=0)
    wbr_t = np.ascontiguousarray(wbr.reshape(NYC, 128, KC, 128).transpose(2, 1, 0, 3))
    wo_t = np.ascontiguousarray(inp["w_mix_out"][l].reshape(KC, 128, KC, 128).transpose(2, 1, 0, 3))
    par = np.concatenate([cols16(inp["ssd_norm"][l]), cols16(gate1)], axis=1)
    ones = np.ones((128, 128), np.float32)
    in_maps = []
    for i in range(NCORES):
        ts = slice(i * TOK, (i + 1) * TOK)
        in_maps.append({"xT": np.ascontiguousarray(xT[:, ts]), "hT": np.ascontiguousarray(hT[:, ts]),
                        "YT": np.ascontiguousarray(YT[:, ts]), "wg": wg_t, "wbr": wbr_t, "wo": wo_t, "par": par, "ones": ones})
    r = run(prog("mixC", build_mixC), in_maps)
    return np.concatenate([r[i]["xo"] for i in range(NCORES)], axis=1)


def kernel(x, c, w_ada, b_ada, norm_gain, w_ffn_in, w_ffn_out, w_mix_in, qk_norm, rel_bias,
           w_ssd_conv, b_ssd_conv, ssd_dt_bias, ssd_a_log, ssd_d, ssd_norm, w_sc_conv,
           w_br_attn, w_br_ssd, w_br_conv, w_mix_out):
    inp = dict(w_mix_in=w_mix_in, qk_norm=qk_norm, rel_bias=rel_bias, w_ssd_conv=w_ssd_conv, b_ssd_conv=b_ssd_conv,
               ssd_dt_bias=ssd_dt_bias, ssd_a_log=ssd_a_log, ssd_d=ssd_d, ssd_norm=ssd_norm, w_sc_conv=w_sc_conv,
               w_br_attn=w_br_attn, w_br_ssd=w_br_ssd, w_br_conv=w_br_conv, w_mix_out=w_mix_out)
    inp = {k: np.asarray(v, np.float32) for k, v in inp.items()}
    w_ada = np.asarray(w_ada, np.float32)
    b_ada = np.asarray(b_ada, np.float32)
    norm_gain = np.asarray(norm_gain, np.float32)
    w_ffn_in = np.asarray(w_ffn_in, np.float32)
    w_ffn_out = np.asarray(w_ffn_out, np.float32)
    ada = compute_ada(np.asarray(c, np.float32), w_ada, b_ada)
    xT = np.ascontiguousarray(np.asarray(x, np.float32)[0].T)
    for l in range(2):
        par = np.concatenate([par_block(norm_gain[l, 0], ada[l, 0]), par_block(norm_gain[l, 1], ada[l, 1])], axis=1)
        xT, hT = run_ffn(xT, par, w_ffn_in[l, 0], w_ffn_out[l, 0], True)
        YT = run_mixB(hT, l, inp, NT=16)
        xT = run_mixC(xT, hT, YT, l, inp, ada[l, 1, 2])
        par = par_block(norm_gain[l, 2], ada[l, 2])
        xT, _ = run_ffn(xT, par, w_ffn_in[l, 1], w_ffn_out[l, 1], False)
    return np.ascontiguousarray(xT.T)[None].astype(np.float32)
```

```python
import contextlib
import threading
import numpy as np
import concourse.bass as bass
import concourse.mybir as mybir
from concourse.bass_utils import run_bass_kernel_spmd

F32 = mybir.dt.float32
BF16 = mybir.dt.bfloat16
AF = mybir.ActivationFunctionType
ALU = mybir.AluOpType
AX = mybir.AxisListType

NCORES = 8
D = 2048
KC = 16
SEQ = 8192
TOK = SEQ // NCORES
FFH = 5632
NJ = FFH // 128
NQ = 4
JQ = NJ // NQ
EPS = 1e-6


class T:
    __slots__ = ("name", "last_w", "readers", "excl")

    def __init__(self, name="", excl=False):
        self.name = name
        self.last_w = None
        self.readers = []
        self.excl = excl


class Op:
    __slots__ = ("eng", "fn", "deps", "is_dma", "sig", "sigval", "dsem", "dval", "dprev")


ENGS = ("pe", "act", "dve", "pool", "sp")
N_DMA_SEMS = {"pe": 1, "act": 1, "dve": 1, "pool": 16, "sp": 8}


class Sched:
    def __init__(self, nc):
        self.nc = nc
        self.ops = []
        self.dma_rr = {e: 0 for e in ENGS}
        self.dma_cnt = {}
        self.hook = None

    def add(self, eng, fn, reads=(), writes=(), dma=False):
        op = Op()
        op.eng = eng
        op.fn = fn
        op.is_dma = dma
        op.sig = False
        op.sigval = None
        deps = []
        excl_r = [t for t in reads if t.excl and t not in writes]
        reads = [t for t in reads if not t.excl]
        writes = list(writes) + excl_r
        for t in reads:
            if t.last_w is not None:
                deps.append(t.last_w)
        for t in writes:
            if t.last_w is not None:
                deps.append(t.last_w)
            deps.extend(t.readers)
        for t in reads:
            t.readers.append(op)
        for t in writes:
            t.last_w = op
            t.readers = []
        op.deps = [d for d in dict.fromkeys(deps) if d is not op]
        if dma:
            k = self.dma_rr[eng]
            self.dma_rr[eng] = (k + 1) % N_DMA_SEMS[eng]
            key = (eng, k)
            c = self.dma_cnt.get(key, 0) + 1
            self.dma_cnt[key] = c
            op.dsem = key
            op.dval = 16 * c
            op.dprev = 16 * (c - 1)
        self.ops.append(op)
        if self.hook is not None:
            self.hook()
        return op

    def emit(self):
        nc = self.nc
        ops = self.ops
        for op in ops:
            for d in op.deps:
                if d.is_dma:
                    continue
                if d.eng == op.eng and d.eng == "pe" and not op.is_dma:
                    continue
                d.sig = True
        cnt = {e: 0 for e in ENGS}
        for op in ops:
            if not op.is_dma and op.sig:
                cnt[op.eng] += 1
                op.sigval = cnt[op.eng]
        with contextlib.ExitStack() as st:
            esem = {e: st.enter_context(nc.semaphore("s_" + e)) for e in ENGS}
            dsem = {}
            for key in self.dma_cnt:
                dsem[key] = st.enter_context(nc.semaphore("d_%s%d" % key))
            block = st.enter_context(nc.Block())
            by_eng = {e: [o for o in ops if o.eng == e] for e in ENGS}

            def run(eng_name, eng):
                known = {}

                def wait(sem_key, sem, val):
                    if known.get(sem_key, 0) >= val:
                        return
                    known[sem_key] = val
                    eng.wait_ge(sem, val)

                for op in by_eng[eng_name]:
                    for d in op.deps:
                        if d.is_dma:
                            wait(d.dsem, dsem[d.dsem], d.dval)
                        elif d.sigval is not None:
                            wait(d.eng, esem[d.eng], d.sigval)
                    if op.is_dma:
                        if op.dprev > 0:
                            wait(op.dsem, dsem[op.dsem], op.dprev)
                        op.fn(eng).then_inc(dsem[op.dsem], 16)
                    else:
                        ins = op.fn(eng)
                        if op.sig:
                            ins.then_inc(esem[op.eng], 1)
                if eng_name == "sp":
                    for key, c in self.dma_cnt.items():
                        wait(key, dsem[key], 16 * c)

            block.tensor(lambda e: run("pe", e))
            block.scalar(lambda e: run("act", e))
            block.vector(lambda e: run("dve", e))
            block.gpsimd(lambda e: run("pool", e))
            block.sync(lambda e: run("sp", e))


class Ctx:
    def __init__(self):
        self.nc = bass.Bass("TRN2", target_bir_lowering=False)
        self.S = Sched(self.nc)
        self.st = contextlib.ExitStack()
        self.ps = []
        self.ps_i = 0
        self.n = 0
        self.tls = threading.local()
        self.pool_i = {}
        self.pool_banks = [[0, 1, 2, 3], [4, 5, 6, 7]]

    def din(self, name, shape, dt=F32):
        return self.nc.dram_tensor(name, list(shape), dt, kind="ExternalInput").ap()

    def dout(self, name, shape, dt=F32):
        return self.nc.dram_tensor(name, list(shape), dt, kind="ExternalOutput").ap()

    def sb(self, shape, dt, name=None):
        self.n += 1
        t = self.st.enter_context(self.nc.sbuf_tensor("%s_%d" % (name or "t", self.n), list(shape), dt))
        return t, T(name or "t")

    def init_psum(self):
        for i in range(8):
            t = self.st.enter_context(self.nc.psum_tensor("ps%d" % i, [128, 512], F32))
            self.ps.append((t, T("ps%d" % i, excl=True)))

    def psum(self):
        pool = getattr(self.tls, "pool", 0)
        if pool == 0:
            r = self.ps[self.ps_i]
            self.ps_i = (self.ps_i + 1) % 8
            return r
        banks = self.pool_banks[pool - 1]
        k = self.pool_i.get(pool, 0)
        self.pool_i[pool] = (k + 1) % len(banks)
        return self.ps[banks[k]]

    def finish(self):
        self.S.emit()
        self.st.close()
        return self.nc


def interleave(C, fns, banks=None):
    S = C.S
    n = len(fns)
    if n == 0:
        return
    if n == 1:
        fns[0]()
        return
    if banks is not None:
        C.pool_banks = banks
        C.pool_i = {}
    sems = [threading.Semaphore(0) for _ in range(n)]
    alive = [True] * n
    idx = {}
    err = []
    done = threading.Event()

    def nxt(i):
        for k in range(1, n + 1):
            j = (i + k) % n
            if alive[j]:
                return j
        return None

    def hook():
        i = idx[threading.get_ident()]
        j = nxt(i)
        if j is not None and j != i:
            sems[j].release()
            sems[i].acquire()

    def worker(i):
        sems[i].acquire()
        idx[threading.get_ident()] = i
        try:
            C.tls.pool = i + 1
            fns[i]()
        except BaseException as ex:
            err.append(ex)
        finally:
            alive[i] = False
            j = nxt(i)
            if j is not None:
                sems[j].release()
            else:
                done.set()

    ths = [threading.Thread(target=worker, args=(i,)) for i in range(n)]
    S.hook = hook
    for t in ths:
        t.start()
    sems[0].release()
    done.wait()
    for t in ths:
        t.join()
    S.hook = None
    if err:
        raise err[0]


def mm_group(S, out_ap, pairs, reads, writes):
    n = len(pairs)

    def fn(e):
        ins = None
        for i, (l, r) in enumerate(pairs):
            ins = e.matmul(out_ap, l, r, start=(i == 0), stop=(i == n - 1))
        return ins

    return S.add("pe", fn, reads=reads, writes=writes)


def load_consts(C, ones_ap):
    ones, t_ones = C.sb([128, 128], F32, "ones")
    C.S.add("sp", lambda e: e.dma_start(out=ones[:], in_=ones_ap), writes=[t_ones], dma=True)
    return ones, t_ones


def modulate(C, xT, t_x, hT, t_h, a_col, shift_col, t_par, ones, t_ones, ntok, out_dram=None):
    S = C.S
    sq = [C.sb([128, 512], F32, "sq") for _ in range(2)]
    rs, t_rs = C.sb([128, 512], F32, "rstd")
    tmp = [C.sb([128, 512], F32, "modtmp") for _ in range(2)]
    tmp32 = [C.sb([128, 512], F32, "modtmp32") for _ in range(2)] if out_dram is not None else None
    for tt in range(ntok // 512):
        sl = slice(tt * 512, (tt + 1) * 512)
        ps, t_ps = C.psum()
        for kc in range(KC):
            sqt, t_sq = sq[kc % 2]
            S.add("act", lambda e, sqt=sqt, kc=kc, sl=sl: e.activation(out=sqt[:], in_=xT[:, kc, sl], func=AF.Square),
                  reads=[t_x], writes=[t_sq])
            S.add("pe", lambda e, sqt=sqt, kc=kc, ps=ps: e.matmul(ps[:, :], ones[:], sqt[:], start=(kc == 0), stop=(kc == KC - 1)),
                  reads=[t_sq, t_ones], writes=[t_ps])
        S.add("act", lambda e, ps=ps: e.activation(out=rs[:], in_=ps[:, :], func=AF.Sqrt, scale=1.0 / D, bias=EPS),
              reads=[t_ps], writes=[t_rs])
        S.add("dve", lambda e: e.reciprocal(out=rs[:], in_=rs[:]), reads=[t_rs], writes=[t_rs])
        for kc in range(KC):
            tm, t_tm = tmp[kc % 2]
            S.add("dve", lambda e, tm=tm, kc=kc, sl=sl: e.scalar_tensor_tensor(
                out=tm[:], in0=xT[:, kc, sl], scalar=a_col[:, kc:kc + 1], in1=rs[:], op0=ALU.mult, op1=ALU.mult),
                reads=[t_x, t_rs] + t_par, writes=[t_tm])
            S.add("act", lambda e, tm=tm, kc=kc, sl=sl: e.activation(
                out=hT[:, kc, sl], in_=tm[:], func=AF.Identity, bias=shift_col[:, kc:kc + 1], scale=1.0),
                reads=[t_tm] + t_par, writes=[t_h])
            if out_dram is not None:
                t32, t_t32 = tmp32[kc % 2]
                S.add("act", lambda e, tm=tm, t32=t32, kc=kc: e.activation(
                    out=t32[:], in_=tm[:], func=AF.Identity, bias=shift_col[:, kc:kc + 1], scale=1.0),
                    reads=[t_tm] + t_par, writes=[t_t32])
                S.add("sp", lambda e, t32=t32, kc=kc, sl=sl: e.dma_start(out=out_dram[kc * 128:(kc + 1) * 128, sl], in_=t32[:]),
                      reads=[t_t32], dma=True)


def prep_mod_params(C, par_ap, n_sets):
    S = C.S
    par, t_par = C.sb([128, n_sets * 4 * KC], F32, "par")
    acol, t_a = C.sb([128, n_sets * KC], F32, "acol")
    S.add("sp", lambda e: e.dma_start(out=par[:], in_=par_ap), writes=[t_par], dma=True)
    out = []
    for s in range(n_sets):
        b = s * 4 * KC
        S.add("dve", lambda e, b=b, s=s: e.scalar_tensor_tensor(
            out=acol[:, s * KC:(s + 1) * KC], in0=par[:, b + 2 * KC:b + 3 * KC], scalar=1.0, in1=par[:, b:b + KC],
            op0=ALU.add, op1=ALU.mult), reads=[t_par], writes=[t_a])
        out.append((acol[:, s * KC:(s + 1) * KC], par[:, b + KC:b + 2 * KC], par[:, b + 3 * KC:b + 4 * KC]))
    return out, [t_par, t_a]


ADA_COLS = 2 * 18432 // NCORES
ADA_NB = ADA_COLS // 512


def build_ada():
    C = Ctx()
    S = C.S
    c_ap = C.din("c", [128, KC])
    w_ap = C.din("w", [ADA_NB, 128, KC, 512])
    b_ap = C.din("b", [1, ADA_COLS])
    o_ap = C.dout("o", [1, ADA_COLS])
    C.init_psum()
    ct, t_c = C.sb([128, KC], F32, "c")
    bt, t_b = C.sb([1, ADA_COLS], F32, "b")
    ot, t_o = C.sb([1, ADA_COLS], F32, "o")
    wt = [C.sb([128, KC, 512], F32, "w") for _ in range(2)]
    S.add("sp", lambda e: e.dma_start(out=ct[:], in_=c_ap), writes=[t_c], dma=True)
    S.add("sp", lambda e: e.dma_start(out=bt[:], in_=b_ap), writes=[t_b], dma=True)
    S.add("act", lambda e: e.activation(out=ct[:], in_=ct[:], func=AF.Silu), reads=[t_c], writes=[t_c])
    for nb in range(ADA_NB):
        w, t_w = wt[nb % 2]
        for kc in range(KC):
            S.add("sp", lambda e, w=w, nb=nb, kc=kc: e.dma_start(out=w[:, kc, :], in_=w_ap[nb, :, kc, :]), writes=[t_w], dma=True)
        ps, t_ps = C.psum()
        mm_group(S, ps[0:1, :], [(ct[:, kc:kc + 1], w[:, kc, :]) for kc in range(KC)], [t_c, t_w], [t_ps])
        S.add("dve", lambda e, ps=ps, nb=nb: e.tensor_tensor(
            out=ot[:, nb * 512:(nb + 1) * 512], in0=ps[0:1, :], in1=bt[:, nb * 512:(nb + 1) * 512], op=ALU.add),
            reads=[t_ps, t_b], writes=[t_o])
    S.add("sp", lambda e: e.dma_start(out=o_ap, in_=ot[:]), reads=[t_o], dma=True)
    return C.finish()


def build_ffn(emit_h_next, dbg=None):
    C = Ctx()
    S = C.S
    nsets = 2 if emit_h_next else 1
    x_ap = C.din("xT", [D, TOK])
    par_ap = C.din("par", [128, nsets * 4 * KC])
    win_ap = C.din("win", [NJ, 128, KC, 256])
    wout_ap = C.din("wout", [NQ, KC, 128, JQ, 128])
    ones_ap = C.din("ones", [128, 128])
    xo_ap = C.dout("xo", [D, TOK])
    ho_ap = C.dout("ho", [D, TOK]) if emit_h_next else None
    C.init_psum()
    ones, t_ones = load_consts(C, ones_ap)
    xT, t_x = C.sb([128, KC, TOK], F32, "xT")
    hT, t_h = C.sb([128, KC, TOK], BF16, "hT")
    actT, t_act = C.sb([128, JQ, TOK], BF16, "actT")
    wi = [C.sb([128, KC, 256], BF16, "wi") for _ in range(2)]
    wo = [C.sb([128, JQ, 128], BF16, "wo") for _ in range(2)]
    sg = [C.sb([128, 512], F32, "sg") for _ in range(2)]
    g05, t_g = C.sb([128, KC], F32, "g05")
    for kc in range(KC):
        S.add("sp", lambda e, kc=kc: e.dma_start(out=xT[:, kc, :], in_=x_ap[kc * 128:(kc + 1) * 128, :]),
              writes=[t_x], dma=True)
    sets, t_par = prep_mod_params(C, par_ap, nsets)
    a_col, shift_col, gate_col = sets[0]
    S.add("dve", lambda e: e.tensor_scalar(out=g05[:], in0=gate_col, scalar1=0.5, scalar2=None, op0=ALU.mult),
          reads=t_par, writes=[t_g])
    modulate(C, xT, t_x, hT, t_h, a_col, shift_col, t_par, ones, t_ones, TOK, out_dram=(ho_ap if dbg == "mod" else None))
    if dbg == "mod":
        for kc in range(KC):
            S.add("sp", lambda e, kc=kc: e.dma_start(out=xo_ap[kc * 128:(kc + 1) * 128, :], in_=xT[:, kc, :]),
                  reads=[t_x], dma=True)
        return C.finish()
    nwi = 0
    nwo = 0
    nsg = 0
    for qh in range(NQ):
        for jj in range(JQ):
            j = qh * JQ + jj
            w, t_w = wi[nwi % 2]
            nwi += 1
            S.add("pool", lambda e, w=w, j=j: e.dma_start(out=w[:], in_=win_ap[j]), writes=[t_w], dma=True)
            for tt in range(TOK // 512):
                sl = slice(tt * 512, (tt + 1) * 512)
                pg, t_pg = C.psum()
                pu, t_pu = C.psum()
                mm_group(S, pg[:, :], [(w[:, kc, 0:128], hT[:, kc, sl]) for kc in range(KC)], [t_w, t_h], [t_pg])
                mm_group(S, pu[:, :], [(w[:, kc, 128:256], hT[:, kc, sl]) for kc in range(KC)], [t_w, t_h], [t_pu])
                s_, t_s = sg[nsg % 2]
                nsg += 1
                S.add("act", lambda e, s_=s_, pg=pg: e.activation(out=s_[:], in_=pg[:, :], func=AF.Silu),
                      reads=[t_pg], writes=[t_s])
                S.add("dve", lambda e, s_=s_, pu=pu, jj=jj, sl=sl: e.tensor_tensor(
                    out=actT[:, jj, sl], in0=s_[:], in1=pu[:, :], op=ALU.mult),
                    reads=[t_s, t_pu], writes=[t_act])
        for m in range(KC):
            w, t_w = wo[nwo % 2]
            nwo += 1
            S.add("pool", lambda e, w=w, qh=qh, m=m: e.dma_start(out=w[:], in_=wout_ap[qh, m]), writes=[t_w], dma=True)
            for tt in range(TOK // 512):
                sl = slice(tt * 512, (tt + 1) * 512)
                po, t_po = C.psum()
                mm_group(S, po[:, :], [(w[:, jj, :], actT[:, jj, sl]) for jj in range(JQ)], [t_w, t_act], [t_po])
                S.add("dve", lambda e, po=po, m=m, sl=sl: e.scalar_tensor_tensor(
                    out=xT[:, m, sl], in0=po[:, :], scalar=g05[:, m:m + 1], in1=xT[:, m, sl], op0=ALU.mult, op1=ALU.add),
                    reads=[t_po, t_g, t_x], writes=[t_x])
    for kc in range(KC):
        S.add("sp", lambda e, kc=kc: e.dma_start(out=xo_ap[kc * 128:(kc + 1) * 128, :], in_=xT[:, kc, :]),
              reads=[t_x], dma=True)
    if emit_h_next:
        a2, sh2, _ = sets[1]
        modulate(C, xT, t_x, hT, t_h, a2, sh2, t_par, ones, t_ones, TOK, out_dram=ho_ap)
    return C.finish()


def cols16(v):
    return np.ascontiguousarray(np.asarray(v, np.float32).reshape(KC, 128).T)


def tile_win(w_in):
    w = w_in.reshape(KC, 128, 2, NJ, 128)
    return np.ascontiguousarray(w.transpose(3, 1, 0, 2, 4).reshape(NJ, 128, KC, 256))


def tile_wout(w_out):
    w = w_out.reshape(NQ, JQ, 128, KC, 128)
    return np.ascontiguousarray(w.transpose(0, 3, 2, 1, 4))


_PROG = {}


def prog(name, fn, *a):
    key = (name,) + a
    if key not in _PROG:
        _PROG[key] = fn(*a)
    return _PROG[key]


def run(nc, in_maps):
    res = run_bass_kernel_spmd(nc, in_maps, core_ids=list(range(NCORES)))
    return res.results


def compute_ada(c, w_ada, b_ada):
    cc = cols16(c.reshape(-1))
    wa = np.concatenate([w_ada[0], w_ada[1]], axis=1)
    ba = np.concatenate([b_ada[0], b_ada[1]], axis=0)
    in_maps = []
    for i in range(NCORES):
        ws = wa[:, i * ADA_COLS:(i + 1) * ADA_COLS].reshape(KC, 128, ADA_NB, 512)
        in_maps.append({
            "c": cc,
            "w": np.ascontiguousarray(ws.transpose(2, 1, 0, 3)),
            "b": np.ascontiguousarray(ba[i * ADA_COLS:(i + 1) * ADA_COLS].reshape(1, -1)),
        })
    r = run(prog("ada", build_ada), in_maps)
    ada = np.concatenate([r[i]["o"].reshape(-1) for i in range(NCORES)])
    return ada.reshape(2, 3, 3, D)


def par_block(norm_gain_li, ada_li):
    return np.concatenate([cols16(norm_gain_li), cols16(ada_li[0]), cols16(ada_li[1]), cols16(ada_li[2])], axis=1)


def run_ffn(xT, par, w_in, w_out, emit_h_next, dbg=None):
    win_t = tile_win(w_in)
    wout_t = tile_wout(w_out)
    ones = np.ones((128, 128), np.float32)
    in_maps = []
    for i in range(NCORES):
        in_maps.append({"xT": np.ascontiguousarray(xT[:, i * TOK:(i + 1) * TOK]), "par": par,
                        "win": win_t, "wout": wout_t, "ones": ones})
    r = run(prog("ffn", build_ffn, emit_h_next, dbg), in_maps)
    xo = np.concatenate([r[i]["xo"] for i in range(NCORES)], axis=1)
    ho = np.concatenate([r[i]["ho"] for i in range(NCORES)], axis=1) if emit_h_next else None
    return xo, ho


NCOLB = 1540
C_Q, C_K, C_Z, C_XS, C_B, C_C, C_CB, C_CC, C_CX, C_V = 0, 128, 256, 512, 768, 896, 1024, 1152, 1280, 1408
SCALE = 128 ** -0.5
NEG = -1e30


PARTS = ('conv', 'ssd', 'attn')
STOP = 0


def build_mixB(NT):
    C = Ctx()
    S = C.S
    ntok = NT * 512
    h_ap = C.din("hT", [D, ntok])
    w_ap = C.din("wB", [128, KC, NCOLB])
    sp_ap = C.din("sp", [128, 24])
    sp64_ap = C.din("sp64", [64, 24])
    cst_ap = C.din("cst", [128, 5 * 128 + 512 + 512])
    ya_ap = C.dout("ya", [ntok, 128])
    yg_ap = C.dout("yg", [256, ntok])
    yc_ap = C.dout("yc", [128, ntok])
    C.init_psum()
    cst, t_cst = C.sb([128, 5 * 128 + 1024], F32, "cst")
    S.add("sp", lambda e: e.dma_start(out=cst[:], in_=cst_ap), writes=[t_cst], dma=True)
    ident = cst[:, 0:128]
    ones = cst[:, 128:256]
    TA = cst[:, 256:384]
    TB = cst[:, 384:512]
    NEGM = cst[:, 512:640]
    U0 = cst[:, 640:896]
    UTRI = cst[:, 640:768]
    U1 = cst[:, 896:1152]
    SELH = cst[0:4, 1152:1664]
    spt, t_sp = C.sb([128, 24], F32, "sp")
    sp64, t_sp64 = C.sb([64, 24], F32, "sp64")
    S.add("sp", lambda e: e.dma_start(out=spt[:], in_=sp_ap), writes=[t_sp], dma=True)
    S.add("sp", lambda e: e.dma_start(out=sp64[:], in_=sp64_ap), writes=[t_sp64], dma=True)
    S.add("dve", lambda e: e.tensor_tensor(out=TA, in0=TA, in1=NEGM, op=ALU.add), reads=[t_cst], writes=[t_cst])
    aneg, t_an = C.sb([128, 4], F32, "aneg")
    S.add("act", lambda e: e.activation(out=aneg[:], in_=spt[:, 20:24], func=AF.Exp), reads=[t_sp], writes=[t_an])
    S.add("dve", lambda e: e.tensor_scalar(out=aneg[:], in0=aneg[:], scalar1=-1.0, scalar2=None, op0=ALU.mult),
          reads=[t_an], writes=[t_an])
    wB, _ = C.sb([128, KC, NCOLB], BF16, "wB")
    t_wB = [T("wB%d" % kc) for kc in range(KC)]
    for kc in range(KC):
        S.add("pool", lambda e, kc=kc: e.dma_start(out=wB[:, kc, :], in_=w_ap[:, kc, :]), writes=[t_wB[kc]], dma=True)
    hbuf = [C.sb([128, KC, 512], BF16, "hb")[0] for _ in range(2)]
    t_hb = [[T("hb") for _ in range(KC)] for _ in range(2)]
    KT, t_KT = C.sb([128, ntok], BF16, "KT")
    vaug, t_v = C.sb([128, NT * 4, 130], BF16, "vaug")
    S.add("pool", lambda e: e.memset(vaug[:], 1.0), writes=[t_v])
    kmean, t_km = C.sb([128, 32], F32, "kmean")
    S.add("pool", lambda e: e.memset(kmean[:], 0.0), writes=[t_km])

    def sbt(shape, dt, name):
        return C.sb(shape, dt, name)

    sq, t_sq = sbt([128, 512], F32, "sq")
    rs, t_rs = sbt([128, 512], F32, "rs")
    qn32, t_qn = sbt([128, 512], F32, "qn32")
    qnb, t_qnb = sbt([128, 512], BF16, "qnb")
    kn32, t_kn = sbt([128, 2, 256], F32, "kn32")
    kms, t_kms = sbt([128, 2], F32, "kms")
    sz = [sbt([64, 512], F32, "sz") for _ in range(4)]
    xpre = [sbt([64, 515], F32, "xpre") for _ in range(4)]
    bpre, t_bpre = sbt([128, 515], F32, "bpre")
    cpre, t_cpre = sbt([128, 515], F32, "cpre")
    for (t_, tt_) in xpre + [(bpre, t_bpre), (cpre, t_cpre)]:
        S.add("pool", lambda e, t_=t_: e.memset(t_[:, 0:3], 0.0), writes=[tt_])
    xsT = [sbt([64, 512], F32, "xsT") for _ in range(4)]
    BT32, t_BT32 = sbt([128, 512], F32, "BT32")
    CT32, t_CT32 = sbt([128, 512], F32, "CT32")
    BTb, t_BTb = sbt([128, 512], BF16, "BTb")
    CTb, t_CTb = sbt([128, 512], BF16, "CTb")
    cacc = [sbt([128, 512], F32, "cacc") for _ in range(2)]
    cBs, t_cBs = sbt([128, 512], F32, "cBs")
    cCs, t_cCs = sbt([128, 512], F32, "cCs")
    ucv, t_ucv = sbt([128, 514], F32, "ucv")
    S.add("pool", lambda e: e.memset(ucv[:, 0:2], 0.0), writes=[t_ucv])
    ycs, t_ycs = sbt([128, 512], F32, "ycs")
    dtraw, t_dtr = sbt([128, 4, 4], F32, "dtraw")
    dtt, t_dtt = sbt([128, 4, 4], F32, "dtt")
    dtA, t_dtA = sbt([128, 4, 4], F32, "dtA")
    wd, t_wd = sbt([128, 4, 4], F32, "wd")
    xd = [sbt([128, 256], BF16, "xd") for _ in range(4)]
    xdd = [sbt([128, 256], BF16, "xdd") for _ in range(4)]
    Btok = [sbt([128, 128], BF16, "Btok") for _ in range(4)]
    acs, t_acs = sbt([128, 12], F32, "acs")
    d2, t_d2 = sbt([128, 8], F32, "d2")
    acsT, t_acsT = sbt([4, 256], F32, "acsT")
    cbm, t_cbm = sbt([128, 384], F32, "cbm")
    E = [sbt([128, 256], F32, "E") for _ in range(4)]
    Cdec, t_Cdec = sbt([128, 256], BF16, "Cdec")
    Dm, t_Dm = sbt([128, 384], F32, "Dm")
    MT, t_MT = sbt([128, 384], BF16, "MT")
    prev32, t_p32 = sbt([128, 256], F32, "prev32")
    prevb, t_pb = sbt([128, 256], BF16, "prevb")
    S.add("pool", lambda e: e.memset(prev32[:], 0.0), writes=[t_p32])
    S.add("pool", lambda e: e.memset(prevb[:], 0.0), writes=[t_pb])
    yt1, t_yt1 = sbt([64, 256], F32, "yt1")
    ygs = [sbt([64, 256], F32, "ygs") for _ in range(2)]
    gsbL = [sbt([128, 32], F32, "gsb") for _ in range(4)]
    top8L = [sbt([128, 8], F32, "top8") for _ in range(4)]
    mselL = [sbt([128, 32], F32, "msel") for _ in range(4)]
    accL = [sbt([128, 130], F32, "acc") for _ in range(4)]
    tmpS2L = [[sbt([128, 256], F32, "tmpS")] * 2 for _ in range(4)]
    pt2L = [[sbt([128, 2, 128], BF16, "pt") for _ in range(2)] for _ in range(4)]
    nptL = [0, 0, 0, 0]
    recL = [sbt([128, 1], F32, "rec") for _ in range(4)]
    yqL = [[sbt([128, 128], F32, "yq")] * 2 for _ in range(4)]
    nyqL = [0, 0, 0, 0]
    nyg = 0
    nyq = 0
    nosb = 0
    if STOP == 1:
        return C.finish()

    def load_h(tt_):
        hb_ = hbuf[tt_ % 2]
        for kc in range(KC):
            S.add("pool", lambda e, kc=kc, hb_=hb_, t0=tt_ * 512: e.dma_start(
                out=hb_[:, kc, :], in_=h_ap[kc * 128:(kc + 1) * 128, t0:t0 + 512]), writes=[t_hb[tt_ % 2][kc]], dma=True)

    load_h(0)
    for tt in range(NT):
        tok0 = tt * 512
        hb = hbuf[tt % 2]
        thb = t_hb[tt % 2]

        def proj(col0, M):
            ps, t_ps = C.psum()
            mm_group(S, ps[0:M, :], [(wB[:, kc, col0:col0 + M], hb[:, kc, :]) for kc in range(KC)], thb + t_wB, [t_ps])
            return ps, t_ps

        def sec_qk():
            for which in range(2):
                ps, t_ps = proj(C_Q if which == 0 else C_K, 128)
                S.add("act", lambda e, ps=ps: e.activation(out=sq[:], in_=ps[:, :], func=AF.Square), reads=[t_ps], writes=[t_sq])
                ps2, t_ps2 = C.psum()
                S.add("pe", lambda e, ps2=ps2: e.matmul(ps2[:, :], ones, sq[:], start=True, stop=True),
                      reads=[t_sq, t_cst], writes=[t_ps2])
                S.add("act", lambda e, ps2=ps2: e.activation(out=rs[:], in_=ps2[:, :], func=AF.Sqrt, scale=1.0 / 128, bias=EPS),
                      reads=[t_ps2], writes=[t_rs])
                S.add("dve", lambda e: e.reciprocal(out=rs[:], in_=rs[:]), reads=[t_rs], writes=[t_rs])
                if which == 0:
                    S.add("dve", lambda e, ps=ps: e.scalar_tensor_tensor(
                        out=qn32[:], in0=ps[:, :], scalar=spt[:, 0:1], in1=rs[:], op0=ALU.mult, op1=ALU.mult),
                        reads=[t_ps, t_rs, t_sp], writes=[t_qn])
                    S.add("act", lambda e: e.activation(out=qnb[:], in_=qn32[:], func=AF.Identity), reads=[t_qn], writes=[t_qnb])
                else:
                    for a in range(2):
                        S.add("dve", lambda e, ps=ps, a=a: e.scalar_tensor_tensor(
                            out=kn32[:, a, :], in0=ps[:, a * 256:(a + 1) * 256], scalar=spt[:, 1:2], in1=rs[:, a * 256:(a + 1) * 256],
                            op0=ALU.mult, op1=ALU.mult), reads=[t_ps, t_rs, t_sp], writes=[t_kn])
                        S.add("act", lambda e, a=a, tok0=tok0: e.activation(
                            out=KT[:, tok0 + a * 256:tok0 + (a + 1) * 256], in_=kn32[:, a, :], func=AF.Identity),
                            reads=[t_kn], writes=[t_KT])
                    S.add("dve", lambda e: e.tensor_reduce(out=kms[:], in_=kn32[:], axis=AX.X, op=ALU.add), reads=[t_kn], writes=[t_kms])
                    S.add("dve", lambda e, tt=tt: e.tensor_scalar(
                        out=kmean[:, 2 * tt:2 * tt + 2], in0=kms[:], scalar1=1.0 / 256, scalar2=None, op0=ALU.mult),
                        reads=[t_kms], writes=[t_km])

        def sec_rest():
            for h in range(4):
                ps, t_ps = proj(C_Z + h * 64, 64)
                S.add("act", lambda e, ps=ps, h=h: e.activation(out=sz[h][0][:], in_=ps[0:64, :], func=AF.Silu),
                      reads=[t_ps], writes=[sz[h][1]])
            for h in range(4):
                ps, t_ps = proj(C_XS + h * 64, 64)
                S.add("act", lambda e, ps=ps, h=h: e.activation(out=xpre[h][0][:, 3:515], in_=ps[0:64, :], func=AF.Identity),
                      reads=[t_ps], writes=[xpre[h][1]])
            ps, t_ps = proj(C_B, 128)
            S.add("act", lambda e, ps=ps: e.activation(out=bpre[:, 3:515], in_=ps[:, :], func=AF.Identity), reads=[t_ps], writes=[t_bpre])
            ps, t_ps = proj(C_C, 128)
            S.add("act", lambda e, ps=ps: e.activation(out=cpre[:, 3:515], in_=ps[:, :], func=AF.Identity), reads=[t_ps], writes=[t_cpre])
            ps, t_ps = proj(C_CB, 128)
            S.add("act", lambda e, ps=ps: e.activation(out=cBs[:], in_=ps[:, :], func=AF.Identity), reads=[t_ps], writes=[t_cBs])
            ps, t_ps = proj(C_CC, 128)
            S.add("act", lambda e, ps=ps: e.activation(out=cCs[:], in_=ps[:, :], func=AF.Identity), reads=[t_ps], writes=[t_cCs])
            ps, t_ps = proj(C_CX, 128)
            S.add("dve", lambda e, ps=ps: e.tensor_tensor(out=ucv[:, 2:514], in0=cCs[:], in1=ps[:, :], op=ALU.mult),
                  reads=[t_ps, t_cCs], writes=[t_ucv])
            for s in range(4):
                ps, t_ps = C.psum()
                mm_group(S, ps[:, 0:128], [(hb[:, kc, s * 128:(s + 1) * 128], wB[:, kc, C_V:C_V + 128]) for kc in range(KC)],
                         thb + t_wB, [t_ps])
                if STOP != 8:
                    mm_group(S, ps[:, 128:132], [(hb[:, kc, s * 128:(s + 1) * 128], wB[:, kc, C_V + 128:C_V + 132]) for kc in range(KC)],
                             thb + t_wB, [t_ps])
                if STOP == 4:
                    continue
                S.add("act", lambda e, ps=ps, s=s, tt=tt: e.activation(out=vaug[:, 4 * tt + s, 0:128], in_=ps[:, 0:128], func=AF.Identity),
                      reads=[t_ps], writes=[t_v])
                if STOP in (5, 7):
                    continue
                S.add("act", lambda e, ps=ps, s=s: e.activation(out=dtraw[:, s, :], in_=ps[:, 128:132], func=AF.Identity),
                      reads=[t_ps], writes=[t_dtr])
                S.add("pool", lambda e, s=s: e.tensor_tensor(out=dtraw[:, s, :], in0=dtraw[:, s, :], in1=spt[:, 16:20], op=ALU.add),
                      reads=[t_dtr, t_sp], writes=[t_dtr])
                if STOP == 6:
                    continue

        interleave(C, [sec_qk, sec_rest], banks=[[0, 1, 2], [3, 4, 5, 6, 7]])
        if tt + 1 < NT:
            load_h(tt + 1)
        if 'conv' in PARTS:
            a0, t_a0 = cacc[0]
            S.add("dve", lambda e: e.tensor_scalar(out=a0[:], in0=ucv[:, 0:512], scalar1=spt[:, 13:14], scalar2=None, op0=ALU.mult),
                  reads=[t_ucv, t_sp], writes=[t_a0])
            for j in (1, 2):
                S.add("dve", lambda e, j=j: e.scalar_tensor_tensor(
                    out=a0[:], in0=ucv[:, j:j + 512], scalar=spt[:, 13 + j:14 + j], in1=a0[:], op0=ALU.mult, op1=ALU.add),
                    reads=[t_ucv, t_sp, t_a0], writes=[t_a0])
            S.add("dve", lambda e: e.tensor_tensor(out=ycs[:], in0=cBs[:], in1=a0[:], op=ALU.mult), reads=[t_cBs, t_a0], writes=[t_ycs])
            S.add("sp", lambda e, tok0=tok0: e.dma_start(out=yc_ap[:, tok0:tok0 + 512], in_=ycs[:]), reads=[t_ycs], dma=True)
            S.add("dve", lambda e: e.tensor_copy(out=ucv[:, 0:2], in_=ucv[:, 512:514]), reads=[t_ucv], writes=[t_ucv])
        def sec_ssd():
            nonlocal nyg
            def ssd_conv(pre, t_pre, P, wsrc, t_w, c0, outs):
                ac, t_ac = cacc[1]
                S.add("dve", lambda e: e.tensor_scalar(out=ac[0:P, :], in0=pre[:, 0:512], scalar1=wsrc[:, c0:c0 + 1], scalar2=None, op0=ALU.mult),
                      reads=[t_pre, t_w], writes=[t_ac])
                for j in (1, 2, 3):
                    S.add("dve", lambda e, j=j: e.scalar_tensor_tensor(
                        out=ac[0:P, :], in0=pre[:, j:j + 512], scalar=wsrc[:, c0 + j:c0 + j + 1], in1=ac[0:P, :], op0=ALU.mult, op1=ALU.add),
                        reads=[t_pre, t_w, t_ac], writes=[t_ac])
                o, t_o = outs
                S.add("act", lambda e: e.activation(out=o[:], in_=ac[0:P, :], func=AF.Silu, bias=wsrc[:, c0 + 4:c0 + 5], scale=1.0),
                      reads=[t_ac, t_w], writes=[t_o])
                S.add("dve", lambda e: e.tensor_copy(out=pre[:, 0:3], in_=pre[:, 512:515]), reads=[t_pre], writes=[t_pre])

            for h in range(4):
                ssd_conv(xpre[h][0], xpre[h][1], 64, sp64, t_sp64, h * 6, xsT[h])
            ssd_conv(bpre, t_bpre, 128, spt, t_sp, 3, (BT32, t_BT32))
            ssd_conv(cpre, t_cpre, 128, spt, t_sp, 8, (CT32, t_CT32))
            S.add("pool", lambda e: e.tensor_copy(out=BTb[:], in_=BT32[:]), reads=[t_BT32], writes=[t_BTb])
            S.add("pool", lambda e: e.tensor_copy(out=CTb[:], in_=CT32[:]), reads=[t_CT32], writes=[t_CTb])
            S.add("act", lambda e: e.activation(out=dtt[:], in_=dtraw[:], func=AF.Exp), reads=[t_dtr], writes=[t_dtt])
            S.add("act", lambda e: e.activation(out=dtt[:], in_=dtt[:], func=AF.Ln, bias=1.0, scale=1.0), reads=[t_dtt], writes=[t_dtt])
            for s in range(4):
                S.add("dve", lambda e, s=s: e.tensor_tensor(out=dtA[:, s, :], in0=dtt[:, s, :], in1=aneg[:], op=ALU.mult),
                      reads=[t_dtt, t_an], writes=[t_dtA])
            for cc in range(2):
                ccol = slice(cc * 256, (cc + 1) * 256)
                s0, s1 = 2 * cc, 2 * cc + 1
                psa, t_psa = C.psum()
                S.add("pe", lambda e, psa=psa, s0=s0: e.matmul(psa[:, 0:4], UTRI, dtA[:, s0, :], start=True, stop=True),
                      reads=[t_dtA, t_cst], writes=[t_psa])
                mm_group(S, psa[:, 4:8], [(ones, dtA[:, s0, :]), (UTRI, dtA[:, s1, :])], [t_dtA, t_cst], [t_psa])
                mm_group(S, psa[:, 8:12], [(ones, dtA[:, s0, :]), (ones, dtA[:, s1, :])], [t_dtA, t_cst], [t_psa])
                S.add("dve", lambda e, psa=psa: e.tensor_copy(out=acs[:], in_=psa[:, 0:12]), reads=[t_psa], writes=[t_acs])
                for s in range(2):
                    S.add("dve", lambda e, s=s: e.tensor_tensor(out=d2[:, 4 * s:4 * s + 4], in0=acs[:, 8:12], in1=acs[:, 4 * s:4 * s + 4], op=ALU.subtract),
                          reads=[t_acs], writes=[t_d2])
                S.add("act", lambda e: e.activation(out=d2[:], in_=d2[:], func=AF.Exp), reads=[t_d2], writes=[t_d2])
                for s in range(2):
                    S.add("dve", lambda e, s=s, cc=cc: e.tensor_tensor(out=wd[:, 2 * cc + s, :], in0=dtt[:, 2 * cc + s, :], in1=d2[:, 4 * s:4 * s + 4], op=ALU.mult),
                          reads=[t_dtt, t_d2], writes=[t_wd])
                pst, t_pst = C.psum()
                mm_group(S, pst[0:4, 0:256], [(dtA[:, s0, :], U0), (dtA[:, s1, :], U1)], [t_dtA, t_cst], [t_pst])
                S.add("dve", lambda e, pst=pst: e.tensor_copy(out=acsT[:], in_=pst[0:4, 0:256]), reads=[t_pst], writes=[t_acsT])
                for s in (s0, s1):
                    psx, t_psx = C.psum()

                    def tfn(e, psx=psx, s=s):
                        ins = None
                        for h in range(4):
                            ins = e.transpose(out=psx[:, h * 64:(h + 1) * 64], in_=xsT[h][0][:, s * 128:(s + 1) * 128], identity=cst[0:64, 0:64])
                        return ins
                    S.add("pe", tfn, reads=[x[1] for x in xsT] + [t_cst], writes=[t_psx])
                    for h in range(4):
                        S.add("dve", lambda e, psx=psx, s=s, h=h: e.tensor_scalar(
                            out=xd[s][0][:, h * 64:(h + 1) * 64], in0=psx[:, h * 64:(h + 1) * 64], scalar1=dtt[:, s, h:h + 1], scalar2=None, op0=ALU.mult),
                            reads=[t_psx, t_dtt], writes=[xd[s][1]])
                        S.add("dve", lambda e, psx=psx, s=s, h=h: e.tensor_scalar(
                            out=xdd[s][0][:, h * 64:(h + 1) * 64], in0=psx[:, h * 64:(h + 1) * 64], scalar1=wd[:, s, h:h + 1], scalar2=None, op0=ALU.mult),
                            reads=[t_psx, t_wd], writes=[xdd[s][1]])
                    psb, t_psb = C.psum()
                    S.add("pe", lambda e, psb=psb, s=s: e.transpose(out=psb[:, 0:128], in_=BT32[:, s * 128:(s + 1) * 128], identity=ident),
                          reads=[t_BT32, t_cst], writes=[t_psb])
                    S.add("act", lambda e, psb=psb, s=s: e.activation(out=Btok[s][0][:], in_=psb[:, 0:128], func=AF.Identity),
                          reads=[t_psb], writes=[Btok[s][1]])
                psc, t_psc = C.psum()
                S.add("pe", lambda e, psc=psc, cc=cc: e.matmul(psc[:, 0:256], BTb[:, cc * 256:cc * 256 + 128], CTb[:, cc * 256:cc * 256 + 256], start=True, stop=True),
                      reads=[t_BTb, t_CTb], writes=[t_psc])
                S.add("pe", lambda e, psc=psc, cc=cc: e.matmul(psc[:, 256:384], BTb[:, cc * 256 + 128:cc * 256 + 256], CTb[:, cc * 256 + 128:cc * 256 + 256], start=True, stop=True),
                      reads=[t_BTb, t_CTb], writes=[t_psc])
                S.add("dve", lambda e, psc=psc: e.tensor_tensor(out=cbm[:, 0:256], in0=psc[:, 0:256], in1=U0, op=ALU.mult),
                      reads=[t_psc, t_cst], writes=[t_cbm])
                S.add("dve", lambda e, psc=psc: e.tensor_tensor(out=cbm[:, 256:384], in0=psc[:, 256:384], in1=UTRI, op=ALU.mult),
                      reads=[t_psc, t_cst], writes=[t_cbm])
                for h in range(4):
                    psbc, t_psbc = C.psum()
                    S.add("pe", lambda e, psbc=psbc, h=h: e.matmul(psbc[:, 0:256], SELH[:, h * 128:(h + 1) * 128], acsT[:], start=True, stop=True),
                          reads=[t_acsT, t_cst], writes=[t_psbc])
                    Eh, t_Eh = E[h]
                    S.add("act", lambda e, psbc=psbc, Eh=Eh: e.activation(out=Eh[:], in_=psbc[:, 0:256], func=AF.Exp), reads=[t_psbc], writes=[t_Eh])
                    S.add("pool", lambda e, Eh=Eh, ccol=ccol: e.tensor_tensor(out=Cdec[:], in0=CT32[:, ccol], in1=Eh[:], op=ALU.mult),
                          reads=[t_CT32, t_Eh], writes=[t_Cdec])
                    S.add("dve", lambda e, psbc=psbc, h=h: e.tensor_scalar(
                        out=Dm[:, 0:256], in0=psbc[:, 0:256], scalar1=acs[:, h:h + 1], scalar2=0.0, op0=ALU.subtract, op1=ALU.min),
                        reads=[t_psbc, t_acs], writes=[t_Dm])
                    S.add("dve", lambda e, psbc=psbc, h=h: e.tensor_scalar(
                        out=Dm[:, 256:384], in0=psbc[:, 128:256], scalar1=acs[:, 4 + h:5 + h], scalar2=0.0, op0=ALU.subtract, op1=ALU.min),
                        reads=[t_psbc, t_acs], writes=[t_Dm])
                    S.add("act", lambda e: e.activation(out=Dm[:], in_=Dm[:], func=AF.Exp), reads=[t_Dm], writes=[t_Dm])
                    S.add("dve", lambda e: e.tensor_tensor(out=MT[:], in0=Dm[:], in1=cbm[:], op=ALU.mult), reads=[t_Dm, t_cbm], writes=[t_MT])
                    psy, t_psy = C.psum()
                    hs = slice(h * 64, (h + 1) * 64)

                    def yfn(e, psy=psy, hs=hs, s0=s0, s1=s1):
                        e.matmul(psy[0:64, 0:256], prevb[:, hs], Cdec[:], start=True, stop=False)
                        e.matmul(psy[0:64, 0:256], xd[s0][0][:, hs], MT[:, 0:256], start=False, stop=False)
                        return e.matmul(psy[0:64, 128:256], xd[s1][0][:, hs], MT[:, 256:384], start=False, stop=True)
                    S.add("pe", yfn, reads=[t_pb, t_Cdec, xd[s0][1], xd[s1][1], t_MT], writes=[t_psy])
                    S.add("dve", lambda e, psy=psy, h=h, ccol=ccol: e.scalar_tensor_tensor(
                        out=yt1[:], in0=xsT[h][0][:, ccol], scalar=sp64[:, h * 6 + 5:h * 6 + 6], in1=psy[0:64, 0:256], op0=ALU.mult, op1=ALU.add),
                        reads=[xsT[h][1], t_sp64, t_psy], writes=[t_yt1])
                    yg, t_yg = ygs[nyg % 2]
                    nyg += 1
                    S.add("pool", lambda e, yg=yg, h=h, ccol=ccol: e.tensor_tensor(out=yg[:], in0=yt1[:], in1=sz[h][0][:, ccol], op=ALU.mult),
                          reads=[t_yt1, sz[h][1]], writes=[t_yg])
                    S.add("sp", lambda e, yg=yg, h=h, tok0=tok0, cc=cc: e.dma_start(
                        out=yg_ap[h * 64:(h + 1) * 64, tok0 + cc * 256:tok0 + (cc + 1) * 256], in_=yg[:]), reads=[t_yg], dma=True)
                pss, t_pss = C.psum()
                mm_group(S, pss[:, 0:256], [(Btok[s0][0][:], xdd[s0][0][:]), (Btok[s1][0][:], xdd[s1][0][:])],
                         [Btok[s0][1], Btok[s1][1], xdd[s0][1], xdd[s1][1]], [t_pss])
                for h in range(4):
                    hs = slice(h * 64, (h + 1) * 64)
                    S.add("dve", lambda e, h=h, hs=hs, pss=pss: e.scalar_tensor_tensor(
                        out=prev32[:, hs], in0=prev32[:, hs], scalar=E[h][0][:, 255:256], in1=pss[:, hs], op0=ALU.mult, op1=ALU.add),
                        reads=[t_p32, E[h][1], t_pss], writes=[t_p32])
                S.add("act", lambda e: e.activation(out=prevb[:], in_=prev32[:], func=AF.Identity), reads=[t_p32], writes=[t_pb])
        def sec_attn(qis=(0, 1, 2, 3), bs=0):
            gsb, t_gsb = gsbL[bs]
            top8, t_top8 = top8L[bs]
            msel, t_msel = mselL[bs]
            acc, t_acc = accL[bs]
            rec, t_rec = recL[bs]
            pt2 = pt2L[bs]
            tmpS2 = tmpS2L[bs]
            yq = yqL[bs]
            for i in qis:
                qi = 4 * tt + i
                J = qi // 2
                eo = qi % 2
                qc = slice(i * 128, (i + 1) * 128)
                use_sel = J > 3
                if use_sel:
                    psg, t_psg = C.psum()
                    S.add("pe", lambda e, psg=psg, qc=qc: e.matmul(psg[:, 0:32], qn32[:, qc], kmean[:], start=True, stop=True),
                          reads=[t_qn, t_km], writes=[t_psg])
                    S.add("pool", lambda e: e.memset(gsb[:], NEG), writes=[t_gsb])
                    S.add("dve", lambda e, psg=psg, J=J: e.tensor_copy(out=gsb[:, 0:J], in_=psg[:, 0:J]), reads=[t_psg], writes=[t_gsb])
                    S.add("dve", lambda e: e.max(out=top8[:], in_=gsb[:]), reads=[t_gsb], writes=[t_top8])
                    S.add("dve", lambda e: e.tensor_scalar(out=msel[:], in0=gsb[:], scalar1=top8[:, 2:3], scalar2=None, op0=ALU.is_ge),
                          reads=[t_gsb, t_top8], writes=[t_msel])
                blocks = [J] + list(range(J))
                for bi, n in enumerate(blocks):
                    if n == J:
                        halves = [(0, "A")] if eo == 0 else [(0, "B"), (1, "A")]
                    elif n == J - 1:
                        halves = [(0, "c"), (1, "B")] if eo == 0 else [(0, "c"), (1, "c")]
                    else:
                        halves = [(0, "c"), (1, "c")]
                    pss2, t_pss2 = C.psum()
                    pt, t_pt = pt2[nptL[bs] % 2]
                    tmpS, t_tmpS = tmpS2[nptL[bs] % 2]
                    nptL[bs] += 1

                    def sfn(e, pss2=pss2, halves=halves, n=n, qc=qc):
                        ins = None
                        for hh, _k in halves:
                            ins = e.matmul(pss2[:, hh * 128:(hh + 1) * 128], KT[:, n * 256 + hh * 128:n * 256 + (hh + 1) * 128], qnb[:, qc], start=True, stop=True)
                        return ins
                    S.add("pe", sfn, reads=[t_KT, t_qnb], writes=[t_pss2])
                    if STOP == 10:
                        continue
                    for hh, kind in halves:
                        if (STOP == 14 and kind != "c") or (STOP == 15 and kind == "c"):
                            continue
                        if kind == "c":
                            S.add("act", lambda e, pss2=pss2, hh=hh, pt=pt: e.activation(
                                out=pt[:, hh, :], in_=pss2[:, hh * 128:(hh + 1) * 128], func=AF.Exp, bias=spt[:, 2:3], scale=SCALE),
                                reads=[t_pss2, t_sp], writes=[t_pt])
                        else:
                            bt_ = TA if kind == "A" else TB
                            S.add("dve", lambda e, pss2=pss2, hh=hh, bt_=bt_, tmpS=tmpS: e.scalar_tensor_tensor(
                                out=tmpS[:, hh * 128:(hh + 1) * 128], in0=pss2[:, hh * 128:(hh + 1) * 128], scalar=SCALE, in1=bt_, op0=ALU.mult, op1=ALU.add),
                                reads=[t_pss2, t_cst], writes=[t_tmpS])
                            S.add("act", lambda e, hh=hh, pt=pt, tmpS=tmpS: e.activation(out=pt[:, hh, :], in_=tmpS[:, hh * 128:(hh + 1) * 128], func=AF.Exp),
                                  reads=[t_tmpS], writes=[t_pt])
                    if STOP in (11, 14, 15):
                        continue
                    pso, t_pso = C.psum()
                    nh = len(halves)

                    def ofn(e, pso=pso, halves=halves, n=n, nh=nh, pt=pt):
                        ins = None
                        for ii, (hh, _k) in enumerate(halves):
                            ins = e.matmul(pso[:, 0:130], pt[:, hh, :], vaug[:, 2 * n + hh, :], start=(ii == 0), stop=(ii == nh - 1))
                        return ins
                    S.add("pe", ofn, reads=[t_pt, t_v], writes=[t_pso])
                    if STOP == 12:
                        continue
                    if bi == 0:
                        S.add("act", lambda e, pso=pso: e.activation(out=acc[:], in_=pso[:, 0:130], func=AF.Identity), reads=[t_pso], writes=[t_acc])
                    else:
                        sc_ = msel[:, n:n + 1] if use_sel else 1.0
                        S.add("dve", lambda e, pso=pso, sc_=sc_: e.scalar_tensor_tensor(
                            out=acc[:], in0=pso[:, 0:130], scalar=sc_, in1=acc[:], op0=ALU.mult, op1=ALU.add),
                            reads=[t_pso, t_msel, t_acc], writes=[t_acc])
                if STOP in (10, 11, 12, 13, 14, 15):
                    continue
                S.add("dve", lambda e: e.reciprocal(out=rec[:], in_=acc[:, 128:129]), reads=[t_acc], writes=[t_rec])
                yq_, t_yq = yq[nyqL[bs] % 2]
                nyqL[bs] += 1
                S.add("dve", lambda e, yq_=yq_: e.tensor_scalar(out=yq_[:], in0=acc[:, 0:128], scalar1=rec[:, 0:1], scalar2=None, op0=ALU.mult),
                      reads=[t_acc, t_rec], writes=[t_yq])
                S.add("sp", lambda e, yq_=yq_, qi=qi: e.dma_start(out=ya_ap[qi * 128:(qi + 1) * 128, :], in_=yq_[:]), reads=[t_yq], dma=True)
        secs = []
        if 'ssd' in PARTS:
            secs.append(sec_ssd)
        if 'attn' in PARTS:
            for qb in range(4):
                secs.append(lambda qb=qb: sec_attn((qb,), qb))
        interleave(C, secs, banks=[[0, 1, 2, 3], [4], [5], [6], [7]] if len(secs) == 5 else [[0, 1], [2, 3], [4, 5], [6, 7]])
    return C.finish()


def t5_bucket_np(dist):
    n = np.maximum(dist, 0)
    max_exact = 16
    ratio = np.log(np.maximum(n, 1).astype(np.float32) / max_exact) / np.float32(np.log(128 / max_exact))
    large = max_exact + (ratio * (32 - max_exact)).astype(np.int32)
    large = np.minimum(large, 31)
    return np.where(n < max_exact, n, large)


def mix_consts(rel_bias, head):
    kk = np.arange(128)[:, None]
    qq = np.arange(128)[None, :]
    cst = np.zeros((128, 5 * 128 + 1024), np.float32)
    cst[:, 0:128] = np.eye(128, dtype=np.float32)
    cst[:, 128:256] = 1.0
    cst[:, 256:384] = rel_bias[t5_bucket_np(qq - kk), head]
    cst[:, 384:512] = rel_bias[t5_bucket_np(128 + qq - kk), head]
    cst[:, 512:640] = np.where(qq >= kk, 0.0, NEG)
    utri = (kk <= qq).astype(np.float32)
    cst[:, 640:768] = utri
    cst[:, 768:896] = 1.0
    cst[:, 896:1024] = 0.0
    cst[:, 1024:1152] = utri
    for h in range(4):
        cst[h, 1152 + h * 128:1152 + (h + 1) * 128] = 1.0
    return cst


def run_mixB(hT, l, inp, NT=16):
    w = inp["w_mix_in"][l]
    in_maps = []
    for c in range(NCORES):
        g = c // 2
        cols = np.concatenate([
            np.arange(c * 128, (c + 1) * 128), 1024 + np.arange(c * 128, (c + 1) * 128),
            3072 + np.arange(c * 256, (c + 1) * 256), 5120 + np.arange(c * 256, (c + 1) * 256),
            5120 + 2048 + np.arange(g * 128, (g + 1) * 128), 5120 + 2560 + np.arange(g * 128, (g + 1) * 128),
            8224 + np.arange(c * 128, (c + 1) * 128), 9248 + np.arange(c * 128, (c + 1) * 128),
            10272 + np.arange(c * 128, (c + 1) * 128), 2048 + np.arange(c * 128, (c + 1) * 128),
            8192 + np.arange(c * 4, (c + 1) * 4)])
        wB = np.ascontiguousarray(w[:, cols].reshape(KC, 128, NCOLB).transpose(1, 0, 2))
        sp = np.zeros((128, 24), np.float32)
        sp[:, 0] = inp["qk_norm"][l, 0]
        sp[:, 1] = inp["qk_norm"][l, 1]
        sp[:, 2] = inp["rel_bias"][31, c]
        wc = inp["w_ssd_conv"][l]
        bc = inp["b_ssd_conv"][l]
        chB = 2048 + g * 128 + np.arange(128)
        chC = 2560 + g * 128 + np.arange(128)
        sp[:, 3:7] = wc[:, chB].T
        sp[:, 7] = bc[chB]
        sp[:, 8:12] = wc[:, chC].T
        sp[:, 12] = bc[chC]
        sp[:, 13:16] = inp["w_sc_conv"][l][:, c * 128:(c + 1) * 128].T
        sp[:, 16:20] = inp["ssd_dt_bias"][l][None, 4 * c:4 * c + 4]
        sp[:, 20:24] = inp["ssd_a_log"][l][None, 4 * c:4 * c + 4]
        sp64 = np.zeros((64, 24), np.float32)
        for h in range(4):
            ch = c * 256 + h * 64 + np.arange(64)
            sp64[:, h * 6:h * 6 + 4] = wc[:, ch].T
            sp64[:, h * 6 + 4] = bc[ch]
            sp64[:, h * 6 + 5] = inp["ssd_d"][l][4 * c + h]
        in_maps.append({"hT": hT, "wB": wB, "sp": sp, "sp64": sp64, "cst": mix_consts(inp["rel_bias"], c)})
    r = run(prog("mixB", build_mixB, NT), in_maps)
    ntok = NT * 512
    YT = np.zeros((4096, ntok), np.float32)
    for c in range(NCORES):
        YT[c * 128:(c + 1) * 128] = r[c]["ya"].T
        YT[1024 + c * 256:1024 + (c + 1) * 256] = r[c]["yg"]
        YT[3072 + c * 128:3072 + (c + 1) * 128] = r[c]["yc"]
    return YT


NYC = 32
BR_CH = ((0, 8), (8, 24), (24, 32))


def build_mixC():
    C = Ctx()
    S = C.S
    x_ap = C.din("xT", [D, TOK])
    h_ap = C.din("hT", [D, TOK])
    y_ap = C.din("YT", [4096, TOK])
    wg_ap = C.din("wg", [KC, 128, KC, 384])
    wbr_ap = C.din("wbr", [KC, 128, NYC, 128])
    wo_ap = C.din("wo", [KC, 128, KC, 128])
    par_ap = C.din("par", [128, 2 * KC])
    ones_ap = C.din("ones", [128, 128])
    xo_ap = C.dout("xo", [D, TOK])
    C.init_psum()
    ones, t_ones = load_consts(C, ones_ap)
    par, t_par = C.sb([128, 2 * KC], F32, "par")
    S.add("sp", lambda e: e.dma_start(out=par[:], in_=par_ap), writes=[t_par], dma=True)
    xT, _ = C.sb([128, KC, 512], F32, "xT")
    t_x = [T("x%d" % k) for k in range(KC)]
    hb, _ = C.sb([128, KC, 512], BF16, "hb")
    t_h = [T("h%d" % k) for k in range(KC)]
    Yb, _ = C.sb([128, NYC, 512], BF16, "Yb")
    t_y = [T("y%d" % k) for k in range(NYC)]
    stg = [C.sb([128, 512], F32, "stg") for _ in range(4)]
    sq = [C.sb([128, 512], F32, "sq") for _ in range(2)]
    rs, t_rs = C.sb([128, 512], F32, "rs")
    tmpn = [C.sb([128, 512], F32, "tmpn") for _ in range(2)]
    mg, t_mg = C.sb([128, 512], F32, "mg")
    sig = [C.sb([128, 512], F32, "sig") for _ in range(2)]
    tmpm, t_tmpm = C.sb([128, 512], F32, "tmpm")
    mT, _ = C.sb([128, KC, 512], BF16, "mT")
    t_m = [T("m%d" % k) for k in range(KC)]
    wg = [C.sb([128, KC, 384], BF16, "wg") for _ in range(2)]
    wbr = [C.sb([128, NYC, 128], BF16, "wbr") for _ in range(2)]
    wo = [C.sb([128, KC, 128], BF16, "wo") for _ in range(2)]
    nsig = 0
    nw = 0
    nwo = 0
    for tt in range(TOK // 512):
        sl = slice(tt * 512, (tt + 1) * 512)
        for kc in range(KC):
            S.add("sp", lambda e, kc=kc, sl=sl: e.dma_start(out=xT[:, kc, :], in_=x_ap[kc * 128:(kc + 1) * 128, sl]),
                  writes=[t_x[kc]], dma=True)
            S.add("pool", lambda e, kc=kc, sl=sl: e.dma_start(out=hb[:, kc, :], in_=h_ap[kc * 128:(kc + 1) * 128, sl]),
                  writes=[t_h[kc]], dma=True)
        for ch in list(range(0, 8)) + list(range(24, 32)):
            S.add("pool", lambda e, ch=ch, sl=sl: e.dma_start(out=Yb[:, ch, :], in_=y_ap[ch * 128:(ch + 1) * 128, sl]),
                  writes=[t_y[ch]], dma=True)
        for gq in range(4):
            ps, t_ps = C.psum()
            for c4 in range(4):
                ch = 8 + gq * 4 + c4
                st_, t_st = stg[c4]
                S.add("sp", lambda e, st_=st_, ch=ch, sl=sl: e.dma_start(out=st_[:], in_=y_ap[ch * 128:(ch + 1) * 128, sl]),
                      writes=[t_st], dma=True)
                sq_, t_sq = sq[c4 % 2]
                S.add("act", lambda e, sq_=sq_, st_=st_: e.activation(out=sq_[:], in_=st_[:], func=AF.Square), reads=[t_st], writes=[t_sq])
                S.add("pe", lambda e, ps=ps, sq_=sq_, c4=c4: e.matmul(ps[:, :], ones[:], sq_[:], start=(c4 == 0), stop=(c4 == 3)),
                      reads=[t_sq, t_ones], writes=[t_ps])
            S.add("act", lambda e, ps=ps: e.activation(out=rs[:], in_=ps[:, :], func=AF.Sqrt, scale=1.0 / 512, bias=EPS),
                  reads=[t_ps], writes=[t_rs])
            S.add("dve", lambda e: e.reciprocal(out=rs[:], in_=rs[:]), reads=[t_rs], writes=[t_rs])
            for c4 in range(4):
                ch = 8 + gq * 4 + c4
                st_, t_st = stg[c4]
                S.add("dve", lambda e, st_=st_, ch=ch: e.scalar_tensor_tensor(
                    out=Yb[:, ch, :], in0=st_[:], scalar=par[:, ch - 8:ch - 7], in1=rs[:], op0=ALU.mult, op1=ALU.mult),
                    reads=[t_st, t_rs, t_par], writes=[t_y[ch]])
        for m in range(KC):
            wg_, t_wg = wg[nw % 2]
            wbr_, t_wbr = wbr[nw % 2]
            nw += 1
            S.add("pool", lambda e, wg_=wg_, m=m: e.dma_start(out=wg_[:], in_=wg_ap[m]), writes=[t_wg], dma=True)
            S.add("pool", lambda e, wbr_=wbr_, m=m: e.dma_start(out=wbr_[:], in_=wbr_ap[m]), writes=[t_wbr], dma=True)
            for b in range(3):
                pg, t_pg = C.psum()
                mm_group(S, pg[:, :], [(wg_[:, kc, b * 128:(b + 1) * 128], hb[:, kc, :]) for kc in range(KC)], t_h + [t_wg], [t_pg])
                sg_, t_sg = sig[nsig % 2]
                nsig += 1
                S.add("act", lambda e, sg_=sg_, pg=pg: e.activation(out=sg_[:], in_=pg[:, :], func=AF.Sigmoid), reads=[t_pg], writes=[t_sg])
                pb, t_pb = C.psum()
                c0, c1 = BR_CH[b]
                mm_group(S, pb[:, :], [(wbr_[:, ch, :], Yb[:, ch, :]) for ch in range(c0, c1)], t_y[c0:c1] + [t_wbr], [t_pb])
                if b == 0:
                    S.add("dve", lambda e, sg_=sg_, pb=pb: e.tensor_tensor(out=mg[:], in0=sg_[:], in1=pb[:, :], op=ALU.mult),
                          reads=[t_sg, t_pb], writes=[t_mg])
                else:
                    S.add("dve", lambda e, sg_=sg_, pb=pb: e.tensor_tensor(out=tmpm[:], in0=sg_[:], in1=pb[:, :], op=ALU.mult),
                          reads=[t_sg, t_pb], writes=[t_tmpm])
                    if b == 1:
                        S.add("dve", lambda e: e.tensor_tensor(out=mg[:], in0=mg[:], in1=tmpm[:], op=ALU.add),
                              reads=[t_mg, t_tmpm], writes=[t_mg])
                    else:
                        S.add("dve", lambda e, m=m: e.tensor_tensor(out=mT[:, m, :], in0=mg[:], in1=tmpm[:], op=ALU.add),
                              reads=[t_mg, t_tmpm], writes=[t_m[m]])
        for m in range(KC):
            wo_, t_wo = wo[nwo % 2]
            nwo += 1
            S.add("pool", lambda e, wo_=wo_, m=m: e.dma_start(out=wo_[:], in_=wo_ap[m]), writes=[t_wo], dma=True)
            po, t_po = C.psum()
            mm_group(S, po[:, :], [(wo_[:, kc, :], mT[:, kc, :]) for kc in range(KC)], t_m + [t_wo], [t_po])
            S.add("dve", lambda e, po=po, m=m: e.scalar_tensor_tensor(
                out=xT[:, m, :], in0=po[:, :], scalar=par[:, KC + m:KC + m + 1], in1=xT[:, m, :], op0=ALU.mult, op1=ALU.add),
                reads=[t_po, t_par, t_x[m]], writes=[t_x[m]])
            S.add("sp", lambda e, m=m, sl=sl: e.dma_start(out=xo_ap[m * 128:(m + 1) * 128, sl], in_=xT[:, m, :]),
                  reads=[t_x[m]], dma=True)
    return C.finish()


def run_mixC(xT, hT, YT, l, inp, gate1):
    w = inp["w_mix_in"][l]
    G0 = 11296
    wg = np.stack([w[:, G0 + b * 2048:G0 + (b + 1) * 2048].reshape(KC, 128, KC, 128) for b in range(3)], axis=0)
    wg_t = np.ascontiguousarray(wg.transpose(3, 2, 1, 0, 4).reshape(KC, 128, KC, 384))
    wbr = np.concatenate([inp["w_br_attn"][l], inp["w_br_ssd"][l], inp["w_br_conv"][l]], axis=0)
    wbr_t = np.ascontiguousarray(wbr.reshape(NYC, 128, KC, 128).transpose(2, 1, 0, 3))
    wo_t = np.ascontiguousarray(inp["w_mix_out"][l].reshape(KC, 128, KC, 128).transpose(2, 1, 0, 3))
    par = np.concatenate([cols16(inp["ssd_norm"][l]), cols16(gate1)], axis=1)
    ones = np.ones((128, 128), np.float32)
    in_maps = []
    for i in range(NCORES):
        ts = slice(i * TOK, (i + 1) * TOK)
        in_maps.append({"xT": np.ascontiguousarray(xT[:, ts]), "hT": np.ascontiguousarray(hT[:, ts]),
                        "YT": np.ascontiguousarray(YT[:, ts]), "wg": wg_t, "wbr": wbr_t, "wo": wo_t, "par": par, "ones": ones})
    r = run(prog("mixC", build_mixC), in_maps)
    return np.concatenate([r[i]["xo"] for i in range(NCORES)], axis=1)


def kernel(x, c, w_ada, b_ada, norm_gain, w_ffn_in, w_ffn_out, w_mix_in, qk_norm, rel_bias,
           w_ssd_conv, b_ssd_conv, ssd_dt_bias, ssd_a_log, ssd_d, ssd_norm, w_sc_conv,
           w_br_attn, w_br_ssd, w_br_conv, w_mix_out):
    inp = dict(w_mix_in=w_mix_in, qk_norm=qk_norm, rel_bias=rel_bias, w_ssd_conv=w_ssd_conv, b_ssd_conv=b_ssd_conv,
               ssd_dt_bias=ssd_dt_bias, ssd_a_log=ssd_a_log, ssd_d=ssd_d, ssd_norm=ssd_norm, w_sc_conv=w_sc_conv,
               w_br_attn=w_br_attn, w_br_ssd=w_br_ssd, w_br_conv=w_br_conv, w_mix_out=w_mix_out)
    inp = {k: np.asarray(v, np.float32) for k, v in inp.items()}
    w_ada = np.asarray(w_ada, np.float32)
    b_ada = np.asarray(b_ada, np.float32)
    norm_gain = np.asarray(norm_gain, np.float32)
    w_ffn_in = np.asarray(w_ffn_in, np.float32)
    w_ffn_out = np.asarray(w_ffn_out, np.float32)
    ada = compute_ada(np.asarray(c, np.float32), w_ada, b_ada)
    xT = np.ascontiguousarray(np.asarray(x, np.float32)[0].T)
    for l in range(2):
        par = np.concatenate([par_block(norm_gain[l, 0], ada[l, 0]), par_block(norm_gain[l, 1], ada[l, 1])], axis=1)
        xT, hT = run_ffn(xT, par, w_ffn_in[l, 0], w_ffn_out[l, 0], True)
        YT = run_mixB(hT, l, inp, NT=16)
        xT = run_mixC(xT, hT, YT, l, inp, ada[l, 1, 2])
        par = par_block(norm_gain[l, 2], ada[l, 2])
        xT, _ = run_ffn(xT, par, w_ffn_in[l, 1], w_ffn_out[l, 1], False)
    return np.ascontiguousarray(xT.T)[None].astype(np.float32)
```

```python
import contextlib
import threading
import numpy as np
import concourse.bass as bass
import concourse.mybir as mybir
from concourse.bass_utils import run_bass_kernel_spmd

F32 = mybir.dt.float32
BF16 = mybir.dt.bfloat16
AF = mybir.ActivationFunctionType
ALU = mybir.AluOpType
AX = mybir.AxisListType

NCORES = 8
D = 2048
KC = 16
SEQ = 8192
TOK = SEQ // NCORES
FFH = 5632
NJ = FFH // 128
NQ = 4
JQ = NJ // NQ
EPS = 1e-6


class T:
    __slots__ = ("name", "last_w", "readers", "excl")

    def __init__(self, name="", excl=False):
        self.name = name
        self.last_w = None
        self.readers = []
        self.excl = excl


class Op:
    __slots__ = ("eng", "fn", "deps", "is_dma", "sig", "sigval", "dsem", "dval", "dprev")


ENGS = ("pe", "act", "dve", "pool", "sp")
N_DMA_SEMS = {"pe": 1, "act": 1, "dve": 1, "pool": 16, "sp": 8}


class Sched:
    def __init__(self, nc):
        self.nc = nc
        self.ops = []
        self.dma_rr = {e: 0 for e in ENGS}
        self.dma_cnt = {}
        self.hook = None

    def add(self, eng, fn, reads=(), writes=(), dma=False):
        op = Op()
        op.eng = eng
        op.fn = fn
        op.is_dma = dma
        op.sig = False
        op.sigval = None
        deps = []
        excl_r = [t for t in reads if t.excl and t not in writes]
        reads = [t for t in reads if not t.excl]
        writes = list(writes) + excl_r
        for t in reads:
            if t.last_w is not None:
                deps.append(t.last_w)
        for t in writes:
            if t.last_w is not None:
                deps.append(t.last_w)
            deps.extend(t.readers)
        for t in reads:
            t.readers.append(op)
        for t in writes:
            t.last_w = op
            t.readers = []
        op.deps = [d for d in dict.fromkeys(deps) if d is not op]
        if dma:
            k = self.dma_rr[eng]
            self.dma_rr[eng] = (k + 1) % N_DMA_SEMS[eng]
            key = (eng, k)
            c = self.dma_cnt.get(key, 0) + 1
            self.dma_cnt[key] = c
            op.dsem = key
            op.dval = 16 * c
            op.dprev = 16 * (c - 1)
        self.ops.append(op)
        if self.hook is not None:
            self.hook()
        return op

    def emit(self):
        nc = self.nc
        ops = self.ops
        for op in ops:
            for d in op.deps:
                if d.is_dma:
                    continue
                if d.eng == op.eng and d.eng == "pe" and not op.is_dma:
                    continue
                d.sig = True
        cnt = {e: 0 for e in ENGS}
        for op in ops:
            if not op.is_dma and op.sig:
                cnt[op.eng] += 1
                op.sigval = cnt[op.eng]
        with contextlib.ExitStack() as st:
            esem = {e: st.enter_context(nc.semaphore("s_" + e)) for e in ENGS}
            dsem = {}
            for key in self.dma_cnt:
                dsem[key] = st.enter_context(nc.semaphore("d_%s%d" % key))
            block = st.enter_context(nc.Block())
            by_eng = {e: [o for o in ops if o.eng == e] for e in ENGS}

            def run(eng_name, eng):
                known = {}

                def wait(sem_key, sem, val):
                    if known.get(sem_key, 0) >= val:
                        return
                    known[sem_key] = val
                    eng.wait_ge(sem, val)

                for op in by_eng[eng_name]:
                    for d in op.deps:
                        if d.is_dma:
                            wait(d.dsem, dsem[d.dsem], d.dval)
                        elif d.sigval is not None:
                            wait(d.eng, esem[d.eng], d.sigval)
                    if op.is_dma:
                        if op.dprev > 0:
                            wait(op.dsem, dsem[op.dsem], op.dprev)
                        op.fn(eng).then_inc(dsem[op.dsem], 16)
                    else:
                        ins = op.fn(eng)
                        if op.sig:
                            ins.then_inc(esem[op.eng], 1)
                if eng_name == "sp":
                    for key, c in self.dma_cnt.items():
                        wait(key, dsem[key], 16 * c)

            block.tensor(lambda e: run("pe", e))
            block.scalar(lambda e: run("act", e))
            block.vector(lambda e: run("dve", e))
            block.gpsimd(lambda e: run("pool", e))
            block.sync(lambda e: run("sp", e))


class Ctx:
    def __init__(self):
        self.nc = bass.Bass("TRN2", target_bir_lowering=False)
        self.S = Sched(self.nc)
        self.st = contextlib.ExitStack()
        self.ps = []
        self.ps_i = 0
        self.n = 0
        self.tls = threading.local()
        self.pool_i = {}
        self.pool_banks = [[0, 1, 2, 3], [4, 5, 6, 7]]

    def din(self, name, shape, dt=F32):
        return self.nc.dram_tensor(name, list(shape), dt, kind="ExternalInput").ap()

    def dout(self, name, shape, dt=F32):
        return self.nc.dram_tensor(name, list(shape), dt, kind="ExternalOutput").ap()

    def sb(self, shape, dt, name=None):
        self.n += 1
        t = self.st.enter_context(self.nc.sbuf_tensor("%s_%d" % (name or "t", self.n), list(shape), dt))
        return t, T(name or "t")

    def init_psum(self):
        for i in range(8):
            t = self.st.enter_context(self.nc.psum_tensor("ps%d" % i, [128, 512], F32))
            self.ps.append((t, T("ps%d" % i, excl=True)))

    def psum(self):
        pool = getattr(self.tls, "pool", 0)
        if pool == 0:
            r = self.ps[self.ps_i]
            self.ps_i = (self.ps_i + 1) % 8
            return r
        banks = self.pool_banks[pool - 1]
        k = self.pool_i.get(pool, 0)
        self.pool_i[pool] = (k + 1) % len(banks)
        return self.ps[banks[k]]

    def finish(self):
        self.S.emit()
        self.st.close()
        return self.nc


def interleave(C, fns, banks=None):
    S = C.S
    n = len(fns)
    if n == 0:
        return
    if n == 1:
        fns[0]()
        return
    if banks is not None:
        C.pool_banks = banks
        C.pool_i = {}
    sems = [threading.Semaphore(0) for _ in range(n)]
    alive = [True] * n
    idx = {}
    err = []
    done = threading.Event()

    def nxt(i):
        for k in range(1, n + 1):
            j = (i + k) % n
            if alive[j]:
                return j
        return None

    def hook():
        i = idx[threading.get_ident()]
        j = nxt(i)
        if j is not None and j != i:
            sems[j].release()
            sems[i].acquire()

    def worker(i):
        sems[i].acquire()
        idx[threading.get_ident()] = i
        try:
            C.tls.pool = i + 1
            fns[i]()
        except BaseException as ex:
            err.append(ex)
        finally:
            alive[i] = False
            j = nxt(i)
            if j is not None:
                sems[j].release()
            else:
                done.set()

    ths = [threading.Thread(target=worker, args=(i,)) for i in range(n)]
    S.hook = hook
    for t in ths:
        t.start()
    sems[0].release()
    done.wait()
    for t in ths:
        t.join()
    S.hook = None
    if err:
        raise err[0]


def mm_group(S, out_ap, pairs, reads, writes):
    n = len(pairs)

    def fn(e):
        ins = None
        for i, (l, r) in enumerate(pairs):
            ins = e.matmul(out_ap, l, r, start=(i == 0), stop=(i == n - 1))
        return ins

    return S.add("pe", fn, reads=reads, writes=writes)


def load_consts(C, ones_ap):
    ones, t_ones = C.sb([128, 128], F32, "ones")
    C.S.add("sp", lambda e: e.dma_start(out=ones[:], in_=ones_ap), writes=[t_ones], dma=True)
    return ones, t_ones


def modulate(C, xT, t_x, hT, t_h, a_col, shift_col, t_par, ones, t_ones, ntok, out_dram=None):
    S = C.S
    sq = [C.sb([128, 512], F32, "sq") for _ in range(2)]
    rs, t_rs = C.sb([128, 512], F32, "rstd")
    tmp = [C.sb([128, 512], F32, "modtmp") for _ in range(2)]
    tmp32 = [C.sb([128, 512], F32, "modtmp32") for _ in range(2)] if out_dram is not None else None
    for tt in range(ntok // 512):
        sl = slice(tt * 512, (tt + 1) * 512)
        ps, t_ps = C.psum()
        for kc in range(KC):
            sqt, t_sq = sq[kc % 2]
            S.add("act", lambda e, sqt=sqt, kc=kc, sl=sl: e.activation(out=sqt[:], in_=xT[:, kc, sl], func=AF.Square),
                  reads=[t_x], writes=[t_sq])
            S.add("pe", lambda e, sqt=sqt, kc=kc, ps=ps: e.matmul(ps[:, :], ones[:], sqt[:], start=(kc == 0), stop=(kc == KC - 1)),
                  reads=[t_sq, t_ones], writes=[t_ps])
        S.add("act", lambda e, ps=ps: e.activation(out=rs[:], in_=ps[:, :], func=AF.Sqrt, scale=1.0 / D, bias=EPS),
              reads=[t_ps], writes=[t_rs])
        S.add("dve", lambda e: e.reciprocal(out=rs[:], in_=rs[:]), reads=[t_rs], writes=[t_rs])
        for kc in range(KC):
            tm, t_tm = tmp[kc % 2]
            S.add("dve", lambda e, tm=tm, kc=kc, sl=sl: e.scalar_tensor_tensor(
                out=tm[:], in0=xT[:, kc, sl], scalar=a_col[:, kc:kc + 1], in1=rs[:], op0=ALU.mult, op1=ALU.mult),
                reads=[t_x, t_rs] + t_par, writes=[t_tm])
            S.add("act", lambda e, tm=tm, kc=kc, sl=sl: e.activation(
                out=hT[:, kc, sl], in_=tm[:], func=AF.Identity, bias=shift_col[:, kc:kc + 1], scale=1.0),
                reads=[t_tm] + t_par, writes=[t_h])
            if out_dram is not None:
                t32, t_t32 = tmp32[kc % 2]
                S.add("act", lambda e, tm=tm, t32=t32, kc=kc: e.activation(
                    out=t32[:], in_=tm[:], func=AF.Identity, bias=shift_col[:, kc:kc + 1], scale=1.0),
                    reads=[t_tm] + t_par, writes=[t_t32])
                S.add("sp", lambda e, t32=t32, kc=kc, sl=sl: e.dma_start(out=out_dram[kc * 128:(kc + 1) * 128, sl], in_=t32[:]),
                      reads=[t_t32], dma=True)


def prep_mod_params(C, par_ap, n_sets):
    S = C.S
    par, t_par = C.sb([128, n_sets * 4 * KC], F32, "par")
    acol, t_a = C.sb([128, n_sets * KC], F32, "acol")
    S.add("sp", lambda e: e.dma_start(out=par[:], in_=par_ap), writes=[t_par], dma=True)
    out = []
    for s in range(n_sets):
        b = s * 4 * KC
        S.add("dve", lambda e, b=b, s=s: e.scalar_tensor_tensor(
            out=acol[:, s * KC:(s + 1) * KC], in0=par[:, b + 2 * KC:b + 3 * KC], scalar=1.0, in1=par[:, b:b + KC],
            op0=ALU.add, op1=ALU.mult), reads=[t_par], writes=[t_a])
        out.append((acol[:, s * KC:(s + 1) * KC], par[:, b + KC:b + 2 * KC], par[:, b + 3 * KC:b + 4 * KC]))
    return out, [t_par, t_a]


ADA_COLS = 2 * 18432 // NCORES
ADA_NB = ADA_COLS // 512


def build_ada():
    C = Ctx()
    S = C.S
    c_ap = C.din("c", [128, KC])
    w_ap = C.din("w", [ADA_NB, 128, KC, 512])
    b_ap = C.din("b", [1, ADA_COLS])
    o_ap = C.dout("o", [1, ADA_COLS])
    C.init_psum()
    ct, t_c = C.sb([128, KC], F32, "c")
    bt, t_b = C.sb([1, ADA_COLS], F32, "b")
    ot, t_o = C.sb([1, ADA_COLS], F32, "o")
    wt = [C.sb([128, KC, 512], F32, "w") for _ in range(2)]
    S.add("sp", lambda e: e.dma_start(out=ct[:], in_=c_ap), writes=[t_c], dma=True)
    S.add("sp", lambda e: e.dma_start(out=bt[:], in_=b_ap), writes=[t_b], dma=True)
    S.add("act", lambda e: e.activation(out=ct[:], in_=ct[:], func=AF.Silu), reads=[t_c], writes=[t_c])
    for nb in range(ADA_NB):
        w, t_w = wt[nb % 2]
        for kc in range(KC):
            S.add("sp", lambda e, w=w, nb=nb, kc=kc: e.dma_start(out=w[:, kc, :], in_=w_ap[nb, :, kc, :]), writes=[t_w], dma=True)
        ps, t_ps = C.psum()
        mm_group(S, ps[0:1, :], [(ct[:, kc:kc + 1], w[:, kc, :]) for kc in range(KC)], [t_c, t_w], [t_ps])
        S.add("dve", lambda e, ps=ps, nb=nb: e.tensor_tensor(
            out=ot[:, nb * 512:(nb + 1) * 512], in0=ps[0:1, :], in1=bt[:, nb * 512:(nb + 1) * 512], op=ALU.add),
            reads=[t_ps, t_b], writes=[t_o])
    S.add("sp", lambda e: e.dma_start(out=o_ap, in_=ot[:]), reads=[t_o], dma=True)
    return C.finish()


def build_ffn(emit_h_next, dbg=None):
    C = Ctx()
    S = C.S
    nsets = 2 if emit_h_next else 1
    x_ap = C.din("xT", [D, TOK])
    par_ap = C.din("par", [128, nsets * 4 * KC])
    win_ap = C.din("win", [NJ, 128, KC, 256])
    wout_ap = C.din("wout", [NQ, KC, 128, JQ, 128])
    ones_ap = C.din("ones", [128, 128])
    xo_ap = C.dout("xo", [D, TOK])
    ho_ap = C.dout("ho", [D, TOK]) if emit_h_next else None
    C.init_psum()
    ones, t_ones = load_consts(C, ones_ap)
    xT, t_x = C.sb([128, KC, TOK], F32, "xT")
    hT, t_h = C.sb([128, KC, TOK], BF16, "hT")
    actT, t_act = C.sb([128, JQ, TOK], BF16, "actT")
    wi = [C.sb([128, KC, 256], BF16, "wi") for _ in range(2)]
    wo = [C.sb([128, JQ, 128], BF16, "wo") for _ in range(2)]
    sg = [C.sb([128, 512], F32, "sg") for _ in range(2)]
    g05, t_g = C.sb([128, KC], F32, "g05")
    t_xk = [T("xk%d" % kc) for kc in range(KC)]
    for kc in range(KC):
        S.add("sp", lambda e, kc=kc: e.dma_start(out=xT[:, kc, :], in_=x_ap[kc * 128:(kc + 1) * 128, :]),
              writes=[t_xk[kc]], dma=True)
    sets, t_par = prep_mod_params(C, par_ap, nsets)
    a_col, shift_col, gate_col = sets[0]
    S.add("dve", lambda e: e.tensor_scalar(out=g05[:], in0=gate_col, scalar1=0.5, scalar2=None, op0=ALU.mult),
          reads=t_par + t_xk, writes=[t_g, t_x])
    modulate(C, xT, t_x, hT, t_h, a_col, shift_col, t_par, ones, t_ones, TOK, out_dram=(ho_ap if dbg == "mod" else None))
    if dbg == "mod":
        for kc in range(KC):
            S.add("sp", lambda e, kc=kc: e.dma_start(out=xo_ap[kc * 128:(kc + 1) * 128, :], in_=xT[:, kc, :]),
                  reads=[t_x], dma=True)
        return C.finish()
    nwi = 0
    nwo = 0
    nsg = 0
    for qh in range(NQ):
        for jj in range(JQ):
            j = qh * JQ + jj
            w, t_w = wi[nwi % 2]
            nwi += 1
            S.add("pool", lambda e, w=w, j=j: e.dma_start(out=w[:], in_=win_ap[j]), writes=[t_w], dma=True)
            for tt in range(TOK // 512):
                sl = slice(tt * 512, (tt + 1) * 512)
                pg, t_pg = C.psum()
                pu, t_pu = C.psum()
                mm_group(S, pg[:, :], [(w[:, kc, 0:128], hT[:, kc, sl]) for kc in range(KC)], [t_w, t_h], [t_pg])
                mm_group(S, pu[:, :], [(w[:, kc, 128:256], hT[:, kc, sl]) for kc in range(KC)], [t_w, t_h], [t_pu])
                s_, t_s = sg[nsg % 2]
                nsg += 1
                S.add("act", lambda e, s_=s_, pg=pg: e.activation(out=s_[:], in_=pg[:, :], func=AF.Silu),
                      reads=[t_pg], writes=[t_s])
                S.add("dve", lambda e, s_=s_, pu=pu, jj=jj, sl=sl: e.tensor_tensor(
                    out=actT[:, jj, sl], in0=s_[:], in1=pu[:, :], op=ALU.mult),
                    reads=[t_s, t_pu], writes=[t_act])
        for m in range(KC):
            w, t_w = wo[nwo % 2]
            nwo += 1
            S.add("pool", lambda e, w=w, qh=qh, m=m: e.dma_start(out=w[:], in_=wout_ap[qh, m]), writes=[t_w], dma=True)
            for tt in range(TOK // 512):
                sl = slice(tt * 512, (tt + 1) * 512)
                po, t_po = C.psum()
                mm_group(S, po[:, :], [(w[:, jj, :], actT[:, jj, sl]) for jj in range(JQ)], [t_w, t_act], [t_po])
                S.add("dve", lambda e, po=po, m=m, sl=sl: e.scalar_tensor_tensor(
                    out=xT[:, m, sl], in0=po[:, :], scalar=g05[:, m:m + 1], in1=xT[:, m, sl], op0=ALU.mult, op1=ALU.add),
                    reads=[t_po, t_g, t_x], writes=[t_x])
    for kc in range(KC):
        S.add("sp", lambda e, kc=kc: e.dma_start(out=xo_ap[kc * 128:(kc + 1) * 128, :], in_=xT[:, kc, :]),
              reads=[t_x], dma=True)
    if emit_h_next:
        a2, sh2, _ = sets[1]
        modulate(C, xT, t_x, hT, t_h, a2, sh2, t_par, ones, t_ones, TOK, out_dram=ho_ap)
    return C.finish()


def cols16(v):
    return np.ascontiguousarray(np.asarray(v, np.float32).reshape(KC, 128).T)


def tile_win(w_in):
    w = w_in.reshape(KC, 128, 2, NJ, 128)
    return np.ascontiguousarray(w.transpose(3, 1, 0, 2, 4).reshape(NJ, 128, KC, 256))


def tile_wout(w_out):
    w = w_out.reshape(NQ, JQ, 128, KC, 128)
    return np.ascontiguousarray(w.transpose(0, 3, 2, 1, 4))


_PROG = {}


def prog(name, fn, *a):
    key = (name,) + a
    if key not in _PROG:
        _PROG[key] = fn(*a)
    return _PROG[key]


def run(nc, in_maps):
    res = run_bass_kernel_spmd(nc, in_maps, core_ids=list(range(NCORES)))
    return res.results


def compute_ada(c, w_ada, b_ada):
    cc = cols16(c.reshape(-1))
    wa = np.concatenate([w_ada[0], w_ada[1]], axis=1)
    ba = np.concatenate([b_ada[0], b_ada[1]], axis=0)
    in_maps = []
    for i in range(NCORES):
        ws = wa[:, i * ADA_COLS:(i + 1) * ADA_COLS].reshape(KC, 128, ADA_NB, 512)
        in_maps.append({
            "c": cc,
            "w": np.ascontiguousarray(ws.transpose(2, 1, 0, 3)),
            "b": np.ascontiguousarray(ba[i * ADA_COLS:(i + 1) * ADA_COLS].reshape(1, -1)),
        })
    r = run(prog("ada", build_ada), in_maps)
    ada = np.concatenate([r[i]["o"].reshape(-1) for i in range(NCORES)])
    return ada.reshape(2, 3, 3, D)


def par_block(norm_gain_li, ada_li):
    return np.concatenate([cols16(norm_gain_li), cols16(ada_li[0]), cols16(ada_li[1]), cols16(ada_li[2])], axis=1)


def run_ffn(xT, par, w_in, w_out, emit_h_next, dbg=None):
    win_t = tile_win(w_in)
    wout_t = tile_wout(w_out)
    ones = np.ones((128, 128), np.float32)
    in_maps = []
    for i in range(NCORES):
        in_maps.append({"xT": np.ascontiguousarray(xT[:, i * TOK:(i + 1) * TOK]), "par": par,
                        "win": win_t, "wout": wout_t, "ones": ones})
    r = run(prog("ffn", build_ffn, emit_h_next, dbg), in_maps)
    xo = np.concatenate([r[i]["xo"] for i in range(NCORES)], axis=1)
    ho = np.concatenate([r[i]["ho"] for i in range(NCORES)], axis=1) if emit_h_next else None
    return xo, ho


NCOLB = 1540
C_Q, C_K, C_Z, C_XS, C_B, C_C, C_CB, C_CC, C_CX, C_V = 0, 128, 256, 512, 768, 896, 1024, 1152, 1280, 1408
SCALE = 128 ** -0.5
NEG = -1e30


PARTS = ('conv', 'ssd', 'attn')
STOP = 0


def build_mixB(NT):
    C = Ctx()
    S = C.S
    ntok = NT * 512
    h_ap = C.din("hT", [D, ntok])
    w_ap = C.din("wB", [128, KC, NCOLB])
    sp_ap = C.din("sp", [128, 24])
    sp64_ap = C.din("sp64", [64, 24])
    cst_ap = C.din("cst", [128, 5 * 128 + 512 + 512])
    ya_ap = C.dout("ya", [ntok, 128])
    yg_ap = C.dout("yg", [256, ntok])
    yc_ap = C.dout("yc", [128, ntok])
    C.init_psum()
    cst, t_cst = C.sb([128, 5 * 128 + 1024], F32, "cst")
    S.add("sp", lambda e: e.dma_start(out=cst[:], in_=cst_ap), writes=[t_cst], dma=True)
    ident = cst[:, 0:128]
    ones = cst[:, 128:256]
    TA = cst[:, 256:384]
    TB = cst[:, 384:512]
    NEGM = cst[:, 512:640]
    U0 = cst[:, 640:896]
    UTRI = cst[:, 640:768]
    U1 = cst[:, 896:1152]
    SELH = cst[0:4, 1152:1664]
    spt, t_sp = C.sb([128, 24], F32, "sp")
    sp64, t_sp64 = C.sb([64, 24], F32, "sp64")
    S.add("sp", lambda e: e.dma_start(out=spt[:], in_=sp_ap), writes=[t_sp], dma=True)
    S.add("sp", lambda e: e.dma_start(out=sp64[:], in_=sp64_ap), writes=[t_sp64], dma=True)
    S.add("dve", lambda e: e.tensor_tensor(out=TA, in0=TA, in1=NEGM, op=ALU.add), reads=[t_cst], writes=[t_cst])
    aneg, t_an = C.sb([128, 4], F32, "aneg")
    S.add("act", lambda e: e.activation(out=aneg[:], in_=spt[:, 20:24], func=AF.Exp), reads=[t_sp], writes=[t_an])
    S.add("dve", lambda e: e.tensor_scalar(out=aneg[:], in0=aneg[:], scalar1=-1.0, scalar2=None, op0=ALU.mult),
          reads=[t_an], writes=[t_an])
    wB, _ = C.sb([128, KC, NCOLB], BF16, "wB")
    t_wB = [T("wB%d" % kc) for kc in range(KC)]
    for kc in range(KC):
        S.add("pool", lambda e, kc=kc: e.dma_start(out=wB[:, kc, :], in_=w_ap[:, kc, :]), writes=[t_wB[kc]], dma=True)
    hbuf = [C.sb([128, KC, 512], BF16, "hb")[0] for _ in range(2)]
    t_hb = [[T("hb") for _ in range(KC)] for _ in range(2)]
    KT, t_KT = C.sb([128, ntok], BF16, "KT")
    vaug, t_v = C.sb([128, NT * 4, 130], BF16, "vaug")
    S.add("pool", lambda e: e.memset(vaug[:], 1.0), writes=[t_v])
    kmean, t_km = C.sb([128, 32], F32, "kmean")
    S.add("pool", lambda e: e.memset(kmean[:], 0.0), writes=[t_km])

    def sbt(shape, dt, name):
        return C.sb(shape, dt, name)

    sq, t_sq = sbt([128, 512], F32, "sq")
    rs, t_rs = sbt([128, 512], F32, "rs")
    qn32, t_qn = sbt([128, 512], F32, "qn32")
    qnb, t_qnb = sbt([128, 512], BF16, "qnb")
    kn32, t_kn = sbt([128, 2, 256], F32, "kn32")
    kms, t_kms = sbt([128, 2], F32, "kms")
    sz = [sbt([64, 512], F32, "sz") for _ in range(4)]
    xpre = [sbt([64, 515], F32, "xpre") for _ in range(4)]
    bpre, t_bpre = sbt([128, 515], F32, "bpre")
    cpre, t_cpre = sbt([128, 515], F32, "cpre")
    for (t_, tt_) in xpre + [(bpre, t_bpre), (cpre, t_cpre)]:
        S.add("pool", lambda e, t_=t_: e.memset(t_[:, 0:3], 0.0), writes=[tt_])
    xsT = [sbt([64, 512], F32, "xsT") for _ in range(4)]
    BT32, t_BT32 = sbt([128, 512], F32, "BT32")
    CT32, t_CT32 = sbt([128, 512], F32, "CT32")
    BTb, t_BTb = sbt([128, 512], BF16, "BTb")
    CTb, t_CTb = sbt([128, 512], BF16, "CTb")
    cacc = [sbt([128, 512], F32, "cacc") for _ in range(2)]
    cBs, t_cBs = sbt([128, 512], F32, "cBs")
    cCs, t_cCs = sbt([128, 512], F32, "cCs")
    ucv, t_ucv = sbt([128, 514], F32, "ucv")
    S.add("pool", lambda e: e.memset(ucv[:, 0:2], 0.0), writes=[t_ucv])
    ycs, t_ycs = sbt([128, 512], F32, "ycs")
    dtraw, t_dtr = sbt([128, 4, 4], F32, "dtraw")
    dtt, t_dtt = sbt([128, 4, 4], F32, "dtt")
    dtA, t_dtA = sbt([128, 4, 4], F32, "dtA")
    wd, t_wd = sbt([128, 4, 4], F32, "wd")
    xd = [sbt([128, 256], BF16, "xd") for _ in range(4)]
    xdd = [sbt([128, 256], BF16, "xdd") for _ in range(4)]
    Btok = [sbt([128, 128], BF16, "Btok") for _ in range(4)]
    acs, t_acs = sbt([128, 12], F32, "acs")
    d2, t_d2 = sbt([128, 8], F32, "d2")
    acsT, t_acsT = sbt([4, 256], F32, "acsT")
    cbm, t_cbm = sbt([128, 384], F32, "cbm")
    E = [sbt([128, 256], F32, "E") for _ in range(4)]
    Cdec, t_Cdec = sbt([128, 256], BF16, "Cdec")
    Dm, t_Dm = sbt([128, 384], F32, "Dm")
    MT, t_MT = sbt([128, 384], BF16, "MT")
    prev32, t_p32 = sbt([128, 256], F32, "prev32")
    prevb, t_pb = sbt([128, 256], BF16, "prevb")
    S.add("pool", lambda e: e.memset(prev32[:], 0.0), writes=[t_p32])
    S.add("pool", lambda e: e.memset(prevb[:], 0.0), writes=[t_pb])
    yt1, t_yt1 = sbt([64, 256], F32, "yt1")
    ygs = [sbt([64, 256], F32, "ygs") for _ in range(2)]
    gsbL = [sbt([128, 32], F32, "gsb") for _ in range(4)]
    top8L = [sbt([128, 8], F32, "top8") for _ in range(4)]
    mselL = [sbt([128, 32], F32, "msel") for _ in range(4)]
    accL = [sbt([128, 130], F32, "acc") for _ in range(4)]
    tmpS2L = [[sbt([128, 256], F32, "tmpS")] * 2 for _ in range(4)]
    pt2L = [[sbt([128, 2, 128], BF16, "pt") for _ in range(2)] for _ in range(4)]
    nptL = [0, 0, 0, 0]
    recL = [sbt([128, 1], F32, "rec") for _ in range(4)]
    yqL = [[sbt([128, 128], F32, "yq")] * 2 for _ in range(4)]
    nyqL = [0, 0, 0, 0]
    nyg = 0
    nyq = 0
    nosb = 0
    if STOP == 1:
        return C.finish()

    def load_h(tt_):
        hb_ = hbuf[tt_ % 2]
        for kc in range(KC):
            S.add("pool", lambda e, kc=kc, hb_=hb_, t0=tt_ * 512: e.dma_start(
                out=hb_[:, kc, :], in_=h_ap[kc * 128:(kc + 1) * 128, t0:t0 + 512]), writes=[t_hb[tt_ % 2][kc]], dma=True)

    load_h(0)
    for tt in range(NT):
        tok0 = tt * 512
        hb = hbuf[tt % 2]
        thb = t_hb[tt % 2]

        def proj(col0, M):
            ps, t_ps = C.psum()
            mm_group(S, ps[0:M, :], [(wB[:, kc, col0:col0 + M], hb[:, kc, :]) for kc in range(KC)], thb + t_wB, [t_ps])
            return ps, t_ps

        def sec_qk():
            for which in range(2):
                ps, t_ps = proj(C_Q if which == 0 else C_K, 128)
                S.add("act", lambda e, ps=ps: e.activation(out=sq[:], in_=ps[:, :], func=AF.Square), reads=[t_ps], writes=[t_sq])
                ps2, t_ps2 = C.psum()
                S.add("pe", lambda e, ps2=ps2: e.matmul(ps2[:, :], ones, sq[:], start=True, stop=True),
                      reads=[t_sq, t_cst], writes=[t_ps2])
                S.add("act", lambda e, ps2=ps2: e.activation(out=rs[:], in_=ps2[:, :], func=AF.Sqrt, scale=1.0 / 128, bias=EPS),
                      reads=[t_ps2], writes=[t_rs])
                S.add("dve", lambda e: e.reciprocal(out=rs[:], in_=rs[:]), reads=[t_rs], writes=[t_rs])
                if which == 0:
                    S.add("dve", lambda e, ps=ps: e.scalar_tensor_tensor(
                        out=qn32[:], in0=ps[:, :], scalar=spt[:, 0:1], in1=rs[:], op0=ALU.mult, op1=ALU.mult),
                        reads=[t_ps, t_rs, t_sp], writes=[t_qn])
                    S.add("act", lambda e: e.activation(out=qnb[:], in_=qn32[:], func=AF.Identity), reads=[t_qn], writes=[t_qnb])
                else:
                    for a in range(2):
                        S.add("dve", lambda e, ps=ps, a=a: e.scalar_tensor_tensor(
                            out=kn32[:, a, :], in0=ps[:, a * 256:(a + 1) * 256], scalar=spt[:, 1:2], in1=rs[:, a * 256:(a + 1) * 256],
                            op0=ALU.mult, op1=ALU.mult), reads=[t_ps, t_rs, t_sp], writes=[t_kn])
                        S.add("act", lambda e, a=a, tok0=tok0: e.activation(
                            out=KT[:, tok0 + a * 256:tok0 + (a + 1) * 256], in_=kn32[:, a, :], func=AF.Identity),
                            reads=[t_kn], writes=[t_KT])
                    S.add("dve", lambda e: e.tensor_reduce(out=kms[:], in_=kn32[:], axis=AX.X, op=ALU.add), reads=[t_kn], writes=[t_kms])
                    S.add("dve", lambda e, tt=tt: e.tensor_scalar(
                        out=kmean[:, 2 * tt:2 * tt + 2], in0=kms[:], scalar1=1.0 / 256, scalar2=None, op0=ALU.mult),
                        reads=[t_kms], writes=[t_km])

        def sec_rest():
            for h in range(4):
                ps, t_ps = proj(C_Z + h * 64, 64)
                S.add("act", lambda e, ps=ps, h=h: e.activation(out=sz[h][0][:], in_=ps[0:64, :], func=AF.Silu),
                      reads=[t_ps], writes=[sz[h][1]])
            for h in range(4):
                ps, t_ps = proj(C_XS + h * 64, 64)
                S.add("act", lambda e, ps=ps, h=h: e.activation(out=xpre[h][0][:, 3:515], in_=ps[0:64, :], func=AF.Identity),
                      reads=[t_ps], writes=[xpre[h][1]])
            ps, t_ps = proj(C_B, 128)
            S.add("act", lambda e, ps=ps: e.activation(out=bpre[:, 3:515], in_=ps[:, :], func=AF.Identity), reads=[t_ps], writes=[t_bpre])
            ps, t_ps = proj(C_C, 128)
            S.add("act", lambda e, ps=ps: e.activation(out=cpre[:, 3:515], in_=ps[:, :], func=AF.Identity), reads=[t_ps], writes=[t_cpre])
            ps, t_ps = proj(C_CB, 128)
            S.add("act", lambda e, ps=ps: e.activation(out=cBs[:], in_=ps[:, :], func=AF.Identity), reads=[t_ps], writes=[t_cBs])
            ps, t_ps = proj(C_CC, 128)
            S.add("act", lambda e, ps=ps: e.activation(out=cCs[:], in_=ps[:, :], func=AF.Identity), reads=[t_ps], writes=[t_cCs])
            ps, t_ps = proj(C_CX, 128)
            S.add("dve", lambda e, ps=ps: e.tensor_tensor(out=ucv[:, 2:514], in0=cCs[:], in1=ps[:, :], op=ALU.mult),
                  reads=[t_ps, t_cCs], writes=[t_ucv])
            for s in range(4):
                ps, t_ps = C.psum()
                mm_group(S, ps[:, 0:128], [(hb[:, kc, s * 128:(s + 1) * 128], wB[:, kc, C_V:C_V + 128]) for kc in range(KC)],
                         thb + t_wB, [t_ps])
                if STOP != 8:
                    mm_group(S, ps[:, 128:132], [(hb[:, kc, s * 128:(s + 1) * 128], wB[:, kc, C_V + 128:C_V + 132]) for kc in range(KC)],
                             thb + t_wB, [t_ps])
                if STOP == 4:
                    continue
                S.add("act", lambda e, ps=ps, s=s, tt=tt: e.activation(out=vaug[:, 4 * tt + s, 0:128], in_=ps[:, 0:128], func=AF.Identity),
                      reads=[t_ps], writes=[t_v])
                if STOP in (5, 7):
                    continue
                S.add("act", lambda e, ps=ps, s=s: e.activation(out=dtraw[:, s, :], in_=ps[:, 128:132], func=AF.Identity),
                      reads=[t_ps], writes=[t_dtr])
                S.add("pool", lambda e, s=s: e.tensor_tensor(out=dtraw[:, s, :], in0=dtraw[:, s, :], in1=spt[:, 16:20], op=ALU.add),
                      reads=[t_dtr, t_sp], writes=[t_dtr])
                if STOP == 6:
                    continue

        interleave(C, [sec_qk, sec_rest], banks=[[0, 1, 2], [3, 4, 5, 6, 7]])
        if tt + 1 < NT:
            load_h(tt + 1)
        if 'conv' in PARTS:
            a0, t_a0 = cacc[0]
            S.add("dve", lambda e: e.tensor_scalar(out=a0[:], in0=ucv[:, 0:512], scalar1=spt[:, 13:14], scalar2=None, op0=ALU.mult),
                  reads=[t_ucv, t_sp], writes=[t_a0])
            for j in (1, 2):
                S.add("dve", lambda e, j=j: e.scalar_tensor_tensor(
                    out=a0[:], in0=ucv[:, j:j + 512], scalar=spt[:, 13 + j:14 + j], in1=a0[:], op0=ALU.mult, op1=ALU.add),
                    reads=[t_ucv, t_sp, t_a0], writes=[t_a0])
            S.add("dve", lambda e: e.tensor_tensor(out=ycs[:], in0=cBs[:], in1=a0[:], op=ALU.mult), reads=[t_cBs, t_a0], writes=[t_ycs])
            S.add("sp", lambda e, tok0=tok0: e.dma_start(out=yc_ap[:, tok0:tok0 + 512], in_=ycs[:]), reads=[t_ycs], dma=True)
            S.add("dve", lambda e: e.tensor_copy(out=ucv[:, 0:2], in_=ucv[:, 512:514]), reads=[t_ucv], writes=[t_ucv])
        def sec_ssd():
            nonlocal nyg
            def ssd_conv(pre, t_pre, P, wsrc, t_w, c0, outs):
                ac, t_ac = cacc[1]
                S.add("dve", lambda e: e.tensor_scalar(out=ac[0:P, :], in0=pre[:, 0:512], scalar1=wsrc[:, c0:c0 + 1], scalar2=None, op0=ALU.mult),
                      reads=[t_pre, t_w], writes=[t_ac])
                for j in (1, 2, 3):
                    S.add("dve", lambda e, j=j: e.scalar_tensor_tensor(
                        out=ac[0:P, :], in0=pre[:, j:j + 512], scalar=wsrc[:, c0 + j:c0 + j + 1], in1=ac[0:P, :], op0=ALU.mult, op1=ALU.add),
                        reads=[t_pre, t_w, t_ac], writes=[t_ac])
                o, t_o = outs
                S.add("act", lambda e: e.activation(out=o[:], in_=ac[0:P, :], func=AF.Silu, bias=wsrc[:, c0 + 4:c0 + 5], scale=1.0),
                      reads=[t_ac, t_w], writes=[t_o])
                S.add("dve", lambda e: e.tensor_copy(out=pre[:, 0:3], in_=pre[:, 512:515]), reads=[t_pre], writes=[t_pre])

            for h in range(4):
                ssd_conv(xpre[h][0], xpre[h][1], 64, sp64, t_sp64, h * 6, xsT[h])
            ssd_conv(bpre, t_bpre, 128, spt, t_sp, 3, (BT32, t_BT32))
            ssd_conv(cpre, t_cpre, 128, spt, t_sp, 8, (CT32, t_CT32))
            S.add("pool", lambda e: e.tensor_copy(out=BTb[:], in_=BT32[:]), reads=[t_BT32], writes=[t_BTb])
            S.add("pool", lambda e: e.tensor_copy(out=CTb[:], in_=CT32[:]), reads=[t_CT32], writes=[t_CTb])
            S.add("act", lambda e: e.activation(out=dtt[:], in_=dtraw[:], func=AF.Exp), reads=[t_dtr], writes=[t_dtt])
            S.add("act", lambda e: e.activation(out=dtt[:], in_=dtt[:], func=AF.Ln, bias=1.0, scale=1.0), reads=[t_dtt], writes=[t_dtt])
            for s in range(4):
                S.add("dve", lambda e, s=s: e.tensor_tensor(out=dtA[:, s, :], in0=dtt[:, s, :], in1=aneg[:], op=ALU.mult),
                      reads=[t_dtt, t_an], writes=[t_dtA])
            for cc in range(2):
                ccol = slice(cc * 256, (cc + 1) * 256)
                s0, s1 = 2 * cc, 2 * cc + 1
                psa, t_psa = C.psum()
                S.add("pe", lambda e, psa=psa, s0=s0: e.matmul(psa[:, 0:4], UTRI, dtA[:, s0, :], start=True, stop=True),
                      reads=[t_dtA, t_cst], writes=[t_psa])
                mm_group(S, psa[:, 4:8], [(ones, dtA[:, s0, :]), (UTRI, dtA[:, s1, :])], [t_dtA, t_cst], [t_psa])
                mm_group(S, psa[:, 8:12], [(ones, dtA[:, s0, :]), (ones, dtA[:, s1, :])], [t_dtA, t_cst], [t_psa])
                S.add("dve", lambda e, psa=psa: e.tensor_copy(out=acs[:], in_=psa[:, 0:12]), reads=[t_psa], writes=[t_acs])
                for s in range(2):
                    S.add("dve", lambda e, s=s: e.tensor_tensor(out=d2[:, 4 * s:4 * s + 4], in0=acs[:, 8:12], in1=acs[:, 4 * s:4 * s + 4], op=ALU.subtract),
                          reads=[t_acs], writes=[t_d2])
                S.add("act", lambda e: e.activation(out=d2[:], in_=d2[:], func=AF.Exp), reads=[t_d2], writes=[t_d2])
                for s in range(2):
                    S.add("dve", lambda e, s=s, cc=cc: e.tensor_tensor(out=wd[:, 2 * cc + s, :], in0=dtt[:, 2 * cc + s, :], in1=d2[:, 4 * s:4 * s + 4], op=ALU.mult),
                          reads=[t_dtt, t_d2], writes=[t_wd])
                pst, t_pst = C.psum()
                mm_group(S, pst[0:4, 0:256], [(dtA[:, s0, :], U0), (dtA[:, s1, :], U1)], [t_dtA, t_cst], [t_pst])
                S.add("dve", lambda e, pst=pst: e.tensor_copy(out=acsT[:], in_=pst[0:4, 0:256]), reads=[t_pst], writes=[t_acsT])
                for s in (s0, s1):
                    psx, t_psx = C.psum()

                    def tfn(e, psx=psx, s=s):
                        ins = None
                        for h in range(4):
                            ins = e.transpose(out=psx[:, h * 64:(h + 1) * 64], in_=xsT[h][0][:, s * 128:(s + 1) * 128], identity=cst[0:64, 0:64])
                        return ins
                    S.add("pe", tfn, reads=[x[1] for x in xsT] + [t_cst], writes=[t_psx])
                    for h in range(4):
                        S.add("dve", lambda e, psx=psx, s=s, h=h: e.tensor_scalar(
                            out=xd[s][0][:, h * 64:(h + 1) * 64], in0=psx[:, h * 64:(h + 1) * 64], scalar1=dtt[:, s, h:h + 1], scalar2=None, op0=ALU.mult),
                            reads=[t_psx, t_dtt], writes=[xd[s][1]])
                        S.add("dve", lambda e, psx=psx, s=s, h=h: e.tensor_scalar(
                            out=xdd[s][0][:, h * 64:(h + 1) * 64], in0=psx[:, h * 64:(h + 1) * 64], scalar1=wd[:, s, h:h + 1], scalar2=None, op0=ALU.mult),
                            reads=[t_psx, t_wd], writes=[xdd[s][1]])
                    psb, t_psb = C.psum()
                    S.add("pe", lambda e, psb=psb, s=s: e.transpose(out=psb[:, 0:128], in_=BT32[:, s * 128:(s + 1) * 128], identity=ident),
                          reads=[t_BT32, t_cst], writes=[t_psb])
                    S.add("act", lambda e, psb=psb, s=s: e.activation(out=Btok[s][0][:], in_=psb[:, 0:128], func=AF.Identity),
                          reads=[t_psb], writes=[Btok[s][1]])
                psc, t_psc = C.psum()
                S.add("pe", lambda e, psc=psc, cc=cc: e.matmul(psc[:, 0:256], BTb[:, cc * 256:cc * 256 + 128], CTb[:, cc * 256:cc * 256 + 256], start=True, stop=True),
                      reads=[t_BTb, t_CTb], writes=[t_psc])
                S.add("pe", lambda e, psc=psc, cc=cc: e.matmul(psc[:, 256:384], BTb[:, cc * 256 + 128:cc * 256 + 256], CTb[:, cc * 256 + 128:cc * 256 + 256], start=True, stop=True),
                      reads=[t_BTb, t_CTb], writes=[t_psc])
                S.add("dve", lambda e, psc=psc: e.tensor_tensor(out=cbm[:, 0:256], in0=psc[:, 0:256], in1=U0, op=ALU.mult),
                      reads=[t_psc, t_cst], writes=[t_cbm])
                S.add("dve", lambda e, psc=psc: e.tensor_tensor(out=cbm[:, 256:384], in0=psc[:, 256:384], in1=UTRI, op=ALU.mult),
                      reads=[t_psc, t_cst], writes=[t_cbm])
                for h in range(4):
                    psbc, t_psbc = C.psum()
                    S.add("pe", lambda e, psbc=psbc, h=h: e.matmul(psbc[:, 0:256], SELH[:, h * 128:(h + 1) * 128], acsT[:], start=True, stop=True),
                          reads=[t_acsT, t_cst], writes=[t_psbc])
                    Eh, t_Eh = E[h]
                    S.add("act", lambda e, psbc=psbc, Eh=Eh: e.activation(out=Eh[:], in_=psbc[:, 0:256], func=AF.Exp), reads=[t_psbc], writes=[t_Eh])
                    S.add("pool", lambda e, Eh=Eh, ccol=ccol: e.tensor_tensor(out=Cdec[:], in0=CT32[:, ccol], in1=Eh[:], op=ALU.mult),
                          reads=[t_CT32, t_Eh], writes=[t_Cdec])
                    S.add("dve", lambda e, psbc=psbc, h=h: e.tensor_scalar(
                        out=Dm[:, 0:256], in0=psbc[:, 0:256], scalar1=acs[:, h:h + 1], scalar2=0.0, op0=ALU.subtract, op1=ALU.min),
                        reads=[t_psbc, t_acs], writes=[t_Dm])
                    S.add("dve", lambda e, psbc=psbc, h=h: e.tensor_scalar(
                        out=Dm[:, 256:384], in0=psbc[:, 128:256], scalar1=acs[:, 4 + h:5 + h], scalar2=0.0, op0=ALU.subtract, op1=ALU.min),
                        reads=[t_psbc, t_acs], writes=[t_Dm])
                    S.add("act", lambda e: e.activation(out=Dm[:], in_=Dm[:], func=AF.Exp), reads=[t_Dm], writes=[t_Dm])
                    S.add("dve", lambda e: e.tensor_tensor(out=MT[:], in0=Dm[:], in1=cbm[:], op=ALU.mult), reads=[t_Dm, t_cbm], writes=[t_MT])
                    psy, t_psy = C.psum()
                    hs = slice(h * 64, (h + 1) * 64)

                    def yfn(e, psy=psy, hs=hs, s0=s0, s1=s1):
                        e.matmul(psy[0:64, 0:256], prevb[:, hs], Cdec[:], start=True, stop=False)
                        e.matmul(psy[0:64, 0:256], xd[s0][0][:, hs], MT[:, 0:256], start=False, stop=False)
                        return e.matmul(psy[0:64, 128:256], xd[s1][0][:, hs], MT[:, 256:384], start=False, stop=True)
                    S.add("pe", yfn, reads=[t_pb, t_Cdec, xd[s0][1], xd[s1][1], t_MT], writes=[t_psy])
                    S.add("dve", lambda e, psy=psy, h=h, ccol=ccol: e.scalar_tensor_tensor(
                        out=yt1[:], in0=xsT[h][0][:, ccol], scalar=sp64[:, h * 6 + 5:h * 6 + 6], in1=psy[0:64, 0:256], op0=ALU.mult, op1=ALU.add),
                        reads=[xsT[h][1], t_sp64, t_psy], writes=[t_yt1])
                    yg, t_yg = ygs[nyg % 2]
                    nyg += 1
                    S.add("pool", lambda e, yg=yg, h=h, ccol=ccol: e.tensor_tensor(out=yg[:], in0=yt1[:], in1=sz[h][0][:, ccol], op=ALU.mult),
                          reads=[t_yt1, sz[h][1]], writes=[t_yg])
                    S.add("sp", lambda e, yg=yg, h=h, tok0=tok0, cc=cc: e.dma_start(
                        out=yg_ap[h * 64:(h + 1) * 64, tok0 + cc * 256:tok0 + (cc + 1) * 256], in_=yg[:]), reads=[t_yg], dma=True)
                pss, t_pss = C.psum()
                mm_group(S, pss[:, 0:256], [(Btok[s0][0][:], xdd[s0][0][:]), (Btok[s1][0][:], xdd[s1][0][:])],
                         [Btok[s0][1], Btok[s1][1], xdd[s0][1], xdd[s1][1]], [t_pss])
                for h in range(4):
                    hs = slice(h * 64, (h + 1) * 64)
                    S.add("dve", lambda e, h=h, hs=hs, pss=pss: e.scalar_tensor_tensor(
                        out=prev32[:, hs], in0=prev32[:, hs], scalar=E[h][0][:, 255:256], in1=pss[:, hs], op0=ALU.mult, op1=ALU.add),
                        reads=[t_p32, E[h][1], t_pss], writes=[t_p32])
                S.add("act", lambda e: e.activation(out=prevb[:], in_=prev32[:], func=AF.Identity), reads=[t_p32], writes=[t_pb])
        def sec_attn(qis=(0, 1, 2, 3), bs=0):
            gsb, t_gsb = gsbL[bs]
            top8, t_top8 = top8L[bs]
            msel, t_msel = mselL[bs]
            acc, t_acc = accL[bs]
            rec, t_rec = recL[bs]
            pt2 = pt2L[bs]
            tmpS2 = tmpS2L[bs]
            yq = yqL[bs]
            for i in qis:
                qi = 4 * tt + i
                J = qi // 2
                eo = qi % 2
                qc = slice(i * 128, (i + 1) * 128)
                use_sel = J > 3
                if use_sel:
                    psg, t_psg = C.psum()
                    S.add("pe", lambda e, psg=psg, qc=qc: e.matmul(psg[:, 0:32], qn32[:, qc], kmean[:], start=True, stop=True),
                          reads=[t_qn, t_km], writes=[t_psg])
                    S.add("pool", lambda e: e.memset(gsb[:], NEG), writes=[t_gsb])
                    S.add("dve", lambda e, psg=psg, J=J: e.tensor_copy(out=gsb[:, 0:J], in_=psg[:, 0:J]), reads=[t_psg], writes=[t_gsb])
                    S.add("dve", lambda e: e.max(out=top8[:], in_=gsb[:]), reads=[t_gsb], writes=[t_top8])
                    S.add("dve", lambda e: e.tensor_scalar(out=msel[:], in0=gsb[:], scalar1=top8[:, 2:3], scalar2=None, op0=ALU.is_ge),
                          reads=[t_gsb, t_top8], writes=[t_msel])
                blocks = [J] + list(range(J))
                for bi, n in enumerate(blocks):
                    if n == J:
                        halves = [(0, "A")] if eo == 0 else [(0, "B"), (1, "A")]
                    elif n == J - 1:
                        halves = [(0, "c"), (1, "B")] if eo == 0 else [(0, "c"), (1, "c")]
                    else:
                        halves = [(0, "c"), (1, "c")]
                    pss2, t_pss2 = C.psum()
                    pt, t_pt = pt2[nptL[bs] % 2]
                    tmpS, t_tmpS = tmpS2[nptL[bs] % 2]
                    nptL[bs] += 1

                    def sfn(e, pss2=pss2, halves=halves, n=n, qc=qc):
                        ins = None
                        for hh, _k in halves:
                            ins = e.matmul(pss2[:, hh * 128:(hh + 1) * 128], KT[:, n * 256 + hh * 128:n * 256 + (hh + 1) * 128], qnb[:, qc], start=True, stop=True)
                        return ins
                    S.add("pe", sfn, reads=[t_KT, t_qnb], writes=[t_pss2])
                    if STOP == 10:
                        continue
                    for hh, kind in halves:
                        if (STOP == 14 and kind != "c") or (STOP == 15 and kind == "c"):
                            continue
                        if kind == "c":
                            S.add("act", lambda e, pss2=pss2, hh=hh, pt=pt: e.activation(
                                out=pt[:, hh, :], in_=pss2[:, hh * 128:(hh + 1) * 128], func=AF.Exp, bias=spt[:, 2:3], scale=SCALE),
                                reads=[t_pss2, t_sp], writes=[t_pt])
                        else:
                            bt_ = TA if kind == "A" else TB
                            S.add("dve", lambda e, pss2=pss2, hh=hh, bt_=bt_, tmpS=tmpS: e.scalar_tensor_tensor(
                                out=tmpS[:, hh * 128:(hh + 1) * 128], in0=pss2[:, hh * 128:(hh + 1) * 128], scalar=SCALE, in1=bt_, op0=ALU.mult, op1=ALU.add),
                                reads=[t_pss2, t_cst], writes=[t_tmpS])
                            S.add("act", lambda e, hh=hh, pt=pt, tmpS=tmpS: e.activation(out=pt[:, hh, :], in_=tmpS[:, hh * 128:(hh + 1) * 128], func=AF.Exp),
                                  reads=[t_tmpS], writes=[t_pt])
                    if STOP in (11, 14, 15):
                        continue
                    pso, t_pso = C.psum()
                    nh = len(halves)

                    def ofn(e, pso=pso, halves=halves, n=n, nh=nh, pt=pt):
                        ins = None
                        for ii, (hh, _k) in enumerate(halves):
                            ins = e.matmul(pso[:, 0:130], pt[:, hh, :], vaug[:, 2 * n + hh, :], start=(ii == 0), stop=(ii == nh - 1))
                        return ins
                    S.add("pe", ofn, reads=[t_pt, t_v], writes=[t_pso])
                    if STOP == 12:
                        continue
                    if bi == 0:
                        S.add("act", lambda e, pso=pso: e.activation(out=acc[:], in_=pso[:, 0:130], func=AF.Identity), reads=[t_pso], writes=[t_acc])
                    else:
                        sc_ = msel[:, n:n + 1] if use_sel else 1.0
                        S.add("dve", lambda e, pso=pso, sc_=sc_: e.scalar_tensor_tensor(
                            out=acc[:], in0=pso[:, 0:130], scalar=sc_, in1=acc[:], op0=ALU.mult, op1=ALU.add),
                            reads=[t_pso, t_msel, t_acc], writes=[t_acc])
                if STOP in (10, 11, 12, 13, 14, 15):
                    continue
                S.add("dve", lambda e: e.reciprocal(out=rec[:], in_=acc[:, 128:129]), reads=[t_acc], writes=[t_rec])
                yq_, t_yq = yq[nyqL[bs] % 2]
                nyqL[bs] += 1
                S.add("dve", lambda e, yq_=yq_: e.tensor_scalar(out=yq_[:], in0=acc[:, 0:128], scalar1=rec[:, 0:1], scalar2=None, op0=ALU.mult),
                      reads=[t_acc, t_rec], writes=[t_yq])
                S.add("sp", lambda e, yq_=yq_, qi=qi: e.dma_start(out=ya_ap[qi * 128:(qi + 1) * 128, :], in_=yq_[:]), reads=[t_yq], dma=True)
        secs = []
        if 'ssd' in PARTS:
            secs.append(sec_ssd)
        if 'attn' in PARTS:
            for qb in range(4):
                secs.append(lambda qb=qb: sec_attn((qb,), qb))
        interleave(C, secs, banks=[[0, 1, 2, 3], [4], [5], [6], [7]] if len(secs) == 5 else [[0, 1], [2, 3], [4, 5], [6, 7]])
    return C.finish()


def t5_bucket_np(dist):
    n = np.maximum(dist, 0)
    max_exact = 16
    ratio = np.log(np.maximum(n, 1).astype(np.float32) / max_exact) / np.float32(np.log(128 / max_exact))
    large = max_exact + (ratio * (32 - max_exact)).astype(np.int32)
    large = np.minimum(large, 31)
    return np.where(n < max_exact, n, large)


def mix_consts(rel_bias, head):
    kk = np.arange(128)[:, None]
    qq = np.arange(128)[None, :]
    cst = np.zeros((128, 5 * 128 + 1024), np.float32)
    cst[:, 0:128] = np.eye(128, dtype=np.float32)
    cst[:, 128:256] = 1.0
    cst[:, 256:384] = rel_bias[t5_bucket_np(qq - kk), head]
    cst[:, 384:512] = rel_bias[t5_bucket_np(128 + qq - kk), head]
    cst[:, 512:640] = np.where(qq >= kk, 0.0, NEG)
    utri = (kk <= qq).astype(np.float32)
    cst[:, 640:768] = utri
    cst[:, 768:896] = 1.0
    cst[:, 896:1024] = 0.0
    cst[:, 1024:1152] = utri
    for h in range(4):
        cst[h, 1152 + h * 128:1152 + (h + 1) * 128] = 1.0
    return cst


def run_mixB(hT, l, inp, NT=16):
    w = inp["w_mix_in"][l]
    in_maps = []
    for c in range(NCORES):
        g = c // 2
        cols = np.concatenate([
            np.arange(c * 128, (c + 1) * 128), 1024 + np.arange(c * 128, (c + 1) * 128),
            3072 + np.arange(c * 256, (c + 1) * 256), 5120 + np.arange(c * 256, (c + 1) * 256),
            5120 + 2048 + np.arange(g * 128, (g + 1) * 128), 5120 + 2560 + np.arange(g * 128, (g + 1) * 128),
            8224 + np.arange(c * 128, (c + 1) * 128), 9248 + np.arange(c * 128, (c + 1) * 128),
            10272 + np.arange(c * 128, (c + 1) * 128), 2048 + np.arange(c * 128, (c + 1) * 128),
            8192 + np.arange(c * 4, (c + 1) * 4)])
        wB = np.ascontiguousarray(w[:, cols].reshape(KC, 128, NCOLB).transpose(1, 0, 2))
        sp = np.zeros((128, 24), np.float32)
        sp[:, 0] = inp["qk_norm"][l, 0]
        sp[:, 1] = inp["qk_norm"][l, 1]
        sp[:, 2] = inp["rel_bias"][31, c]
        wc = inp["w_ssd_conv"][l]
        bc = inp["b_ssd_conv"][l]
        chB = 2048 + g * 128 + np.arange(128)
        chC = 2560 + g * 128 + np.arange(128)
        sp[:, 3:7] = wc[:, chB].T
        sp[:, 7] = bc[chB]
        sp[:, 8:12] = wc[:, chC].T
        sp[:, 12] = bc[chC]
        sp[:, 13:16] = inp["w_sc_conv"][l][:, c * 128:(c + 1) * 128].T
        sp[:, 16:20] = inp["ssd_dt_bias"][l][None, 4 * c:4 * c + 4]
        sp[:, 20:24] = inp["ssd_a_log"][l][None, 4 * c:4 * c + 4]
        sp64 = np.zeros((64, 24), np.float32)
        for h in range(4):
            ch = c * 256 + h * 64 + np.arange(64)
            sp64[:, h * 6:h * 6 + 4] = wc[:, ch].T
            sp64[:, h * 6 + 4] = bc[ch]
            sp64[:, h * 6 + 5] = inp["ssd_d"][l][4 * c + h]
        in_maps.append({"hT": hT, "wB": wB, "sp": sp, "sp64": sp64, "cst": mix_consts(inp["rel_bias"], c)})
    r = run(prog("mixB", build_mixB, NT), in_maps)
    ntok = NT * 512
    YT = np.zeros((4096, ntok), np.float32)
    for c in range(NCORES):
        YT[c * 128:(c + 1) * 128] = r[c]["ya"].T
        YT[1024 + c * 256:1024 + (c + 1) * 256] = r[c]["yg"]
        YT[3072 + c * 128:3072 + (c + 1) * 128] = r[c]["yc"]
    return YT


NYC = 32
BR_CH = ((0, 8), (8, 24), (24, 32))


def build_mixC():
    C = Ctx()
    S = C.S
    x_ap = C.din("xT", [D, TOK])
    h_ap = C.din("hT", [D, TOK])
    y_ap = C.din("YT", [4096, TOK])
    wg_ap = C.din("wg", [KC, 128, KC, 384])
    wbr_ap = C.din("wbr", [KC, 128, NYC, 128])
    wo_ap = C.din("wo", [KC, 128, KC, 128])
    par_ap = C.din("par", [128, 2 * KC])
    ones_ap = C.din("ones", [128, 128])
    xo_ap = C.dout("xo", [D, TOK])
    C.init_psum()
    ones, t_ones = load_consts(C, ones_ap)
    par, t_par = C.sb([128, 2 * KC], F32, "par")
    S.add("sp", lambda e: e.dma_start(out=par[:], in_=par_ap), writes=[t_par], dma=True)
    xT, _ = C.sb([128, KC, 512], F32, "xT")
    t_x = [T("x%d" % k) for k in range(KC)]
    hb, _ = C.sb([128, KC, 512], BF16, "hb")
    t_h = [T("h%d" % k) for k in range(KC)]
    Yb, _ = C.sb([128, NYC, 512], BF16, "Yb")
    t_y = [T("y%d" % k) for k in range(NYC)]
    stg = [C.sb([128, 512], F32, "stg") for _ in range(4)]
    sq = [C.sb([128, 512], F32, "sq") for _ in range(2)]
    rs, t_rs = C.sb([128, 512], F32, "rs")
    tmpn = [C.sb([128, 512], F32, "tmpn") for _ in range(2)]
    mg, t_mg = C.sb([128, 512], F32, "mg")
    sig = [C.sb([128, 512], F32, "sig") for _ in range(2)]
    tmpm, t_tmpm = C.sb([128, 512], F32, "tmpm")
    mT, _ = C.sb([128, KC, 512], BF16, "mT")
    t_m = [T("m%d" % k) for k in range(KC)]
    wg = [C.sb([128, KC, 384], BF16, "wg") for _ in range(2)]
    wbr = [C.sb([128, NYC, 128], BF16, "wbr") for _ in range(2)]
    wo = [C.sb([128, KC, 128], BF16, "wo") for _ in range(2)]
    nsig = 0
    nw = 0
    nwo = 0
    for tt in range(TOK // 512):
        sl = slice(tt * 512, (tt + 1) * 512)
        for kc in range(KC):
            S.add("sp", lambda e, kc=kc, sl=sl: e.dma_start(out=xT[:, kc, :], in_=x_ap[kc * 128:(kc + 1) * 128, sl]),
                  writes=[t_x[kc]], dma=True)
            S.add("pool", lambda e, kc=kc, sl=sl: e.dma_start(out=hb[:, kc, :], in_=h_ap[kc * 128:(kc + 1) * 128, sl]),
                  writes=[t_h[kc]], dma=True)
        for ch in list(range(0, 8)) + list(range(24, 32)):
            S.add("pool", lambda e, ch=ch, sl=sl: e.dma_start(out=Yb[:, ch, :], in_=y_ap[ch * 128:(ch + 1) * 128, sl]),
                  writes=[t_y[ch]], dma=True)
        for gq in range(4):
            ps, t_ps = C.psum()
            for c4 in range(4):
                ch = 8 + gq * 4 + c4
                st_, t_st = stg[c4]
                S.add("sp", lambda e, st_=st_, ch=ch, sl=sl: e.dma_start(out=st_[:], in_=y_ap[ch * 128:(ch + 1) * 128, sl]),
                      writes=[t_st], dma=True)
                sq_, t_sq = sq[c4 % 2]
                S.add("act", lambda e, sq_=sq_, st_=st_: e.activation(out=sq_[:], in_=st_[:], func=AF.Square), reads=[t_st], writes=[t_sq])
                S.add("pe", lambda e, ps=ps, sq_=sq_, c4=c4: e.matmul(ps[:, :], ones[:], sq_[:], start=(c4 == 0), stop=(c4 == 3)),
                      reads=[t_sq, t_ones], writes=[t_ps])
            S.add("act", lambda e, ps=ps: e.activation(out=rs[:], in_=ps[:, :], func=AF.Sqrt, scale=1.0 / 512, bias=EPS),
                  reads=[t_ps], writes=[t_rs])
            S.add("dve", lambda e: e.reciprocal(out=rs[:], in_=rs[:]), reads=[t_rs], writes=[t_rs])
            for c4 in range(4):
                ch = 8 + gq * 4 + c4
                st_, t_st = stg[c4]
                S.add("dve", lambda e, st_=st_, ch=ch: e.scalar_tensor_tensor(
                    out=Yb[:, ch, :], in0=st_[:], scalar=par[:, ch - 8:ch - 7], in1=rs[:], op0=ALU.mult, op1=ALU.mult),
                    reads=[t_st, t_rs, t_par], writes=[t_y[ch]])
        for m in range(KC):
            wg_, t_wg = wg[nw % 2]
            wbr_, t_wbr = wbr[nw % 2]
            nw += 1
            S.add("pool", lambda e, wg_=wg_, m=m: e.dma_start(out=wg_[:], in_=wg_ap[m]), writes=[t_wg], dma=True)
            S.add("pool", lambda e, wbr_=wbr_, m=m: e.dma_start(out=wbr_[:], in_=wbr_ap[m]), writes=[t_wbr], dma=True)
            for b in range(3):
                pg, t_pg = C.psum()
                mm_group(S, pg[:, :], [(wg_[:, kc, b * 128:(b + 1) * 128], hb[:, kc, :]) for kc in range(KC)], t_h + [t_wg], [t_pg])
                sg_, t_sg = sig[nsig % 2]
                nsig += 1
                S.add("act", lambda e, sg_=sg_, pg=pg: e.activation(out=sg_[:], in_=pg[:, :], func=AF.Sigmoid), reads=[t_pg], writes=[t_sg])
                pb, t_pb = C.psum()
                c0, c1 = BR_CH[b]
                mm_group(S, pb[:, :], [(wbr_[:, ch, :], Yb[:, ch, :]) for ch in range(c0, c1)], t_y[c0:c1] + [t_wbr], [t_pb])
                if b == 0:
                    S.add("dve", lambda e, sg_=sg_, pb=pb: e.tensor_tensor(out=mg[:], in0=sg_[:], in1=pb[:, :], op=ALU.mult),
                          reads=[t_sg, t_pb], writes=[t_mg])
                else:
                    S.add("dve", lambda e, sg_=sg_, pb=pb: e.tensor_tensor(out=tmpm[:], in0=sg_[:], in1=pb[:, :], op=ALU.mult),
                          reads=[t_sg, t_pb], writes=[t_tmpm])
                    if b == 1:
                        S.add("dve", lambda e: e.tensor_tensor(out=mg[:], in0=mg[:], in1=tmpm[:], op=ALU.add),
                              reads=[t_mg, t_tmpm], writes=[t_mg])
                    else:
                        S.add("dve", lambda e, m=m: e.tensor_tensor(out=mT[:, m, :], in0=mg[:], in1=tmpm[:], op=ALU.add),
                              reads=[t_mg, t_tmpm], writes=[t_m[m]])
        for m in range(KC):
            wo_, t_wo = wo[nwo % 2]
            nwo += 1
            S.add("pool", lambda e, wo_=wo_, m=m: e.dma_start(out=wo_[:], in_=wo_ap[m]), writes=[t_wo], dma=True)
            po, t_po = C.psum()
            mm_group(S, po[:, :], [(wo_[:, kc, :], mT[:, kc, :]) for kc in range(KC)], t_m + [t_wo], [t_po])
            S.add("dve", lambda e, po=po, m=m: e.scalar_tensor_tensor(
                out=xT[:, m, :], in0=po[:, :], scalar=par[:, KC + m:KC + m + 1], in1=xT[:, m, :], op0=ALU.mult, op1=ALU.add),
                reads=[t_po, t_par, t_x[m]], writes=[t_x[m]])
            S.add("sp", lambda e, m=m, sl=sl: e.dma_start(out=xo_ap[m * 128:(m + 1) * 128, sl], in_=xT[:, m, :]),
                  reads=[t_x[m]], dma=True)
    return C.finish()


def run_mixC(xT, hT, YT, l, inp, gate1):
    w = inp["w_mix_in"][l]
    G0 = 11296
    wg = np.stack([w[:, G0 + b * 2048:G0 + (b + 1) * 2048].reshape(KC, 128, KC, 128) for b in range(3)], axis=0)
    wg_t = np.ascontiguousarray(wg.transpose(3, 2, 1, 0, 4).reshape(KC, 128, KC, 384))
    wbr = np.concatenate([inp["w_br_attn"][l], inp["w_br_ssd"][l], inp["w_br_conv"][l]], axis=0)
    wbr_t = np.ascontiguousarray(wbr.reshape(NYC, 128, KC, 128).transpose(2, 1, 0, 3))
    wo_t = np.ascontiguousarray(inp["w_mix_out"][l].reshape(KC, 128, KC, 128).transpose(2, 1, 0, 3))
    par = np.concatenate([cols16(inp["ssd_norm"][l]), cols16(gate1)], axis=1)
    ones = np.ones((128, 128), np.float32)
    in_maps = []
    for i in range(NCORES):
        ts = slice(i * TOK, (i + 1) * TOK)
        in_maps.append({"xT": np.ascontiguousarray(xT[:, ts]), "hT": np.ascontiguousarray(hT[:, ts]),
                        "YT": np.ascontiguousarray(YT[:, ts]), "wg": wg_t, "wbr": wbr_t, "wo": wo_t, "par": par, "ones": ones})
    r = run(prog("mixC", build_mixC), in_maps)
    return np.concatenate([r[i]["xo"] for i in range(NCORES)], axis=1)


def kernel(x, c, w_ada, b_ada, norm_gain, w_ffn_in, w_ffn_out, w_mix_in, qk_norm, rel_bias,
           w_ssd_conv, b_ssd_conv, ssd_dt_bias, ssd_a_log, ssd_d, ssd_norm, w_sc_conv,
           w_br_attn, w_br_ssd, w_br_conv, w_mix_out):
    inp = dict(w_mix_in=w_mix_in, qk_norm=qk_norm, rel_bias=rel_bias, w_ssd_conv=w_ssd_conv, b_ssd_conv=b_ssd_conv,
               ssd_dt_bias=ssd_dt_bias, ssd_a_log=ssd_a_log, ssd_d=ssd_d, ssd_norm=ssd_norm, w_sc_conv=w_sc_conv,
               w_br_attn=w_br_attn, w_br_ssd=w_br_ssd, w_br_conv=w_br_conv, w_mix_out=w_mix_out)
    inp = {k: np.asarray(v, np.float32) for k, v in inp.items()}
    w_ada = np.asarray(w_ada, np.float32)
    b_ada = np.asarray(b_ada, np.float32)
    norm_gain = np.asarray(norm_gain, np.float32)
    w_ffn_in = np.asarray(w_ffn_in, np.float32)
    w_ffn_out = np.asarray(w_ffn_out, np.float32)
    ada = compute_ada(np.asarray(c, np.float32), w_ada, b_ada)
    xT = np.ascontiguousarray(np.asarray(x, np.float32)[0].T)
    for l in range(2):
        par = np.concatenate([par_block(norm_gain[l, 0], ada[l, 0]), par_block(norm_gain[l, 1], ada[l, 1])], axis=1)
        xT, hT = run_ffn(xT, par, w_ffn_in[l, 0], w_ffn_out[l, 0], True)
        YT = run_mixB(hT, l, inp, NT=16)
        xT = run_mixC(xT, hT, YT, l, inp, ada[l, 1, 2])
        par = par_block(norm_gain[l, 2], ada[l, 2])
        xT, _ = run_ffn(xT, par, w_ffn_in[l, 1], w_ffn_out[l, 1], False)
    return np.ascontiguousarray(xT.T)[None].astype(np.float32)
```
